# Optimizing a Trainium2 kernel written in Bass

```python
import math
import jax
import jax.numpy as jnp
from jax import lax
import numpy as np

D_MODEL = 1024
BATCH = 4
SEQ = 4096
DEPTH = 2

GRID_W = 64
CTX_LEN = 256
EPS = 1e-6
F32 = jnp.float32

SSD_HEADS = 12
SSD_HEAD_DIM = 64
SSD_WIDTH = SSD_HEADS * SSD_HEAD_DIM
SSD_GROUPS = 2
SSD_REP = SSD_HEADS // SSD_GROUPS
SSD_STATE = 128
SSD_BC = SSD_GROUPS * SSD_STATE
SSD_XBC = SSD_WIDTH + 2 * SSD_BC
SSD_CONV_LEN = 5
SSD_CHUNK = 128
ATTN_HEADS = 12
ATTN_KV_HEADS = 4
ATTN_REP = ATTN_HEADS // ATTN_KV_HEADS
HEAD_DIM = 64
ATTN_WIDTH = ATTN_HEADS * HEAD_DIM
KV_WIDTH = ATTN_KV_HEADS * HEAD_DIM
Q_BLOCK = 128
ROPE_BASE = 10000.0
ROPE_NF = HEAD_DIM // 4
CM_WIDTH = 512
CM_CONV_LEN = 31

MIX_WIDTH = SSD_WIDTH + ATTN_WIDTH + CM_WIDTH
IN_SIZES = (SSD_WIDTH, SSD_XBC, 2 * SSD_HEADS, ATTN_WIDTH, KV_WIDTH, KV_WIDTH, ATTN_WIDTH, CM_WIDTH, CM_WIDTH, CM_WIDTH)
IN_WIDTH = sum(IN_SIZES)
IN_OFFSETS = tuple(sum(IN_SIZES[:i + 1]) for i in range(len(IN_SIZES) - 1))

kernel_name = 'hybrid_ssd_gqa_conformer_prefix_dit'


def rmsnorm(x, g):
    xf = x.astype(F32)
    y = xf * lax.rsqrt(jnp.mean(xf * xf, axis=-1, keepdims=True) + EPS)
    return (y * g.astype(F32)).astype(x.dtype)


def layernorm(x, g, b):
    xf = x.astype(F32)
    xc = xf - jnp.mean(xf, axis=-1, keepdims=True)
    y = xc * lax.rsqrt(jnp.mean(xc * xc, axis=-1, keepdims=True) + EPS)
    return (y * g.astype(F32) + b.astype(F32)).astype(x.dtype)


def modulate(x, g, shift, scale):
    return rmsnorm(x, g) * (1 + scale) + shift


def dwconv(x, w, b):
    pad = w.shape[0] // 2
    y = lax.conv_general_dilated(x, w[:, None, :].astype(x.dtype), window_strides=(1,), padding=[(pad, pad)], dimension_numbers=('NWC', 'WIO', 'NWC'), feature_group_count=x.shape[-1])
    return y + b.astype(x.dtype)


def split_cols(p):
    return jnp.split(p, IN_OFFSETS, axis=-1)


def axial_rope_angles(seq):
    n_rows = seq // GRID_W
    rows = jnp.repeat(jnp.arange(n_rows, dtype=F32), GRID_W)
    cols = jnp.tile(jnp.arange(GRID_W, dtype=F32), n_rows)
    inv = ROPE_BASE ** (-jnp.arange(ROPE_NF, dtype=F32) / ROPE_NF)
    ang = jnp.stack([rows[:, None] * inv, cols[:, None] * inv], axis=1)
    return jnp.cos(ang), jnp.sin(ang)


def apply_axial_rope(x, cos, sin):
    b, s, h, d = x.shape
    xf = x.astype(F32).reshape(b, s, h, 2, 2, ROPE_NF)
    x1, x2 = xf[..., 0, :], xf[..., 1, :]
    cs, sn = cos[None, :, None], sin[None, :, None]
    out = jnp.stack([x1 * cs - x2 * sn, x2 * cs + x1 * sn], axis=-2)
    return out.reshape(b, s, h, d).astype(x.dtype)


def attend(q, keys, vals):
    b, L = q.shape[:2]
    nb = L // Q_BLOCK
    qb = q.reshape(b, nb, Q_BLOCK, ATTN_KV_HEADS, ATTN_REP, HEAD_DIM).transpose(1, 0, 2, 3, 4, 5)
    scale = HEAD_DIM ** -0.5

    def one_block(qi):
        s = jnp.einsum('bqkrd,bskd->bkrqs', qi, keys, preferred_element_type=F32) * scale
        p = jax.nn.softmax(s, axis=-1).astype(vals.dtype)
        return jnp.einsum('bkrqs,bskd->bqkrd', p, vals)

    out = lax.map(one_block, qb)
    return out.transpose(1, 0, 2, 3, 4, 5).reshape(b, L, ATTN_WIDTH)


def ssd_chunk_scan(x, dt, A, Bm, Cm, h0, return_y):
    b, L = x.shape[:2]
    nc, l = L // SSD_CHUNK, SSD_CHUNK
    G, R, P, N = SSD_GROUPS, SSD_REP, SSD_HEAD_DIM, SSD_STATE
    xdt = (x.astype(F32) * dt[..., None]).reshape(b, nc, l, G, R, P)
    Bc = Bm.astype(F32).reshape(b, nc, l, G, N)
    Cc = Cm.astype(F32).reshape(b, nc, l, G, N)
    cum = jnp.cumsum((dt * A).reshape(b, nc, l, G, R), axis=2)
    total = cum[:, :, -1]
    states = jnp.einsum('bclgn,bclgr,bclgrp->bcgrpn', Bc, jnp.exp(total[:, :, None] - cum), xdt)

    def step(h, inp):
        st, tot = inp
        return h * jnp.exp(tot)[..., None, None] + st, h

    h_final, h_in = lax.scan(step, h0, (jnp.moveaxis(states, 1, 0), jnp.moveaxis(total, 1, 0)))
    if not return_y:
        return None, h_final
    h_in = jnp.moveaxis(h_in, 0, 1)
    seg = cum[:, :, :, None] - cum[:, :, None, :]
    mask = jnp.tril(jnp.ones((l, l), dtype=bool))[None, None, :, :, None, None]
    lmat = jnp.exp(jnp.where(mask, seg, -jnp.inf))
    cb = jnp.einsum('bclgn,bcsgn->bclsg', Cc, Bc)
    y_diag = jnp.einsum('bclsg,bclsgr,bcsgrp->bclgrp', cb, lmat, xdt)
    y_off = jnp.einsum('bclgn,bcgrpn,bclgr->bclgrp', Cc, h_in, jnp.exp(cum))
    return (y_diag + y_off).reshape(b, L, G * R, P), h_final


def ssd_branch(z, xbc, dt_raw, h0_f, h0_b, conv_w, conv_b, dt_bias, a_log, d_skip, norm_g, return_y):
    b, L, _ = xbc.shape
    xbc = jax.nn.silu(dwconv(xbc, conv_w, conv_b))
    xs = xbc[..., :SSD_WIDTH].reshape(b, L, SSD_HEADS, SSD_HEAD_DIM)
    Bm = xbc[..., SSD_WIDTH:SSD_WIDTH + SSD_BC].reshape(b, L, SSD_GROUPS, SSD_STATE)
    Cm = xbc[..., SSD_WIDTH + SSD_BC:].reshape(b, L, SSD_GROUPS, SSD_STATE)
    dt = jax.nn.softplus(dt_raw.astype(F32) + dt_bias.astype(F32))
    A = -jnp.exp(a_log.astype(F32))
    H = SSD_HEADS
    flip = lambda t: jnp.flip(t, axis=1)
    y_f, h_f = ssd_chunk_scan(xs, dt[..., :H], A[:H], Bm, Cm, h0_f, return_y)
    y_b, h_b = ssd_chunk_scan(flip(xs), flip(dt[..., H:]), A[H:], flip(Bm), flip(Cm), h0_b, return_y)
    if not return_y:
        return None, h_f, h_b
    y = y_f + flip(y_b) + d_skip.astype(F32)[:, None] * xs.astype(F32)
    y = y.reshape(b, L, SSD_WIDTH) * jax.nn.silu(z.astype(F32))
    return rmsnorm(y, norm_g).astype(z.dtype), h_f, h_b


def conformer_branch(ua, ub, gate, conv_w, conv_b, ln_g, ln_b, pw_w, pw_b):
    u = ua * jax.nn.sigmoid(ub)
    u = jax.nn.silu(layernorm(dwconv(u, conv_w, conv_b), ln_g, ln_b))
    u = u @ pw_w + pw_b
    return u * jax.nn.silu(gate)


def setup_inputs(seed: int = 0) -> dict:
    key = jax.random.key(seed)
    ks = jax.random.split(key, 24)
    L, D = DEPTH, D_MODEL
    nrm = lambda k, shape, s: jax.random.normal(k, shape, F32) * s
    gain = lambda k, shape: 1.0 + 0.02 * jax.random.normal(k, shape, F32)
    dt0 = jnp.exp(jax.random.uniform(ks[10], (L, 2 * SSD_HEADS), F32, math.log(1e-3), math.log(1e-1)))
    return {
        'x': nrm(ks[0], (BATCH, SEQ, D), 1.0),
        'c': nrm(ks[1], (BATCH, D), 1.0),
        'ctx': nrm(ks[2], (BATCH, CTX_LEN, D), 1.0),
        'c_ctx': nrm(ks[3], (D,), 1.0),
        'norm_g': gain(ks[4], (L, D)),
        'ada_w': nrm(ks[5], (L, D, 3 * D), 0.5 * D ** -0.5),
        'ada_b': nrm(ks[6], (L, 3 * D), 0.02),
        'w_in': nrm(ks[7], (L, D, IN_WIDTH), D ** -0.5),
        'ssd_conv_w': nrm(ks[8], (L, SSD_CONV_LEN, SSD_XBC), SSD_CONV_LEN ** -0.5),
        'ssd_conv_b': nrm(ks[9], (L, SSD_XBC), 0.02),
        'ssd_dt_bias': dt0 + jnp.log(-jnp.expm1(-dt0)),
        'ssd_a_log': jnp.log(jax.random.uniform(ks[11], (L, 2 * SSD_HEADS), F32, 1.0, 16.0)),
        'ssd_d': gain(ks[12], (L, SSD_HEADS)),
        'ssd_norm_g': gain(ks[13], (L, SSD_WIDTH)),
        'q_norm_g': gain(ks[14], (L, HEAD_DIM)),
        'k_norm_g': gain(ks[15], (L, HEAD_DIM)),
        'cm_conv_w': nrm(ks[16], (L, CM_CONV_LEN, CM_WIDTH), CM_CONV_LEN ** -0.5),
        'cm_conv_b': nrm(ks[17], (L, CM_WIDTH), 0.02),
        'cm_ln_g': gain(ks[18], (L, CM_WIDTH)),
        'cm_ln_b': nrm(ks[19], (L, CM_WIDTH), 0.02),
        'cm_pw_w': nrm(ks[20], (L, CM_WIDTH, CM_WIDTH), CM_WIDTH ** -0.5),
        'cm_pw_b': nrm(ks[21], (L, CM_WIDTH), 0.02),
        'w_out': nrm(ks[22], (L, MIX_WIDTH, D), MIX_WIDTH ** -0.5),
        'final_norm_g': gain(ks[23], (D,)),
    }


def reference(x, c, ctx, c_ctx, norm_g, ada_w, ada_b, w_in, ssd_conv_w, ssd_conv_b, ssd_dt_bias, ssd_a_log, ssd_d, ssd_norm_g, q_norm_g, k_norm_g, cm_conv_w, cm_conv_b, cm_ln_g, cm_ln_b, cm_pw_w, cm_pw_b, w_out, final_norm_g):
    b, seq, _ = x.shape
    n_ctx = ctx.shape[1]
    cos, sin = axial_rope_angles(seq)
    silu_c, silu_cc = jax.nn.silu(c), jax.nn.silu(c_ctx)
    h0 = jnp.zeros((b, SSD_GROUPS, SSD_REP, SSD_HEAD_DIM, SSD_STATE), F32)
    for i in range(DEPTH):
        last = i == DEPTH - 1
        shift, scale, gate = jnp.split(silu_c @ ada_w[i] + ada_b[i], 3, axis=-1)
        shift_c, scale_c, gate_c = jnp.split(silu_cc @ ada_w[i] + ada_b[i], 3, axis=-1)
        h = modulate(x, norm_g[i], shift[:, None], scale[:, None])
        h_c = modulate(ctx, norm_g[i], shift_c, scale_c)
        z, xbc, dtr, q, k, v, ga, ua, ub, gc = split_cols(h @ w_in[i])
        z_c, xbc_c, dtr_c, q_c, k_c, v_c, ga_c, ua_c, ub_c, gc_c = split_cols(h_c @ w_in[i])
        ssd_p = (ssd_conv_w[i], ssd_conv_b[i], ssd_dt_bias[i], ssd_a_log[i], ssd_d[i], ssd_norm_g[i])
        cm_p = (cm_conv_w[i], cm_conv_b[i], cm_ln_g[i], cm_ln_b[i], cm_pw_w[i], cm_pw_b[i])
        ssd_ctx, hf, hb = ssd_branch(z_c, xbc_c, dtr_c, h0, h0, *ssd_p, return_y=not last)
        ssd_lat, _, _ = ssd_branch(z, xbc, dtr, hf, hb, *ssd_p, return_y=True)
        kc = rmsnorm(k_c.reshape(b, n_ctx, ATTN_KV_HEADS, HEAD_DIM), k_norm_g[i])
        vc = v_c.reshape(b, n_ctx, ATTN_KV_HEADS, HEAD_DIM)
        ql = apply_axial_rope(rmsnorm(q.reshape(b, seq, ATTN_HEADS, HEAD_DIM), q_norm_g[i]), cos, sin)
        kl = apply_axial_rope(rmsnorm(k.reshape(b, seq, ATTN_KV_HEADS, HEAD_DIM), k_norm_g[i]), cos, sin)
        vl = v.reshape(b, seq, ATTN_KV_HEADS, HEAD_DIM)
        attn_lat = attend(ql, jnp.concatenate([kl, kc], axis=1), jnp.concatenate([vl, vc], axis=1)) * jax.nn.silu(ga)
        conv_lat = conformer_branch(ua, ub, gc, *cm_p)
        y = jnp.concatenate([ssd_lat, attn_lat, conv_lat], axis=-1) @ w_out[i]
        x = x + gate[:, None] * y
        if not last:
            qc = rmsnorm(q_c.reshape(b, n_ctx, ATTN_HEADS, HEAD_DIM), q_norm_g[i])
            attn_ctx = attend(qc, kc, vc) * jax.nn.silu(ga_c)
            conv_ctx = conformer_branch(ua_c, ub_c, gc_c, *cm_p)
            y_c = jnp.concatenate([ssd_ctx, attn_ctx, conv_ctx], axis=-1) @ w_out[i]
            ctx = ctx + gate_c * y_c
    return rmsnorm(x, final_norm_g)
```

```python
import numpy as np
import ml_dtypes
import concourse.bass as bass
import concourse.mybir as mybir
from concourse.bass_utils import run_bass_kernel_spmd

F32 = mybir.dt.float32
BF16 = mybir.dt.bfloat16
AF = mybir.ActivationFunctionType
ALU = mybir.AluOpType
AX = mybir.AxisListType

D = 1024
SEQ = 4096
NCTX = 256
T = SEQ + NCTX
DEPTH = 2
INW = 5656
EPS = 1e-6
O_Z, O_X, O_B, O_C, O_DT, O_Q, O_K, O_V, O_GA, O_UA, O_UB, O_GC = 0, 768, 1536, 1792, 2048, 2072, 2840, 3096, 3352, 4120, 4632, 5144
NEG = -30000.0


class Buf:
    __slots__ = ("name", "w", "r", "excl", "semi", "dcount")

    def __init__(self, name, excl=False):
        self.name = name
        self.w = {}
        self.r = {}
        self.excl = excl
        self.semi = None
        self.dcount = 0


class Op:
    __slots__ = ("eng", "idx", "fn", "deps", "dma", "sig", "semi", "dval")


def _merge(dst, src):
    for k, v in src.items():
        if dst.get(k, -1) < v:
            dst[k] = v


class Rec:
    ENG = ("pe", "act", "dve", "pool", "sp")

    def __init__(self, nc):
        self.nc = nc
        self.ops = {e: [] for e in self.ENG}
        self.ndma_sems = 0
        self.free = []
        self.semcount = {}
        self.eobj = {"pe": nc.tensor, "act": nc.scalar, "dve": nc.vector, "pool": nc.gpsimd, "sp": nc.sync}

    def add(self, eng, fn, reads=(), writes=(), dma=None, part=False):
        if dma is True:
            dma = writes[0]
        op = Op()
        op.eng, op.fn, op.dma, op.sig = eng, fn, dma is not None, False
        op.idx = len(self.ops[eng])
        raw, oth = {}, {}
        for b in reads:
            _merge(raw, b.w)
            if b.excl:
                _merge(oth, b.r)
        for b in writes:
            _merge(oth, b.r)
            if not part or b.excl:
                _merge(oth, b.w)
        if dma is None:
            oth.pop(("c", eng), None)
            if eng == "pe":
                raw.pop(("c", eng), None)
        deps = raw
        _merge(deps, oth)
        op.deps = deps
        if dma is not None:
            b = dma
            if b.semi is None:
                if self.free:
                    b.semi, b.dcount = self.free.pop()
                    _merge(deps, {("d", b.semi): b.dcount})
                else:
                    b.semi = self.ndma_sems
                    self.ndma_sems += 1
            b.dcount += 16
            self.semcount[b.semi] = b.dcount
            op.semi, op.dval = b.semi, b.dcount
            my = {("d", b.semi): b.dcount}
        else:
            my = {("c", eng): op.idx}
        for b in reads:
            _merge(b.r, my)
        for b in writes:
            if part:
                _merge(b.w, my)
            else:
                b.w = dict(my)
                b.r = {}
        self.ops[eng].append(op)
        return op

    def release(self, bufs):
        for b in bufs:
            if b.semi is not None:
                self.free.append((b.semi, b.dcount))
                b.semi = None

    def barrier(self):
        deps = {("c", e): len(self.ops[e]) - 1 for e in self.ENG if self.ops[e]}
        for s_, c_ in self.semcount.items():
            deps[("d", s_)] = c_
        for e in self.ENG:
            op = Op()
            op.eng, op.fn, op.dma, op.sig, op.idx = e, None, False, False, len(self.ops[e])
            op.deps = {k: v for k, v in deps.items() if k != ("c", e)}
            self.ops[e].append(op)

    def emit(self):
        nc = self.nc
        for e in self.ENG:
            for op in self.ops[e]:
                nd = {}
                for k, v in op.deps.items():
                    if k[0] == "c":
                        lst = self.ops[k[1]]
                        while v >= 0 and lst[v].fn is None:
                            v -= 1
                        if v < 0:
                            continue
                        lst[v].sig = True
                    nd[k] = v
                op.deps = nd
        pref = {}
        for e in self.ENG:
            c = 0
            arr = []
            for op in self.ops[e]:
                if op.sig and not op.dma:
                    c += 1
                arr.append(c)
            pref[e] = arr
        assert self.ndma_sems + 5 <= 98, self.ndma_sems
        esem = {e: nc.alloc_semaphore(name="es_" + e) for e in self.ENG}
        dsem = [nc.alloc_semaphore(name="ds_%d" % i) for i in range(self.ndma_sems)]
        for e in self.ENG:
            eo = self.eobj[e]
            known = {}
            for op in self.ops[e]:
                for k, v in op.deps.items():
                    if k[0] == "c":
                        sem, val, key = esem[k[1]], pref[k[1]][v], k
                    else:
                        sem, val, key = dsem[k[1]], v, k
                    if known.get(key, 0) >= val:
                        continue
                    known[key] = val
                    eo.wait_ge(sem, val)
                if op.fn is None:
                    continue
                ins = op.fn()
                if op.dma:
                    ins.then_inc(dsem[op.semi], 16)
                elif op.sig:
                    ins.then_inc(esem[e], 1)


class SB:
    ARENA = None
    ABYTES = 204800

    def __init__(self, nc, base=0, limit=None):
        if SB.ARENA is None or SB.ARENA[0] is not nc:
            SB.ARENA = (nc, nc.alloc_sbuf_tensor("arena", [128, SB.ABYTES // 4], F32))
        self.nc, self.off, self.limit = nc, base, (limit or SB.ABYTES)

    def alloc(self, name, shape, dt):
        return self.at(name, shape, dt, None)

    def at(self, name, shape, dt, off):
        esz = 4 if dt == F32 else 2
        n = int(np.prod(shape[1:]))
        nb = (n * esz + 63) // 64 * 64
        if off is None:
            off = self.off
            self.off += nb
        assert off % 4 == 0 and off + nb <= self.limit, (name, off, nb, self.limit)
        A = SB.ARENA[1]
        ap = A[0:shape[0], off // 4: off // 4 + nb // 4]
        if dt != F32:
            ap = ap.bitcast(dt)
        ap = ap[:, 0:n]
        if len(shape) > 2:
            names = " ".join("d%d" % i for i in range(len(shape) - 1))
            kw = {"d%d" % i: int(shape[i + 1]) for i in range(len(shape) - 1)}
            ap = ap.rearrange("p (%s) -> p %s" % (names, names), **kw)
        return ap


def fap(ap, dims):
    return bass.AP(ap.tensor, ap.offset, [list(ap.ap[0])] + [list(d) for d in dims])


def build(nc, phases=99, nlayers=DEPTH, taps=()):
    R = Rec(nc)
    dram_in = {}

    def din(name, shape, dt=F32):
        dram_in[name] = nc.dram_tensor(name, list(shape), dt, kind="ExternalInput")
        return dram_in[name]

    x_in = din("x", [SEQ, D])
    ctx_in = din("ctx", [NCTX, D])
    cvec = din("cvec", [128, 8, 2])
    ada_w = din("ada_w", [DEPTH, D, 3 * D])
    ada_b_fm = din("ada_b_fm", [DEPTH, 128, 24])
    ada_b_gate = din("ada_b_gate", [DEPTH, D])
    norm_g_fm = din("norm_g_fm", [DEPTH, 128, 8])
    w_in = din("w_in", [DEPTH, D, INW])
    scw = din("ssd_conv_w_fm", [DEPTH, 128, 10, 5])
    scb = din("ssd_conv_b_fm", [DEPTH, 128, 10])
    dtb = din("ssd_dt_bias", [DEPTH, 24])
    alog = din("ssd_a_log", [DEPTH, 24])
    ssdd = din("ssd_d", [DEPTH, 12])
    ssdng = din("ssd_norm_g", [DEPTH, 768])
    qkg = din("qk_g_fm", [DEPTH, 128, 2])
    ccw = din("cm_conv_w_fm", [DEPTH, 128, 4, 31])
    cmv = din("cm_vec_fm", [DEPTH, 128, 4, 4])
    cpw = din("cm_pw_w", [DEPTH, 512, 512])
    w_out = din("w_out", [DEPTH, 2048, D])
    fng = din("final_norm_g", [D])
    consts = din("consts", [128, 6, 128])
    masks = din("masks", [128, 2, 128])
    rope = din("rope", [128, 2, T])
    out_d = nc.dram_tensor("out", [SEQ, D], F32, kind="ExternalOutput")

    def scratch(name, shape, dt):
        kind = "ExternalOutput" if name in taps else "Internal"
        return nc.dram_tensor(name, list(shape), dt, kind=kind)

    gate_d = scratch("gate_d", [DEPTH, 2, 128, D], F32)
    x1_d = scratch("x1_d", [T, D], F32)
    xbc_d = scratch("xbc_d", [1280, T], F32)
    dt_d = scratch("dt_d", [T, 24], F32)
    zs_d = scratch("zs_d", [T, 768], F32)
    qT_d = scratch("qT_d", [768, T], BF16)
    kT_d = scratch("kT_d", [256, T], BF16)
    v_d = scratch("v_d", [T, 256], BF16)
    gas_d = scratch("gas_d", [768, T], F32)
    cv_d = scratch("cv_d", [512, T], F32)
    gcs_d = scratch("gcs_d", [512, T], F32)
    cat_d = scratch("cat_d", [2048, T], BF16)
    hb_d = scratch("hb_d", [34, 128, 768], BF16)
    B_gate = Buf("gate_d"); B_x1 = Buf("x1_d"); B_xbc = Buf("xbc_d"); B_dt = Buf("dt_d"); B_zs = Buf("zs_d")
    B_qT = Buf("qT_d"); B_kT = Buf("kT_d"); B_v = Buf("v_d"); B_gas = Buf("gas_d"); B_cv = Buf("cv_d")
    B_gcs = Buf("gcs_d"); B_cat = Buf("cat_d"); B_hb = Buf("hb_d"); B_out = Buf("out")
    B_in = Buf("inputs")

    PS = [nc.alloc_psum_tensor("ps%d" % i, [128, 512], F32) for i in range(8)]
    PB = [Buf("psb%d" % i, excl=True) for i in range(8)]

    sbp = SB(nc, 0, 24 * 1024)
    c_t = sbp.alloc("consts", [128, 6, 128], F32)
    cb_t = sbp.alloc("constsb", [128, 6, 128], BF16)
    mk_t = sbp.alloc("masks", [128, 2, 128], F32)
    eps_t = sbp.alloc("eps", [128, 2], F32)
    AB_t = sbp.alloc("AB", [128, DEPTH, 2, 8, 2], F32)
    B_c = Buf("consts"); B_AB = Buf("AB")
    IDENT, UIN, UTR, PERM, BDM, ONES = range(6)
    R.add("sp", lambda: nc.sync.dma_start(out=c_t[:], in_=consts.ap()), [B_in], [B_c], dma=True)
    B_mk = Buf("mk")
    R.add("sp", lambda: nc.sync.dma_start(out=mk_t[:], in_=masks.ap()), [B_in], [B_mk], dma=True)
    B_cb = Buf("cb")
    R.add("dve", lambda: nc.vector.tensor_copy(out=cb_t[:], in_=c_t[:]), [B_c], [B_cb])
    B_eps = Buf("eps")
    R.add("pool", lambda: nc.gpsimd.memset(eps_t[:, 0:1], EPS), [], [B_eps])
    R.add("pool", lambda: nc.gpsimd.memset(eps_t[:, 1:2], 1.0), [], [B_eps], part=True)

    WORK0 = 24 * 1024

    def phase0():
        sb = SB(nc, WORK0)
        aw = sb.alloc("aw", [128, 8, 3072], F32)
        B_aw = [Buf("aw%d" % k) for k in range(8)]
        cv = sb.alloc("cv", [128, 8, 2], F32)
        sc = sb.alloc("sc", [128, 8, 2], F32)
        screp = sb.alloc("screp", [128, 8, 2, 128], F32)
        abf = sb.alloc("abf", [128, 24], F32)
        ngf = sb.alloc("ngf", [128, 8], F32)
        abg = sb.alloc("abg", [128, D], F32)
        mod = sb.alloc("mod", [128, 16, 2], F32)
        gt = sb.alloc("gt", [128, 2, D], F32)
        B_cv, B_sc, B_screp, B_abf, B_ngf, B_abg, B_mod, B_gt = [Buf(n) for n in "cv sc screp abf ngf abg mod gt".split()]
        R.add("sp", lambda: nc.sync.dma_start(out=cv[:], in_=cvec.ap()), [B_in], [B_cv], dma=True)
        R.add("act", lambda: nc.scalar.activation(out=sc[:], in_=cv[:], func=AF.Silu), [B_cv], [B_sc])
        for kc in range(8):
            for j in range(2):
                R.add("dve", lambda kc=kc, j=j: nc.vector.tensor_copy(
                    out=screp[:, kc, j, :], in_=fap(sc[:, kc, j:j + 1], [[0, 128]])), [B_sc], [B_screp], part=True)
        for li in range(nlayers):
            for kc in range(8):
                R.add("sp", lambda kc=kc, li=li: nc.sync.dma_start(
                    out=aw[:, kc, :], in_=ada_w[li, kc * 128:(kc + 1) * 128, :]), [B_in], [B_aw[kc]], dma=True)
            R.add("sp", lambda li=li: nc.sync.dma_start(out=abf[:], in_=ada_b_fm[li]), [B_in], [B_abf], dma=True)
            R.add("sp", lambda li=li: nc.sync.dma_start(out=ngf[:], in_=norm_g_fm[li]), [B_in], [B_ngf], dma=True)
            R.add("sp", lambda li=li: nc.sync.dma_start(
                out=abg[:], in_=bass.AP(ada_b_gate.ap().tensor, li * D, [[0, 128], [1, D]])), [B_in], [B_abg], dma=True)
            for fc in range(16):
                for kc in range(8):
                    R.add("pe", lambda fc=fc, kc=kc: nc.tensor.matmul(
                        PS[0][:, fc * 2:fc * 2 + 2], lhsT=aw[:, kc, fc * 128:(fc + 1) * 128], rhs=sc[:, kc, :],
                        start=(kc == 0), stop=(kc == 7)), [B_aw[kc], B_sc], [PB[0]], part=not (fc == 0 and kc == 0))
            R.add("dve", lambda: nc.vector.tensor_tensor(
                out=mod[:], in0=fap(PS[0][:, 0:32], [[2, 16], [1, 2]]), in1=fap(abf[:, 0:16], [[1, 16], [0, 2]]),
                op=ALU.add), [PB[0], B_abf], [B_mod])
            R.add("dve", lambda li=li: nc.vector.scalar_tensor_tensor(
                out=AB_t[:, li, 0, :, :], in0=mod[:, 8:16, :], scalar=1.0, in1=fap(ngf[:, 0:8], [[1, 8], [0, 2]]),
                op0=ALU.add, op1=ALU.mult), [B_mod, B_ngf], [B_AB], part=True)
            R.add("dve", lambda li=li: nc.vector.tensor_copy(out=AB_t[:, li, 1, :, :], in_=mod[:, 0:8, :]),
                  [B_mod], [B_AB], part=True)
            for j in range(2):
                for cc in range(2):
                    pb = 1 + (j * 2 + cc) % 2
                    for kc in range(8):
                        R.add("pe", lambda j=j, cc=cc, kc=kc, pb=pb: nc.tensor.matmul(
                            PS[pb][:, :], lhsT=screp[:, kc, j, :], rhs=aw[:, kc, 2048 + cc * 512:2048 + (cc + 1) * 512],
                            start=(kc == 0), stop=(kc == 7)), [B_screp, B_aw[kc]], [PB[pb]], part=(kc > 0))
                    R.add("dve", lambda j=j, cc=cc, pb=pb: nc.vector.tensor_tensor(
                        out=gt[:, j, cc * 512:(cc + 1) * 512], in0=PS[pb][:, :], in1=abg[:, cc * 512:(cc + 1) * 512],
                        op=ALU.add), [PB[pb], B_abg], [B_gt], part=not (j == 0 and cc == 0))
            for j in range(2):
                R.add("sp", lambda li=li, j=j: nc.sync.dma_start(out=gate_d[li, j], in_=gt[:, j, :]),
                      [B_gt], [B_gate], dma=B_gt, part=True)

    phase0()
    if phases <= 0:
        R.emit()
        return dram_in

    HT_OFF = WORK0
    hT = SB(nc).at("hT", [128, 8, T], BF16, HT_OFF)
    HT_BYTES = 8 * T * 2
    B_hT = [Buf("hT%d" % i) for i in range(34)]
    WORK1 = HT_OFF + HT_BYTES

    def phase1(li):
        sb = SB(nc, WORK1)
        xt = [sb.alloc("xt", [128, D], F32) for _ in range(2)]
        xn = [sb.alloc("xn", [128, D], F32) for _ in range(2)]
        junk = sb.alloc("junk", [128, D], BF16)
        st = [sb.alloc("st", [128, 4], F32) for _ in range(2)]
        B_xt = [Buf("xt%d" % i) for i in range(2)]
        B_xn = [Buf("xn%d" % i) for i in range(2)]
        B_junk = Buf("junk")
        B_st = [Buf("st%d" % i) for i in range(2)]
        for tt in range(34):
            s = tt % 2
            j = 0 if tt < 32 else 1
            if li == 0:
                src = x_in[tt * 128:(tt + 1) * 128, :] if tt < 32 else ctx_in[(tt - 32) * 128:(tt - 31) * 128, :]
                srcb = B_in
            else:
                src = x1_d[tt * 128:(tt + 1) * 128, :]
                srcb = B_x1
            R.add("sp", lambda s=s, src=src: nc.sync.dma_start(out=xt[s][:], in_=src), [srcb], [B_xt[s]], dma=True)
            R.add("act", lambda s=s: nc.scalar.activation(out=junk[:], in_=xt[s][:], func=AF.Square,
                                                          accum_out=st[s][:, 0:1]), [B_xt[s]], [B_junk, B_st[s]])
            R.add("act", lambda s=s: nc.scalar.activation(out=st[s][:, 1:2], in_=st[s][:, 0:1], func=AF.Sqrt,
                                                          bias=eps_t[:, 0:1], scale=1.0 / D), [B_st[s], B_eps], [B_st[s]], part=True)
            R.add("dve", lambda s=s: nc.vector.reciprocal(out=st[s][:, 2:3], in_=st[s][:, 1:2]), [B_st[s]], [B_st[s]], part=True)
            R.add("dve", lambda s=s: nc.vector.tensor_scalar(out=xn[s][:], in0=xt[s][:], scalar1=st[s][:, 2:3], scalar2=None,
                                                             op0=ALU.mult), [B_xt[s], B_st[s]], [B_xn[s]])
            pb0 = 2 * (tt % 2)
            for c in range(8):
                pb = pb0 + c // 4
                R.add("pe", lambda s=s, c=c, pb=pb: nc.tensor.transpose(
                    PS[pb][:, (c % 4) * 128:(c % 4 + 1) * 128], xn[s][:, c * 128:(c + 1) * 128], c_t[:, IDENT, :]),
                    [B_xn[s], B_c], [PB[pb]], part=(c % 4 > 0))
            for c in range(8):
                pb = pb0 + c // 4
                if c % 2 == 0:
                    R.add("act", lambda c=c, pb=pb, tt=tt, j=j: nc.scalar.activation(
                        out=hT[:, c, tt * 128:(tt + 1) * 128], in_=PS[pb][:, (c % 4) * 128:(c % 4 + 1) * 128],
                        func=AF.Identity, bias=AB_t[:, li, 1, c, j:j + 1], scale=AB_t[:, li, 0, c, j:j + 1]),
                        [PB[pb], B_AB], [B_hT[tt]], part=True)
                else:
                    R.add("dve", lambda c=c, pb=pb, tt=tt, j=j: nc.vector.tensor_scalar(
                        out=hT[:, c, tt * 128:(tt + 1) * 128], in0=PS[pb][:, (c % 4) * 128:(c % 4 + 1) * 128],
                        scalar1=AB_t[:, li, 0, c, j:j + 1], scalar2=AB_t[:, li, 1, c, j:j + 1],
                        op0=ALU.mult, op1=ALU.add), [PB[pb], B_AB], [B_hT[tt]], part=True)

    def wsrc(li, c0, ncol):
        return w_in[li, :, c0:c0 + ncol].rearrange("(kc p) n -> p kc n", p=128)

    def phaseA(li):
        sb = SB(nc, WORK1)
        wst = [sb.alloc("wst", [128, 8, 128], F32) for _ in range(2)]
        wbf = [sb.alloc("wbf", [128, 8, 128], BF16) for _ in range(2)]
        B_wst = [Buf("wst%d" % i) for i in range(2)]
        B_wbf = [Buf("wbf%d" % i) for i in range(2)]
        ot = [sb.alloc("ot", [128, 512], F32) for _ in range(3)]
        B_ot = [Buf("ot%d" % i) for i in range(3)]
        obt = [sb.alloc("obt", [128, 512], BF16) for _ in range(2)]
        B_obt = [Buf("obt%d" % i) for i in range(2)]
        vecs = sb.alloc("vecs", [128, 10 * 5 + 10 + 2 + 4 * 31 + 16], F32)
        B_vecs = Buf("vecs")
        V_SCW, V_SCB, V_QKG, V_CCW, V_CMV = 0, 50, 60, 62, 62 + 124
        R.add("sp", lambda: nc.sync.dma_start(out=vecs[:, V_SCW:V_SCW + 50], in_=scw[li].rearrange("p a b -> p (a b)")), [B_in], [B_vecs], dma=True)
        R.add("sp", lambda: nc.sync.dma_start(out=vecs[:, V_SCB:V_SCB + 10], in_=scb[li]), [B_in], [B_vecs], dma=True, part=True)
        R.add("sp", lambda: nc.sync.dma_start(out=vecs[:, V_QKG:V_QKG + 2], in_=qkg[li]), [B_in], [B_vecs], dma=True, part=True)
        R.add("sp", lambda: nc.sync.dma_start(out=vecs[:, V_CCW:V_CCW + 124], in_=ccw[li].rearrange("p a b -> p (a b)")), [B_in], [B_vecs], dma=True, part=True)
        R.add("sp", lambda: nc.sync.dma_start(out=vecs[:, V_CMV:V_CMV + 16], in_=cmv[li].rearrange("p a b -> p (a b)")), [B_in], [B_vecs], dma=True, part=True)
        sub0 = sb.off
        cnt = {"w": 0, "ot": 0, "obt": 0, "ps": 0}

        def wload(c0):
            s_ = cnt["w"] % 2
            cnt["w"] += 1
            R.add("sp", lambda: nc.sync.dma_start(out=wst[s_][:], in_=wsrc(li, c0, 128)), [B_in], [B_wst[s_]], dma=True)
            R.add("pool", lambda: nc.gpsimd.tensor_copy(out=wbf[s_][:], in_=wst[s_][:]), [B_wst[s_]], [B_wbf[s_]])
            return s_

        def proj(ws, j, pb):
            w_ = 512 if j < 8 else 256
            for kc in range(8):
                R.add("pe", lambda kc=kc: nc.tensor.matmul(PS[pb][:, 0:w_], lhsT=wbf[ws][:, kc, :], rhs=hT[:, kc, j * 512:j * 512 + w_],
                                                          start=(kc == 0), stop=(kc == 7)),
                      [B_wbf[ws]] + B_hT[j * 4:j * 4 + w_ // 128], [PB[pb]], part=(kc > 0))
            return w_

        def next_ot():
            s_ = cnt["ot"] % 3
            cnt["ot"] += 1
            return s_

        def next_obt():
            s_ = cnt["obt"] % 2
            cnt["obt"] += 1
            return s_

        for (c0, nch, dst, B_dst) in ((O_GA, 6, gas_d, B_gas), (O_GC, 4, gcs_d, B_gcs)):
            for ch in range(nch):
                ws = wload(c0 + ch * 128)
                for j in range(9):
                    pb = cnt["ps"] % 2
                    cnt["ps"] += 1
                    w_ = proj(ws, j, pb)
                    o_ = next_ot()
                    R.add("act", lambda pb=pb, o_=o_, w_=w_: nc.scalar.activation(out=ot[o_][:, 0:w_], in_=PS[pb][:, 0:w_], func=AF.Silu),
                          [PB[pb]], [B_ot[o_]])
                    R.add("sp", lambda o_=o_, w_=w_, ch=ch, j=j, dst=dst: nc.sync.dma_start(
                        out=dst[ch * 128:(ch + 1) * 128, j * 512:j * 512 + w_], in_=ot[o_][:, 0:w_]), [B_ot[o_]], [B_dst], dma=B_ot[o_], part=True)

        sbq = SB(nc, sub0)
        ropet = sbq.alloc("rope", [128, 2, T], F32)
        B_rope = Buf("rope")
        R.add("sp", lambda: nc.sync.dma_start(out=ropet[:, 0, :], in_=rope[:, 0, :]), [B_in], [B_rope], dma=True)
        R.add("sp", lambda: nc.sync.dma_start(out=ropet[:, 1, :], in_=rope[:, 1, :]), [B_in], [B_rope], dma=True, part=True)
        sqb = sbq.alloc("sqb", [128, 512], BF16); B_sqb = Buf("sqb")
        qf = sbq.alloc("qf", [128, 512], F32); B_qf = Buf("qf")
        sd = sbq.alloc("sd", [128, 512], F32); B_sd = Buf("sd")
        rs = sbq.alloc("rs", [128, 512], F32); B_rs = Buf("rs")
        qn = sbq.alloc("qn", [128, 512], F32); B_qn = Buf("qn")
        qnb = sbq.alloc("qnb", [128, 512], BF16); B_qnb = Buf("qnb")
        t1 = sbq.alloc("t1", [128, 512], F32); B_t1 = Buf("t1")
        t2 = sbq.alloc("t2", [128, 512], F32); B_t2 = Buf("t2")
        for (c0, nch, dst, B_dst, gi) in ((O_Q, 6, qT_d, B_qT, 0), (O_K, 2, kT_d, B_kT, 1)):
            for ch in range(nch):
                ws = wload(c0 + ch * 128)
                for j in range(9):
                    pb = cnt["ps"] % 2
                    cnt["ps"] += 1
                    w_ = proj(ws, j, pb)
                    cs = slice(j * 512, j * 512 + w_)
                    R.add("act", lambda pb=pb, w_=w_: nc.scalar.activation(out=sqb[:, 0:w_], in_=PS[pb][:, 0:w_], func=AF.Square), [PB[pb]], [B_sqb])
                    R.add("act", lambda pb=pb, w_=w_: nc.scalar.activation(out=qf[:, 0:w_], in_=PS[pb][:, 0:w_], func=AF.Copy), [PB[pb]], [B_qf])
                    R.add("pe", lambda w_=w_: nc.tensor.matmul(PS[4][:, 0:w_], lhsT=cb_t[:, BDM, :], rhs=sqb[:, 0:w_], start=True, stop=True),
                          [B_cb, B_sqb], [PB[4]])
                    R.add("act", lambda w_=w_: nc.scalar.activation(out=sd[:, 0:w_], in_=PS[4][:, 0:w_], func=AF.Sqrt, bias=eps_t[:, 0:1], scale=1.0),
                          [PB[4], B_eps], [B_sd])
                    R.add("dve", lambda w_=w_: nc.vector.reciprocal(out=rs[:, 0:w_], in_=sd[:, 0:w_]), [B_sd], [B_rs])
                    R.add("dve", lambda w_=w_, gi=gi: nc.vector.scalar_tensor_tensor(
                        out=qn[:, 0:w_], in0=qf[:, 0:w_], scalar=vecs[:, V_QKG + gi:V_QKG + gi + 1], in1=rs[:, 0:w_], op0=ALU.mult, op1=ALU.mult),
                        [B_qf, B_vecs, B_rs], [B_qn])
                    R.add("act", lambda w_=w_: nc.scalar.activation(out=qnb[:, 0:w_], in_=qn[:, 0:w_], func=AF.Copy), [B_qn], [B_qnb])
                    R.add("pe", lambda w_=w_: nc.tensor.matmul(PS[5][:, 0:w_], lhsT=cb_t[:, PERM, :], rhs=qnb[:, 0:w_], start=True, stop=True),
                          [B_cb, B_qnb], [PB[5]])
                    R.add("pool", lambda w_=w_, cs=cs: nc.gpsimd.tensor_tensor(out=t1[:, 0:w_], in0=qn[:, 0:w_], in1=ropet[:, 0, cs], op=ALU.mult),
                          [B_qn, B_rope], [B_t1])
                    R.add("dve", lambda w_=w_, cs=cs: nc.vector.tensor_tensor(out=t2[:, 0:w_], in0=PS[5][:, 0:w_], in1=ropet[:, 1, cs], op=ALU.mult),
                          [PB[5], B_rope], [B_t2])
                    o_ = next_obt()
                    R.add("dve", lambda w_=w_, o_=o_: nc.vector.tensor_tensor(out=obt[o_][:, 0:w_], in0=t1[:, 0:w_], in1=t2[:, 0:w_], op=ALU.add),
                          [B_t1, B_t2], [B_obt[o_]])
                    R.add("sp", lambda o_=o_, w_=w_, ch=ch, cs=cs, dst=dst: nc.sync.dma_start(
                        out=dst[ch * 128:(ch + 1) * 128, cs], in_=obt[o_][:, 0:w_]), [B_obt[o_]], [B_dst], dma=B_obt[o_], part=True)
        R.barrier()

        sbx = SB(nc, sub0)
        dg5 = sbx.alloc("dg5", [128, 10, 5, 128], BF16); B_dg5 = Buf("dg5")
        RBW = 2 + SEQ + 2 + 2 + NCTX + 2
        rb = [sbx.alloc("rb", [128, RBW], BF16) for _ in range(2)]
        B_rb = [Buf("rb%d" % i) for i in range(2)]
        for ch in range(10):
            for k in range(5):
                R.add("dve", lambda ch=ch, k=k: nc.vector.tensor_scalar(
                    out=dg5[:, ch, k, :], in0=c_t[:, IDENT, :], scalar1=vecs[:, V_SCW + ch * 5 + k:V_SCW + ch * 5 + k + 1], scalar2=None, op0=ALU.mult),
                    [B_c, B_vecs], [B_dg5], part=True)
        for i in range(2):
            R.add("pool", lambda i=i: nc.gpsimd.memset(rb[i][:], 0.0), [], [B_rb[i]])

        def rbcol(j, pad):
            return pad + j * 512 if j < 8 else pad + SEQ + 2 * pad

        def xproj(ch):
            ws = wload(O_X + ch * 128)
            r_ = ch % 2
            for j in range(9):
                pb = cnt["ps"] % 2
                cnt["ps"] += 1
                w_ = proj(ws, j, pb)
                c0_ = rbcol(j, 2)
                R.add("act", lambda pb=pb, w_=w_, c0_=c0_: nc.scalar.activation(out=rb[r_][:, c0_:c0_ + w_], in_=PS[pb][:, 0:w_], func=AF.Copy),
                      [PB[pb]], [B_rb[r_]], part=(j > 0))

        def xconv(ch):
            r_ = ch % 2
            for j in range(9):
                w_ = 512 if j < 8 else 256
                pb = 2 + j % 2
                st_ = rbcol(j, 2) - 2
                for k in range(5):
                    R.add("pe", lambda k=k, pb=pb, w_=w_, st_=st_: nc.tensor.matmul(
                        PS[pb][:, 0:w_], lhsT=dg5[:, ch, k, :], rhs=rb[r_][:, st_ + k:st_ + k + w_], start=(k == 0), stop=(k == 4)),
                        [B_dg5, B_rb[r_]], [PB[pb]], part=(k > 0))
                o_ = next_ot()
                R.add("act", lambda pb=pb, o_=o_, w_=w_: nc.scalar.activation(
                    out=ot[o_][:, 0:w_], in_=PS[pb][:, 0:w_], func=AF.Silu, bias=vecs[:, V_SCB + ch:V_SCB + ch + 1], scale=1.0),
                    [PB[pb], B_vecs], [B_ot[o_]])
                R.add("sp", lambda o_=o_, w_=w_, j=j: nc.sync.dma_start(
                    out=xbc_d[ch * 128:(ch + 1) * 128, j * 512:j * 512 + w_], in_=ot[o_][:, 0:w_]), [B_ot[o_]], [B_xbc], dma=B_ot[o_], part=True)

        xproj(0)
        for ch in range(10):
            if ch + 1 < 10:
                xproj(ch + 1)
            xconv(ch)
        R.barrier()

        sbc = SB(nc, sub0)
        dg31 = sbc.alloc("dg31", [128, 4, 31, 128], BF16); B_dg31 = Buf("dg31")
        RUW = 15 + SEQ + 15 + 15 + NCTX + 15
        ru = [sbc.alloc("ru", [128, RUW], BF16) for _ in range(2)]
        B_ru = [Buf("ru%d" % i) for i in range(2)]
        sg = [sbc.alloc("sg", [128, 512], F32) for _ in range(2)]
        B_sg = [Buf("sg%d" % i) for i in range(2)]
        for ch in range(4):
            for k in range(31):
                eng, eo = ("dve", nc.vector) if k % 2 == 0 else ("pool", nc.gpsimd)
                R.add(eng, lambda ch=ch, k=k, eo=eo: eo.tensor_scalar(
                    out=dg31[:, ch, k, :], in0=c_t[:, IDENT, :], scalar1=vecs[:, V_CCW + ch * 31 + k:V_CCW + ch * 31 + k + 1], scalar2=None, op0=ALU.mult),
                    [B_c, B_vecs], [B_dg31], part=True)
        for i in range(2):
            R.add("pool", lambda i=i: nc.gpsimd.memset(ru[i][:], 0.0), [], [B_ru[i]])

        def uproj(ch):
            wa = wload(O_UA + ch * 128)
            wb = wload(O_UB + ch * 128)
            r_ = ch % 2
            for j in range(9):
                w_ = proj(wa, j, 0)
                proj(wb, j, 1)
                s_ = j % 2
                R.add("act", lambda s_=s_, w_=w_: nc.scalar.activation(out=sg[s_][:, 0:w_], in_=PS[1][:, 0:w_], func=AF.Sigmoid), [PB[1]], [B_sg[s_]])
                c0_ = rbcol(j, 15)
                R.add("dve", lambda s_=s_, w_=w_, c0_=c0_: nc.vector.tensor_tensor(
                    out=ru[r_][:, c0_:c0_ + w_], in0=PS[0][:, 0:w_], in1=sg[s_][:, 0:w_], op=ALU.mult), [PB[0], B_sg[s_]], [B_ru[r_]], part=(j > 0))

        def uconv(ch):
            r_ = ch % 2
            for j in range(9):
                w_ = 512 if j < 8 else 256
                pb = 2 + j % 2
                st_ = rbcol(j, 15) - 15
                for k in range(31):
                    R.add("pe", lambda k=k, pb=pb, w_=w_, st_=st_: nc.tensor.matmul(
                        PS[pb][:, 0:w_], lhsT=dg31[:, ch, k, :], rhs=ru[r_][:, st_ + k:st_ + k + w_], start=(k == 0), stop=(k == 30)),
                        [B_dg31, B_ru[r_]], [PB[pb]], part=(k > 0))
                o_ = next_ot()
                R.add("act", lambda pb=pb, o_=o_, w_=w_: nc.scalar.activation(
                    out=ot[o_][:, 0:w_], in_=PS[pb][:, 0:w_], func=AF.Identity, bias=vecs[:, V_CMV + ch * 4:V_CMV + ch * 4 + 1], scale=1.0),
                    [PB[pb], B_vecs], [B_ot[o_]])
                R.add("sp", lambda o_=o_, w_=w_, j=j: nc.sync.dma_start(
                    out=cv_d[ch * 128:(ch + 1) * 128, j * 512:j * 512 + w_], in_=ot[o_][:, 0:w_]), [B_ot[o_]], [B_cv], dma=B_ot[o_], part=True)

        uproj(0)
        for ch in range(4):
            if ch + 1 < 4:
                uproj(ch + 1)
            uconv(ch)
        R.barrier()

        sbt = SB(nc, sub0)
        NZ = 768 + 24 + 256
        wzs = sbt.alloc("wzs", [128, 8, 384], F32); B_wzs = Buf("wzs")
        wz = sbt.alloc("wz", [128, 8, NZ], BF16); B_wz = Buf("wz")
        for (dc, c0, n_) in ((0, O_Z, 384), (384, O_Z + 384, 384), (768, O_DT, 24), (792, O_V, 256)):
            R.add("sp", lambda c0=c0, n_=n_: nc.sync.dma_start(out=wzs[:, :, 0:n_], in_=wsrc(li, c0, n_)), [B_in], [B_wzs], dma=True)
            R.add("pool", lambda dc=dc, n_=n_: nc.gpsimd.tensor_copy(out=wz[:, :, dc:dc + n_], in_=wzs[:, :, 0:n_]), [B_wzs], [B_wz], part=(dc > 0))
        zt = [sbt.alloc("zt", [128, 768], F32) for _ in range(2)]
        B_zt = [Buf("zt%d" % i) for i in range(2)]
        vt = [sbt.alloc("vt", [128, 256], BF16) for _ in range(2)]
        B_vt = [Buf("vt%d" % i) for i in range(2)]
        dta = sbt.alloc("dta", [128, 34, 24], F32); B_dta = Buf("dta")
        dtw = sbt.alloc("dtw", [128, 3, 34 * 24], F32); B_dtw = Buf("dtw")
        dtbt = sbt.alloc("dtbt", [128, 24], F32); B_dtbt = Buf("dtbt")
        R.add("sp", lambda: nc.sync.dma_start(out=dtbt[:], in_=bass.AP(dtb.ap().tensor, li * 24, [[0, 128], [1, 24]])), [B_in], [B_dtbt], dma=True)
        for tt in range(34):
            s_ = tt % 2
            pbs = (0, 1, 2) if tt % 2 == 0 else (3, 4, 5)
            for (pb, dc, n_) in ((pbs[0], 0, 384), (pbs[1], 384, 384), (pbs[2], 768, 280)):
                for kc in range(8):
                    R.add("pe", lambda kc=kc, pb=pb, dc=dc, n_=n_, tt=tt: nc.tensor.matmul(
                        PS[pb][:, 0:n_], lhsT=hT[:, kc, tt * 128:(tt + 1) * 128], rhs=wz[:, kc, dc:dc + n_], start=(kc == 0), stop=(kc == 7)),
                        [B_hT[tt], B_wz], [PB[pb]], part=(kc > 0))
            for hf_ in range(2):
                R.add("act", lambda hf_=hf_, s_=s_, pbs=pbs: nc.scalar.activation(
                    out=zt[s_][:, hf_ * 384:(hf_ + 1) * 384], in_=PS[pbs[hf_]][:, 0:384], func=AF.Silu), [PB[pbs[hf_]]], [B_zt[s_]], part=(hf_ > 0))
            R.add("sp", lambda s_=s_, tt=tt: nc.sync.dma_start(out=zs_d[tt * 128:(tt + 1) * 128, :], in_=zt[s_][:]), [B_zt[s_]], [B_zs], dma=B_zt[s_], part=True)
            R.add("dve", lambda s_=s_, pbs=pbs: nc.vector.tensor_copy(out=vt[s_][:], in_=PS[pbs[2]][:, 24:280]), [PB[pbs[2]]], [B_vt[s_]])
            R.add("sp", lambda s_=s_, tt=tt: nc.sync.dma_start(out=v_d[tt * 128:(tt + 1) * 128, :], in_=vt[s_][:]), [B_vt[s_]], [B_v], dma=B_vt[s_], part=True)
            R.add("dve", lambda tt=tt, pbs=pbs: nc.vector.tensor_tensor(out=dta[:, tt, :], in0=PS[pbs[2]][:, 0:24], in1=dtbt[:], op=ALU.add),
                  [PB[pbs[2]], B_dtbt], [B_dta], part=True)
        dflat = dta.rearrange("p a b -> p (a b)")
        R.add("act", lambda: nc.scalar.activation(out=dtw[:, 0, :], in_=dflat, func=AF.Abs), [B_dta], [B_dtw])
        R.add("act", lambda: nc.scalar.activation(out=dtw[:, 1, :], in_=dtw[:, 0, :], func=AF.Exp, scale=-1.0), [B_dtw], [B_dtw], part=True)
        R.add("act", lambda: nc.scalar.activation(out=dtw[:, 2, :], in_=dtw[:, 1, :], func=AF.Ln, bias=eps_t[:, 1:2], scale=1.0), [B_dtw, B_eps], [B_dtw], part=True)
        R.add("dve", lambda: nc.vector.scalar_tensor_tensor(out=dtw[:, 0, :], in0=dflat, scalar=0.0, in1=dtw[:, 2, :], op0=ALU.max, op1=ALU.add),
              [B_dta, B_dtw], [B_dtw], part=True)
        R.add("sp", lambda: nc.sync.dma_start(out=dt_d.ap().rearrange("(a p) h -> p a h", p=128),
                                              in_=dtw[:, 0, :].rearrange("p (a h) -> p a h", h=24)), [B_dtw], [B_dt], dma=B_dtw)
        R.barrier()
        R.release(B_wst + B_ot + B_obt + [B_vecs, B_rope, B_wzs, B_dtbt, B_dtw] + B_zt + B_vt)

    def phaseB(li, last):
        sb = SB(nc, WORK0)
        KT = sb.alloc("KT", [128, 2, T], BF16); B_KT = Buf("KT")
        VA = sb.alloc("VA", [128, 34, 4, 128], BF16); B_VA = Buf("VA")
        qs = [sb.alloc("qs", [128, 3, 512], BF16) for _ in range(2)]
        B_qs = [Buf("qs%d" % i) for i in range(2)]
        pT = [sb.alloc("pT", [128, 512], BF16) for _ in range(4)]
        B_pT = [Buf("pT%d" % i) for i in range(4)]
        rsb = [sb.alloc("rsb", [128, 512], F32) for _ in range(2)]
        B_rsb = [Buf("rsb%d" % i) for i in range(2)]
        ob = [sb.alloc("ob", [128, 512], F32) for _ in range(2)]
        B_ob = [Buf("ob%d" % i) for i in range(2)]
        gs = [sb.alloc("gs", [128, 512], F32) for _ in range(2)]
        B_gs = [Buf("gs%d" % i) for i in range(2)]
        obb = [sb.alloc("obb", [128, 512], BF16) for _ in range(2)]
        B_obb = [Buf("obb%d" % i) for i in range(2)]
        R.add("sp", lambda: nc.sync.dma_start(out=KT[:], in_=kT_d.ap().rearrange("(c p) t -> p c t", p=128)), [B_kT], [B_KT], dma=True)
        R.add("pool", lambda: nc.gpsimd.memset(VA[:, :, :, 64:128], 1.0), [], [B_VA])
        for tt in range(34):
            R.add("sp", lambda tt=tt: nc.sync.dma_start(out=VA[:, tt, :, 0:64], in_=v_d[tt * 128:(tt + 1) * 128, :].rearrange("p (g d) -> p g d", d=64)),
                  [B_v], [B_VA], dma=True, part=True)
        fin = 0
        for j in range(9):
            if last and j == 8:
                continue
            w_ = 512 if j < 8 else 256
            kts = list(range(34)) if j < 8 else [32, 33]
            for g in range(4):
                pbase = (g % 2) * 64
                qi = (j * 4 + g) % 2
                R.add("sp", lambda g=g, j=j, w_=w_, qi=qi, pbase=pbase: nc.sync.dma_start(
                    out=qs[qi][pbase:pbase + 64, :, 0:w_],
                    in_=qT_d[g * 192:(g + 1) * 192, j * 512:j * 512 + w_].rearrange("(h d) t -> d h t", d=64)), [B_qT], [B_qs[qi]], dma=True)
                steps = [(kt, hh) for kt in kts for hh in range(3)]
                n = len(steps)

                def S(s_):
                    kt, hh = steps[s_]
                    bk = s_ % 4
                    R.add("pe", lambda kt=kt, hh=hh, bk=bk, g=g, pbase=pbase, qi=qi, w_=w_: nc.tensor.matmul(PS[bk][:, 0:w_], lhsT=KT[pbase:pbase + 64, g // 2, kt * 128:(kt + 1) * 128],
                                                         rhs=qs[qi][pbase:pbase + 64, hh, 0:w_], start=True, stop=True), [B_KT, B_qs[qi]], [PB[bk]])
                    R.add("act", lambda bk=bk, w_=w_: nc.scalar.activation(out=pT[bk][:, 0:w_], in_=PS[bk][:, 0:w_], func=AF.Exp, scale=0.125), [PB[bk]], [B_pT[bk]])

                def PV(s_):
                    kt, hh = steps[s_]
                    bk = s_ % 4
                    R.add("pe", lambda kt=kt, hh=hh, bk=bk, g=g, w_=w_, k0=kts[0], k1=kts[-1]: nc.tensor.matmul(
                        PS[4 + hh][:, 0:w_], lhsT=VA[:, kt, g, :], rhs=pT[bk][:, 0:w_],
                        start=(kt == k0), stop=(kt == k1)), [B_VA, B_pT[bk]], [PB[4 + hh]], part=(kt != kts[0]))

                for s_ in range(n + 2):
                    if s_ < n:
                        S(s_)
                    if s_ >= 2:
                        PV(s_ - 2)
                for hh in range(3):
                    h_ = g * 3 + hh
                    f_ = fin % 2
                    fin += 1
                    R.add("sp", lambda h_=h_, f_=f_, j=j, w_=w_: nc.sync.dma_start(
                        out=gs[f_][0:64, 0:w_], in_=gas_d[h_ * 64:(h_ + 1) * 64, j * 512:j * 512 + w_]), [B_gas], [B_gs[f_]], dma=True)
                    R.add("dve", lambda hh=hh, f_=f_, w_=w_: nc.vector.reciprocal(out=rsb[f_][64:128, 0:w_], in_=PS[4 + hh][64:128, 0:w_]), [PB[4 + hh]], [B_rsb[f_]])
                    R.add("dve", lambda hh=hh, f_=f_, w_=w_: nc.vector.tensor_tensor(out=ob[f_][0:64, 0:w_], in0=PS[4 + hh][0:64, 0:w_], in1=rsb[f_][64:128, 0:w_], op=ALU.mult),
                          [PB[4 + hh], B_rsb[f_]], [B_ob[f_]])
                    R.add("pool", lambda f_=f_, w_=w_: nc.gpsimd.tensor_tensor(out=obb[f_][0:64, 0:w_], in0=ob[f_][0:64, 0:w_], in1=gs[f_][0:64, 0:w_], op=ALU.mult),
                          [B_ob[f_], B_gs[f_]], [B_obb[f_]])
                    R.add("sp", lambda h_=h_, f_=f_, j=j, w_=w_: nc.sync.dma_start(
                        out=cat_d[768 + h_ * 64:768 + (h_ + 1) * 64, j * 512:j * 512 + w_], in_=obb[f_][0:64, 0:w_]), [B_obb[f_]], [B_cat], dma=B_obb[f_], part=True)
        R.barrier()
        R.release([B_KT, B_VA] + B_qs + B_gs + B_obb)

    def bc12(ap2d, n=64):
        return fap(ap2d, [[1, ap2d.shape[1]], [0, n]])

    def v3(ap2d, a, b):
        return fap(ap2d, [[b, a], [1, b]])

    def phaseC(li, last):
        sb = SB(nc, WORK0)
        if "dbgC" in taps:
            dbgY = nc.dram_tensor("dbgY", [2, 4, 128, 768], F32, kind="ExternalOutput")
            dbgM = nc.dram_tensor("dbgM", [2, 2, 128, 1536], BF16, kind="ExternalOutput")
            B_dbgC = Buf("dbgC")
        Abc = sb.alloc("Abc", [128, 24], F32); B_Abc = Buf("Abc")
        Dbc = sb.alloc("Dbc", [128, 12], F32); B_Dbc = Buf("Dbc")
        Gbc = sb.alloc("Gbc", [128, 768], F32); B_Gbc = Buf("Gbc")
        mrep = sb.alloc("mrep", [128, 2, 4, 128], F32); B_mrep = Buf("mrep")
        R.add("sp", lambda: nc.sync.dma_start(out=Abc[:], in_=bass.AP(alog.ap().tensor, li * 24, [[0, 128], [1, 24]])), [B_in], [B_Abc], dma=True)
        R.add("act", lambda: nc.scalar.activation(out=Abc[:], in_=Abc[:], func=AF.Exp), [B_Abc], [B_Abc])
        R.add("dve", lambda: nc.vector.tensor_scalar(out=Abc[:], in0=Abc[:], scalar1=-1.0, scalar2=None, op0=ALU.mult), [B_Abc], [B_Abc])
        R.add("sp", lambda: nc.sync.dma_start(out=Dbc[:], in_=bass.AP(ssdd.ap().tensor, li * 12, [[0, 128], [1, 12]])), [B_in], [B_Dbc], dma=True)
        R.add("sp", lambda: nc.sync.dma_start(out=Gbc[:], in_=bass.AP(ssdng.ap().tensor, li * 768, [[0, 128], [1, 768]])), [B_in], [B_Gbc], dma=True)
        for dr in range(2):
            R.add("dve", lambda dr=dr: nc.vector.tensor_copy(out=mrep[:, dr, :, :], in_=fap(mk_t[:, dr, :], [[0, 4], [1, 128]])), [B_mk], [B_mrep], part=(dr > 0))
        P2 = range(2)
        xT = [sb.alloc("xT", [128, 6, 128], F32) for _ in P2]; B_xT = [Buf("xT%d" % i) for i in P2]
        bcT = [sb.alloc("bcT", [128, 4, 128], F32) for _ in P2]; B_bcT = [Buf("bcT%d" % i) for i in P2]
        dtc = [sb.alloc("dtc", [128, 24], F32) for _ in P2]; B_dtc = [Buf("dtc%d" % i) for i in P2]
        zc = [sb.alloc("zc", [128, 768], F32) for _ in P2]; B_zc = [Buf("zc%d" % i) for i in P2]
        hbin = [sb.alloc("hbin", [128, 768], BF16) for _ in P2]; B_hbin = [Buf("hbin%d" % i) for i in P2]
        xs = [sb.alloc("xs", [128, 768], F32) for _ in P2]; B_xs = [Buf("xs%d" % i) for i in P2]
        Btm = [sb.alloc("Btm", [128, 256], BF16) for _ in P2]; B_Btm = [Buf("Btm%d" % i) for i in P2]
        bcb = [sb.alloc("bcb", [128, 4, 128], BF16) for _ in P2]; B_bcb = [Buf("bcb%d" % i) for i in P2]
        av = [sb.alloc("av", [128, 24], F32) for _ in P2]; B_av = [Buf("av%d" % i) for i in P2]
        ct = [sb.alloc("ct", [128, 72], F32) for _ in P2]; B_ct = [Buf("ct%d" % i) for i in P2]
        ex = [sb.alloc("ex", [128, 72], F32) for _ in P2]; B_ex = [Buf("ex%d" % i) for i in P2]
        ncum = [sb.alloc("ncum", [128, 24], F32) for _ in P2]; B_ncum = [Buf("ncum%d" % i) for i in P2]
        dtd = [sb.alloc("dtd", [128, 24], F32) for _ in P2]; B_dtd = [Buf("dtd%d" % i) for i in P2]
        xw = [[sb.alloc("xw", [128, 768], BF16) for _ in range(4)] for _ in P2]
        B_xw = [[Buf("xw%d_%d" % (i, k)) for k in range(4)] for i in P2]
        cbs = sb.alloc("cbs", [128, 2, 128], F32); B_cbs = Buf("cbs")
        Dm = sb.alloc("Dm", [128, 12, 128], F32); B_Dm = Buf("Dm")
        Et = sb.alloc("Et", [128, 12, 128], F32); B_Et = Buf("Et")
        Mt = [sb.alloc("Mt", [128, 12, 128], BF16) for _ in P2]; B_Mt = [Buf("Mt%d" % i) for i in P2]
        yo = [sb.alloc("yo", [128, 768], F32) for _ in P2]; B_yo = [Buf("yo%d" % i) for i in P2]
        yv = sb.alloc("yv", [128, 768], F32); B_yv = Buf("yv")
        t3 = sb.alloc("t3", [128, 768], F32); B_t3 = Buf("t3")
        yn = sb.alloc("yn", [128, 768], F32); B_yn = Buf("yn")
        junk = sb.alloc("junk", [128, 768], BF16); B_junk = Buf("junkc")
        st = sb.alloc("st", [128, 4], F32); B_st = Buf("stc")
        catT = [sb.alloc("catT", [128, 6, 128], BF16) for _ in P2]; B_catT = [Buf("catT%d" % i) for i in P2]
        tmp = sb.alloc("tmp", [128, 768], F32); B_tmp = Buf("tmp")
        hst = [sb.alloc("hst", [128, 768], F32) for _ in P2]; B_hst = [Buf("hst%d" % i) for i in P2]
        hfb = sb.alloc("hfb", [128, 768], BF16); B_hfb = Buf("hfb")
        hbb = [sb.alloc("hbb", [128, 768], BF16) for _ in P2]; B_hbb = [Buf("hbb%d" % i) for i in P2]
        for i in P2:
            R.add("pool", lambda i=i: nc.gpsimd.memset(hst[i][:], 0.0), [], [B_hst[i]])
            R.add("pool", lambda i=i: nc.gpsimd.memset(hbb[i][:], 0.0), [], [B_hbb[i]])
        R.add("pool", lambda: nc.gpsimd.memset(hfb[:], 0.0), [], [B_hfb])

        def prep(c, pp, full):
            cs = slice(c * 128, (c + 1) * 128)
            R.add("sp", lambda: nc.sync.dma_start(out=xT[pp][:], in_=xbc_d[0:768, cs].rearrange("(c p) t -> p c t", p=128)), [B_xbc], [B_xT[pp]], dma=True)
            R.add("sp", lambda: nc.sync.dma_start(out=bcT[pp][:], in_=xbc_d[768:1280, cs].rearrange("(c p) t -> p c t", p=128)), [B_xbc], [B_bcT[pp]], dma=True)
            R.add("sp", lambda: nc.sync.dma_start(out=dtc[pp][:], in_=dt_d[cs, :]), [B_dt], [B_dtc[pp]], dma=True)
            if full:
                R.add("sp", lambda: nc.sync.dma_start(out=zc[pp][:], in_=zs_d[cs, :]), [B_zs], [B_zc[pp]], dma=True)
                R.add("sp", lambda: nc.sync.dma_start(out=hbin[pp][:], in_=hb_d[c]), [B_hb], [B_hbin[pp]], dma=True)
            for i in range(6):
                bk, col = (0, i * 128) if i < 4 else (1, (i - 4) * 128)
                R.add("pe", lambda i=i, bk=bk, col=col: nc.tensor.transpose(PS[bk][:, col:col + 128], xT[pp][:, i, :], c_t[:, IDENT, :]),
                      [B_xT[pp], B_c], [PB[bk]], part=(i not in (0, 4)))
            for i in range(2):
                R.add("pe", lambda i=i: nc.tensor.transpose(PS[1][:, 256 + i * 128:384 + i * 128], bcT[pp][:, i, :], c_t[:, IDENT, :]),
                      [B_bcT[pp], B_c], [PB[1]], part=True)
            R.add("act", lambda: nc.scalar.activation(out=xs[pp][:, 0:512], in_=PS[0][:, 0:512], func=AF.Copy), [PB[0]], [B_xs[pp]])
            R.add("act", lambda: nc.scalar.activation(out=xs[pp][:, 512:768], in_=PS[1][:, 0:256], func=AF.Copy), [PB[1]], [B_xs[pp]], part=True)
            R.add("dve", lambda: nc.vector.tensor_copy(out=Btm[pp][:], in_=PS[1][:, 256:512]), [PB[1]], [B_Btm[pp]])
            R.add("pool", lambda: nc.gpsimd.tensor_copy(out=bcb[pp][:], in_=bcT[pp][:]), [B_bcT[pp]], [B_bcb[pp]])
            R.add("dve", lambda: nc.vector.tensor_tensor(out=av[pp][:], in0=dtc[pp][:], in1=Abc[:], op=ALU.mult), [B_dtc[pp], B_Abc], [B_av[pp]])
            R.add("pe", lambda: nc.tensor.matmul(PS[2][:, 0:12], lhsT=c_t[:, UIN, :], rhs=av[pp][:, 0:12], start=True, stop=True), [B_c, B_av[pp]], [PB[2]])
            R.add("pe", lambda: nc.tensor.matmul(PS[2][:, 12:24], lhsT=c_t[:, UTR, :], rhs=av[pp][:, 12:24], start=True, stop=True), [B_c, B_av[pp]], [PB[2]], part=True)
            R.add("pe", lambda: nc.tensor.matmul(PS[2][:, 24:48], lhsT=c_t[:, ONES, :], rhs=av[pp][:, 0:24], start=True, stop=True), [B_c, B_av[pp]], [PB[2]], part=True)
            R.add("dve", lambda: nc.vector.tensor_copy(out=ct[pp][:, 24:72], in_=PS[2][:, 0:48]), [PB[2]], [B_ct[pp]])
            R.add("dve", lambda: nc.vector.tensor_tensor(out=ct[pp][:, 0:24], in0=ct[pp][:, 48:72], in1=ct[pp][:, 24:48], op=ALU.subtract), [B_ct[pp]], [B_ct[pp]], part=True)
            R.add("act", lambda: nc.scalar.activation(out=ex[pp][:], in_=ct[pp][:], func=AF.Exp), [B_ct[pp]], [B_ex[pp]])
            R.add("dve", lambda: nc.vector.tensor_scalar(out=ncum[pp][:], in0=ct[pp][:, 24:48], scalar1=-1.0, scalar2=None, op0=ALU.mult), [B_ct[pp]], [B_ncum[pp]])
            R.add("dve", lambda: nc.vector.tensor_tensor(out=dtd[pp][:], in0=dtc[pp][:], in1=ex[pp][:, 0:24], op=ALU.mult), [B_dtc[pp], B_ex[pp]], [B_dtd[pp]])
            xs3 = v3(xs[pp][:, 0:768], 12, 64)
            R.add("dve", lambda: nc.vector.tensor_tensor(out=v3(xw[pp][1][:, 0:768], 12, 64), in0=xs3, in1=bc12(dtd[pp][:, 12:24]), op=ALU.mult),
                  [B_xs[pp], B_dtd[pp]], [B_xw[pp][1]])
            if full:
                R.add("pool", lambda: nc.gpsimd.tensor_tensor(out=v3(xw[pp][0][:, 0:768], 12, 64), in0=xs3, in1=bc12(dtd[pp][:, 0:12]), op=ALU.mult),
                      [B_xs[pp], B_dtd[pp]], [B_xw[pp][0]])
                R.add("dve", lambda: nc.vector.tensor_tensor(out=v3(xw[pp][2][:, 0:768], 12, 64), in0=xs3, in1=bc12(dtc[pp][:, 0:12]), op=ALU.mult),
                      [B_xs[pp], B_dtc[pp]], [B_xw[pp][2]])
                R.add("pool", lambda: nc.gpsimd.tensor_tensor(out=v3(xw[pp][3][:, 0:768], 12, 64), in0=xs3, in1=bc12(dtc[pp][:, 12:24]), op=ALU.mult),
                      [B_xs[pp], B_dtc[pp]], [B_xw[pp][3]])

        def state_update(pp, d, bks, outb, B_outb):
            for g in range(2):
                R.add("pe", lambda g=g: nc.tensor.matmul(PS[bks[g]][:, 0:384], lhsT=Btm[pp][:, g * 128:(g + 1) * 128], rhs=xw[pp][d][:, g * 384:(g + 1) * 384],
                                                        start=True, stop=True), [B_Btm[pp], B_xw[pp][d]], [PB[bks[g]]])
            R.add("dve", lambda: nc.vector.tensor_tensor(out=v3(tmp[:, 0:768], 12, 64), in0=v3(hst[d][:, 0:768], 12, 64),
                                                         in1=bc12(ex[pp][:, 48 + d * 12:60 + d * 12]), op=ALU.mult), [B_hst[d], B_ex[pp]], [B_tmp])
            for g in range(2):
                R.add("dve", lambda g=g: nc.vector.tensor_tensor(out=hst[d][:, g * 384:(g + 1) * 384], in0=PS[bks[g]][:, 0:384], in1=tmp[:, g * 384:(g + 1) * 384], op=ALU.add),
                      [PB[bks[g]], B_tmp], [B_hst[d]], part=(g > 0))
            R.add("act", lambda: nc.scalar.activation(out=outb[:], in_=hst[d][:], func=AF.Copy), [B_hst[d]], [B_outb])

        order_b = [33, 32] + list(range(31, -1, -1))
        for n_, c in enumerate(order_b):
            pp = n_ % 2
            prep(c, pp, False)
            R.add("sp", lambda c=c, pp=pp: nc.sync.dma_start(out=hb_d[c], in_=hbb[pp][:]), [B_hbb[pp]], [B_hb], dma=B_hbb[pp], part=True)
            state_update(pp, 1, (3, 4), hbb[1 - pp], B_hbb[1 - pp])

        order_f = [32, 33] + list(range(32))
        for n_, c in enumerate(order_f):
            pp = n_ % 2
            prep(c, pp, True)
            for g in range(2):
                R.add("pe", lambda g=g, pp=pp: nc.tensor.matmul(PS[3][:, g * 128:(g + 1) * 128], lhsT=bcb[pp][:, g, :], rhs=bcb[pp][:, 2 + g, :], start=True, stop=True),
                      [B_bcb[pp]], [PB[3]], part=(g > 0))
            R.add("dve", lambda: nc.vector.tensor_copy(out=cbs[:].rearrange("p a b -> p (a b)"), in_=PS[3][:, 0:256]), [PB[3]], [B_cbs])
            for dr in range(2):
                um = UIN if dr == 0 else UTR
                R.add("pool", lambda dr=dr, um=um, pp=pp: nc.gpsimd.tensor_tensor(
                    out=Dm[:], in0=fap(c_t[:, um, :], [[0, 12], [1, 128]]), in1=bc12(av[pp][:, dr * 12:dr * 12 + 12], 128), op=ALU.mult),
                    [B_c, B_av[pp]], [B_Dm])
                for q in range(3):
                    R.add("pe", lambda q=q: nc.tensor.matmul(PS[4 + q][:, :], lhsT=c_t[:, ONES, :], rhs=Dm[:, q * 4:(q + 1) * 4, :].rearrange("p a b -> p (a b)"),
                                                             start=True, stop=False), [B_c, B_Dm], [PB[4 + q]])
                    R.add("pe", lambda q=q, dr=dr: nc.tensor.matmul(PS[4 + q][:, :], lhsT=c_t[:, IDENT, :], rhs=mrep[:, dr, :, :].rearrange("p a b -> p (a b)"),
                                                                    start=False, stop=True), [B_c, B_mrep], [PB[4 + q]], part=True)
                for h in range(12):
                    R.add("act", lambda h=h, dr=dr, pp=pp: nc.scalar.activation(
                        out=Et[:, h, :], in_=PS[4 + h // 4][:, (h % 4) * 128:(h % 4 + 1) * 128], func=AF.Exp,
                        bias=ncum[pp][:, dr * 12 + h:dr * 12 + h + 1], scale=1.0), [PB[4 + h // 4], B_ncum[pp]], [B_Et], part=(h > 0))
                for g in range(2):
                    R.add("dve", lambda g=g, dr=dr: nc.vector.tensor_tensor(out=Mt[dr][:, g * 6:(g + 1) * 6, :], in0=Et[:, g * 6:(g + 1) * 6, :],
                                                                          in1=fap(cbs[:, g, :], [[0, 6], [1, 128]]), op=ALU.mult), [B_Et, B_cbs], [B_Mt[dr]], part=(g > 0))
                hin, B_hin = (hfb, B_hfb) if dr == 0 else (hbin[pp], B_hbin[pp])
                for g in range(2):
                    bk = 7 if g == 0 else 2
                    R.add("pe", lambda g=g, bk=bk, hin=hin, pp=pp: nc.tensor.matmul(PS[bk][:, 0:384], lhsT=bcb[pp][:, 2 + g, :], rhs=hin[:, g * 384:(g + 1) * 384],
                                                                                 start=True, stop=True), [B_bcb[pp], B_hin], [PB[bk]])
                    R.add("dve", lambda g=g, bk=bk, dr=dr, pp=pp: nc.vector.tensor_tensor(
                        out=v3(yo[dr][:, g * 384:(g + 1) * 384], 6, 64), in0=v3(PS[bk][:, 0:384], 6, 64),
                        in1=bc12(ex[pp][:, 24 + dr * 12 + g * 6:24 + dr * 12 + g * 6 + 6]), op=ALU.mult), [PB[bk], B_ex[pp]], [B_yo[dr]], part=(g > 0))
            for h in range(12):
                bk, col = (0, h * 64) if h < 8 else (1, (h - 8) * 64)
                for dr in range(2):
                    R.add("pe", lambda h=h, dr=dr, bk=bk, col=col, pp=pp: nc.tensor.matmul(
                        PS[bk][:, col:col + 64], lhsT=Mt[dr][:, h, :], rhs=xw[pp][2 + dr][:, h * 64:(h + 1) * 64], start=(dr == 0), stop=(dr == 1)),
                        [B_Mt[dr], B_xw[pp][2 + dr]], [PB[bk]], part=not (dr == 0 and h in (0, 8)))
            skip_out = last and c >= 32
            if not skip_out:
                R.add("dve", lambda: nc.vector.tensor_tensor(out=yv[:, 0:512], in0=PS[0][:, 0:512], in1=yo[0][:, 0:512], op=ALU.add), [PB[0], B_yo[0]], [B_yv])
                R.add("dve", lambda: nc.vector.tensor_tensor(out=yv[:, 512:768], in0=PS[1][:, 0:256], in1=yo[0][:, 512:768], op=ALU.add), [PB[1], B_yo[0]], [B_yv], part=True)
                R.add("pool", lambda: nc.gpsimd.tensor_tensor(out=yv[:], in0=yv[:], in1=yo[1][:], op=ALU.add), [B_yv, B_yo[1]], [B_yv])
                R.add("pool", lambda pp=pp: nc.gpsimd.tensor_tensor(out=v3(t3[:, 0:768], 12, 64), in0=v3(xs[pp][:, 0:768], 12, 64), in1=bc12(Dbc[:, 0:12]), op=ALU.mult),
                      [B_xs[pp], B_Dbc], [B_t3])
                R.add("pool", lambda: nc.gpsimd.tensor_tensor(out=yv[:], in0=yv[:], in1=t3[:], op=ALU.add), [B_yv, B_t3], [B_yv])
                if "dbgC" in taps and c in (32, 0):
                    di = 0 if c == 32 else 1
                    R.add("sp", lambda di=di: nc.sync.dma_start(out=dbgY[di, 0], in_=yv[:]), [B_yv], [B_dbgC], dma=True, part=True)
                    R.add("sp", lambda di=di: nc.sync.dma_start(out=dbgY[di, 1], in_=yo[0][:]), [B_yo[0]], [B_dbgC], dma=True, part=True)
                    R.add("sp", lambda di=di: nc.sync.dma_start(out=dbgY[di, 2], in_=yo[1][:]), [B_yo[1]], [B_dbgC], dma=True, part=True)
                    R.add("sp", lambda di=di: nc.sync.dma_start(out=dbgY[di, 3], in_=t3[:]), [B_t3], [B_dbgC], dma=True, part=True)
                    R.add("sp", lambda di=di: nc.sync.dma_start(out=dbgM[di, 0], in_=Mt[0][:].rearrange("p a b -> p (a b)")), [B_Mt[0]], [B_dbgC], dma=True, part=True)
                    R.add("sp", lambda di=di: nc.sync.dma_start(out=dbgM[di, 1], in_=Mt[1][:].rearrange("p a b -> p (a b)")), [B_Mt[1]], [B_dbgC], dma=True, part=True)
                    R.add("sp", None, [B_dbgC], [])
                R.add("pool", lambda pp=pp: nc.gpsimd.tensor_tensor(out=yv[:], in0=yv[:], in1=zc[pp][:], op=ALU.mult), [B_yv, B_zc[pp]], [B_yv])
                R.add("act", lambda: nc.scalar.activation(out=junk[:], in_=yv[:], func=AF.Square, accum_out=st[:, 0:1]), [B_yv], [B_junk, B_st])
                R.add("act", lambda: nc.scalar.activation(out=st[:, 1:2], in_=st[:, 0:1], func=AF.Sqrt, bias=eps_t[:, 0:1], scale=1.0 / 768), [B_st, B_eps], [B_st], part=True)
                R.add("dve", lambda: nc.vector.reciprocal(out=st[:, 2:3], in_=st[:, 1:2]), [B_st], [B_st], part=True)
                R.add("dve", lambda: nc.vector.scalar_tensor_tensor(out=yn[:], in0=yv[:], scalar=st[:, 2:3], in1=Gbc[:], op0=ALU.mult, op1=ALU.mult),
                      [B_yv, B_st, B_Gbc], [B_yn])
                for i in range(6):
                    bk, col = (4, i * 128) if i < 4 else (5, (i - 4) * 128)
                    R.add("pe", lambda i=i, bk=bk, col=col: nc.tensor.transpose(PS[bk][:, col:col + 128], yn[:, i * 128:(i + 1) * 128], c_t[:, IDENT, :]),
                          [B_yn, B_c], [PB[bk]], part=(i not in (0, 4)))
                R.add("act", lambda pp=pp: nc.scalar.activation(out=catT[pp][:, 0:4, :].rearrange("p a b -> p (a b)"), in_=PS[4][:, 0:512], func=AF.Copy), [PB[4]], [B_catT[pp]])
                R.add("act", lambda pp=pp: nc.scalar.activation(out=catT[pp][:, 4:6, :].rearrange("p a b -> p (a b)"), in_=PS[5][:, 0:256], func=AF.Copy), [PB[5]], [B_catT[pp]], part=True)
                R.add("sp", lambda c=c, pp=pp: nc.sync.dma_start(out=cat_d[0:768, c * 128:(c + 1) * 128].rearrange("(c p) t -> p c t", p=128), in_=catT[pp][:]),
                      [B_catT[pp]], [B_cat], dma=B_catT[pp], part=True)
            state_update(pp, 0, (6, 7), hfb, B_hfb)
        R.barrier()
        R.release(B_xT + B_bcT + B_dtc + B_zc + B_hbin + B_hbb + B_catT + [B_Abc, B_Dbc, B_Gbc])

    def phaseD(li, last):
        sb = SB(nc, WORK0)
        vec = sb.alloc("vecD", [128, 16], F32); B_vec = Buf("vecD")
        R.add("sp", lambda: nc.sync.dma_start(out=vec[:], in_=cmv[li].rearrange("p a b -> p (a b)")), [B_in], [B_vec], dma=True)
        pws = sb.alloc("pws", [128, 4, 512], F32); B_pws = Buf("pws")
        pwb = sb.alloc("pwb", [128, 4, 512], BF16); B_pwb = Buf("pwb")
        R.add("sp", lambda: nc.sync.dma_start(out=pws[:], in_=cpw[li].rearrange("(c p) n -> p c n", p=128)), [B_in], [B_pws], dma=True)
        R.add("pool", lambda: nc.gpsimd.tensor_copy(out=pwb[:], in_=pws[:]), [B_pws], [B_pwb])
        P2 = range(2)
        cvt = [sb.alloc("cvt", [128, 4, 512], F32) for _ in P2]; B_cvt = [Buf("cvt%d" % i) for i in P2]
        gct = [sb.alloc("gct", [128, 4, 512], F32) for _ in P2]; B_gct = [Buf("gct%d" % i) for i in P2]
        sq = sb.alloc("sq", [128, 4, 512], F32); B_sq = Buf("sq")
        mean = sb.alloc("mean", [128, 512], F32); B_mean = Buf("mean")
        m2 = sb.alloc("m2", [128, 512], F32); B_m2 = Buf("m2")
        var = sb.alloc("var", [128, 512], F32); B_var = Buf("var")
        sdv = sb.alloc("sdv", [128, 512], F32); B_sdv = Buf("sdv")
        rstd = sb.alloc("rstd", [128, 512], F32); B_rstd = Buf("rstd")
        xc_ = sb.alloc("xc", [128, 512], F32); B_xc = Buf("xc")
        xn_ = sb.alloc("xn", [128, 512], F32); B_xn = Buf("xnD")
        act = sb.alloc("act", [128, 4, 512], BF16); B_act = Buf("act")
        oD = [sb.alloc("oD", [128, 512], BF16) for _ in P2]; B_oD = [Buf("oD%d" % i) for i in P2]
        k_ = 0
        for j in range(9):
            if last and j == 8:
                continue
            w_ = 512 if j < 8 else 256
            cs = slice(j * 512, j * 512 + w_)
            pp = j % 2
            R.add("sp", lambda pp=pp, cs=cs, w_=w_: nc.sync.dma_start(out=cvt[pp][:, :, 0:w_], in_=cv_d[:, cs].rearrange("(c p) t -> p c t", p=128)), [B_cv], [B_cvt[pp]], dma=True)
            R.add("sp", lambda pp=pp, cs=cs, w_=w_: nc.sync.dma_start(out=gct[pp][:, :, 0:w_], in_=gcs_d[:, cs].rearrange("(c p) t -> p c t", p=128)), [B_gcs], [B_gct[pp]], dma=True)
            R.add("act", lambda pp=pp, w_=w_: nc.scalar.activation(out=sq[:, :, 0:w_], in_=cvt[pp][:, :, 0:w_], func=AF.Square), [B_cvt[pp]], [B_sq])
            for c in range(4):
                R.add("pe", lambda c=c, pp=pp, w_=w_: nc.tensor.matmul(PS[0][:, 0:w_], lhsT=c_t[:, ONES, :], rhs=cvt[pp][:, c, 0:w_], start=(c == 0), stop=(c == 3)),
                      [B_c, B_cvt[pp]], [PB[0]], part=(c > 0))
            for c in range(4):
                R.add("pe", lambda c=c, w_=w_: nc.tensor.matmul(PS[1][:, 0:w_], lhsT=c_t[:, ONES, :], rhs=sq[:, c, 0:w_], start=(c == 0), stop=(c == 3)),
                      [B_c, B_sq], [PB[1]], part=(c > 0))
            R.add("dve", lambda w_=w_: nc.vector.tensor_scalar(out=mean[:, 0:w_], in0=PS[0][:, 0:w_], scalar1=1.0 / 512, scalar2=None, op0=ALU.mult), [PB[0]], [B_mean])
            R.add("dve", lambda w_=w_: nc.vector.tensor_tensor(out=m2[:, 0:w_], in0=mean[:, 0:w_], in1=mean[:, 0:w_], op=ALU.mult), [B_mean], [B_m2])
            R.add("dve", lambda w_=w_: nc.vector.scalar_tensor_tensor(out=var[:, 0:w_], in0=PS[1][:, 0:w_], scalar=1.0 / 512, in1=m2[:, 0:w_], op0=ALU.mult, op1=ALU.subtract),
                  [PB[1], B_m2], [B_var])
            R.add("act", lambda w_=w_: nc.scalar.activation(out=sdv[:, 0:w_], in_=var[:, 0:w_], func=AF.Sqrt, bias=eps_t[:, 0:1], scale=1.0), [B_var, B_eps], [B_sdv])
            R.add("dve", lambda w_=w_: nc.vector.reciprocal(out=rstd[:, 0:w_], in_=sdv[:, 0:w_]), [B_sdv], [B_rstd])
            for c in range(4):
                R.add("dve", lambda c=c, pp=pp, w_=w_: nc.vector.tensor_tensor(out=xc_[:, 0:w_], in0=cvt[pp][:, c, 0:w_], in1=mean[:, 0:w_], op=ALU.subtract),
                      [B_cvt[pp], B_mean], [B_xc])
                R.add("pool", lambda w_=w_: nc.gpsimd.tensor_tensor(out=xn_[:, 0:w_], in0=xc_[:, 0:w_], in1=rstd[:, 0:w_], op=ALU.mult), [B_xc, B_rstd], [B_xn])
                R.add("act", lambda c=c, w_=w_: nc.scalar.activation(out=act[:, c, 0:w_], in_=xn_[:, 0:w_], func=AF.Silu,
                                                                     bias=vec[:, c * 4 + 2:c * 4 + 3], scale=vec[:, c * 4 + 1:c * 4 + 2]), [B_xn, B_vec], [B_act], part=(c > 0))
            for oc in range(4):
                bk = 2 + oc % 2
                for c in range(4):
                    R.add("pe", lambda oc=oc, c=c, bk=bk, w_=w_: nc.tensor.matmul(PS[bk][:, 0:w_], lhsT=pwb[:, c, oc * 128:(oc + 1) * 128], rhs=act[:, c, 0:w_],
                                                                                 start=(c == 0), stop=(c == 3)), [B_pwb, B_act], [PB[bk]], part=(c > 0))
                o_ = k_ % 2
                k_ += 1
                R.add("dve", lambda oc=oc, bk=bk, o_=o_, pp=pp, w_=w_: nc.vector.scalar_tensor_tensor(
                    out=oD[o_][:, 0:w_], in0=PS[bk][:, 0:w_], scalar=vec[:, oc * 4 + 3:oc * 4 + 4], in1=gct[pp][:, oc, 0:w_], op0=ALU.add, op1=ALU.mult),
                    [PB[bk], B_vec, B_gct[pp]], [B_oD[o_]])
                R.add("sp", lambda oc=oc, o_=o_, cs=cs, w_=w_: nc.sync.dma_start(out=cat_d[1536 + oc * 128:1536 + (oc + 1) * 128, cs], in_=oD[o_][:, 0:w_]),
                      [B_oD[o_]], [B_cat], dma=B_oD[o_], part=True)
        R.barrier()
        R.release([B_vec, B_pws] + B_cvt + B_gct + B_oD)

    def phaseE(li, last):
        sb = SB(nc, WORK0)
        wos = sb.alloc("wos", [128, 4, D], F32); B_wos = Buf("wos")
        wo = sb.alloc("wo", [128, 16, D], BF16); B_wo = Buf("wo")
        for q in range(4):
            R.add("sp", lambda q=q: nc.sync.dma_start(out=wos[:], in_=w_out[li, q * 512:(q + 1) * 512, :].rearrange("(c p) n -> p c n", p=128)), [B_in], [B_wos], dma=True)
            R.add("pool", lambda q=q: nc.gpsimd.tensor_copy(out=wo[:, q * 4:(q + 1) * 4, :], in_=wos[:]), [B_wos], [B_wo], part=(q > 0))
        gB = sb.alloc("gB", [128, 2, D], F32); B_gB = Buf("gB")
        for j in range(2):
            R.add("sp", lambda j=j: nc.sync.dma_start(out=gB[:, j, :], in_=gate_d[li, j]), [B_gate], [B_gB], dma=True, part=(j > 0))
        fg = sb.alloc("fg", [128, D], F32); B_fg = Buf("fg")
        R.add("sp", lambda: nc.sync.dma_start(out=fg[:], in_=bass.AP(fng.ap().tensor, 0, [[0, 128], [1, D]])), [B_in], [B_fg], dma=True)
        P2 = range(2)
        ct_ = [sb.alloc("ctE", [128, 16, 512], BF16) for _ in P2]; B_ct = [Buf("ctE%d" % i) for i in P2]
        xr = [sb.alloc("xr", [128, D], F32) for _ in P2]; B_xr = [Buf("xr%d" % i) for i in P2]
        ty = [sb.alloc("ty", [128, D], F32) for _ in P2]; B_ty = [Buf("ty%d" % i) for i in P2]
        xo = [sb.alloc("xo", [128, D], F32) for _ in P2]; B_xo = [Buf("xo%d" % i) for i in P2]
        junk = sb.alloc("junkE", [128, D], BF16); B_junk = Buf("junkE")
        st = [sb.alloc("stE", [128, 4], F32) for _ in P2]; B_st = [Buf("stE%d" % i) for i in P2]
        fo = [sb.alloc("fo", [128, D], F32) for _ in P2]; B_fo = [Buf("fo%d" % i) for i in P2]
        for j in range(9):
            if last and j == 8:
                continue
            w_ = 512 if j < 8 else 256
            jp = j % 2
            R.add("sp", lambda jp=jp, j=j, w_=w_: nc.sync.dma_start(out=ct_[jp][:, :, 0:w_], in_=cat_d[:, j * 512:j * 512 + w_].rearrange("(c p) t -> p c t", p=128)),
                  [B_cat], [B_ct[jp]], dma=True)
            for q in range(w_ // 128):
                tt = j * 4 + q
                pp = tt % 2
                jj = 0 if tt < 32 else 1
                if li == 0:
                    src = x_in[tt * 128:(tt + 1) * 128, :] if tt < 32 else ctx_in[(tt - 32) * 128:(tt - 31) * 128, :]
                    srcb = B_in
                else:
                    src = x1_d[tt * 128:(tt + 1) * 128, :]
                    srcb = B_x1
                R.add("sp", lambda pp=pp, src=src: nc.sync.dma_start(out=xr[pp][:], in_=src), [srcb], [B_xr[pp]], dma=True)
                for hf_ in range(2):
                    bk = 2 * pp + hf_
                    for c in range(16):
                        R.add("pe", lambda c=c, hf_=hf_, bk=bk, jp=jp, q=q: nc.tensor.matmul(
                            PS[bk][:, :], lhsT=ct_[jp][:, c, q * 128:(q + 1) * 128], rhs=wo[:, c, hf_ * 512:(hf_ + 1) * 512], start=(c == 0), stop=(c == 15)),
                            [B_ct[jp], B_wo], [PB[bk]], part=(c > 0))
                    R.add("dve", lambda hf_=hf_, bk=bk, pp=pp, jj=jj: nc.vector.tensor_tensor(
                        out=ty[pp][:, hf_ * 512:(hf_ + 1) * 512], in0=PS[bk][:, :], in1=gB[:, jj, hf_ * 512:(hf_ + 1) * 512], op=ALU.mult),
                        [PB[bk], B_gB], [B_ty[pp]], part=(hf_ > 0))
                R.add("pool", lambda pp=pp: nc.gpsimd.tensor_tensor(out=xo[pp][:], in0=ty[pp][:], in1=xr[pp][:], op=ALU.add), [B_ty[pp], B_xr[pp]], [B_xo[pp]])
                if not last:
                    R.add("sp", lambda pp=pp, tt=tt: nc.sync.dma_start(out=x1_d[tt * 128:(tt + 1) * 128, :], in_=xo[pp][:]), [B_xo[pp]], [B_x1], dma=B_xo[pp], part=True)
                else:
                    R.add("act", lambda pp=pp: nc.scalar.activation(out=junk[:], in_=xo[pp][:], func=AF.Square, accum_out=st[pp][:, 0:1]), [B_xo[pp]], [B_junk, B_st[pp]])
                    R.add("act", lambda pp=pp: nc.scalar.activation(out=st[pp][:, 1:2], in_=st[pp][:, 0:1], func=AF.Sqrt, bias=eps_t[:, 0:1], scale=1.0 / D),
                          [B_st[pp], B_eps], [B_st[pp]], part=True)
                    R.add("dve", lambda pp=pp: nc.vector.reciprocal(out=st[pp][:, 2:3], in_=st[pp][:, 1:2]), [B_st[pp]], [B_st[pp]], part=True)
                    R.add("dve", lambda pp=pp: nc.vector.scalar_tensor_tensor(out=fo[pp][:], in0=xo[pp][:], scalar=st[pp][:, 2:3], in1=fg[:], op0=ALU.mult, op1=ALU.mult),
                          [B_xo[pp], B_st[pp], B_fg], [B_fo[pp]])
                    R.add("sp", lambda pp=pp, tt=tt: nc.sync.dma_start(out=out_d[tt * 128:(tt + 1) * 128, :], in_=fo[pp][:]), [B_fo[pp]], [B_out], dma=B_fo[pp], part=True)
        R.barrier()
        R.release([B_wos, B_gB, B_fg] + B_ct + B_xr + B_xo + B_fo)


    R.barrier()
    for li in range(nlayers):
        phase1(li)
        R.barrier()
        if phases >= 2:
            phaseA(li)
        last = (li == DEPTH - 1) and nlayers == DEPTH
        if phases >= 3 and "skipB" not in taps:
            phaseB(li, last)
        if phases >= 4 and "skipC" not in taps:
            phaseC(li, last)
        if phases >= 5:
            phaseD(li, last)
        if phases >= 6:
            phaseE(li, last)
    dbg_fence = []
    if "hT" in taps:
        hT_dbg = nc.dram_tensor("hT_dbg", [128, 8, T], BF16, kind="ExternalOutput")
        B_dbg = Buf("dbg")
        R.add("sp", lambda: nc.sync.dma_start(out=hT_dbg.ap(), in_=hT[:]), B_hT, [B_dbg], dma=True)
        dbg_fence.append(B_dbg)
    R.barrier()
    R.emit()
    return dram_in


def host_inputs(inp, b):
    f = np.float32
    fm = lambda v, nch: np.ascontiguousarray(np.asarray(v, f).reshape(nch, 128).T)
    m = {}
    m["x"] = np.ascontiguousarray(inp["x"][b], f)
    m["ctx"] = np.ascontiguousarray(inp["ctx"][b], f)
    m["cvec"] = np.ascontiguousarray(np.stack([fm(inp["c"][b], 8), fm(inp["c_ctx"], 8)], axis=-1))
    m["ada_w"] = np.ascontiguousarray(inp["ada_w"], f)
    m["ada_b_fm"] = np.stack([fm(inp["ada_b"][l], 24) for l in range(DEPTH)])
    m["ada_b_gate"] = np.ascontiguousarray(inp["ada_b"][:, 2048:3072], f)
    m["norm_g_fm"] = np.stack([fm(inp["norm_g"][l], 8) for l in range(DEPTH)])
    m["w_in"] = np.ascontiguousarray(inp["w_in"], f)
    m["ssd_conv_w_fm"] = np.ascontiguousarray(
        np.asarray(inp["ssd_conv_w"], f).reshape(DEPTH, 5, 10, 128).transpose(0, 3, 2, 1))
    m["ssd_conv_b_fm"] = np.stack([fm(inp["ssd_conv_b"][l], 10) for l in range(DEPTH)])
    m["ssd_dt_bias"] = np.ascontiguousarray(inp["ssd_dt_bias"], f)
    m["ssd_a_log"] = np.ascontiguousarray(inp["ssd_a_log"], f)
    m["ssd_d"] = np.ascontiguousarray(inp["ssd_d"], f)
    m["ssd_norm_g"] = np.ascontiguousarray(inp["ssd_norm_g"], f)
    qg = np.asarray(inp["q_norm_g"], f)
    kg = np.asarray(inp["k_norm_g"], f)
    m["qk_g_fm"] = np.ascontiguousarray(np.stack([np.tile(qg, (1, 2)), np.tile(kg, (1, 2))], axis=-1))
    m["cm_conv_w_fm"] = np.ascontiguousarray(
        np.asarray(inp["cm_conv_w"], f).reshape(DEPTH, 31, 4, 128).transpose(0, 3, 2, 1))
    m["cm_vec_fm"] = np.ascontiguousarray(np.stack(
        [np.stack([fm(inp[k][l], 4) for k in ("cm_conv_b", "cm_ln_g", "cm_ln_b", "cm_pw_b")], axis=-1)
         for l in range(DEPTH)]))
    m["cm_pw_w"] = np.ascontiguousarray(inp["cm_pw_w"], f)
    m["w_out"] = np.ascontiguousarray(inp["w_out"], f)
    m["final_norm_g"] = np.ascontiguousarray(inp["final_norm_g"], f)
    m.update(const_inputs())
    return m


_CONST = None


def const_inputs():
    global _CONST
    if _CONST is not None:
        return _CONST
    f = np.float32
    i = np.arange(128)
    ident = np.eye(128, dtype=f)
    U = (i[:, None] <= i[None, :]).astype(f)
    UT = (i[:, None] >= i[None, :]).astype(f)
    d = i % 64
    half = (d % 32) // 16
    partner = np.where(half == 0, i + 16, i - 16)
    perm = np.zeros((128, 128), f)
    perm[partner, i] = 1.0
    bd = ((i[:, None] // 64) == (i[None, :] // 64)).astype(f) / 64.0
    ones = np.ones((128, 128), f)
    consts = np.stack([ident, U, UT, perm, bd, ones], axis=1)
    mf = np.where(i[:, None] <= i[None, :], 0.0, NEG).astype(f)
    mb = np.where(i[:, None] >= i[None, :], 0.0, NEG).astype(f)
    masks = np.stack([mf, mb], axis=1)
    u = np.arange(SEQ)
    pos = np.stack([u // 64, u % 64], axis=0).astype(np.float64)
    inv = 10000.0 ** (-np.arange(16, dtype=np.float64) / 16)
    fq = d % 16
    axis = d // 32
    ang = pos[axis][:, :] * inv[fq][:, None]
    ang = (pos[axis].astype(f) * inv.astype(f)[fq][:, None]).astype(f)
    cos = np.cos(ang).astype(f)
    sin = np.sin(ang).astype(f)
    sgn = np.where(half == 0, -1.0, 1.0).astype(f)[:, None]
    COS = np.concatenate([cos, np.ones((128, NCTX), f)], axis=1)
    SIN = np.concatenate([sin * sgn, np.zeros((128, NCTX), f)], axis=1)
    rope = np.stack([COS, SIN], axis=1)
    _CONST = {"consts": np.ascontiguousarray(consts), "masks": np.ascontiguousarray(masks),
              "rope": np.ascontiguousarray(rope)}
    return _CONST


def kernel(**inputs):
    inputs = {k: np.asarray(v) for k, v in inputs.items()}
    nc = bass.Bass("TRN2", target_bir_lowering=False)
    build(nc)
    in_maps = [host_inputs(inputs, c % 4) for c in range(8)]
    res = run_bass_kernel_spmd(nc, in_maps, core_ids=list(range(8)))
    return np.stack([res.results[b]["out"] for b in range(4)], axis=0).astype(np.float32)
```

```python
import numpy as np
import ml_dtypes
import concourse.bass as bass
import concourse.mybir as mybir
from concourse.bass_utils import run_bass_kernel_spmd

F32 = mybir.dt.float32
BF16 = mybir.dt.bfloat16
AF = mybir.ActivationFunctionType
ALU = mybir.AluOpType
AX = mybir.AxisListType

D = 1024
SEQ = 4096
NCTX = 256
T = SEQ + NCTX
DEPTH = 2
INW = 5656
EPS = 1e-6
O_Z, O_X, O_B, O_C, O_DT, O_Q, O_K, O_V, O_GA, O_UA, O_UB, O_GC = 0, 768, 1536, 1792, 2048, 2072, 2840, 3096, 3352, 4120, 4632, 5144
NEG = -30000.0


class Buf:
    __slots__ = ("name", "w", "r", "excl", "semi", "dcount")

    def __init__(self, name, excl=False):
        self.name = name
        self.w = {}
        self.r = {}
        self.excl = excl
        self.semi = None
        self.dcount = 0


class Op:
    __slots__ = ("eng", "idx", "fn", "deps", "dma", "sig", "semi", "dval")


def _merge(dst, src):
    for k, v in src.items():
        if dst.get(k, -1) < v:
            dst[k] = v


class Rec:
    ENG = ("pe", "act", "dve", "pool", "sp")

    def __init__(self, nc):
        self.nc = nc
        self.ops = {e: [] for e in self.ENG}
        self.ndma_sems = 0
        self.free = []
        self.semcount = {}
        self.eobj = {"pe": nc.tensor, "act": nc.scalar, "dve": nc.vector, "pool": nc.gpsimd, "sp": nc.sync}

    def add(self, eng, fn, reads=(), writes=(), dma=None, part=False):
        if dma is True:
            dma = writes[0]
        op = Op()
        op.eng, op.fn, op.dma, op.sig = eng, fn, dma is not None, False
        op.idx = len(self.ops[eng])
        raw, oth = {}, {}
        for b in reads:
            _merge(raw, b.w)
            if b.excl:
                _merge(oth, b.r)
        for b in writes:
            _merge(oth, b.r)
            if not part or b.excl:
                _merge(oth, b.w)
        if dma is None:
            oth.pop(("c", eng), None)
            if eng == "pe":
                raw.pop(("c", eng), None)
        deps = raw
        _merge(deps, oth)
        op.deps = deps
        if dma is not None:
            b = dma
            if b.semi is None:
                if self.free:
                    b.semi, b.dcount = self.free.pop()
                    _merge(deps, {("d", b.semi): b.dcount})
                else:
                    b.semi = self.ndma_sems
                    self.ndma_sems += 1
            b.dcount += 16
            self.semcount[b.semi] = b.dcount
            op.semi, op.dval = b.semi, b.dcount
            my = {("d", b.semi): b.dcount}
        else:
            my = {("c", eng): op.idx}
        for b in reads:
            _merge(b.r, my)
        for b in writes:
            if part:
                _merge(b.w, my)
            else:
                b.w = dict(my)
                b.r = {}
        self.ops[eng].append(op)
        return op

    def release(self, bufs):
        for b in bufs:
            if b.semi is not None:
                self.free.append((b.semi, b.dcount))
                b.semi = None

    def barrier(self):
        deps = {("c", e): len(self.ops[e]) - 1 for e in self.ENG if self.ops[e]}
        for s_, c_ in self.semcount.items():
            deps[("d", s_)] = c_
        for e in self.ENG:
            op = Op()
            op.eng, op.fn, op.dma, op.sig, op.idx = e, None, False, False, len(self.ops[e])
            op.deps = {k: v for k, v in deps.items() if k != ("c", e)}
            self.ops[e].append(op)

    def emit(self):
        nc = self.nc
        for e in self.ENG:
            for op in self.ops[e]:
                nd = {}
                for k, v in op.deps.items():
                    if k[0] == "c":
                        lst = self.ops[k[1]]
                        while v >= 0 and lst[v].fn is None:
                            v -= 1
                        if v < 0:
                            continue
                        lst[v].sig = True
                    nd[k] = v
                op.deps = nd
        pref = {}
        for e in self.ENG:
            c = 0
            arr = []
            for op in self.ops[e]:
                if op.sig and not op.dma:
                    c += 1
                arr.append(c)
            pref[e] = arr
        assert self.ndma_sems + 5 <= 98, self.ndma_sems
        esem = {e: nc.alloc_semaphore(name="es_" + e) for e in self.ENG}
        dsem = [nc.alloc_semaphore(name="ds_%d" % i) for i in range(self.ndma_sems)]
        for e in self.ENG:
            eo = self.eobj[e]
            known = {}
            for op in self.ops[e]:
                for k, v in op.deps.items():
                    if k[0] == "c":
                        sem, val, key = esem[k[1]], pref[k[1]][v], k
                    else:
                        sem, val, key = dsem[k[1]], v, k
                    if known.get(key, 0) >= val:
                        continue
                    known[key] = val
                    eo.wait_ge(sem, val)
                if op.fn is None:
                    continue
                ins = op.fn()
                if op.dma:
                    ins.then_inc(dsem[op.semi], 16)
                elif op.sig:
                    ins.then_inc(esem[e], 1)


class SB:
    ARENA = None
    ABYTES = 204800

    def __init__(self, nc, base=0, limit=None):
        if SB.ARENA is None or SB.ARENA[0] is not nc:
            SB.ARENA = (nc, nc.alloc_sbuf_tensor("arena", [128, SB.ABYTES // 4], F32))
        self.nc, self.off, self.limit = nc, base, (limit or SB.ABYTES)

    def alloc(self, name, shape, dt):
        return self.at(name, shape, dt, None)

    def at(self, name, shape, dt, off):
        esz = 4 if dt == F32 else 2
        n = int(np.prod(shape[1:]))
        nb = (n * esz + 63) // 64 * 64
        if off is None:
            off = self.off
            self.off += nb
        assert off % 4 == 0 and off + nb <= self.limit, (name, off, nb, self.limit)
        A = SB.ARENA[1]
        ap = A[0:shape[0], off // 4: off // 4 + nb // 4]
        if dt != F32:
            ap = ap.bitcast(dt)
        ap = ap[:, 0:n]
        if len(shape) > 2:
            names = " ".join("d%d" % i for i in range(len(shape) - 1))
            kw = {"d%d" % i: int(shape[i + 1]) for i in range(len(shape) - 1)}
            ap = ap.rearrange("p (%s) -> p %s" % (names, names), **kw)
        return ap


def fap(ap, dims):
    return bass.AP(ap.tensor, ap.offset, [list(ap.ap[0])] + [list(d) for d in dims])


def build(nc, phases=99, nlayers=DEPTH, taps=()):
    R = Rec(nc)
    dram_in = {}

    def din(name, shape, dt=F32):
        dram_in[name] = nc.dram_tensor(name, list(shape), dt, kind="ExternalInput")
        return dram_in[name]

    x_in = din("x", [SEQ, D])
    ctx_in = din("ctx", [NCTX, D])
    cvec = din("cvec", [128, 8, 2])
    ada_w = din("ada_w", [DEPTH, D, 3 * D])
    ada_b_fm = din("ada_b_fm", [DEPTH, 128, 24])
    ada_b_gate = din("ada_b_gate", [DEPTH, D])
    norm_g_fm = din("norm_g_fm", [DEPTH, 128, 8])
    w_in = din("w_in", [DEPTH, D, INW])
    scw = din("ssd_conv_w_fm", [DEPTH, 128, 10, 5])
    scb = din("ssd_conv_b_fm", [DEPTH, 128, 10])
    dtb = din("ssd_dt_bias", [DEPTH, 24])
    alog = din("ssd_a_log", [DEPTH, 24])
    ssdd = din("ssd_d", [DEPTH, 12])
    ssdng = din("ssd_norm_g", [DEPTH, 768])
    qkg = din("qk_g_fm", [DEPTH, 128, 2])
    ccw = din("cm_conv_w_fm", [DEPTH, 128, 4, 31])
    cmv = din("cm_vec_fm", [DEPTH, 128, 4, 4])
    cpw = din("cm_pw_w", [DEPTH, 512, 512])
    w_out = din("w_out", [DEPTH, 2048, D])
    fng = din("final_norm_g", [D])
    consts = din("consts", [128, 6, 128])
    masks = din("masks", [128, 2, 128])
    rope = din("rope", [128, 2, T])
    out_d = nc.dram_tensor("out", [SEQ, D], F32, kind="ExternalOutput")

    def scratch(name, shape, dt):
        kind = "ExternalOutput" if name in taps else "Internal"
        return nc.dram_tensor(name, list(shape), dt, kind=kind)

    gate_d = scratch("gate_d", [DEPTH, 2, 128, D], F32)
    x1_d = scratch("x1_d", [T, D], F32)
    xbc_d = scratch("xbc_d", [1280, T], F32)
    dt_d = scratch("dt_d", [T, 24], F32)
    zs_d = scratch("zs_d", [T, 768], F32)
    qT_d = scratch("qT_d", [768, T], BF16)
    kT_d = scratch("kT_d", [256, T], BF16)
    v_d = scratch("v_d", [T, 256], BF16)
    gas_d = scratch("gas_d", [768, T], F32)
    cv_d = scratch("cv_d", [512, T], F32)
    gcs_d = scratch("gcs_d", [512, T], F32)
    cat_d = scratch("cat_d", [2048, T], BF16)
    hb_d = scratch("hb_d", [34, 128, 768], BF16)
    B_gate = Buf("gate_d"); B_x1 = Buf("x1_d"); B_xbc = Buf("xbc_d"); B_dt = Buf("dt_d"); B_zs = Buf("zs_d")
    B_qT = Buf("qT_d"); B_kT = Buf("kT_d"); B_v = Buf("v_d"); B_gas = Buf("gas_d"); B_cv = Buf("cv_d")
    B_gcs = Buf("gcs_d"); B_cat = Buf("cat_d"); B_hb = Buf("hb_d"); B_out = Buf("out")
    B_in = Buf("inputs")

    PS = [nc.alloc_psum_tensor("ps%d" % i, [128, 512], F32) for i in range(8)]
    PB = [Buf("psb%d" % i, excl=True) for i in range(8)]

    sbp = SB(nc, 0, 24 * 1024)
    c_t = sbp.alloc("consts", [128, 6, 128], F32)
    cb_t = sbp.alloc("constsb", [128, 6, 128], BF16)
    mk_t = sbp.alloc("masks", [128, 2, 128], F32)
    eps_t = sbp.alloc("eps", [128, 2], F32)
    AB_t = sbp.alloc("AB", [128, DEPTH, 2, 8, 2], F32)
    B_c = Buf("consts"); B_AB = Buf("AB")
    IDENT, UIN, UTR, PERM, BDM, ONES = range(6)
    R.add("sp", lambda: nc.sync.dma_start(out=c_t[:], in_=consts.ap()), [B_in], [B_c], dma=True)
    B_mk = Buf("mk")
    R.add("sp", lambda: nc.sync.dma_start(out=mk_t[:], in_=masks.ap()), [B_in], [B_mk], dma=True)
    B_cb = Buf("cb")
    R.add("dve", lambda: nc.vector.tensor_copy(out=cb_t[:], in_=c_t[:]), [B_c], [B_cb])
    B_eps = Buf("eps")
    R.add("pool", lambda: nc.gpsimd.memset(eps_t[:, 0:1], EPS), [], [B_eps])
    R.add("pool", lambda: nc.gpsimd.memset(eps_t[:, 1:2], 1.0), [], [B_eps], part=True)

    WORK0 = 24 * 1024

    def phase0():
        sb = SB(nc, WORK0)
        aw = sb.alloc("aw", [128, 8, 3072], F32)
        B_aw = [Buf("aw%d" % k) for k in range(8)]
        cv = sb.alloc("cv", [128, 8, 2], F32)
        sc = sb.alloc("sc", [128, 8, 2], F32)
        screp = sb.alloc("screp", [128, 8, 2, 128], F32)
        abf = sb.alloc("abf", [128, 24], F32)
        ngf = sb.alloc("ngf", [128, 8], F32)
        abg = sb.alloc("abg", [128, D], F32)
        mod = sb.alloc("mod", [128, 16, 2], F32)
        gt = sb.alloc("gt", [128, 2, D], F32)
        B_cv, B_sc, B_screp, B_abf, B_ngf, B_abg, B_mod, B_gt = [Buf(n) for n in "cv sc screp abf ngf abg mod gt".split()]
        R.add("sp", lambda: nc.sync.dma_start(out=cv[:], in_=cvec.ap()), [B_in], [B_cv], dma=True)
        R.add("act", lambda: nc.scalar.activation(out=sc[:], in_=cv[:], func=AF.Silu), [B_cv], [B_sc])
        for kc in range(8):
            for j in range(2):
                R.add("dve", lambda kc=kc, j=j: nc.vector.tensor_copy(
                    out=screp[:, kc, j, :], in_=fap(sc[:, kc, j:j + 1], [[0, 128]])), [B_sc], [B_screp], part=True)
        for li in range(nlayers):
            for kc in range(8):
                R.add("sp", lambda kc=kc, li=li: nc.sync.dma_start(
                    out=aw[:, kc, :], in_=ada_w[li, kc * 128:(kc + 1) * 128, :]), [B_in], [B_aw[kc]], dma=True)
            R.add("sp", lambda li=li: nc.sync.dma_start(out=abf[:], in_=ada_b_fm[li]), [B_in], [B_abf], dma=True)
            R.add("sp", lambda li=li: nc.sync.dma_start(out=ngf[:], in_=norm_g_fm[li]), [B_in], [B_ngf], dma=True)
            R.add("sp", lambda li=li: nc.sync.dma_start(
                out=abg[:], in_=bass.AP(ada_b_gate.ap().tensor, li * D, [[0, 128], [1, D]])), [B_in], [B_abg], dma=True)
            for fc in range(16):
                for kc in range(8):
                    R.add("pe", lambda fc=fc, kc=kc: nc.tensor.matmul(
                        PS[0][:, fc * 2:fc * 2 + 2], lhsT=aw[:, kc, fc * 128:(fc + 1) * 128], rhs=sc[:, kc, :],
                        start=(kc == 0), stop=(kc == 7)), [B_aw[kc], B_sc], [PB[0]], part=not (fc == 0 and kc == 0))
            R.add("dve", lambda: nc.vector.tensor_tensor(
                out=mod[:], in0=fap(PS[0][:, 0:32], [[2, 16], [1, 2]]), in1=fap(abf[:, 0:16], [[1, 16], [0, 2]]),
                op=ALU.add), [PB[0], B_abf], [B_mod])
            R.add("dve", lambda li=li: nc.vector.scalar_tensor_tensor(
                out=AB_t[:, li, 0, :, :], in0=mod[:, 8:16, :], scalar=1.0, in1=fap(ngf[:, 0:8], [[1, 8], [0, 2]]),
                op0=ALU.add, op1=ALU.mult), [B_mod, B_ngf], [B_AB], part=True)
            R.add("dve", lambda li=li: nc.vector.tensor_copy(out=AB_t[:, li, 1, :, :], in_=mod[:, 0:8, :]),
                  [B_mod], [B_AB], part=True)
            for j in range(2):
                for cc in range(2):
                    pb = 1 + (j * 2 + cc) % 2
                    for kc in range(8):
                        R.add("pe", lambda j=j, cc=cc, kc=kc, pb=pb: nc.tensor.matmul(
                            PS[pb][:, :], lhsT=screp[:, kc, j, :], rhs=aw[:, kc, 2048 + cc * 512:2048 + (cc + 1) * 512],
                            start=(kc == 0), stop=(kc == 7)), [B_screp, B_aw[kc]], [PB[pb]], part=(kc > 0))
                    R.add("dve", lambda j=j, cc=cc, pb=pb: nc.vector.tensor_tensor(
                        out=gt[:, j, cc * 512:(cc + 1) * 512], in0=PS[pb][:, :], in1=abg[:, cc * 512:(cc + 1) * 512],
                        op=ALU.add), [PB[pb], B_abg], [B_gt], part=not (j == 0 and cc == 0))
            for j in range(2):
                R.add("sp", lambda li=li, j=j: nc.sync.dma_start(out=gate_d[li, j], in_=gt[:, j, :]),
                      [B_gt], [B_gate], dma=B_gt, part=True)

    phase0()
    if phases <= 0:
        R.emit()
        return dram_in

    HT_OFF = WORK0
    hT = SB(nc).at("hT", [128, 8, T], BF16, HT_OFF)
    HT_BYTES = 8 * T * 2
    B_hT = [Buf("hT%d" % i) for i in range(34)]
    WORK1 = HT_OFF + HT_BYTES

    def phase1(li):
        sb = SB(nc, WORK1)
        xt = [sb.alloc("xt", [128, D], F32) for _ in range(2)]
        xn = [sb.alloc("xn", [128, D], F32) for _ in range(2)]
        junk = sb.alloc("junk", [128, D], BF16)
        st = [sb.alloc("st", [128, 4], F32) for _ in range(2)]
        B_xt = [Buf("xt%d" % i) for i in range(2)]
        B_xn = [Buf("xn%d" % i) for i in range(2)]
        B_junk = Buf("junk")
        B_st = [Buf("st%d" % i) for i in range(2)]
        for tt in range(34):
            s = tt % 2
            j = 0 if tt < 32 else 1
            if li == 0:
                src = x_in[tt * 128:(tt + 1) * 128, :] if tt < 32 else ctx_in[(tt - 32) * 128:(tt - 31) * 128, :]
                srcb = B_in
            else:
                src = x1_d[tt * 128:(tt + 1) * 128, :]
                srcb = B_x1
            R.add("sp", lambda s=s, src=src: nc.sync.dma_start(out=xt[s][:], in_=src), [srcb], [B_xt[s]], dma=True)
            R.add("act", lambda s=s: nc.scalar.activation(out=junk[:], in_=xt[s][:], func=AF.Square,
                                                          accum_out=st[s][:, 0:1]), [B_xt[s]], [B_junk, B_st[s]])
            R.add("act", lambda s=s: nc.scalar.activation(out=st[s][:, 1:2], in_=st[s][:, 0:1], func=AF.Sqrt,
                                                          bias=eps_t[:, 0:1], scale=1.0 / D), [B_st[s], B_eps], [B_st[s]], part=True)
            R.add("dve", lambda s=s: nc.vector.reciprocal(out=st[s][:, 2:3], in_=st[s][:, 1:2]), [B_st[s]], [B_st[s]], part=True)
            R.add("dve", lambda s=s: nc.vector.tensor_scalar(out=xn[s][:], in0=xt[s][:], scalar1=st[s][:, 2:3], scalar2=None,
                                                             op0=ALU.mult), [B_xt[s], B_st[s]], [B_xn[s]])
            pb0 = 2 * (tt % 2)
            for c in range(8):
                pb = pb0 + c // 4
                R.add("pe", lambda s=s, c=c, pb=pb: nc.tensor.transpose(
                    PS[pb][:, (c % 4) * 128:(c % 4 + 1) * 128], xn[s][:, c * 128:(c + 1) * 128], c_t[:, IDENT, :]),
                    [B_xn[s], B_c], [PB[pb]], part=(c % 4 > 0))
            for c in range(8):
                pb = pb0 + c // 4
                if c % 2 == 0:
                    R.add("act", lambda c=c, pb=pb, tt=tt, j=j: nc.scalar.activation(
                        out=hT[:, c, tt * 128:(tt + 1) * 128], in_=PS[pb][:, (c % 4) * 128:(c % 4 + 1) * 128],
                        func=AF.Identity, bias=AB_t[:, li, 1, c, j:j + 1], scale=AB_t[:, li, 0, c, j:j + 1]),
                        [PB[pb], B_AB], [B_hT[tt]], part=True)
                else:
                    R.add("dve", lambda c=c, pb=pb, tt=tt, j=j: nc.vector.tensor_scalar(
                        out=hT[:, c, tt * 128:(tt + 1) * 128], in0=PS[pb][:, (c % 4) * 128:(c % 4 + 1) * 128],
                        scalar1=AB_t[:, li, 0, c, j:j + 1], scalar2=AB_t[:, li, 1, c, j:j + 1],
                        op0=ALU.mult, op1=ALU.add), [PB[pb], B_AB], [B_hT[tt]], part=True)

    def wsrc(li, c0, ncol):
        return w_in[li, :, c0:c0 + ncol].rearrange("(kc p) n -> p kc n", p=128)

    def phaseA(li):
        sb = SB(nc, WORK1)
        wst = [sb.alloc("wst", [128, 8, 128], F32) for _ in range(2)]
        wbf = [sb.alloc("wbf", [128, 8, 128], BF16) for _ in range(2)]
        B_wst = [Buf("wst%d" % i) for i in range(2)]
        B_wbf = [Buf("wbf%d" % i) for i in range(2)]
        ot = [sb.alloc("ot", [128, 512], F32) for _ in range(3)]
        B_ot = [Buf("ot%d" % i) for i in range(3)]
        obt = [sb.alloc("obt", [128, 512], BF16) for _ in range(2)]
        B_obt = [Buf("obt%d" % i) for i in range(2)]
        vecs = sb.alloc("vecs", [128, 10 * 5 + 10 + 2 + 4 * 31 + 16], F32)
        B_vecs = Buf("vecs")
        V_SCW, V_SCB, V_QKG, V_CCW, V_CMV = 0, 50, 60, 62, 62 + 124
        R.add("sp", lambda: nc.sync.dma_start(out=vecs[:, V_SCW:V_SCW + 50], in_=scw[li].rearrange("p a b -> p (a b)")), [B_in], [B_vecs], dma=True)
        R.add("sp", lambda: nc.sync.dma_start(out=vecs[:, V_SCB:V_SCB + 10], in_=scb[li]), [B_in], [B_vecs], dma=True, part=True)
        R.add("sp", lambda: nc.sync.dma_start(out=vecs[:, V_QKG:V_QKG + 2], in_=qkg[li]), [B_in], [B_vecs], dma=True, part=True)
        R.add("sp", lambda: nc.sync.dma_start(out=vecs[:, V_CCW:V_CCW + 124], in_=ccw[li].rearrange("p a b -> p (a b)")), [B_in], [B_vecs], dma=True, part=True)
        R.add("sp", lambda: nc.sync.dma_start(out=vecs[:, V_CMV:V_CMV + 16], in_=cmv[li].rearrange("p a b -> p (a b)")), [B_in], [B_vecs], dma=True, part=True)
        sub0 = sb.off
        cnt = {"w": 0, "ot": 0, "obt": 0, "ps": 0}

        def wload(c0):
            s_ = cnt["w"] % 2
            cnt["w"] += 1
            R.add("sp", lambda: nc.sync.dma_start(out=wst[s_][:], in_=wsrc(li, c0, 128)), [B_in], [B_wst[s_]], dma=True)
            R.add("pool", lambda: nc.gpsimd.tensor_copy(out=wbf[s_][:], in_=wst[s_][:]), [B_wst[s_]], [B_wbf[s_]])
            return s_

        def proj(ws, j, pb):
            w_ = 512 if j < 8 else 256
            for kc in range(8):
                R.add("pe", lambda kc=kc: nc.tensor.matmul(PS[pb][:, 0:w_], lhsT=wbf[ws][:, kc, :], rhs=hT[:, kc, j * 512:j * 512 + w_],
                                                          start=(kc == 0), stop=(kc == 7)),
                      [B_wbf[ws]] + B_hT[j * 4:j * 4 + w_ // 128], [PB[pb]], part=(kc > 0))
            return w_

        def next_ot():
            s_ = cnt["ot"] % 3
            cnt["ot"] += 1
            return s_

        def next_obt():
            s_ = cnt["obt"] % 2
            cnt["obt"] += 1
            return s_

        for (c0, nch, dst, B_dst) in ((O_GA, 6, gas_d, B_gas), (O_GC, 4, gcs_d, B_gcs)):
            for ch in range(nch):
                ws = wload(c0 + ch * 128)
                for j in range(9):
                    pb = cnt["ps"] % 2
                    cnt["ps"] += 1
                    w_ = proj(ws, j, pb)
                    o_ = next_ot()
                    R.add("act", lambda pb=pb, o_=o_, w_=w_: nc.scalar.activation(out=ot[o_][:, 0:w_], in_=PS[pb][:, 0:w_], func=AF.Silu),
                          [PB[pb]], [B_ot[o_]])
                    R.add("sp", lambda o_=o_, w_=w_, ch=ch, j=j, dst=dst: nc.sync.dma_start(
                        out=dst[ch * 128:(ch + 1) * 128, j * 512:j * 512 + w_], in_=ot[o_][:, 0:w_]), [B_ot[o_]], [B_dst], dma=B_ot[o_], part=True)

        sbq = SB(nc, sub0)
        ropet = sbq.alloc("rope", [128, 2, T], F32)
        B_rope = Buf("rope")
        R.add("sp", lambda: nc.sync.dma_start(out=ropet[:, 0, :], in_=rope[:, 0, :]), [B_in], [B_rope], dma=True)
        R.add("sp", lambda: nc.sync.dma_start(out=ropet[:, 1, :], in_=rope[:, 1, :]), [B_in], [B_rope], dma=True, part=True)
        sqb = sbq.alloc("sqb", [128, 512], BF16); B_sqb = Buf("sqb")
        qf = sbq.alloc("qf", [128, 512], F32); B_qf = Buf("qf")
        sd = sbq.alloc("sd", [128, 512], F32); B_sd = Buf("sd")
        rs = sbq.alloc("rs", [128, 512], F32); B_rs = Buf("rs")
        qn = sbq.alloc("qn", [128, 512], F32); B_qn = Buf("qn")
        qnb = sbq.alloc("qnb", [128, 512], BF16); B_qnb = Buf("qnb")
        t1 = sbq.alloc("t1", [128, 512], F32); B_t1 = Buf("t1")
        t2 = sbq.alloc("t2", [128, 512], F32); B_t2 = Buf("t2")
        for (c0, nch, dst, B_dst, gi) in ((O_Q, 6, qT_d, B_qT, 0), (O_K, 2, kT_d, B_kT, 1)):
            for ch in range(nch):
                ws = wload(c0 + ch * 128)
                for j in range(9):
                    pb = cnt["ps"] % 2
                    cnt["ps"] += 1
                    w_ = proj(ws, j, pb)
                    cs = slice(j * 512, j * 512 + w_)
                    R.add("act", lambda pb=pb, w_=w_: nc.scalar.activation(out=sqb[:, 0:w_], in_=PS[pb][:, 0:w_], func=AF.Square), [PB[pb]], [B_sqb])
                    R.add("act", lambda pb=pb, w_=w_: nc.scalar.activation(out=qf[:, 0:w_], in_=PS[pb][:, 0:w_], func=AF.Copy), [PB[pb]], [B_qf])
                    R.add("pe", lambda w_=w_: nc.tensor.matmul(PS[4][:, 0:w_], lhsT=cb_t[:, BDM, :], rhs=sqb[:, 0:w_], start=True, stop=True),
                          [B_cb, B_sqb], [PB[4]])
                    R.add("act", lambda w_=w_: nc.scalar.activation(out=sd[:, 0:w_], in_=PS[4][:, 0:w_], func=AF.Sqrt, bias=eps_t[:, 0:1], scale=1.0),
                          [PB[4], B_eps], [B_sd])
                    R.add("dve", lambda w_=w_: nc.vector.reciprocal(out=rs[:, 0:w_], in_=sd[:, 0:w_]), [B_sd], [B_rs])
                    R.add("dve", lambda w_=w_, gi=gi: nc.vector.scalar_tensor_tensor(
                        out=qn[:, 0:w_], in0=qf[:, 0:w_], scalar=vecs[:, V_QKG + gi:V_QKG + gi + 1], in1=rs[:, 0:w_], op0=ALU.mult, op1=ALU.mult),
                        [B_qf, B_vecs, B_rs], [B_qn])
                    R.add("act", lambda w_=w_: nc.scalar.activation(out=qnb[:, 0:w_], in_=qn[:, 0:w_], func=AF.Copy), [B_qn], [B_qnb])
                    R.add("pe", lambda w_=w_: nc.tensor.matmul(PS[5][:, 0:w_], lhsT=cb_t[:, PERM, :], rhs=qnb[:, 0:w_], start=True, stop=True),
                          [B_cb, B_qnb], [PB[5]])
                    R.add("pool", lambda w_=w_, cs=cs: nc.gpsimd.tensor_tensor(out=t1[:, 0:w_], in0=qn[:, 0:w_], in1=ropet[:, 0, cs], op=ALU.mult),
                          [B_qn, B_rope], [B_t1])
                    R.add("dve", lambda w_=w_, cs=cs: nc.vector.tensor_tensor(out=t2[:, 0:w_], in0=PS[5][:, 0:w_], in1=ropet[:, 1, cs], op=ALU.mult),
                          [PB[5], B_rope], [B_t2])
                    o_ = next_obt()
                    R.add("dve", lambda w_=w_, o_=o_: nc.vector.tensor_tensor(out=obt[o_][:, 0:w_], in0=t1[:, 0:w_], in1=t2[:, 0:w_], op=ALU.add),
                          [B_t1, B_t2], [B_obt[o_]])
                    R.add("sp", lambda o_=o_, w_=w_, ch=ch, cs=cs, dst=dst: nc.sync.dma_start(
                        out=dst[ch * 128:(ch + 1) * 128, cs], in_=obt[o_][:, 0:w_]), [B_obt[o_]], [B_dst], dma=B_obt[o_], part=True)
        R.barrier()

        sbx = SB(nc, sub0)
        dg5 = sbx.alloc("dg5", [128, 10, 5, 128], BF16); B_dg5 = Buf("dg5")
        RBW = 2 + SEQ + 2 + 2 + NCTX + 2
        rb = [sbx.alloc("rb", [128, RBW], BF16) for _ in range(2)]
        B_rb = [Buf("rb%d" % i) for i in range(2)]
        for ch in range(10):
            for k in range(5):
                R.add("dve", lambda ch=ch, k=k: nc.vector.tensor_scalar(
                    out=dg5[:, ch, k, :], in0=c_t[:, IDENT, :], scalar1=vecs[:, V_SCW + ch * 5 + k:V_SCW + ch * 5 + k + 1], scalar2=None, op0=ALU.mult),
                    [B_c, B_vecs], [B_dg5], part=True)
        for i in range(2):
            R.add("pool", lambda i=i: nc.gpsimd.memset(rb[i][:], 0.0), [], [B_rb[i]])

        def rbcol(j, pad):
            return pad + j * 512 if j < 8 else pad + SEQ + 2 * pad

        def xproj(ch):
            ws = wload(O_X + ch * 128)
            r_ = ch % 2
            for j in range(9):
                pb = cnt["ps"] % 2
                cnt["ps"] += 1
                w_ = proj(ws, j, pb)
                c0_ = rbcol(j, 2)
                R.add("act", lambda pb=pb, w_=w_, c0_=c0_: nc.scalar.activation(out=rb[r_][:, c0_:c0_ + w_], in_=PS[pb][:, 0:w_], func=AF.Copy),
                      [PB[pb]], [B_rb[r_]], part=(j > 0))

        def xconv(ch):
            r_ = ch % 2
            for j in range(9):
                w_ = 512 if j < 8 else 256
                pb = 2 + j % 2
                st_ = rbcol(j, 2) - 2
                for k in range(5):
                    R.add("pe", lambda k=k, pb=pb, w_=w_, st_=st_: nc.tensor.matmul(
                        PS[pb][:, 0:w_], lhsT=dg5[:, ch, k, :], rhs=rb[r_][:, st_ + k:st_ + k + w_], start=(k == 0), stop=(k == 4)),
                        [B_dg5, B_rb[r_]], [PB[pb]], part=(k > 0))
                o_ = next_ot()
                R.add("act", lambda pb=pb, o_=o_, w_=w_: nc.scalar.activation(
                    out=ot[o_][:, 0:w_], in_=PS[pb][:, 0:w_], func=AF.Silu, bias=vecs[:, V_SCB + ch:V_SCB + ch + 1], scale=1.0),
                    [PB[pb], B_vecs], [B_ot[o_]])
                R.add("sp", lambda o_=o_, w_=w_, j=j: nc.sync.dma_start(
                    out=xbc_d[ch * 128:(ch + 1) * 128, j * 512:j * 512 + w_], in_=ot[o_][:, 0:w_]), [B_ot[o_]], [B_xbc], dma=B_ot[o_], part=True)

        xproj(0)
        for ch in range(10):
            if ch + 1 < 10:
                xproj(ch + 1)
            xconv(ch)
        R.barrier()

        sbc = SB(nc, sub0)
        dg31 = sbc.alloc("dg31", [128, 4, 31, 128], BF16); B_dg31 = Buf("dg31")
        RUW = 15 + SEQ + 15 + 15 + NCTX + 15
        ru = [sbc.alloc("ru", [128, RUW], BF16) for _ in range(2)]
        B_ru = [Buf("ru%d" % i) for i in range(2)]
        sg = [sbc.alloc("sg", [128, 512], F32) for _ in range(2)]
        B_sg = [Buf("sg%d" % i) for i in range(2)]
        for ch in range(4):
            for k in range(31):
                eng, eo = ("dve", nc.vector) if k % 2 == 0 else ("pool", nc.gpsimd)
                R.add(eng, lambda ch=ch, k=k, eo=eo: eo.tensor_scalar(
                    out=dg31[:, ch, k, :], in0=c_t[:, IDENT, :], scalar1=vecs[:, V_CCW + ch * 31 + k:V_CCW + ch * 31 + k + 1], scalar2=None, op0=ALU.mult),
                    [B_c, B_vecs], [B_dg31], part=True)
        for i in range(2):
            R.add("pool", lambda i=i: nc.gpsimd.memset(ru[i][:], 0.0), [], [B_ru[i]])

        def uproj(ch):
            wa = wload(O_UA + ch * 128)
            wb = wload(O_UB + ch * 128)
            r_ = ch % 2
            for j in range(9):
                w_ = proj(wa, j, 0)
                proj(wb, j, 1)
                s_ = j % 2
                R.add("act", lambda s_=s_, w_=w_: nc.scalar.activation(out=sg[s_][:, 0:w_], in_=PS[1][:, 0:w_], func=AF.Sigmoid), [PB[1]], [B_sg[s_]])
                c0_ = rbcol(j, 15)
                R.add("dve", lambda s_=s_, w_=w_, c0_=c0_: nc.vector.tensor_tensor(
                    out=ru[r_][:, c0_:c0_ + w_], in0=PS[0][:, 0:w_], in1=sg[s_][:, 0:w_], op=ALU.mult), [PB[0], B_sg[s_]], [B_ru[r_]], part=(j > 0))

        def uconv(ch):
            r_ = ch % 2
            for j in range(9):
                w_ = 512 if j < 8 else 256
                pb = 2 + j % 2
                st_ = rbcol(j, 15) - 15
                for k in range(31):
                    R.add("pe", lambda k=k, pb=pb, w_=w_, st_=st_: nc.tensor.matmul(
                        PS[pb][:, 0:w_], lhsT=dg31[:, ch, k, :], rhs=ru[r_][:, st_ + k:st_ + k + w_], start=(k == 0), stop=(k == 30)),
                        [B_dg31, B_ru[r_]], [PB[pb]], part=(k > 0))
                o_ = next_ot()
                R.add("act", lambda pb=pb, o_=o_, w_=w_: nc.scalar.activation(
                    out=ot[o_][:, 0:w_], in_=PS[pb][:, 0:w_], func=AF.Identity, bias=vecs[:, V_CMV + ch * 4:V_CMV + ch * 4 + 1], scale=1.0),
                    [PB[pb], B_vecs], [B_ot[o_]])
                R.add("sp", lambda o_=o_, w_=w_, j=j: nc.sync.dma_start(
                    out=cv_d[ch * 128:(ch + 1) * 128, j * 512:j * 512 + w_], in_=ot[o_][:, 0:w_]), [B_ot[o_]], [B_cv], dma=B_ot[o_], part=True)

        uproj(0)
        for ch in range(4):
            if ch + 1 < 4:
                uproj(ch + 1)
            uconv(ch)
        R.barrier()

        sbt = SB(nc, sub0)
        NZ = 768 + 24 + 256
        wzs = sbt.alloc("wzs", [128, 8, 384], F32); B_wzs = Buf("wzs")
        wz = sbt.alloc("wz", [128, 8, NZ], BF16); B_wz = Buf("wz")
        for (dc, c0, n_) in ((0, O_Z, 384), (384, O_Z + 384, 384), (768, O_DT, 24), (792, O_V, 256)):
            R.add("sp", lambda c0=c0, n_=n_: nc.sync.dma_start(out=wzs[:, :, 0:n_], in_=wsrc(li, c0, n_)), [B_in], [B_wzs], dma=True)
            R.add("pool", lambda dc=dc, n_=n_: nc.gpsimd.tensor_copy(out=wz[:, :, dc:dc + n_], in_=wzs[:, :, 0:n_]), [B_wzs], [B_wz], part=(dc > 0))
        zt = [sbt.alloc("zt", [128, 768], F32) for _ in range(2)]
        B_zt = [Buf("zt%d" % i) for i in range(2)]
        vt = [sbt.alloc("vt", [128, 256], BF16) for _ in range(2)]
        B_vt = [Buf("vt%d" % i) for i in range(2)]
        dta = sbt.alloc("dta", [128, 34, 24], F32); B_dta = Buf("dta")
        dtw = sbt.alloc("dtw", [128, 3, 34 * 24], F32); B_dtw = Buf("dtw")
        dtbt = sbt.alloc("dtbt", [128, 24], F32); B_dtbt = Buf("dtbt")
        R.add("sp", lambda: nc.sync.dma_start(out=dtbt[:], in_=bass.AP(dtb.ap().tensor, li * 24, [[0, 128], [1, 24]])), [B_in], [B_dtbt], dma=True)
        for tt in range(34):
            s_ = tt % 2
            pbs = (0, 1, 2) if tt % 2 == 0 else (3, 4, 5)
            for (pb, dc, n_) in ((pbs[0], 0, 384), (pbs[1], 384, 384), (pbs[2], 768, 280)):
                for kc in range(8):
                    R.add("pe", lambda kc=kc, pb=pb, dc=dc, n_=n_, tt=tt: nc.tensor.matmul(
                        PS[pb][:, 0:n_], lhsT=hT[:, kc, tt * 128:(tt + 1) * 128], rhs=wz[:, kc, dc:dc + n_], start=(kc == 0), stop=(kc == 7)),
                        [B_hT[tt], B_wz], [PB[pb]], part=(kc > 0))
            for hf_ in range(2):
                R.add("act", lambda hf_=hf_, s_=s_, pbs=pbs: nc.scalar.activation(
                    out=zt[s_][:, hf_ * 384:(hf_ + 1) * 384], in_=PS[pbs[hf_]][:, 0:384], func=AF.Silu), [PB[pbs[hf_]]], [B_zt[s_]], part=(hf_ > 0))
            R.add("sp", lambda s_=s_, tt=tt: nc.sync.dma_start(out=zs_d[tt * 128:(tt + 1) * 128, :], in_=zt[s_][:]), [B_zt[s_]], [B_zs], dma=B_zt[s_], part=True)
            R.add("dve", lambda s_=s_, pbs=pbs: nc.vector.tensor_copy(out=vt[s_][:], in_=PS[pbs[2]][:, 24:280]), [PB[pbs[2]]], [B_vt[s_]])
            R.add("sp", lambda s_=s_, tt=tt: nc.sync.dma_start(out=v_d[tt * 128:(tt + 1) * 128, :], in_=vt[s_][:]), [B_vt[s_]], [B_v], dma=B_vt[s_], part=True)
            R.add("dve", lambda tt=tt, pbs=pbs: nc.vector.tensor_tensor(out=dta[:, tt, :], in0=PS[pbs[2]][:, 0:24], in1=dtbt[:], op=ALU.add),
                  [PB[pbs[2]], B_dtbt], [B_dta], part=True)
        dflat = dta.rearrange("p a b -> p (a b)")
        R.add("act", lambda: nc.scalar.activation(out=dtw[:, 0, :], in_=dflat, func=AF.Abs), [B_dta], [B_dtw])
        R.add("act", lambda: nc.scalar.activation(out=dtw[:, 1, :], in_=dtw[:, 0, :], func=AF.Exp, scale=-1.0), [B_dtw], [B_dtw], part=True)
        R.add("act", lambda: nc.scalar.activation(out=dtw[:, 2, :], in_=dtw[:, 1, :], func=AF.Ln, bias=eps_t[:, 1:2], scale=1.0), [B_dtw, B_eps], [B_dtw], part=True)
        R.add("dve", lambda: nc.vector.scalar_tensor_tensor(out=dtw[:, 0, :], in0=dflat, scalar=0.0, in1=dtw[:, 2, :], op0=ALU.max, op1=ALU.add),
              [B_dta, B_dtw], [B_dtw], part=True)
        R.add("sp", lambda: nc.sync.dma_start(out=dt_d.ap().rearrange("(a p) h -> p a h", p=128),
                                              in_=dtw[:, 0, :].rearrange("p (a h) -> p a h", h=24)), [B_dtw], [B_dt], dma=B_dtw)
        R.barrier()
        R.release(B_wst + B_ot + B_obt + [B_vecs, B_rope, B_wzs, B_dtbt, B_dtw] + B_zt + B_vt)

    def phaseB(li, last):
        sb = SB(nc, WORK0)
        KT = sb.alloc("KT", [128, 4, T], BF16); B_KT = Buf("KT")
        VA = sb.alloc("VA", [128, 34, 4, 128], BF16); B_VA = Buf("VA")
        qs = [sb.alloc("qs", [128, 3, 512], BF16) for _ in range(2)]
        B_qs = [Buf("qs%d" % i) for i in range(2)]
        pT = [sb.alloc("pT", [128, 512], BF16) for _ in range(4)]
        B_pT = [Buf("pT%d" % i) for i in range(4)]
        rsb = [sb.alloc("rsb", [128, 512], F32) for _ in range(2)]
        B_rsb = [Buf("rsb%d" % i) for i in range(2)]
        ob = [sb.alloc("ob", [128, 512], F32) for _ in range(2)]
        B_ob = [Buf("ob%d" % i) for i in range(2)]
        gs = [sb.alloc("gs", [128, 512], F32) for _ in range(2)]
        B_gs = [Buf("gs%d" % i) for i in range(2)]
        obb = [sb.alloc("obb", [128, 512], BF16) for _ in range(2)]
        B_obb = [Buf("obb%d" % i) for i in range(2)]
        R.add("pool", lambda: nc.gpsimd.memset(KT[64:128, :, :], 0.0), [], [B_KT])
        R.add("sp", lambda: nc.sync.dma_start(out=KT[0:64, :, :], in_=kT_d.ap().rearrange("(g d) t -> d g t", d=64)), [B_kT], [B_KT], dma=True, part=True)
        for i in range(2):
            R.add("pool", lambda i=i: nc.gpsimd.memset(qs[i][64:128, :, :], 0.0), [], [B_qs[i]])
        R.add("pool", lambda: nc.gpsimd.memset(VA[:, :, :, 64:128], 1.0), [], [B_VA])
        for tt in range(34):
            R.add("sp", lambda tt=tt: nc.sync.dma_start(out=VA[:, tt, :, 0:64], in_=v_d[tt * 128:(tt + 1) * 128, :].rearrange("p (g d) -> p g d", d=64)),
                  [B_v], [B_VA], dma=True, part=True)
        fin = 0
        for j in range(9):
            if last and j == 8:
                continue
            w_ = 512 if j < 8 else 256
            kts = list(range(34)) if j < 8 else [32, 33]
            for g in range(4):
                pbase = (g % 2) * 64
                qi = (j * 4 + g) % 2
                R.add("sp", lambda g=g, j=j, w_=w_, qi=qi, pbase=pbase: nc.sync.dma_start(
                    out=qs[qi][0:64, :, 0:w_],
                    in_=qT_d[g * 192:(g + 1) * 192, j * 512:j * 512 + w_].rearrange("(h d) t -> d h t", d=64)), [B_qT], [B_qs[qi]], dma=True, part=True)
                steps = [(kt, hh) for kt in kts for hh in range(3)]
                n = len(steps)

                def S(s_):
                    kt, hh = steps[s_]
                    bk = s_ % 4
                    R.add("pe", lambda kt=kt, hh=hh, bk=bk, g=g, pbase=pbase, qi=qi, w_=w_: nc.tensor.matmul(PS[bk][:, 0:w_], lhsT=KT[:, g, kt * 128:(kt + 1) * 128],
                                                         rhs=qs[qi][:, hh, 0:w_], start=True, stop=True), [B_KT, B_qs[qi]], [PB[bk]])
                    R.add("act", lambda bk=bk, w_=w_: nc.scalar.activation(out=pT[bk][:, 0:w_], in_=PS[bk][:, 0:w_], func=AF.Exp, scale=0.125), [PB[bk]], [B_pT[bk]])

                def PV(s_):
                    kt, hh = steps[s_]
                    bk = s_ % 4
                    R.add("pe", lambda kt=kt, hh=hh, bk=bk, g=g, w_=w_, k0=kts[0], k1=kts[-1]: nc.tensor.matmul(
                        PS[4 + hh][:, 0:w_], lhsT=VA[:, kt, g, :], rhs=pT[bk][:, 0:w_],
                        start=(kt == k0), stop=(kt == k1)), [B_VA, B_pT[bk]], [PB[4 + hh]], part=(kt != kts[0]))

                for s_ in range(n + 2):
                    if s_ < n:
                        S(s_)
                    if s_ >= 2:
                        PV(s_ - 2)
                for hh in range(3):
                    h_ = g * 3 + hh
                    f_ = fin % 2
                    fin += 1
                    R.add("sp", lambda h_=h_, f_=f_, j=j, w_=w_: nc.sync.dma_start(
                        out=gs[f_][0:64, 0:w_], in_=gas_d[h_ * 64:(h_ + 1) * 64, j * 512:j * 512 + w_]), [B_gas], [B_gs[f_]], dma=True)
                    R.add("dve", lambda hh=hh, f_=f_, w_=w_: nc.vector.reciprocal(out=rsb[f_][64:128, 0:w_], in_=PS[4 + hh][64:128, 0:w_]), [PB[4 + hh]], [B_rsb[f_]])
                    R.add("dve", lambda hh=hh, f_=f_, w_=w_: nc.vector.tensor_tensor(out=ob[f_][0:64, 0:w_], in0=PS[4 + hh][0:64, 0:w_], in1=rsb[f_][64:128, 0:w_], op=ALU.mult),
                          [PB[4 + hh], B_rsb[f_]], [B_ob[f_]])
                    R.add("pool", lambda f_=f_, w_=w_: nc.gpsimd.tensor_tensor(out=obb[f_][0:64, 0:w_], in0=ob[f_][0:64, 0:w_], in1=gs[f_][0:64, 0:w_], op=ALU.mult),
                          [B_ob[f_], B_gs[f_]], [B_obb[f_]])
                    R.add("sp", lambda h_=h_, f_=f_, j=j, w_=w_: nc.sync.dma_start(
                        out=cat_d[768 + h_ * 64:768 + (h_ + 1) * 64, j * 512:j * 512 + w_], in_=obb[f_][0:64, 0:w_]), [B_obb[f_]], [B_cat], dma=B_obb[f_], part=True)
        R.barrier()
        R.release([B_KT, B_VA] + B_qs + B_gs + B_obb)

    def bc12(ap2d, n=64):
        return fap(ap2d, [[1, ap2d.shape[1]], [0, n]])

    def v3(ap2d, a, b):
        return fap(ap2d, [[b, a], [1, b]])

    def phaseC(li, last):
        sb = SB(nc, WORK0)
        if "dbgC" in taps:
            dbgY = nc.dram_tensor("dbgY", [2, 4, 128, 768], F32, kind="ExternalOutput")
            dbgM = nc.dram_tensor("dbgM", [2, 2, 128, 1536], BF16, kind="ExternalOutput")
            B_dbgC = Buf("dbgC")
        Abc = sb.alloc("Abc", [128, 24], F32); B_Abc = Buf("Abc")
        Dbc = sb.alloc("Dbc", [128, 12], F32); B_Dbc = Buf("Dbc")
        Gbc = sb.alloc("Gbc", [128, 768], F32); B_Gbc = Buf("Gbc")
        mrep = sb.alloc("mrep", [128, 2, 4, 128], F32); B_mrep = Buf("mrep")
        R.add("sp", lambda: nc.sync.dma_start(out=Abc[:], in_=bass.AP(alog.ap().tensor, li * 24, [[0, 128], [1, 24]])), [B_in], [B_Abc], dma=True)
        R.add("act", lambda: nc.scalar.activation(out=Abc[:], in_=Abc[:], func=AF.Exp), [B_Abc], [B_Abc])
        R.add("dve", lambda: nc.vector.tensor_scalar(out=Abc[:], in0=Abc[:], scalar1=-1.0, scalar2=None, op0=ALU.mult), [B_Abc], [B_Abc])
        R.add("sp", lambda: nc.sync.dma_start(out=Dbc[:], in_=bass.AP(ssdd.ap().tensor, li * 12, [[0, 128], [1, 12]])), [B_in], [B_Dbc], dma=True)
        R.add("sp", lambda: nc.sync.dma_start(out=Gbc[:], in_=bass.AP(ssdng.ap().tensor, li * 768, [[0, 128], [1, 768]])), [B_in], [B_Gbc], dma=True)
        for dr in range(2):
            R.add("dve", lambda dr=dr: nc.vector.tensor_copy(out=mrep[:, dr, :, :], in_=fap(mk_t[:, dr, :], [[0, 4], [1, 128]])), [B_mk], [B_mrep], part=(dr > 0))
        P2 = range(2)
        xT = [sb.alloc("xT", [128, 6, 128], F32) for _ in P2]; B_xT = [Buf("xT%d" % i) for i in P2]
        bcT = [sb.alloc("bcT", [128, 4, 128], F32) for _ in P2]; B_bcT = [Buf("bcT%d" % i) for i in P2]
        dtc = [sb.alloc("dtc", [128, 24], F32) for _ in P2]; B_dtc = [Buf("dtc%d" % i) for i in P2]
        zc = [sb.alloc("zc", [128, 768], F32) for _ in P2]; B_zc = [Buf("zc%d" % i) for i in P2]
        hbin = [sb.alloc("hbin", [128, 768], BF16) for _ in P2]; B_hbin = [Buf("hbin%d" % i) for i in P2]
        xs = [sb.alloc("xs", [128, 768], F32) for _ in P2]; B_xs = [Buf("xs%d" % i) for i in P2]
        Btm = [sb.alloc("Btm", [128, 256], BF16) for _ in P2]; B_Btm = [Buf("Btm%d" % i) for i in P2]
        bcb = [sb.alloc("bcb", [128, 4, 128], BF16) for _ in P2]; B_bcb = [Buf("bcb%d" % i) for i in P2]
        av = [sb.alloc("av", [128, 24], F32) for _ in P2]; B_av = [Buf("av%d" % i) for i in P2]
        ct = [sb.alloc("ct", [128, 72], F32) for _ in P2]; B_ct = [Buf("ct%d" % i) for i in P2]
        ex = [sb.alloc("ex", [128, 72], F32) for _ in P2]; B_ex = [Buf("ex%d" % i) for i in P2]
        ncum = [sb.alloc("ncum", [128, 24], F32) for _ in P2]; B_ncum = [Buf("ncum%d" % i) for i in P2]
        dtd = [sb.alloc("dtd", [128, 24], F32) for _ in P2]; B_dtd = [Buf("dtd%d" % i) for i in P2]
        xw = [[sb.alloc("xw", [128, 768], BF16) for _ in range(4)] for _ in P2]
        B_xw = [[Buf("xw%d_%d" % (i, k)) for k in range(4)] for i in P2]
        cbs = sb.alloc("cbs", [128, 2, 128], F32); B_cbs = Buf("cbs")
        Dm = sb.alloc("Dm", [128, 12, 128], F32); B_Dm = Buf("Dm")
        Et = sb.alloc("Et", [128, 12, 128], F32); B_Et = Buf("Et")
        Mt = [sb.alloc("Mt", [128, 12, 128], BF16) for _ in P2]; B_Mt = [Buf("Mt%d" % i) for i in P2]
        yo = [sb.alloc("yo", [128, 768], F32) for _ in P2]; B_yo = [Buf("yo%d" % i) for i in P2]
        yv = sb.alloc("yv", [128, 768], F32); B_yv = Buf("yv")
        t3 = sb.alloc("t3", [128, 768], F32); B_t3 = Buf("t3")
        yn = sb.alloc("yn", [128, 768], F32); B_yn = Buf("yn")
        junk = sb.alloc("junk", [128, 768], BF16); B_junk = Buf("junkc")
        st = sb.alloc("st", [128, 4], F32); B_st = Buf("stc")
        catT = [sb.alloc("catT", [128, 6, 128], BF16) for _ in P2]; B_catT = [Buf("catT%d" % i) for i in P2]
        tmp = sb.alloc("tmp", [128, 768], F32); B_tmp = Buf("tmp")
        hst = [sb.alloc("hst", [128, 768], F32) for _ in P2]; B_hst = [Buf("hst%d" % i) for i in P2]
        hfb = sb.alloc("hfb", [128, 768], BF16); B_hfb = Buf("hfb")
        hbb = [sb.alloc("hbb", [128, 768], BF16) for _ in P2]; B_hbb = [Buf("hbb%d" % i) for i in P2]
        for i in P2:
            R.add("pool", lambda i=i: nc.gpsimd.memset(hst[i][:], 0.0), [], [B_hst[i]])
            R.add("pool", lambda i=i: nc.gpsimd.memset(hbb[i][:], 0.0), [], [B_hbb[i]])
        R.add("pool", lambda: nc.gpsimd.memset(hfb[:], 0.0), [], [B_hfb])

        def prep(c, pp, full):
            cs = slice(c * 128, (c + 1) * 128)
            R.add("sp", lambda: nc.sync.dma_start(out=xT[pp][:], in_=xbc_d[0:768, cs].rearrange("(c p) t -> p c t", p=128)), [B_xbc], [B_xT[pp]], dma=True)
            R.add("sp", lambda: nc.sync.dma_start(out=bcT[pp][:], in_=xbc_d[768:1280, cs].rearrange("(c p) t -> p c t", p=128)), [B_xbc], [B_bcT[pp]], dma=True)
            R.add("sp", lambda: nc.sync.dma_start(out=dtc[pp][:], in_=dt_d[cs, :]), [B_dt], [B_dtc[pp]], dma=True)
            if full:
                R.add("sp", lambda: nc.sync.dma_start(out=zc[pp][:], in_=zs_d[cs, :]), [B_zs], [B_zc[pp]], dma=True)
                R.add("sp", lambda: nc.sync.dma_start(out=hbin[pp][:], in_=hb_d[c]), [B_hb], [B_hbin[pp]], dma=True)
            for i in range(6):
                bk, col = (0, i * 128) if i < 4 else (1, (i - 4) * 128)
                R.add("pe", lambda i=i, bk=bk, col=col: nc.tensor.transpose(PS[bk][:, col:col + 128], xT[pp][:, i, :], c_t[:, IDENT, :]),
                      [B_xT[pp], B_c], [PB[bk]], part=(i not in (0, 4)))
            for i in range(2):
                R.add("pe", lambda i=i: nc.tensor.transpose(PS[1][:, 256 + i * 128:384 + i * 128], bcT[pp][:, i, :], c_t[:, IDENT, :]),
                      [B_bcT[pp], B_c], [PB[1]], part=True)
            R.add("act", lambda: nc.scalar.activation(out=xs[pp][:, 0:512], in_=PS[0][:, 0:512], func=AF.Copy), [PB[0]], [B_xs[pp]])
            R.add("act", lambda: nc.scalar.activation(out=xs[pp][:, 512:768], in_=PS[1][:, 0:256], func=AF.Copy), [PB[1]], [B_xs[pp]], part=True)
            R.add("dve", lambda: nc.vector.tensor_copy(out=Btm[pp][:], in_=PS[1][:, 256:512]), [PB[1]], [B_Btm[pp]])
            R.add("pool", lambda: nc.gpsimd.tensor_copy(out=bcb[pp][:], in_=bcT[pp][:]), [B_bcT[pp]], [B_bcb[pp]])
            R.add("dve", lambda: nc.vector.tensor_tensor(out=av[pp][:], in0=dtc[pp][:], in1=Abc[:], op=ALU.mult), [B_dtc[pp], B_Abc], [B_av[pp]])
            R.add("pe", lambda: nc.tensor.matmul(PS[2][:, 0:12], lhsT=c_t[:, UIN, :], rhs=av[pp][:, 0:12], start=True, stop=True), [B_c, B_av[pp]], [PB[2]])
            R.add("pe", lambda: nc.tensor.matmul(PS[2][:, 12:24], lhsT=c_t[:, UTR, :], rhs=av[pp][:, 12:24], start=True, stop=True), [B_c, B_av[pp]], [PB[2]], part=True)
            R.add("pe", lambda: nc.tensor.matmul(PS[2][:, 24:48], lhsT=c_t[:, ONES, :], rhs=av[pp][:, 0:24], start=True, stop=True), [B_c, B_av[pp]], [PB[2]], part=True)
            R.add("dve", lambda: nc.vector.tensor_copy(out=ct[pp][:, 24:72], in_=PS[2][:, 0:48]), [PB[2]], [B_ct[pp]])
            R.add("dve", lambda: nc.vector.tensor_tensor(out=ct[pp][:, 0:24], in0=ct[pp][:, 48:72], in1=ct[pp][:, 24:48], op=ALU.subtract), [B_ct[pp]], [B_ct[pp]], part=True)
            R.add("act", lambda: nc.scalar.activation(out=ex[pp][:], in_=ct[pp][:], func=AF.Exp), [B_ct[pp]], [B_ex[pp]])
            R.add("dve", lambda: nc.vector.tensor_scalar(out=ncum[pp][:], in0=ct[pp][:, 24:48], scalar1=-1.0, scalar2=None, op0=ALU.mult), [B_ct[pp]], [B_ncum[pp]])
            R.add("dve", lambda: nc.vector.tensor_tensor(out=dtd[pp][:], in0=dtc[pp][:], in1=ex[pp][:, 0:24], op=ALU.mult), [B_dtc[pp], B_ex[pp]], [B_dtd[pp]])
            xs3 = v3(xs[pp][:, 0:768], 12, 64)
            R.add("dve", lambda: nc.vector.tensor_tensor(out=v3(xw[pp][1][:, 0:768], 12, 64), in0=xs3, in1=bc12(dtd[pp][:, 12:24]), op=ALU.mult),
                  [B_xs[pp], B_dtd[pp]], [B_xw[pp][1]])
            if full:
                R.add("pool", lambda: nc.gpsimd.tensor_tensor(out=v3(xw[pp][0][:, 0:768], 12, 64), in0=xs3, in1=bc12(dtd[pp][:, 0:12]), op=ALU.mult),
                      [B_xs[pp], B_dtd[pp]], [B_xw[pp][0]])
                R.add("dve", lambda: nc.vector.tensor_tensor(out=v3(xw[pp][2][:, 0:768], 12, 64), in0=xs3, in1=bc12(dtc[pp][:, 0:12]), op=ALU.mult),
                      [B_xs[pp], B_dtc[pp]], [B_xw[pp][2]])
                R.add("pool", lambda: nc.gpsimd.tensor_tensor(out=v3(xw[pp][3][:, 0:768], 12, 64), in0=xs3, in1=bc12(dtc[pp][:, 12:24]), op=ALU.mult),
                      [B_xs[pp], B_dtc[pp]], [B_xw[pp][3]])

        def state_update(pp, d, bks, outb, B_outb):
            for g in range(2):
                R.add("pe", lambda g=g: nc.tensor.matmul(PS[bks[g]][:, 0:384], lhsT=Btm[pp][:, g * 128:(g + 1) * 128], rhs=xw[pp][d][:, g * 384:(g + 1) * 384],
                                                        start=True, stop=True), [B_Btm[pp], B_xw[pp][d]], [PB[bks[g]]])
            R.add("dve", lambda: nc.vector.tensor_tensor(out=v3(tmp[:, 0:768], 12, 64), in0=v3(hst[d][:, 0:768], 12, 64),
                                                         in1=bc12(ex[pp][:, 48 + d * 12:60 + d * 12]), op=ALU.mult), [B_hst[d], B_ex[pp]], [B_tmp])
            for g in range(2):
                R.add("dve", lambda g=g: nc.vector.tensor_tensor(out=hst[d][:, g * 384:(g + 1) * 384], in0=PS[bks[g]][:, 0:384], in1=tmp[:, g * 384:(g + 1) * 384], op=ALU.add),
                      [PB[bks[g]], B_tmp], [B_hst[d]], part=(g > 0))
            R.add("act", lambda: nc.scalar.activation(out=outb[:], in_=hst[d][:], func=AF.Copy), [B_hst[d]], [B_outb])

        order_b = [33, 32] + list(range(31, -1, -1))
        for n_, c in enumerate(order_b):
            pp = n_ % 2
            prep(c, pp, False)
            R.add("sp", lambda c=c, pp=pp: nc.sync.dma_start(out=hb_d[c], in_=hbb[pp][:]), [B_hbb[pp]], [B_hb], dma=B_hbb[pp], part=True)
            state_update(pp, 1, (3, 4), hbb[1 - pp], B_hbb[1 - pp])

        order_f = [32, 33] + list(range(32))
        for n_, c in enumerate(order_f):
            pp = n_ % 2
            prep(c, pp, True)
            for g in range(2):
                R.add("pe", lambda g=g, pp=pp: nc.tensor.matmul(PS[3][:, g * 128:(g + 1) * 128], lhsT=bcb[pp][:, g, :], rhs=bcb[pp][:, 2 + g, :], start=True, stop=True),
                      [B_bcb[pp]], [PB[3]], part=(g > 0))
            R.add("dve", lambda: nc.vector.tensor_copy(out=cbs[:].rearrange("p a b -> p (a b)"), in_=PS[3][:, 0:256]), [PB[3]], [B_cbs])
            for dr in range(2):
                um = UIN if dr == 0 else UTR
                R.add("pool", lambda dr=dr, um=um, pp=pp: nc.gpsimd.tensor_tensor(
                    out=Dm[:], in0=fap(c_t[:, um, :], [[0, 12], [1, 128]]), in1=bc12(av[pp][:, dr * 12:dr * 12 + 12], 128), op=ALU.mult),
                    [B_c, B_av[pp]], [B_Dm])
                for q in range(3):
                    R.add("pe", lambda q=q: nc.tensor.matmul(PS[4 + q][:, :], lhsT=c_t[:, ONES, :], rhs=Dm[:, q * 4:(q + 1) * 4, :].rearrange("p a b -> p (a b)"),
                                                             start=True, stop=False), [B_c, B_Dm], [PB[4 + q]])
                    R.add("pe", lambda q=q, dr=dr: nc.tensor.matmul(PS[4 + q][:, :], lhsT=c_t[:, IDENT, :], rhs=mrep[:, dr, :, :].rearrange("p a b -> p (a b)"),
                                                                    start=False, stop=True), [B_c, B_mrep], [PB[4 + q]], part=True)
                for h in range(12):
                    R.add("act", lambda h=h, dr=dr, pp=pp: nc.scalar.activation(
                        out=Et[:, h, :], in_=PS[4 + h // 4][:, (h % 4) * 128:(h % 4 + 1) * 128], func=AF.Exp,
                        bias=ncum[pp][:, dr * 12 + h:dr * 12 + h + 1], scale=1.0), [PB[4 + h // 4], B_ncum[pp]], [B_Et], part=(h > 0))
                for g in range(2):
                    R.add("dve", lambda g=g, dr=dr: nc.vector.tensor_tensor(out=Mt[dr][:, g * 6:(g + 1) * 6, :], in0=Et[:, g * 6:(g + 1) * 6, :],
                                                                          in1=fap(cbs[:, g, :], [[0, 6], [1, 128]]), op=ALU.mult), [B_Et, B_cbs], [B_Mt[dr]], part=(g > 0))
                hin, B_hin = (hfb, B_hfb) if dr == 0 else (hbin[pp], B_hbin[pp])
                for g in range(2):
                    bk = 7 if g == 0 else 2
                    R.add("pe", lambda g=g, bk=bk, hin=hin, pp=pp: nc.tensor.matmul(PS[bk][:, 0:384], lhsT=bcb[pp][:, 2 + g, :], rhs=hin[:, g * 384:(g + 1) * 384],
                                                                                 start=True, stop=True), [B_bcb[pp], B_hin], [PB[bk]])
                    R.add("dve", lambda g=g, bk=bk, dr=dr, pp=pp: nc.vector.tensor_tensor(
                        out=v3(yo[dr][:, g * 384:(g + 1) * 384], 6, 64), in0=v3(PS[bk][:, 0:384], 6, 64),
                        in1=bc12(ex[pp][:, 24 + dr * 12 + g * 6:24 + dr * 12 + g * 6 + 6]), op=ALU.mult), [PB[bk], B_ex[pp]], [B_yo[dr]], part=(g > 0))
            for h in range(12):
                bk, col = (0, h * 64) if h < 8 else (1, (h - 8) * 64)
                for dr in range(2):
                    R.add("pe", lambda h=h, dr=dr, bk=bk, col=col, pp=pp: nc.tensor.matmul(
                        PS[bk][:, col:col + 64], lhsT=Mt[dr][:, h, :], rhs=xw[pp][2 + dr][:, h * 64:(h + 1) * 64], start=(dr == 0), stop=(dr == 1)),
                        [B_Mt[dr], B_xw[pp][2 + dr]], [PB[bk]], part=not (dr == 0 and h in (0, 8)))
            skip_out = last and c >= 32
            if not skip_out:
                R.add("dve", lambda: nc.vector.tensor_tensor(out=yv[:, 0:512], in0=PS[0][:, 0:512], in1=yo[0][:, 0:512], op=ALU.add), [PB[0], B_yo[0]], [B_yv])
                R.add("dve", lambda: nc.vector.tensor_tensor(out=yv[:, 512:768], in0=PS[1][:, 0:256], in1=yo[0][:, 512:768], op=ALU.add), [PB[1], B_yo[0]], [B_yv], part=True)
                R.add("pool", lambda: nc.gpsimd.tensor_tensor(out=yv[:], in0=yv[:], in1=yo[1][:], op=ALU.add), [B_yv, B_yo[1]], [B_yv])
                R.add("pool", lambda pp=pp: nc.gpsimd.tensor_tensor(out=v3(t3[:, 0:768], 12, 64), in0=v3(xs[pp][:, 0:768], 12, 64), in1=bc12(Dbc[:, 0:12]), op=ALU.mult),
                      [B_xs[pp], B_Dbc], [B_t3])
                R.add("pool", lambda: nc.gpsimd.tensor_tensor(out=yv[:], in0=yv[:], in1=t3[:], op=ALU.add), [B_yv, B_t3], [B_yv])
                if "dbgC" in taps and c in (32, 0):
                    di = 0 if c == 32 else 1
                    R.add("sp", lambda di=di: nc.sync.dma_start(out=dbgY[di, 0], in_=yv[:]), [B_yv], [B_dbgC], dma=True, part=True)
                    R.add("sp", lambda di=di: nc.sync.dma_start(out=dbgY[di, 1], in_=yo[0][:]), [B_yo[0]], [B_dbgC], dma=True, part=True)
                    R.add("sp", lambda di=di: nc.sync.dma_start(out=dbgY[di, 2], in_=yo[1][:]), [B_yo[1]], [B_dbgC], dma=True, part=True)
                    R.add("sp", lambda di=di: nc.sync.dma_start(out=dbgY[di, 3], in_=t3[:]), [B_t3], [B_dbgC], dma=True, part=True)
                    R.add("sp", lambda di=di: nc.sync.dma_start(out=dbgM[di, 0], in_=Mt[0][:].rearrange("p a b -> p (a b)")), [B_Mt[0]], [B_dbgC], dma=True, part=True)
                    R.add("sp", lambda di=di: nc.sync.dma_start(out=dbgM[di, 1], in_=Mt[1][:].rearrange("p a b -> p (a b)")), [B_Mt[1]], [B_dbgC], dma=True, part=True)
                    R.add("sp", None, [B_dbgC], [])
                R.add("pool", lambda pp=pp: nc.gpsimd.tensor_tensor(out=yv[:], in0=yv[:], in1=zc[pp][:], op=ALU.mult), [B_yv, B_zc[pp]], [B_yv])
                R.add("act", lambda: nc.scalar.activation(out=junk[:], in_=yv[:], func=AF.Square, accum_out=st[:, 0:1]), [B_yv], [B_junk, B_st])
                R.add("act", lambda: nc.scalar.activation(out=st[:, 1:2], in_=st[:, 0:1], func=AF.Sqrt, bias=eps_t[:, 0:1], scale=1.0 / 768), [B_st, B_eps], [B_st], part=True)
                R.add("dve", lambda: nc.vector.reciprocal(out=st[:, 2:3], in_=st[:, 1:2]), [B_st], [B_st], part=True)
                R.add("dve", lambda: nc.vector.scalar_tensor_tensor(out=yn[:], in0=yv[:], scalar=st[:, 2:3], in1=Gbc[:], op0=ALU.mult, op1=ALU.mult),
                      [B_yv, B_st, B_Gbc], [B_yn])
                for i in range(6):
                    bk, col = (4, i * 128) if i < 4 else (5, (i - 4) * 128)
                    R.add("pe", lambda i=i, bk=bk, col=col: nc.tensor.transpose(PS[bk][:, col:col + 128], yn[:, i * 128:(i + 1) * 128], c_t[:, IDENT, :]),
                          [B_yn, B_c], [PB[bk]], part=(i not in (0, 4)))
                R.add("act", lambda pp=pp: nc.scalar.activation(out=catT[pp][:, 0:4, :].rearrange("p a b -> p (a b)"), in_=PS[4][:, 0:512], func=AF.Copy), [PB[4]], [B_catT[pp]])
                R.add("act", lambda pp=pp: nc.scalar.activation(out=catT[pp][:, 4:6, :].rearrange("p a b -> p (a b)"), in_=PS[5][:, 0:256], func=AF.Copy), [PB[5]], [B_catT[pp]], part=True)
                R.add("sp", lambda c=c, pp=pp: nc.sync.dma_start(out=cat_d[0:768, c * 128:(c + 1) * 128].rearrange("(c p) t -> p c t", p=128), in_=catT[pp][:]),
                      [B_catT[pp]], [B_cat], dma=B_catT[pp], part=True)
            state_update(pp, 0, (6, 7), hfb, B_hfb)
        R.barrier()
        R.release(B_xT + B_bcT + B_dtc + B_zc + B_hbin + B_hbb + B_catT + [B_Abc, B_Dbc, B_Gbc])

    def phaseD(li, last):
        sb = SB(nc, WORK0)
        vec = sb.alloc("vecD", [128, 16], F32); B_vec = Buf("vecD")
        R.add("sp", lambda: nc.sync.dma_start(out=vec[:], in_=cmv[li].rearrange("p a b -> p (a b)")), [B_in], [B_vec], dma=True)
        pws = sb.alloc("pws", [128, 4, 512], F32); B_pws = Buf("pws")
        pwb = sb.alloc("pwb", [128, 4, 512], BF16); B_pwb = Buf("pwb")
        R.add("sp", lambda: nc.sync.dma_start(out=pws[:], in_=cpw[li].rearrange("(c p) n -> p c n", p=128)), [B_in], [B_pws], dma=True)
        R.add("pool", lambda: nc.gpsimd.tensor_copy(out=pwb[:], in_=pws[:]), [B_pws], [B_pwb])
        P2 = range(2)
        cvt = [sb.alloc("cvt", [128, 4, 512], F32) for _ in P2]; B_cvt = [Buf("cvt%d" % i) for i in P2]
        gct = [sb.alloc("gct", [128, 4, 512], F32) for _ in P2]; B_gct = [Buf("gct%d" % i) for i in P2]
        sq = sb.alloc("sq", [128, 4, 512], F32); B_sq = Buf("sq")
        mean = sb.alloc("mean", [128, 512], F32); B_mean = Buf("mean")
        m2 = sb.alloc("m2", [128, 512], F32); B_m2 = Buf("m2")
        var = sb.alloc("var", [128, 512], F32); B_var = Buf("var")
        sdv = sb.alloc("sdv", [128, 512], F32); B_sdv = Buf("sdv")
        rstd = sb.alloc("rstd", [128, 512], F32); B_rstd = Buf("rstd")
        xc_ = sb.alloc("xc", [128, 512], F32); B_xc = Buf("xc")
        xn_ = sb.alloc("xn", [128, 512], F32); B_xn = Buf("xnD")
        act = sb.alloc("act", [128, 4, 512], BF16); B_act = Buf("act")
        oD = [sb.alloc("oD", [128, 512], BF16) for _ in P2]; B_oD = [Buf("oD%d" % i) for i in P2]
        k_ = 0
        for j in range(9):
            if last and j == 8:
                continue
            w_ = 512 if j < 8 else 256
            cs = slice(j * 512, j * 512 + w_)
            pp = j % 2
            R.add("sp", lambda pp=pp, cs=cs, w_=w_: nc.sync.dma_start(out=cvt[pp][:, :, 0:w_], in_=cv_d[:, cs].rearrange("(c p) t -> p c t", p=128)), [B_cv], [B_cvt[pp]], dma=True)
            R.add("sp", lambda pp=pp, cs=cs, w_=w_: nc.sync.dma_start(out=gct[pp][:, :, 0:w_], in_=gcs_d[:, cs].rearrange("(c p) t -> p c t", p=128)), [B_gcs], [B_gct[pp]], dma=True)
            R.add("act", lambda pp=pp, w_=w_: nc.scalar.activation(out=sq[:, :, 0:w_], in_=cvt[pp][:, :, 0:w_], func=AF.Square), [B_cvt[pp]], [B_sq])
            for c in range(4):
                R.add("pe", lambda c=c, pp=pp, w_=w_: nc.tensor.matmul(PS[0][:, 0:w_], lhsT=c_t[:, ONES, :], rhs=cvt[pp][:, c, 0:w_], start=(c == 0), stop=(c == 3)),
                      [B_c, B_cvt[pp]], [PB[0]], part=(c > 0))
            for c in range(4):
                R.add("pe", lambda c=c, w_=w_: nc.tensor.matmul(PS[1][:, 0:w_], lhsT=c_t[:, ONES, :], rhs=sq[:, c, 0:w_], start=(c == 0), stop=(c == 3)),
                      [B_c, B_sq], [PB[1]], part=(c > 0))
            R.add("dve", lambda w_=w_: nc.vector.tensor_scalar(out=mean[:, 0:w_], in0=PS[0][:, 0:w_], scalar1=1.0 / 512, scalar2=None, op0=ALU.mult), [PB[0]], [B_mean])
            R.add("dve", lambda w_=w_: nc.vector.tensor_tensor(out=m2[:, 0:w_], in0=mean[:, 0:w_], in1=mean[:, 0:w_], op=ALU.mult), [B_mean], [B_m2])
            R.add("dve", lambda w_=w_: nc.vector.scalar_tensor_tensor(out=var[:, 0:w_], in0=PS[1][:, 0:w_], scalar=1.0 / 512, in1=m2[:, 0:w_], op0=ALU.mult, op1=ALU.subtract),
                  [PB[1], B_m2], [B_var])
            R.add("act", lambda w_=w_: nc.scalar.activation(out=sdv[:, 0:w_], in_=var[:, 0:w_], func=AF.Sqrt, bias=eps_t[:, 0:1], scale=1.0), [B_var, B_eps], [B_sdv])
            R.add("dve", lambda w_=w_: nc.vector.reciprocal(out=rstd[:, 0:w_], in_=sdv[:, 0:w_]), [B_sdv], [B_rstd])
            for c in range(4):
                R.add("dve", lambda c=c, pp=pp, w_=w_: nc.vector.tensor_tensor(out=xc_[:, 0:w_], in0=cvt[pp][:, c, 0:w_], in1=mean[:, 0:w_], op=ALU.subtract),
                      [B_cvt[pp], B_mean], [B_xc])
                R.add("pool", lambda w_=w_: nc.gpsimd.tensor_tensor(out=xn_[:, 0:w_], in0=xc_[:, 0:w_], in1=rstd[:, 0:w_], op=ALU.mult), [B_xc, B_rstd], [B_xn])
                R.add("act", lambda c=c, w_=w_: nc.scalar.activation(out=act[:, c, 0:w_], in_=xn_[:, 0:w_], func=AF.Silu,
                                                                     bias=vec[:, c * 4 + 2:c * 4 + 3], scale=vec[:, c * 4 + 1:c * 4 + 2]), [B_xn, B_vec], [B_act], part=(c > 0))
            for oc in range(4):
                bk = 2 + oc % 2
                for c in range(4):
                    R.add("pe", lambda oc=oc, c=c, bk=bk, w_=w_: nc.tensor.matmul(PS[bk][:, 0:w_], lhsT=pwb[:, c, oc * 128:(oc + 1) * 128], rhs=act[:, c, 0:w_],
                                                                                 start=(c == 0), stop=(c == 3)), [B_pwb, B_act], [PB[bk]], part=(c > 0))
                o_ = k_ % 2
                k_ += 1
                R.add("dve", lambda oc=oc, bk=bk, o_=o_, pp=pp, w_=w_: nc.vector.scalar_tensor_tensor(
                    out=oD[o_][:, 0:w_], in0=PS[bk][:, 0:w_], scalar=vec[:, oc * 4 + 3:oc * 4 + 4], in1=gct[pp][:, oc, 0:w_], op0=ALU.add, op1=ALU.mult),
                    [PB[bk], B_vec, B_gct[pp]], [B_oD[o_]])
                R.add("sp", lambda oc=oc, o_=o_, cs=cs, w_=w_: nc.sync.dma_start(out=cat_d[1536 + oc * 128:1536 + (oc + 1) * 128, cs], in_=oD[o_][:, 0:w_]),
                      [B_oD[o_]], [B_cat], dma=B_oD[o_], part=True)
        R.barrier()
        R.release([B_vec, B_pws] + B_cvt + B_gct + B_oD)

    def phaseE(li, last):
        sb = SB(nc, WORK0)
        wos = sb.alloc("wos", [128, 4, D], F32); B_wos = Buf("wos")
        wo = sb.alloc("wo", [128, 16, D], BF16); B_wo = Buf("wo")
        for q in range(4):
            R.add("sp", lambda q=q: nc.sync.dma_start(out=wos[:], in_=w_out[li, q * 512:(q + 1) * 512, :].rearrange("(c p) n -> p c n", p=128)), [B_in], [B_wos], dma=True)
            R.add("pool", lambda q=q: nc.gpsimd.tensor_copy(out=wo[:, q * 4:(q + 1) * 4, :], in_=wos[:]), [B_wos], [B_wo], part=(q > 0))
        gB = sb.alloc("gB", [128, 2, D], F32); B_gB = Buf("gB")
        for j in range(2):
            R.add("sp", lambda j=j: nc.sync.dma_start(out=gB[:, j, :], in_=gate_d[li, j]), [B_gate], [B_gB], dma=True, part=(j > 0))
        fg = sb.alloc("fg", [128, D], F32); B_fg = Buf("fg")
        R.add("sp", lambda: nc.sync.dma_start(out=fg[:], in_=bass.AP(fng.ap().tensor, 0, [[0, 128], [1, D]])), [B_in], [B_fg], dma=True)
        P2 = range(2)
        ct_ = [sb.alloc("ctE", [128, 16, 512], BF16) for _ in P2]; B_ct = [Buf("ctE%d" % i) for i in P2]
        xr = [sb.alloc("xr", [128, D], F32) for _ in P2]; B_xr = [Buf("xr%d" % i) for i in P2]
        ty = [sb.alloc("ty", [128, D], F32) for _ in P2]; B_ty = [Buf("ty%d" % i) for i in P2]
        xo = [sb.alloc("xo", [128, D], F32) for _ in P2]; B_xo = [Buf("xo%d" % i) for i in P2]
        junk = sb.alloc("junkE", [128, D], BF16); B_junk = Buf("junkE")
        st = [sb.alloc("stE", [128, 4], F32) for _ in P2]; B_st = [Buf("stE%d" % i) for i in P2]
        fo = [sb.alloc("fo", [128, D], F32) for _ in P2]; B_fo = [Buf("fo%d" % i) for i in P2]
        for j in range(9):
            if last and j == 8:
                continue
            w_ = 512 if j < 8 else 256
            jp = j % 2
            R.add("sp", lambda jp=jp, j=j, w_=w_: nc.sync.dma_start(out=ct_[jp][:, :, 0:w_], in_=cat_d[:, j * 512:j * 512 + w_].rearrange("(c p) t -> p c t", p=128)),
                  [B_cat], [B_ct[jp]], dma=True)
            for q in range(w_ // 128):
                tt = j * 4 + q
                pp = tt % 2
                jj = 0 if tt < 32 else 1
                if li == 0:
                    src = x_in[tt * 128:(tt + 1) * 128, :] if tt < 32 else ctx_in[(tt - 32) * 128:(tt - 31) * 128, :]
                    srcb = B_in
                else:
                    src = x1_d[tt * 128:(tt + 1) * 128, :]
                    srcb = B_x1
                R.add("sp", lambda pp=pp, src=src: nc.sync.dma_start(out=xr[pp][:], in_=src), [srcb], [B_xr[pp]], dma=True)
                for hf_ in range(2):
                    bk = 2 * pp + hf_
                    for c in range(16):
                        R.add("pe", lambda c=c, hf_=hf_, bk=bk, jp=jp, q=q: nc.tensor.matmul(
                            PS[bk][:, :], lhsT=ct_[jp][:, c, q * 128:(q + 1) * 128], rhs=wo[:, c, hf_ * 512:(hf_ + 1) * 512], start=(c == 0), stop=(c == 15)),
                            [B_ct[jp], B_wo], [PB[bk]], part=(c > 0))
                    R.add("dve", lambda hf_=hf_, bk=bk, pp=pp, jj=jj: nc.vector.tensor_tensor(
                        out=ty[pp][:, hf_ * 512:(hf_ + 1) * 512], in0=PS[bk][:, :], in1=gB[:, jj, hf_ * 512:(hf_ + 1) * 512], op=ALU.mult),
                        [PB[bk], B_gB], [B_ty[pp]], part=(hf_ > 0))
                R.add("pool", lambda pp=pp: nc.gpsimd.tensor_tensor(out=xo[pp][:], in0=ty[pp][:], in1=xr[pp][:], op=ALU.add), [B_ty[pp], B_xr[pp]], [B_xo[pp]])
                if not last:
                    R.add("sp", lambda pp=pp, tt=tt: nc.sync.dma_start(out=x1_d[tt * 128:(tt + 1) * 128, :], in_=xo[pp][:]), [B_xo[pp]], [B_x1], dma=B_xo[pp], part=True)
                else:
                    R.add("act", lambda pp=pp: nc.scalar.activation(out=junk[:], in_=xo[pp][:], func=AF.Square, accum_out=st[pp][:, 0:1]), [B_xo[pp]], [B_junk, B_st[pp]])
                    R.add("act", lambda pp=pp: nc.scalar.activation(out=st[pp][:, 1:2], in_=st[pp][:, 0:1], func=AF.Sqrt, bias=eps_t[:, 0:1], scale=1.0 / D),
                          [B_st[pp], B_eps], [B_st[pp]], part=True)
                    R.add("dve", lambda pp=pp: nc.vector.reciprocal(out=st[pp][:, 2:3], in_=st[pp][:, 1:2]), [B_st[pp]], [B_st[pp]], part=True)
                    R.add("dve", lambda pp=pp: nc.vector.scalar_tensor_tensor(out=fo[pp][:], in0=xo[pp][:], scalar=st[pp][:, 2:3], in1=fg[:], op0=ALU.mult, op1=ALU.mult),
                          [B_xo[pp], B_st[pp], B_fg], [B_fo[pp]])
                    R.add("sp", lambda pp=pp, tt=tt: nc.sync.dma_start(out=out_d[tt * 128:(tt + 1) * 128, :], in_=fo[pp][:]), [B_fo[pp]], [B_out], dma=B_fo[pp], part=True)
        R.barrier()
        R.release([B_wos, B_gB, B_fg] + B_ct + B_xr + B_xo + B_fo)


    R.barrier()
    for li in range(nlayers):
        phase1(li)
        R.barrier()
        if phases >= 2:
            phaseA(li)
        last = (li == DEPTH - 1) and nlayers == DEPTH
        if phases >= 3 and "skipB" not in taps:
            phaseB(li, last)
        if phases >= 4 and "skipC" not in taps:
            phaseC(li, last)
        if phases >= 5:
            phaseD(li, last)
        if phases >= 6:
            phaseE(li, last)
    dbg_fence = []
    if "hT" in taps:
        hT_dbg = nc.dram_tensor("hT_dbg", [128, 8, T], BF16, kind="ExternalOutput")
        B_dbg = Buf("dbg")
        R.add("sp", lambda: nc.sync.dma_start(out=hT_dbg.ap(), in_=hT[:]), B_hT, [B_dbg], dma=True)
        dbg_fence.append(B_dbg)
    R.barrier()
    R.emit()
    return dram_in


def host_inputs(inp, b):
    f = np.float32
    fm = lambda v, nch: np.ascontiguousarray(np.asarray(v, f).reshape(nch, 128).T)
    m = {}
    m["x"] = np.ascontiguousarray(inp["x"][b], f)
    m["ctx"] = np.ascontiguousarray(inp["ctx"][b], f)
    m["cvec"] = np.ascontiguousarray(np.stack([fm(inp["c"][b], 8), fm(inp["c_ctx"], 8)], axis=-1))
    m["ada_w"] = np.ascontiguousarray(inp["ada_w"], f)
    m["ada_b_fm"] = np.stack([fm(inp["ada_b"][l], 24) for l in range(DEPTH)])
    m["ada_b_gate"] = np.ascontiguousarray(inp["ada_b"][:, 2048:3072], f)
    m["norm_g_fm"] = np.stack([fm(inp["norm_g"][l], 8) for l in range(DEPTH)])
    m["w_in"] = np.ascontiguousarray(inp["w_in"], f)
    m["ssd_conv_w_fm"] = np.ascontiguousarray(
        np.asarray(inp["ssd_conv_w"], f).reshape(DEPTH, 5, 10, 128).transpose(0, 3, 2, 1))
    m["ssd_conv_b_fm"] = np.stack([fm(inp["ssd_conv_b"][l], 10) for l in range(DEPTH)])
    m["ssd_dt_bias"] = np.ascontiguousarray(inp["ssd_dt_bias"], f)
    m["ssd_a_log"] = np.ascontiguousarray(inp["ssd_a_log"], f)
    m["ssd_d"] = np.ascontiguousarray(inp["ssd_d"], f)
    m["ssd_norm_g"] = np.ascontiguousarray(inp["ssd_norm_g"], f)
    qg = np.asarray(inp["q_norm_g"], f)
    kg = np.asarray(inp["k_norm_g"], f)
    m["qk_g_fm"] = np.ascontiguousarray(np.stack([np.tile(qg, (1, 2)), np.tile(kg, (1, 2))], axis=-1))
    m["cm_conv_w_fm"] = np.ascontiguousarray(
        np.asarray(inp["cm_conv_w"], f).reshape(DEPTH, 31, 4, 128).transpose(0, 3, 2, 1))
    m["cm_vec_fm"] = np.ascontiguousarray(np.stack(
        [np.stack([fm(inp[k][l], 4) for k in ("cm_conv_b", "cm_ln_g", "cm_ln_b", "cm_pw_b")], axis=-1)
         for l in range(DEPTH)]))
    m["cm_pw_w"] = np.ascontiguousarray(inp["cm_pw_w"], f)
    m["w_out"] = np.ascontiguousarray(inp["w_out"], f)
    m["final_norm_g"] = np.ascontiguousarray(inp["final_norm_g"], f)
    m.update(const_inputs())
    return m


_CONST = None


def const_inputs():
    global _CONST
    if _CONST is not None:
        return _CONST
    f = np.float32
    i = np.arange(128)
    ident = np.eye(128, dtype=f)
    U = (i[:, None] <= i[None, :]).astype(f)
    UT = (i[:, None] >= i[None, :]).astype(f)
    d = i % 64
    half = (d % 32) // 16
    partner = np.where(half == 0, i + 16, i - 16)
    perm = np.zeros((128, 128), f)
    perm[partner, i] = 1.0
    bd = ((i[:, None] // 64) == (i[None, :] // 64)).astype(f) / 64.0
    ones = np.ones((128, 128), f)
    consts = np.stack([ident, U, UT, perm, bd, ones], axis=1)
    mf = np.where(i[:, None] <= i[None, :], 0.0, NEG).astype(f)
    mb = np.where(i[:, None] >= i[None, :], 0.0, NEG).astype(f)
    masks = np.stack([mf, mb], axis=1)
    u = np.arange(SEQ)
    pos = np.stack([u // 64, u % 64], axis=0).astype(np.float64)
    inv = 10000.0 ** (-np.arange(16, dtype=np.float64) / 16)
    fq = d % 16
    axis = d // 32
    ang = pos[axis][:, :] * inv[fq][:, None]
    ang = (pos[axis].astype(f) * inv.astype(f)[fq][:, None]).astype(f)
    cos = np.cos(ang).astype(f)
    sin = np.sin(ang).astype(f)
    sgn = np.where(half == 0, -1.0, 1.0).astype(f)[:, None]
    COS = np.concatenate([cos, np.ones((128, NCTX), f)], axis=1)
    SIN = np.concatenate([sin * sgn, np.zeros((128, NCTX), f)], axis=1)
    rope = np.stack([COS, SIN], axis=1)
    _CONST = {"consts": np.ascontiguousarray(consts), "masks": np.ascontiguousarray(masks),
              "rope": np.ascontiguousarray(rope)}
    return _CONST


def kernel(**inputs):
    inputs = {k: np.asarray(v) for k, v in inputs.items()}
    nc = bass.Bass("TRN2", target_bir_lowering=False)
    build(nc)
    in_maps = [host_inputs(inputs, c % 4) for c in range(8)]
    res = run_bass_kernel_spmd(nc, in_maps, core_ids=list(range(8)))
    return np.stack([res.results[b]["out"] for b in range(4)], axis=0).astype(np.float32)
```

```python
import numpy as np
import ml_dtypes
import concourse.bass as bass
import concourse.mybir as mybir
from concourse.bass_utils import run_bass_kernel_spmd

F32 = mybir.dt.float32
BF16 = mybir.dt.bfloat16
AF = mybir.ActivationFunctionType
ALU = mybir.AluOpType
AX = mybir.AxisListType

D = 1024
SEQ = 4096
NCTX = 256
T = SEQ + NCTX
DEPTH = 2
INW = 5656
EPS = 1e-6
O_Z, O_X, O_B, O_C, O_DT, O_Q, O_K, O_V, O_GA, O_UA, O_UB, O_GC = 0, 768, 1536, 1792, 2048, 2072, 2840, 3096, 3352, 4120, 4632, 5144
NEG = -30000.0


class Buf:
    __slots__ = ("name", "w", "r", "excl", "semi", "dcount")

    def __init__(self, name, excl=False):
        self.name = name
        self.w = {}
        self.r = {}
        self.excl = excl
        self.semi = None
        self.dcount = 0


class Op:
    __slots__ = ("eng", "idx", "fn", "deps", "dma", "sig", "semi", "dval")


def _merge(dst, src):
    for k, v in src.items():
        if dst.get(k, -1) < v:
            dst[k] = v


class Rec:
    ENG = ("pe", "act", "dve", "pool", "sp")

    def __init__(self, nc):
        self.nc = nc
        self.ops = {e: [] for e in self.ENG}
        self.ndma_sems = 0
        self.free = []
        self.capture = None
        self.semcount = {}
        self.eobj = {"pe": nc.tensor, "act": nc.scalar, "dve": nc.vector, "pool": nc.gpsimd, "sp": nc.sync}

    def add(self, eng, fn, reads=(), writes=(), dma=None, part=False):
        if dma is True:
            dma = writes[0]
        if self.capture is not None:
            self.capture.append((eng, fn, tuple(reads), tuple(writes), dma, part))
            return None
        op = Op()
        op.eng, op.fn, op.dma, op.sig = eng, fn, dma is not None, False
        op.idx = len(self.ops[eng])
        raw, oth = {}, {}
        for b in reads:
            _merge(raw, b.w)
            if b.excl:
                _merge(oth, b.r)
        for b in writes:
            _merge(oth, b.r)
            if not part or b.excl:
                _merge(oth, b.w)
        if dma is None:
            oth.pop(("c", eng), None)
            if eng == "pe":
                raw.pop(("c", eng), None)
        deps = raw
        _merge(deps, oth)
        op.deps = deps
        if dma is not None:
            b = dma
            if b.semi is None:
                if self.free:
                    b.semi, b.dcount = self.free.pop()
                    _merge(deps, {("d", b.semi): b.dcount})
                else:
                    b.semi = self.ndma_sems
                    self.ndma_sems += 1
            b.dcount += 16
            self.semcount[b.semi] = b.dcount
            op.semi, op.dval = b.semi, b.dcount
            my = {("d", b.semi): b.dcount}
        else:
            my = {("c", eng): op.idx}
        for b in reads:
            _merge(b.r, my)
        for b in writes:
            if part:
                _merge(b.w, my)
            else:
                b.w = dict(my)
                b.r = {}
        self.ops[eng].append(op)
        return op

    def interleave(self, lists):
        assert self.capture is None
        cur = [0] * len(lists)
        lo = 0
        n = len(lists)
        while lo < n:
            act = [i for i in (lo, lo + 1) if i < n]
            progressed = False
            for i in act:
                ops, need, done = lists[i]
                if cur[i] >= len(ops):
                    continue
                if i > lo and cur[i] >= need and cur[i - 1] < lists[i - 1][2]:
                    continue
                if i > lo and cur[i] == 0 and cur[lo] < len(lists[lo][0]) // 2:
                    continue
                self.add(*ops[cur[i]][:4], dma=ops[cur[i]][4], part=ops[cur[i]][5])
                cur[i] += 1
                progressed = True
            while lo < n and cur[lo] >= len(lists[lo][0]):
                lo += 1
            assert progressed or lo >= n

    def release(self, bufs):
        for b in bufs:
            if b.semi is not None:
                self.free.append((b.semi, b.dcount))
                b.semi = None

    def barrier(self):
        deps = {("c", e): len(self.ops[e]) - 1 for e in self.ENG if self.ops[e]}
        for s_, c_ in self.semcount.items():
            deps[("d", s_)] = c_
        for e in self.ENG:
            op = Op()
            op.eng, op.fn, op.dma, op.sig, op.idx = e, None, False, False, len(self.ops[e])
            op.deps = {k: v for k, v in deps.items() if k != ("c", e)}
            self.ops[e].append(op)

    def emit(self):
        nc = self.nc
        for e in self.ENG:
            for op in self.ops[e]:
                nd = {}
                for k, v in op.deps.items():
                    if k[0] == "c":
                        lst = self.ops[k[1]]
                        while v >= 0 and lst[v].fn is None:
                            v -= 1
                        if v < 0:
                            continue
                        lst[v].sig = True
                    nd[k] = v
                op.deps = nd
        pref = {}
        for e in self.ENG:
            c = 0
            arr = []
            for op in self.ops[e]:
                if op.sig and not op.dma:
                    c += 1
                arr.append(c)
            pref[e] = arr
        assert self.ndma_sems + 5 <= 98, self.ndma_sems
        esem = {e: nc.alloc_semaphore(name="es_" + e) for e in self.ENG}
        dsem = [nc.alloc_semaphore(name="ds_%d" % i) for i in range(self.ndma_sems)]
        for e in self.ENG:
            eo = self.eobj[e]
            known = {}
            for op in self.ops[e]:
                for k, v in op.deps.items():
                    if k[0] == "c":
                        sem, val, key = esem[k[1]], pref[k[1]][v], k
                    else:
                        sem, val, key = dsem[k[1]], v, k
                    if known.get(key, 0) >= val:
                        continue
                    known[key] = val
                    eo.wait_ge(sem, val)
                if op.fn is None:
                    continue
                ins = op.fn()
                if op.dma:
                    ins.then_inc(dsem[op.semi], 16)
                elif op.sig:
                    ins.then_inc(esem[e], 1)


class SB:
    ARENA = None
    ABYTES = 204800

    def __init__(self, nc, base=0, limit=None):
        if SB.ARENA is None or SB.ARENA[0] is not nc:
            SB.ARENA = (nc, nc.alloc_sbuf_tensor("arena", [128, SB.ABYTES // 4], F32))
        self.nc, self.off, self.limit = nc, base, (limit or SB.ABYTES)

    def alloc(self, name, shape, dt):
        return self.at(name, shape, dt, None)

    def at(self, name, shape, dt, off):
        esz = 4 if dt == F32 else 2
        n = int(np.prod(shape[1:]))
        nb = (n * esz + 63) // 64 * 64
        if off is None:
            off = self.off
            self.off += nb
        assert off % 4 == 0 and off + nb <= self.limit, (name, off, nb, self.limit)
        A = SB.ARENA[1]
        ap = A[0:shape[0], off // 4: off // 4 + nb // 4]
        if dt != F32:
            ap = ap.bitcast(dt)
        ap = ap[:, 0:n]
        if len(shape) > 2:
            names = " ".join("d%d" % i for i in range(len(shape) - 1))
            kw = {"d%d" % i: int(shape[i + 1]) for i in range(len(shape) - 1)}
            ap = ap.rearrange("p (%s) -> p %s" % (names, names), **kw)
        return ap


def fap(ap, dims):
    return bass.AP(ap.tensor, ap.offset, [list(ap.ap[0])] + [list(d) for d in dims])


def build(nc, phases=99, nlayers=DEPTH, taps=()):
    R = Rec(nc)
    dram_in = {}

    def din(name, shape, dt=F32):
        dram_in[name] = nc.dram_tensor(name, list(shape), dt, kind="ExternalInput")
        return dram_in[name]

    x_in = din("x", [SEQ, D])
    ctx_in = din("ctx", [NCTX, D])
    cvec = din("cvec", [128, 8, 2])
    ada_w = din("ada_w", [DEPTH, D, 3 * D])
    ada_b_fm = din("ada_b_fm", [DEPTH, 128, 24])
    ada_b_gate = din("ada_b_gate", [DEPTH, D])
    norm_g_fm = din("norm_g_fm", [DEPTH, 128, 8])
    w_in = din("w_in", [DEPTH, D, INW])
    scw = din("ssd_conv_w_fm", [DEPTH, 128, 10, 5])
    scb = din("ssd_conv_b_fm", [DEPTH, 128, 10])
    dtb = din("ssd_dt_bias", [DEPTH, 24])
    alog = din("ssd_a_log", [DEPTH, 24])
    ssdd = din("ssd_d", [DEPTH, 12])
    ssdng = din("ssd_norm_g", [DEPTH, 768])
    qkg = din("qk_g_fm", [DEPTH, 128, 2])
    ccw = din("cm_conv_w_fm", [DEPTH, 128, 4, 31])
    cmv = din("cm_vec_fm", [DEPTH, 128, 4, 4])
    cpw = din("cm_pw_w", [DEPTH, 512, 512])
    w_out = din("w_out", [DEPTH, 2048, D])
    fng = din("final_norm_g", [D])
    consts = din("consts", [128, 6, 128])
    masks = din("masks", [128, 2, 128])
    rope = din("rope", [128, 2, T])
    out_d = nc.dram_tensor("out", [SEQ, D], F32, kind="ExternalOutput")

    def scratch(name, shape, dt):
        kind = "ExternalOutput" if name in taps else "Internal"
        return nc.dram_tensor(name, list(shape), dt, kind=kind)

    gate_d = scratch("gate_d", [DEPTH, 2, 128, D], F32)
    x1_d = scratch("x1_d", [T, D], F32)
    xbc_d = scratch("xbc_d", [1280, T], F32)
    dt_d = scratch("dt_d", [T, 24], F32)
    zs_d = scratch("zs_d", [T, 768], F32)
    qT_d = scratch("qT_d", [768, T], BF16)
    kT_d = scratch("kT_d", [256, T], BF16)
    v_d = scratch("v_d", [T, 256], BF16)
    gas_d = scratch("gas_d", [768, T], F32)
    cv_d = scratch("cv_d", [512, T], F32)
    gcs_d = scratch("gcs_d", [512, T], F32)
    cat_d = scratch("cat_d", [2048, T], BF16)
    hb_d = scratch("hb_d", [34, 128, 768], BF16)
    B_gate = Buf("gate_d"); B_x1 = Buf("x1_d"); B_xbc = Buf("xbc_d"); B_dt = Buf("dt_d"); B_zs = Buf("zs_d")
    B_qT = Buf("qT_d"); B_kT = Buf("kT_d"); B_v = Buf("v_d"); B_gas = Buf("gas_d"); B_cv = Buf("cv_d")
    B_gcs = Buf("gcs_d"); B_cat = Buf("cat_d"); B_hb = Buf("hb_d"); B_out = Buf("out")
    B_in = Buf("inputs")

    PS = [nc.alloc_psum_tensor("ps%d" % i, [128, 512], F32) for i in range(8)]
    PB = [Buf("psb%d" % i, excl=True) for i in range(8)]

    sbp = SB(nc, 0, 24 * 1024)
    c_t = sbp.alloc("consts", [128, 6, 128], F32)
    cb_t = sbp.alloc("constsb", [128, 6, 128], BF16)
    mk_t = sbp.alloc("masks", [128, 2, 128], F32)
    eps_t = sbp.alloc("eps", [128, 2], F32)
    AB_t = sbp.alloc("AB", [128, DEPTH, 2, 8, 2], F32)
    B_c = Buf("consts"); B_AB = Buf("AB")
    IDENT, UIN, UTR, PERM, BDM, ONES = range(6)
    R.add("sp", lambda: nc.sync.dma_start(out=c_t[:], in_=consts.ap()), [B_in], [B_c], dma=True)
    B_mk = Buf("mk")
    R.add("sp", lambda: nc.sync.dma_start(out=mk_t[:], in_=masks.ap()), [B_in], [B_mk], dma=True)
    B_cb = Buf("cb")
    R.add("dve", lambda: nc.vector.tensor_copy(out=cb_t[:], in_=c_t[:]), [B_c], [B_cb])
    B_eps = Buf("eps")
    R.add("pool", lambda: nc.gpsimd.memset(eps_t[:, 0:1], EPS), [], [B_eps])
    R.add("pool", lambda: nc.gpsimd.memset(eps_t[:, 1:2], 1.0), [], [B_eps], part=True)

    WORK0 = 24 * 1024

    def phase0():
        sb = SB(nc, WORK0)
        aw = sb.alloc("aw", [128, 8, 3072], F32)
        B_aw = [Buf("aw%d" % k) for k in range(8)]
        cv = sb.alloc("cv", [128, 8, 2], F32)
        sc = sb.alloc("sc", [128, 8, 2], F32)
        screp = sb.alloc("screp", [128, 8, 2, 128], F32)
        abf = sb.alloc("abf", [128, 24], F32)
        ngf = sb.alloc("ngf", [128, 8], F32)
        abg = sb.alloc("abg", [128, D], F32)
        mod = sb.alloc("mod", [128, 16, 2], F32)
        gt = sb.alloc("gt", [128, 2, D], F32)
        B_cv, B_sc, B_screp, B_abf, B_ngf, B_abg, B_mod, B_gt = [Buf(n) for n in "cv sc screp abf ngf abg mod gt".split()]
        R.add("sp", lambda: nc.sync.dma_start(out=cv[:], in_=cvec.ap()), [B_in], [B_cv], dma=True)
        R.add("act", lambda: nc.scalar.activation(out=sc[:], in_=cv[:], func=AF.Silu), [B_cv], [B_sc])
        for kc in range(8):
            for j in range(2):
                R.add("dve", lambda kc=kc, j=j: nc.vector.tensor_copy(
                    out=screp[:, kc, j, :], in_=fap(sc[:, kc, j:j + 1], [[0, 128]])), [B_sc], [B_screp], part=True)
        for li in range(nlayers):
            for kc in range(8):
                R.add("sp", lambda kc=kc, li=li: nc.sync.dma_start(
                    out=aw[:, kc, :], in_=ada_w[li, kc * 128:(kc + 1) * 128, :]), [B_in], [B_aw[kc]], dma=True)
            R.add("sp", lambda li=li: nc.sync.dma_start(out=abf[:], in_=ada_b_fm[li]), [B_in], [B_abf], dma=True)
            R.add("sp", lambda li=li: nc.sync.dma_start(out=ngf[:], in_=norm_g_fm[li]), [B_in], [B_ngf], dma=True)
            R.add("sp", lambda li=li: nc.sync.dma_start(
                out=abg[:], in_=bass.AP(ada_b_gate.ap().tensor, li * D, [[0, 128], [1, D]])), [B_in], [B_abg], dma=True)
            for fc in range(16):
                for kc in range(8):
                    R.add("pe", lambda fc=fc, kc=kc: nc.tensor.matmul(
                        PS[0][:, fc * 2:fc * 2 + 2], lhsT=aw[:, kc, fc * 128:(fc + 1) * 128], rhs=sc[:, kc, :],
                        start=(kc == 0), stop=(kc == 7)), [B_aw[kc], B_sc], [PB[0]], part=not (fc == 0 and kc == 0))
            R.add("dve", lambda: nc.vector.tensor_tensor(
                out=mod[:], in0=fap(PS[0][:, 0:32], [[2, 16], [1, 2]]), in1=fap(abf[:, 0:16], [[1, 16], [0, 2]]),
                op=ALU.add), [PB[0], B_abf], [B_mod])
            R.add("dve", lambda li=li: nc.vector.scalar_tensor_tensor(
                out=AB_t[:, li, 0, :, :], in0=mod[:, 8:16, :], scalar=1.0, in1=fap(ngf[:, 0:8], [[1, 8], [0, 2]]),
                op0=ALU.add, op1=ALU.mult), [B_mod, B_ngf], [B_AB], part=True)
            R.add("dve", lambda li=li: nc.vector.tensor_copy(out=AB_t[:, li, 1, :, :], in_=mod[:, 0:8, :]),
                  [B_mod], [B_AB], part=True)
            for j in range(2):
                for cc in range(2):
                    pb = 1 + (j * 2 + cc) % 2
                    for kc in range(8):
                        R.add("pe", lambda j=j, cc=cc, kc=kc, pb=pb: nc.tensor.matmul(
                            PS[pb][:, :], lhsT=screp[:, kc, j, :], rhs=aw[:, kc, 2048 + cc * 512:2048 + (cc + 1) * 512],
                            start=(kc == 0), stop=(kc == 7)), [B_screp, B_aw[kc]], [PB[pb]], part=(kc > 0))
                    R.add("dve", lambda j=j, cc=cc, pb=pb: nc.vector.tensor_tensor(
                        out=gt[:, j, cc * 512:(cc + 1) * 512], in0=PS[pb][:, :], in1=abg[:, cc * 512:(cc + 1) * 512],
                        op=ALU.add), [PB[pb], B_abg], [B_gt], part=not (j == 0 and cc == 0))
            for j in range(2):
                R.add("sp", lambda li=li, j=j: nc.sync.dma_start(out=gate_d[li, j], in_=gt[:, j, :]),
                      [B_gt], [B_gate], dma=B_gt, part=True)

    phase0()
    if phases <= 0:
        R.emit()
        return dram_in

    HT_OFF = WORK0
    hT = SB(nc).at("hT", [128, 8, T], BF16, HT_OFF)
    HT_BYTES = 8 * T * 2
    B_hT = [Buf("hT%d" % i) for i in range(34)]
    WORK1 = HT_OFF + HT_BYTES

    def phase1(li):
        sb = SB(nc, WORK1)
        xt = [sb.alloc("xt", [128, D], F32) for _ in range(2)]
        xn = [sb.alloc("xn", [128, D], F32) for _ in range(2)]
        junk = sb.alloc("junk", [128, D], BF16)
        st = [sb.alloc("st", [128, 4], F32) for _ in range(2)]
        B_xt = [Buf("xt%d" % i) for i in range(2)]
        B_xn = [Buf("xn%d" % i) for i in range(2)]
        B_junk = Buf("junk")
        B_st = [Buf("st%d" % i) for i in range(2)]
        for tt in range(34):
            s = tt % 2
            j = 0 if tt < 32 else 1
            if li == 0:
                src = x_in[tt * 128:(tt + 1) * 128, :] if tt < 32 else ctx_in[(tt - 32) * 128:(tt - 31) * 128, :]
                srcb = B_in
            else:
                src = x1_d[tt * 128:(tt + 1) * 128, :]
                srcb = B_x1
            R.add("sp", lambda s=s, src=src: nc.sync.dma_start(out=xt[s][:], in_=src), [srcb], [B_xt[s]], dma=True)
            R.add("act", lambda s=s: nc.scalar.activation(out=junk[:], in_=xt[s][:], func=AF.Square,
                                                          accum_out=st[s][:, 0:1]), [B_xt[s]], [B_junk, B_st[s]])
            R.add("act", lambda s=s: nc.scalar.activation(out=st[s][:, 1:2], in_=st[s][:, 0:1], func=AF.Sqrt,
                                                          bias=eps_t[:, 0:1], scale=1.0 / D), [B_st[s], B_eps], [B_st[s]], part=True)
            R.add("dve", lambda s=s: nc.vector.reciprocal(out=st[s][:, 2:3], in_=st[s][:, 1:2]), [B_st[s]], [B_st[s]], part=True)
            R.add("dve", lambda s=s: nc.vector.tensor_scalar(out=xn[s][:], in0=xt[s][:], scalar1=st[s][:, 2:3], scalar2=None,
                                                             op0=ALU.mult), [B_xt[s], B_st[s]], [B_xn[s]])
            pb0 = 2 * (tt % 2)
            for c in range(8):
                pb = pb0 + c // 4
                R.add("pe", lambda s=s, c=c, pb=pb: nc.tensor.transpose(
                    PS[pb][:, (c % 4) * 128:(c % 4 + 1) * 128], xn[s][:, c * 128:(c + 1) * 128], c_t[:, IDENT, :]),
                    [B_xn[s], B_c], [PB[pb]], part=(c % 4 > 0))
            for c in range(8):
                pb = pb0 + c // 4
                if c % 2 == 0:
                    R.add("act", lambda c=c, pb=pb, tt=tt, j=j: nc.scalar.activation(
                        out=hT[:, c, tt * 128:(tt + 1) * 128], in_=PS[pb][:, (c % 4) * 128:(c % 4 + 1) * 128],
                        func=AF.Identity, bias=AB_t[:, li, 1, c, j:j + 1], scale=AB_t[:, li, 0, c, j:j + 1]),
                        [PB[pb], B_AB], [B_hT[tt]], part=True)
                else:
                    R.add("dve", lambda c=c, pb=pb, tt=tt, j=j: nc.vector.tensor_scalar(
                        out=hT[:, c, tt * 128:(tt + 1) * 128], in0=PS[pb][:, (c % 4) * 128:(c % 4 + 1) * 128],
                        scalar1=AB_t[:, li, 0, c, j:j + 1], scalar2=AB_t[:, li, 1, c, j:j + 1],
                        op0=ALU.mult, op1=ALU.add), [PB[pb], B_AB], [B_hT[tt]], part=True)

    def wsrc(li, c0, ncol):
        return w_in[li, :, c0:c0 + ncol].rearrange("(kc p) n -> p kc n", p=128)

    def phaseA(li):
        sb = SB(nc, WORK1)
        wst = [sb.alloc("wst", [128, 8, 128], F32) for _ in range(2)]
        wbf = [sb.alloc("wbf", [128, 8, 128], BF16) for _ in range(2)]
        B_wst = [Buf("wst%d" % i) for i in range(2)]
        B_wbf = [Buf("wbf%d" % i) for i in range(2)]
        ot = [sb.alloc("ot", [128, 512], F32) for _ in range(3)]
        B_ot = [Buf("ot%d" % i) for i in range(3)]
        obt = [sb.alloc("obt", [128, 512], BF16) for _ in range(2)]
        B_obt = [Buf("obt%d" % i) for i in range(2)]
        vecs = sb.alloc("vecs", [128, 10 * 5 + 10 + 2 + 4 * 31 + 16], F32)
        B_vecs = Buf("vecs")
        V_SCW, V_SCB, V_QKG, V_CCW, V_CMV = 0, 50, 60, 62, 62 + 124
        R.add("sp", lambda: nc.sync.dma_start(out=vecs[:, V_SCW:V_SCW + 50], in_=scw[li].rearrange("p a b -> p (a b)")), [B_in], [B_vecs], dma=True)
        R.add("sp", lambda: nc.sync.dma_start(out=vecs[:, V_SCB:V_SCB + 10], in_=scb[li]), [B_in], [B_vecs], dma=True, part=True)
        R.add("sp", lambda: nc.sync.dma_start(out=vecs[:, V_QKG:V_QKG + 2], in_=qkg[li]), [B_in], [B_vecs], dma=True, part=True)
        R.add("sp", lambda: nc.sync.dma_start(out=vecs[:, V_CCW:V_CCW + 124], in_=ccw[li].rearrange("p a b -> p (a b)")), [B_in], [B_vecs], dma=True, part=True)
        R.add("sp", lambda: nc.sync.dma_start(out=vecs[:, V_CMV:V_CMV + 16], in_=cmv[li].rearrange("p a b -> p (a b)")), [B_in], [B_vecs], dma=True, part=True)
        sub0 = sb.off
        cnt = {"w": 0, "ot": 0, "obt": 0, "ps": 0}

        def wload(c0):
            s_ = cnt["w"] % 2
            cnt["w"] += 1
            R.add("sp", lambda: nc.sync.dma_start(out=wst[s_][:], in_=wsrc(li, c0, 128)), [B_in], [B_wst[s_]], dma=True)
            R.add("pool", lambda: nc.gpsimd.tensor_copy(out=wbf[s_][:], in_=wst[s_][:]), [B_wst[s_]], [B_wbf[s_]])
            return s_

        def proj(ws, j, pb):
            w_ = 512 if j < 8 else 256
            for kc in range(8):
                R.add("pe", lambda kc=kc: nc.tensor.matmul(PS[pb][:, 0:w_], lhsT=wbf[ws][:, kc, :], rhs=hT[:, kc, j * 512:j * 512 + w_],
                                                          start=(kc == 0), stop=(kc == 7)),
                      [B_wbf[ws]] + B_hT[j * 4:j * 4 + w_ // 128], [PB[pb]], part=(kc > 0))
            return w_

        def next_ot():
            s_ = cnt["ot"] % 3
            cnt["ot"] += 1
            return s_

        def next_obt():
            s_ = cnt["obt"] % 2
            cnt["obt"] += 1
            return s_

        for (c0, nch, dst, B_dst) in (((O_GA, 6, gas_d, B_gas), (O_GC, 4, gcs_d, B_gcs)) if "noA1" not in taps else ()):
            for ch in range(nch):
                ws = wload(c0 + ch * 128)
                for j in range(9):
                    pb = cnt["ps"] % 2
                    cnt["ps"] += 1
                    w_ = proj(ws, j, pb)
                    o_ = next_ot()
                    R.add("act", lambda pb=pb, o_=o_, w_=w_: nc.scalar.activation(out=ot[o_][:, 0:w_], in_=PS[pb][:, 0:w_], func=AF.Silu),
                          [PB[pb]], [B_ot[o_]])
                    R.add("sp", lambda o_=o_, w_=w_, ch=ch, j=j, dst=dst: nc.sync.dma_start(
                        out=dst[ch * 128:(ch + 1) * 128, j * 512:j * 512 + w_], in_=ot[o_][:, 0:w_]), [B_ot[o_]], [B_dst], dma=B_ot[o_], part=True)

        sbq = SB(nc, sub0)
        ropet = sbq.alloc("rope", [128, 2, T], F32)
        B_rope = Buf("rope")
        R.add("sp", lambda: nc.sync.dma_start(out=ropet[:, 0, :], in_=rope[:, 0, :]), [B_in], [B_rope], dma=True)
        R.add("sp", lambda: nc.sync.dma_start(out=ropet[:, 1, :], in_=rope[:, 1, :]), [B_in], [B_rope], dma=True, part=True)
        P2 = range(2)
        sqb = [sbq.alloc("sqb", [128, 512], BF16) for _ in P2]; B_sqb = [Buf("sqb%d" % i) for i in P2]
        sd = [sbq.alloc("sd", [128, 512], F32) for _ in P2]; B_sd = [Buf("sd%d" % i) for i in P2]
        rs = [sbq.alloc("rs", [128, 512], F32) for _ in P2]; B_rs = [Buf("rs%d" % i) for i in P2]
        qnb = [sbq.alloc("qnb", [128, 512], BF16) for _ in P2]; B_qnb = [Buf("qnb%d" % i) for i in P2]
        t1 = [sbq.alloc("t1", [128, 512], F32) for _ in P2]; B_t1 = [Buf("t1%d" % i) for i in P2]
        t2 = [sbq.alloc("t2", [128, 512], F32) for _ in P2]; B_t2 = [Buf("t2%d" % i) for i in P2]
        lists = []
        tile_no = 0
        qk_chunks = []
        for (c0, nch, dst, B_dst, gi) in (((O_Q, 6, qT_d, B_qT, 0), (O_K, 2, kT_d, B_kT, 1)) if "noA2" not in taps else ()):
            for ch in range(nch):
                qk_chunks.append((c0 + ch * 128, ch, dst, B_dst, gi))
        ws_of = {}
        for kq, (wc0, ch, dst, B_dst, gi) in enumerate(qk_chunks):
            if True:
                for j in range(9):
                    pp = tile_no % 2
                    tile_no += 1
                    ws = None

                    def body(kq=kq, j=j, pp=pp, ch=ch, dst=dst, B_dst=B_dst, gi=gi):
                        P0, P1, P2_ = 3 * pp, 3 * pp + 1, 3 * pp + 2
                        if j == 0:
                            if kq == 0:
                                ws_of[0] = wload(qk_chunks[0][0])
                            if kq + 1 < len(qk_chunks):
                                ws_of[kq + 1] = wload(qk_chunks[kq + 1][0])
                        ws = ws_of[kq]
                        w_ = proj(ws, j, P0)
                        cs = slice(j * 512, j * 512 + w_)
                        R.add("act", lambda: nc.scalar.activation(out=sqb[pp][:, 0:w_], in_=PS[P0][:, 0:w_], func=AF.Square), [PB[P0]], [B_sqb[pp]])
                        R.add("pe", lambda: nc.tensor.matmul(PS[P1][:, 0:w_], lhsT=cb_t[:, BDM, :], rhs=sqb[pp][:, 0:w_], start=True, stop=True),
                              [B_cb, B_sqb[pp]], [PB[P1]])
                        R.add("act", lambda: nc.scalar.activation(out=sd[pp][:, 0:w_], in_=PS[P1][:, 0:w_], func=AF.Ln, bias=eps_t[:, 0:1], scale=1.0),
                              [PB[P1], B_eps], [B_sd[pp]])
                        R.add("act", lambda: nc.scalar.activation(out=rs[pp][:, 0:w_], in_=sd[pp][:, 0:w_], func=AF.Exp, scale=-0.5), [B_sd[pp]], [B_rs[pp]])
                        R.add("dve", lambda: nc.vector.scalar_tensor_tensor(
                            out=qnb[pp][:, 0:w_], in0=PS[P0][:, 0:w_], scalar=vecs[:, V_QKG + gi:V_QKG + gi + 1], in1=rs[pp][:, 0:w_], op0=ALU.mult, op1=ALU.mult),
                            [PB[P0], B_vecs, B_rs[pp]], [B_qnb[pp]])
                        R.add("pe", lambda: nc.tensor.matmul(PS[P2_][:, 0:w_], lhsT=cb_t[:, PERM, :], rhs=qnb[pp][:, 0:w_], start=True, stop=True),
                              [B_cb, B_qnb[pp]], [PB[P2_]])
                        R.add("pool", lambda: nc.gpsimd.tensor_tensor(out=t1[pp][:, 0:w_], in0=qnb[pp][:, 0:w_], in1=ropet[:, 0, cs], op=ALU.mult),
                              [B_qnb[pp], B_rope], [B_t1[pp]])
                        R.add("dve", lambda: nc.vector.tensor_tensor(out=t2[pp][:, 0:w_], in0=PS[P2_][:, 0:w_], in1=ropet[:, 1, cs], op=ALU.mult),
                              [PB[P2_], B_rope], [B_t2[pp]])
                        R.add("dve", lambda: nc.vector.tensor_tensor(out=obt[pp][:, 0:w_], in0=t1[pp][:, 0:w_], in1=t2[pp][:, 0:w_], op=ALU.add),
                              [B_t1[pp], B_t2[pp]], [B_obt[pp]])
                        R.add("sp", lambda: nc.sync.dma_start(out=dst[ch * 128:(ch + 1) * 128, cs], in_=obt[pp][:, 0:w_]), [B_obt[pp]], [B_dst], dma=B_obt[pp], part=True)
                    lst = []
                    R.capture = lst
                    body()
                    R.capture = None
                    lists.append((lst, 10 ** 9, 0))
        R.interleave(lists)
        R.barrier()

        sbx = SB(nc, sub0)
        dg5 = sbx.alloc("dg5", [128, 10, 5, 128], BF16); B_dg5 = Buf("dg5")
        RBW = 2 + SEQ + 2 + 2 + NCTX + 2
        rb = [sbx.alloc("rb", [128, RBW], BF16) for _ in range(2)]
        B_rb = [Buf("rb%d" % i) for i in range(2)]
        for ch in range(10):
            for k in range(5):
                R.add("dve", lambda ch=ch, k=k: nc.vector.tensor_scalar(
                    out=dg5[:, ch, k, :], in0=c_t[:, IDENT, :], scalar1=vecs[:, V_SCW + ch * 5 + k:V_SCW + ch * 5 + k + 1], scalar2=None, op0=ALU.mult),
                    [B_c, B_vecs], [B_dg5], part=True)
        for i in range(2):
            R.add("pool", lambda i=i: nc.gpsimd.memset(rb[i][:], 0.0), [], [B_rb[i]])

        def rbcol(j, pad):
            return pad + j * 512 if j < 8 else pad + SEQ + 2 * pad

        def xproj(ch):
            ws = wload(O_X + ch * 128)
            r_ = ch % 2
            for j in range(9):
                pb = cnt["ps"] % 2
                cnt["ps"] += 1
                w_ = proj(ws, j, pb)
                c0_ = rbcol(j, 2)
                R.add("act", lambda pb=pb, w_=w_, c0_=c0_: nc.scalar.activation(out=rb[r_][:, c0_:c0_ + w_], in_=PS[pb][:, 0:w_], func=AF.Copy),
                      [PB[pb]], [B_rb[r_]], part=(j > 0))

        def xconv(ch):
            r_ = ch % 2
            for j in range(9):
                w_ = 512 if j < 8 else 256
                pb = 2 + j % 2
                st_ = rbcol(j, 2) - 2
                for k in range(5):
                    R.add("pe", lambda k=k, pb=pb, w_=w_, st_=st_: nc.tensor.matmul(
                        PS[pb][:, 0:w_], lhsT=dg5[:, ch, k, :], rhs=rb[r_][:, st_ + k:st_ + k + w_], start=(k == 0), stop=(k == 4)),
                        [B_dg5, B_rb[r_]], [PB[pb]], part=(k > 0))
                o_ = next_ot()
                R.add("act", lambda pb=pb, o_=o_, w_=w_: nc.scalar.activation(
                    out=ot[o_][:, 0:w_], in_=PS[pb][:, 0:w_], func=AF.Silu, bias=vecs[:, V_SCB + ch:V_SCB + ch + 1], scale=1.0),
                    [PB[pb], B_vecs], [B_ot[o_]])
                R.add("sp", lambda o_=o_, w_=w_, j=j: nc.sync.dma_start(
                    out=xbc_d[ch * 128:(ch + 1) * 128, j * 512:j * 512 + w_], in_=ot[o_][:, 0:w_]), [B_ot[o_]], [B_xbc], dma=B_ot[o_], part=True)

        if "noA3" not in taps:
            xproj(0)
        for ch in (range(10) if "noA3" not in taps else ()):
            if ch + 1 < 10:
                xproj(ch + 1)
            xconv(ch)
        R.barrier()

        sbc = SB(nc, sub0)
        dg31 = sbc.alloc("dg31", [128, 4, 31, 128], BF16); B_dg31 = Buf("dg31")
        RUW = 15 + SEQ + 15 + 15 + NCTX + 15
        ru = [sbc.alloc("ru", [128, RUW], BF16) for _ in range(2)]
        B_ru = [Buf("ru%d" % i) for i in range(2)]
        sg = [sbc.alloc("sg", [128, 512], F32) for _ in range(2)]
        B_sg = [Buf("sg%d" % i) for i in range(2)]
        for ch in range(4):
            for k in range(31):
                eng, eo = ("dve", nc.vector) if k % 2 == 0 else ("pool", nc.gpsimd)
                R.add(eng, lambda ch=ch, k=k, eo=eo: eo.tensor_scalar(
                    out=dg31[:, ch, k, :], in0=c_t[:, IDENT, :], scalar1=vecs[:, V_CCW + ch * 31 + k:V_CCW + ch * 31 + k + 1], scalar2=None, op0=ALU.mult),
                    [B_c, B_vecs], [B_dg31], part=True)
        for i in range(2):
            R.add("pool", lambda i=i: nc.gpsimd.memset(ru[i][:], 0.0), [], [B_ru[i]])

        def uproj(ch):
            wa = wload(O_UA + ch * 128)
            wb = wload(O_UB + ch * 128)
            r_ = ch % 2
            for j in range(9):
                w_ = proj(wa, j, 0)
                proj(wb, j, 1)
                s_ = j % 2
                R.add("act", lambda s_=s_, w_=w_: nc.scalar.activation(out=sg[s_][:, 0:w_], in_=PS[1][:, 0:w_], func=AF.Sigmoid), [PB[1]], [B_sg[s_]])
                c0_ = rbcol(j, 15)
                R.add("dve", lambda s_=s_, w_=w_, c0_=c0_: nc.vector.tensor_tensor(
                    out=ru[r_][:, c0_:c0_ + w_], in0=PS[0][:, 0:w_], in1=sg[s_][:, 0:w_], op=ALU.mult), [PB[0], B_sg[s_]], [B_ru[r_]], part=(j > 0))

        def uconv(ch):
            r_ = ch % 2
            for j in range(9):
                w_ = 512 if j < 8 else 256
                pb = 2 + j % 2
                st_ = rbcol(j, 15) - 15
                for k in range(31):
                    R.add("pe", lambda k=k, pb=pb, w_=w_, st_=st_: nc.tensor.matmul(
                        PS[pb][:, 0:w_], lhsT=dg31[:, ch, k, :], rhs=ru[r_][:, st_ + k:st_ + k + w_], start=(k == 0), stop=(k == 30)),
                        [B_dg31, B_ru[r_]], [PB[pb]], part=(k > 0))
                o_ = next_ot()
                R.add("act", lambda pb=pb, o_=o_, w_=w_: nc.scalar.activation(
                    out=ot[o_][:, 0:w_], in_=PS[pb][:, 0:w_], func=AF.Identity, bias=vecs[:, V_CMV + ch * 4:V_CMV + ch * 4 + 1], scale=1.0),
                    [PB[pb], B_vecs], [B_ot[o_]])
                R.add("sp", lambda o_=o_, w_=w_, j=j: nc.sync.dma_start(
                    out=cv_d[ch * 128:(ch + 1) * 128, j * 512:j * 512 + w_], in_=ot[o_][:, 0:w_]), [B_ot[o_]], [B_cv], dma=B_ot[o_], part=True)

        if "noA4" not in taps:
            uproj(0)
        for ch in (range(4) if "noA4" not in taps else ()):
            if ch + 1 < 4:
                uproj(ch + 1)
            uconv(ch)
        R.barrier()

        sbt = SB(nc, sub0)
        NZ = 768 + 24 + 256
        wzs = sbt.alloc("wzs", [128, 8, 384], F32); B_wzs = Buf("wzs")
        wz = sbt.alloc("wz", [128, 8, NZ], BF16); B_wz = Buf("wz")
        for (dc, c0, n_) in ((0, O_Z, 384), (384, O_Z + 384, 384), (768, O_DT, 24), (792, O_V, 256)):
            R.add("sp", lambda c0=c0, n_=n_: nc.sync.dma_start(out=wzs[:, :, 0:n_], in_=wsrc(li, c0, n_)), [B_in], [B_wzs], dma=True)
            R.add("pool", lambda dc=dc, n_=n_: nc.gpsimd.tensor_copy(out=wz[:, :, dc:dc + n_], in_=wzs[:, :, 0:n_]), [B_wzs], [B_wz], part=(dc > 0))
        zt = [sbt.alloc("zt", [128, 768], F32) for _ in range(2)]
        B_zt = [Buf("zt%d" % i) for i in range(2)]
        vt = [sbt.alloc("vt", [128, 256], BF16) for _ in range(2)]
        B_vt = [Buf("vt%d" % i) for i in range(2)]
        dta = sbt.alloc("dta", [128, 34, 24], F32); B_dta = Buf("dta")
        dtw = sbt.alloc("dtw", [128, 3, 34 * 24], F32); B_dtw = Buf("dtw")
        dtbt = sbt.alloc("dtbt", [128, 24], F32); B_dtbt = Buf("dtbt")
        R.add("sp", lambda: nc.sync.dma_start(out=dtbt[:], in_=bass.AP(dtb.ap().tensor, li * 24, [[0, 128], [1, 24]])), [B_in], [B_dtbt], dma=True)
        for tt in (range(34) if "noA5" not in taps else ()):
            s_ = tt % 2
            pbs = (0, 1, 2) if tt % 2 == 0 else (3, 4, 5)
            for (pb, dc, n_) in ((pbs[0], 0, 384), (pbs[1], 384, 384), (pbs[2], 768, 280)):
                for kc in range(8):
                    R.add("pe", lambda kc=kc, pb=pb, dc=dc, n_=n_, tt=tt: nc.tensor.matmul(
                        PS[pb][:, 0:n_], lhsT=hT[:, kc, tt * 128:(tt + 1) * 128], rhs=wz[:, kc, dc:dc + n_], start=(kc == 0), stop=(kc == 7)),
                        [B_hT[tt], B_wz], [PB[pb]], part=(kc > 0))
            for hf_ in range(2):
                R.add("act", lambda hf_=hf_, s_=s_, pbs=pbs: nc.scalar.activation(
                    out=zt[s_][:, hf_ * 384:(hf_ + 1) * 384], in_=PS[pbs[hf_]][:, 0:384], func=AF.Silu), [PB[pbs[hf_]]], [B_zt[s_]], part=(hf_ > 0))
            R.add("sp", lambda s_=s_, tt=tt: nc.sync.dma_start(out=zs_d[tt * 128:(tt + 1) * 128, :], in_=zt[s_][:]), [B_zt[s_]], [B_zs], dma=B_zt[s_], part=True)
            R.add("dve", lambda s_=s_, pbs=pbs: nc.vector.tensor_copy(out=vt[s_][:], in_=PS[pbs[2]][:, 24:280]), [PB[pbs[2]]], [B_vt[s_]])
            R.add("sp", lambda s_=s_, tt=tt: nc.sync.dma_start(out=v_d[tt * 128:(tt + 1) * 128, :], in_=vt[s_][:]), [B_vt[s_]], [B_v], dma=B_vt[s_], part=True)
            R.add("dve", lambda tt=tt, pbs=pbs: nc.vector.tensor_tensor(out=dta[:, tt, :], in0=PS[pbs[2]][:, 0:24], in1=dtbt[:], op=ALU.add),
                  [PB[pbs[2]], B_dtbt], [B_dta], part=True)
        dflat = dta.rearrange("p a b -> p (a b)")
        R.add("act", lambda: nc.scalar.activation(out=dtw[:, 0, :], in_=dflat, func=AF.Abs), [B_dta], [B_dtw])
        R.add("act", lambda: nc.scalar.activation(out=dtw[:, 1, :], in_=dtw[:, 0, :], func=AF.Exp, scale=-1.0), [B_dtw], [B_dtw], part=True)
        R.add("act", lambda: nc.scalar.activation(out=dtw[:, 2, :], in_=dtw[:, 1, :], func=AF.Ln, bias=eps_t[:, 1:2], scale=1.0), [B_dtw, B_eps], [B_dtw], part=True)
        R.add("dve", lambda: nc.vector.scalar_tensor_tensor(out=dtw[:, 0, :], in0=dflat, scalar=0.0, in1=dtw[:, 2, :], op0=ALU.max, op1=ALU.add),
              [B_dta, B_dtw], [B_dtw], part=True)
        R.add("sp", lambda: nc.sync.dma_start(out=dt_d.ap().rearrange("(a p) h -> p a h", p=128),
                                              in_=dtw[:, 0, :].rearrange("p (a h) -> p a h", h=24)), [B_dtw], [B_dt], dma=B_dtw)
        R.barrier()
        R.release(B_wst + B_ot + B_obt + [B_vecs, B_rope, B_wzs, B_dtbt, B_dtw] + B_zt + B_vt)

    def phaseB(li, last):
        sb = SB(nc, WORK0)
        KT = sb.alloc("KT", [128, 4, T], BF16); B_KT = Buf("KT")
        VA = sb.alloc("VA", [128, 34, 4, 128], BF16); B_VA = Buf("VA")
        qs = [sb.alloc("qs", [128, 3, 512], BF16) for _ in range(2)]
        B_qs = [Buf("qs%d" % i) for i in range(2)]
        pT = [sb.alloc("pT", [128, 512], BF16) for _ in range(4)]
        B_pT = [Buf("pT%d" % i) for i in range(4)]
        rsb = [sb.alloc("rsb", [128, 512], F32) for _ in range(2)]
        B_rsb = [Buf("rsb%d" % i) for i in range(2)]
        ob = [sb.alloc("ob", [128, 512], F32) for _ in range(2)]
        B_ob = [Buf("ob%d" % i) for i in range(2)]
        gs = [sb.alloc("gs", [128, 512], F32) for _ in range(2)]
        B_gs = [Buf("gs%d" % i) for i in range(2)]
        obb = [sb.alloc("obb", [128, 512], BF16) for _ in range(2)]
        B_obb = [Buf("obb%d" % i) for i in range(2)]
        R.add("pool", lambda: nc.gpsimd.memset(KT[64:128, :, :], 0.0), [], [B_KT])
        R.add("sp", lambda: nc.sync.dma_start(out=KT[0:64, :, :], in_=kT_d.ap().rearrange("(g d) t -> d g t", d=64)), [B_kT], [B_KT], dma=True, part=True)
        for i in range(2):
            R.add("pool", lambda i=i: nc.gpsimd.memset(qs[i][64:128, :, :], 0.0), [], [B_qs[i]])
        R.add("pool", lambda: nc.gpsimd.memset(VA[:, :, :, 64:128], 1.0), [], [B_VA])
        for tt in range(34):
            R.add("sp", lambda tt=tt: nc.sync.dma_start(out=VA[:, tt, :, 0:64], in_=v_d[tt * 128:(tt + 1) * 128, :].rearrange("p (g d) -> p g d", d=64)),
                  [B_v], [B_VA], dma=True, part=True)
        fin = 0
        for j in range(9):
            if last and j == 8:
                continue
            w_ = 512 if j < 8 else 256
            kts = list(range(34)) if j < 8 else [32, 33]
            for g in range(4):
                pbase = (g % 2) * 64
                qi = (j * 4 + g) % 2
                R.add("sp", lambda g=g, j=j, w_=w_, qi=qi, pbase=pbase: nc.sync.dma_start(
                    out=qs[qi][0:64, :, 0:w_],
                    in_=qT_d[g * 192:(g + 1) * 192, j * 512:j * 512 + w_].rearrange("(h d) t -> d h t", d=64)), [B_qT], [B_qs[qi]], dma=True, part=True)
                steps = [(kt, hh) for kt in kts for hh in range(3)]
                n = len(steps)

                def S(s_):
                    kt, hh = steps[s_]
                    bk = s_ % 4
                    R.add("pe", lambda kt=kt, hh=hh, bk=bk, g=g, pbase=pbase, qi=qi, w_=w_: nc.tensor.matmul(PS[bk][:, 0:w_], lhsT=KT[:, g, kt * 128:(kt + 1) * 128],
                                                         rhs=qs[qi][:, hh, 0:w_], start=True, stop=True), [B_KT, B_qs[qi]], [PB[bk]])
                    R.add("act", lambda bk=bk, w_=w_: nc.scalar.activation(out=pT[bk][:, 0:w_], in_=PS[bk][:, 0:w_], func=AF.Exp, scale=0.125), [PB[bk]], [B_pT[bk]])

                def PV(s_):
                    kt, hh = steps[s_]
                    bk = s_ % 4
                    R.add("pe", lambda kt=kt, hh=hh, bk=bk, g=g, w_=w_, k0=kts[0], k1=kts[-1]: nc.tensor.matmul(
                        PS[4 + hh][:, 0:w_], lhsT=VA[:, kt, g, :], rhs=pT[bk][:, 0:w_],
                        start=(kt == k0), stop=(kt == k1)), [B_VA, B_pT[bk]], [PB[4 + hh]], part=(kt != kts[0]))

                for s_ in range(n + 2):
                    if s_ < n:
                        S(s_)
                    if s_ >= 2:
                        PV(s_ - 2)
                for hh in range(3):
                    h_ = g * 3 + hh
                    f_ = fin % 2
                    fin += 1
                    R.add("sp", lambda h_=h_, f_=f_, j=j, w_=w_: nc.sync.dma_start(
                        out=gs[f_][0:64, 0:w_], in_=gas_d[h_ * 64:(h_ + 1) * 64, j * 512:j * 512 + w_]), [B_gas], [B_gs[f_]], dma=True)
                    R.add("act", lambda hh=hh, f_=f_, w_=w_: nc.scalar.activation(out=rsb[f_][64:128, 0:w_], in_=PS[4 + hh][64:128, 0:w_], func=AF.Ln), [PB[4 + hh]], [B_rsb[f_]])
                    R.add("act", lambda f_=f_, w_=w_: nc.scalar.activation(out=rsb[f_][64:128, 0:w_], in_=rsb[f_][64:128, 0:w_], func=AF.Exp, scale=-1.0), [B_rsb[f_]], [B_rsb[f_]])
                    R.add("dve", lambda hh=hh, f_=f_, w_=w_: nc.vector.tensor_tensor(out=ob[f_][0:64, 0:w_], in0=PS[4 + hh][0:64, 0:w_], in1=rsb[f_][64:128, 0:w_], op=ALU.mult),
                          [PB[4 + hh], B_rsb[f_]], [B_ob[f_]])
                    R.add("pool", lambda f_=f_, w_=w_: nc.gpsimd.tensor_tensor(out=obb[f_][0:64, 0:w_], in0=ob[f_][0:64, 0:w_], in1=gs[f_][0:64, 0:w_], op=ALU.mult),
                          [B_ob[f_], B_gs[f_]], [B_obb[f_]])
                    R.add("sp", lambda h_=h_, f_=f_, j=j, w_=w_: nc.sync.dma_start(
                        out=cat_d[768 + h_ * 64:768 + (h_ + 1) * 64, j * 512:j * 512 + w_], in_=obb[f_][0:64, 0:w_]), [B_obb[f_]], [B_cat], dma=B_obb[f_], part=True)
        R.barrier()
        R.release([B_KT, B_VA] + B_qs + B_gs + B_obb)

    def bc12(ap2d, n=64):
        return fap(ap2d, [[1, ap2d.shape[1]], [0, n]])

    def v3(ap2d, a, b):
        return fap(ap2d, [[b, a], [1, b]])

    def phaseC(li, last):
        sb = SB(nc, WORK0)
        Abc = sb.alloc("Abc", [128, 24], F32); B_Abc = Buf("Abc")
        Dbc = sb.alloc("Dbc", [128, 12], F32); B_Dbc = Buf("Dbc")
        Gbc = sb.alloc("Gbc", [128, 768], F32); B_Gbc = Buf("Gbc")
        mrep = sb.alloc("mrep", [128, 2, 4, 128], BF16); B_mrep = Buf("mrep")
        R.add("sp", lambda: nc.sync.dma_start(out=Abc[:], in_=bass.AP(alog.ap().tensor, li * 24, [[0, 128], [1, 24]])), [B_in], [B_Abc], dma=True)
        R.add("act", lambda: nc.scalar.activation(out=Abc[:], in_=Abc[:], func=AF.Exp), [B_Abc], [B_Abc])
        R.add("dve", lambda: nc.vector.tensor_scalar(out=Abc[:], in0=Abc[:], scalar1=-1.0, scalar2=None, op0=ALU.mult), [B_Abc], [B_Abc])
        R.add("sp", lambda: nc.sync.dma_start(out=Dbc[:], in_=bass.AP(ssdd.ap().tensor, li * 12, [[0, 128], [1, 12]])), [B_in], [B_Dbc], dma=True)
        R.add("sp", lambda: nc.sync.dma_start(out=Gbc[:], in_=bass.AP(ssdng.ap().tensor, li * 768, [[0, 128], [1, 768]])), [B_in], [B_Gbc], dma=True)
        for dr in range(2):
            R.add("dve", lambda dr=dr: nc.vector.tensor_copy(out=mrep[:, dr, :, :], in_=fap(mk_t[:, dr, :], [[0, 4], [1, 128]])), [B_mk], [B_mrep], part=(dr > 0))
        P2 = range(2)

        def mk(name, shape, dt, n=2):
            return [sb.alloc(name, shape, dt) for _ in range(n)], [Buf("%s%d" % (name, i)) for i in range(n)]
        xT, B_xT = mk("xT", [128, 6, 128], F32)
        bcT, B_bcT = mk("bcT", [128, 4, 128], F32)
        dtc, B_dtc = mk("dtc", [128, 24], F32)
        zc, B_zc = mk("zc", [128, 768], F32)
        hbin, B_hbin = mk("hbin", [128, 768], BF16)
        xs, B_xs = mk("xs", [128, 768], F32)
        Btm, B_Btm = mk("Btm", [128, 256], BF16)
        bcb, B_bcb = mk("bcb", [128, 4, 128], BF16)
        av, B_av = mk("av", [128, 24], BF16)
        ct, B_ct = mk("ct", [128, 72], F32)
        ex, B_ex = mk("ex", [128, 72], F32)
        ncum, B_ncum = mk("ncum", [128, 24], F32)
        dtd, B_dtd = mk("dtd", [128, 24], F32)
        xw = [[sb.alloc("xw", [128, 768], BF16) for _ in range(4)] for _ in P2]
        B_xw = [[Buf("xw%d_%d" % (i, k)) for k in range(4)] for i in P2]
        cbs, B_cbs = mk("cbs", [128, 2, 128], BF16)
        Dm = [[sb.alloc("Dm", [128, 12, 128], BF16) for _ in P2] for _ in P2]; B_Dm = [[Buf("Dm%d%d" % (i, k)) for k in P2] for i in P2]
        Et = [[sb.alloc("Et", [128, 12, 128], BF16) for _ in P2] for _ in P2]; B_Et = [[Buf("Et%d%d" % (i, k)) for k in P2] for i in P2]
        Mt = [[sb.alloc("Mt", [128, 12, 128], BF16) for _ in P2] for _ in P2]; B_Mt = [[Buf("Mt%d%d" % (i, k)) for k in P2] for i in P2]
        yo = [[sb.alloc("yo", [128, 768], F32) for _ in P2] for _ in P2]; B_yo = [[Buf("yo%d%d" % (i, k)) for k in P2] for i in P2]
        yv, B_yv = mk("yv", [128, 768], F32)
        t3, B_t3 = mk("t3", [128, 768], F32)
        yn, B_yn = mk("yn", [128, 768], F32)
        junk, B_junk = mk("junkc", [128, 768], BF16)
        st, B_st = mk("stc", [128, 4], F32)
        catT, B_catT = mk("catT", [128, 6, 128], BF16)
        tmp, B_tmp = mk("tmp", [128, 768], F32)
        hst, B_hst = mk("hst", [128, 768], F32)
        hfb = sb.alloc("hfb", [128, 768], BF16); B_hfb = Buf("hfb")
        hbb, B_hbb = mk("hbb", [128, 768], BF16)
        for i in P2:
            R.add("pool", lambda i=i: nc.gpsimd.memset(hst[i][:], 0.0), [], [B_hst[i]])
            R.add("pool", lambda i=i: nc.gpsimd.memset(hbb[i][:], 0.0), [], [B_hbb[i]])
        R.add("pool", lambda: nc.gpsimd.memset(hfb[:], 0.0), [], [B_hfb])

        def prep(c, pp, full):
            S0, S1, S2 = 4 * pp, 4 * pp + 1, 4 * pp + 2
            cs = slice(c * 128, (c + 1) * 128)
            R.add("sp", lambda: nc.sync.dma_start(out=xT[pp][:], in_=xbc_d[0:768, cs].rearrange("(c p) t -> p c t", p=128)), [B_xbc], [B_xT[pp]], dma=True)
            R.add("sp", lambda: nc.sync.dma_start(out=bcT[pp][:], in_=xbc_d[768:1280, cs].rearrange("(c p) t -> p c t", p=128)), [B_xbc], [B_bcT[pp]], dma=True)
            R.add("sp", lambda: nc.sync.dma_start(out=dtc[pp][:], in_=dt_d[cs, :]), [B_dt], [B_dtc[pp]], dma=True)
            if full:
                R.add("sp", lambda: nc.sync.dma_start(out=zc[pp][:], in_=zs_d[cs, :]), [B_zs], [B_zc[pp]], dma=True)
                R.add("sp", lambda: nc.sync.dma_start(out=hbin[pp][:], in_=hb_d[c]), [B_hb], [B_hbin[pp]], dma=True)
            for i in range(6):
                bk, col = (S0, i * 128) if i < 4 else (S1, (i - 4) * 128)
                R.add("pe", lambda i=i, bk=bk, col=col: nc.tensor.transpose(PS[bk][:, col:col + 128], xT[pp][:, i, :], c_t[:, IDENT, :]),
                      [B_xT[pp], B_c], [PB[bk]], part=(i not in (0, 4)))
            for i in range(2):
                R.add("pe", lambda i=i: nc.tensor.transpose(PS[S1][:, 256 + i * 128:384 + i * 128], bcT[pp][:, i, :], c_t[:, IDENT, :]),
                      [B_bcT[pp], B_c], [PB[S1]], part=True)
            R.add("act", lambda: nc.scalar.activation(out=xs[pp][:, 0:512], in_=PS[S0][:, 0:512], func=AF.Copy), [PB[S0]], [B_xs[pp]])
            R.add("act", lambda: nc.scalar.activation(out=xs[pp][:, 512:768], in_=PS[S1][:, 0:256], func=AF.Copy), [PB[S1]], [B_xs[pp]], part=True)
            R.add("dve", lambda: nc.vector.tensor_copy(out=Btm[pp][:], in_=PS[S1][:, 256:512]), [PB[S1]], [B_Btm[pp]])
            R.add("pool", lambda: nc.gpsimd.tensor_copy(out=bcb[pp][:], in_=bcT[pp][:]), [B_bcT[pp]], [B_bcb[pp]])
            R.add("dve", lambda: nc.vector.tensor_tensor(out=av[pp][:], in0=dtc[pp][:], in1=Abc[:], op=ALU.mult), [B_dtc[pp], B_Abc], [B_av[pp]])
            R.add("pe", lambda: nc.tensor.matmul(PS[S2][:, 0:12], lhsT=cb_t[:, UIN, :], rhs=av[pp][:, 0:12], start=True, stop=True), [B_cb, B_av[pp]], [PB[S2]])
            R.add("pe", lambda: nc.tensor.matmul(PS[S2][:, 12:24], lhsT=cb_t[:, UTR, :], rhs=av[pp][:, 12:24], start=True, stop=True), [B_cb, B_av[pp]], [PB[S2]], part=True)
            R.add("pe", lambda: nc.tensor.matmul(PS[S2][:, 24:48], lhsT=cb_t[:, ONES, :], rhs=av[pp][:, 0:24], start=True, stop=True), [B_cb, B_av[pp]], [PB[S2]], part=True)
            R.add("dve", lambda: nc.vector.tensor_copy(out=ct[pp][:, 24:72], in_=PS[S2][:, 0:48]), [PB[S2]], [B_ct[pp]])
            R.add("dve", lambda: nc.vector.tensor_tensor(out=ct[pp][:, 0:24], in0=ct[pp][:, 48:72], in1=ct[pp][:, 24:48], op=ALU.subtract), [B_ct[pp]], [B_ct[pp]], part=True)
            R.add("act", lambda: nc.scalar.activation(out=ex[pp][:], in_=ct[pp][:], func=AF.Exp), [B_ct[pp]], [B_ex[pp]])
            R.add("dve", lambda: nc.vector.tensor_scalar(out=ncum[pp][:], in0=ct[pp][:, 24:48], scalar1=-1.0, scalar2=None, op0=ALU.mult), [B_ct[pp]], [B_ncum[pp]])
            R.add("dve", lambda: nc.vector.tensor_tensor(out=dtd[pp][:], in0=dtc[pp][:], in1=ex[pp][:, 0:24], op=ALU.mult), [B_dtc[pp], B_ex[pp]], [B_dtd[pp]])
            xs3 = v3(xs[pp][:, 0:768], 12, 64)
            R.add("dve", lambda: nc.vector.tensor_tensor(out=v3(xw[pp][1][:, 0:768], 12, 64), in0=xs3, in1=bc12(dtd[pp][:, 12:24]), op=ALU.mult),
                  [B_xs[pp], B_dtd[pp]], [B_xw[pp][1]])
            if full:
                R.add("pool", lambda: nc.gpsimd.tensor_tensor(out=v3(xw[pp][0][:, 0:768], 12, 64), in0=xs3, in1=bc12(dtd[pp][:, 0:12]), op=ALU.mult),
                      [B_xs[pp], B_dtd[pp]], [B_xw[pp][0]])
                R.add("dve", lambda: nc.vector.tensor_tensor(out=v3(xw[pp][2][:, 0:768], 12, 64), in0=xs3, in1=bc12(dtc[pp][:, 0:12]), op=ALU.mult),
                      [B_xs[pp], B_dtc[pp]], [B_xw[pp][2]])
                R.add("pool", lambda: nc.gpsimd.tensor_tensor(out=v3(xw[pp][3][:, 0:768], 12, 64), in0=xs3, in1=bc12(dtc[pp][:, 12:24]), op=ALU.mult),
                      [B_xs[pp], B_dtc[pp]], [B_xw[pp][3]])

        def state_update(pp, d, bks, outb, B_outb):
            for g in range(2):
                R.add("pe", lambda g=g: nc.tensor.matmul(PS[bks[g]][:, 0:384], lhsT=Btm[pp][:, g * 128:(g + 1) * 128], rhs=xw[pp][d][:, g * 384:(g + 1) * 384],
                                                        start=True, stop=True), [B_Btm[pp], B_xw[pp][d]], [PB[bks[g]]])
            R.add("dve", lambda: nc.vector.tensor_tensor(out=v3(tmp[pp][:, 0:768], 12, 64), in0=v3(hst[d][:, 0:768], 12, 64),
                                                         in1=bc12(ex[pp][:, 48 + d * 12:60 + d * 12]), op=ALU.mult), [B_hst[d], B_ex[pp]], [B_tmp[pp]])
            for g in range(2):
                R.add("dve", lambda g=g: nc.vector.tensor_tensor(out=hst[d][:, g * 384:(g + 1) * 384], in0=PS[bks[g]][:, 0:384], in1=tmp[pp][:, g * 384:(g + 1) * 384], op=ALU.add),
                      [PB[bks[g]], B_tmp[pp]], [B_hst[d]], part=(g > 0))
            R.add("act", lambda: nc.scalar.activation(out=outb[:], in_=hst[d][:], func=AF.Copy), [B_hst[d]], [B_outb])

        def capture(fn):
            lst = []
            R.capture = lst
            marks = fn()
            R.capture = None
            return lst, marks

        order_b = [33, 32] + list(range(31, -1, -1))
        lists = []
        for n_, c in enumerate(order_b):
            pp = n_ % 2

            def body(c=c, pp=pp):
                prep(c, pp, False)
                need = len(R.capture)
                R.add("sp", lambda: nc.sync.dma_start(out=hb_d[c], in_=hbb[pp][:]), [B_hbb[pp]], [B_hb], dma=B_hbb[pp], part=True)
                state_update(pp, 1, (4 * pp + 3, 4 * pp + 2), hbb[1 - pp], B_hbb[1 - pp])
                return need, len(R.capture)
            ops, (need, done) = capture(body)
            lists.append((ops, need, done))
        R.interleave(lists)

        order_f = [32, 33] + list(range(32))
        lists = []
        for n_, c in enumerate(order_f):
            pp = n_ % 2

            def body(c=c, pp=pp):
                S0, S1, S2, S3 = 4 * pp, 4 * pp + 1, 4 * pp + 2, 4 * pp + 3
                prep(c, pp, True)
                need = len(R.capture)
                k_ = 0
                for dr in range(2):
                    hin, B_hin = (hfb, B_hfb) if dr == 0 else (hbin[pp], B_hbin[pp])
                    for g in range(2):
                        bk = S3 if k_ % 2 == 0 else S2
                        k_ += 1
                        R.add("pe", lambda g=g, bk=bk, hin=hin: nc.tensor.matmul(PS[bk][:, 0:384], lhsT=bcb[pp][:, 2 + g, :], rhs=hin[:, g * 384:(g + 1) * 384],
                                                                              start=True, stop=True), [B_bcb[pp], B_hin], [PB[bk]])
                        R.add("dve", lambda g=g, bk=bk, dr=dr: nc.vector.tensor_tensor(
                            out=v3(yo[pp][dr][:, g * 384:(g + 1) * 384], 6, 64), in0=v3(PS[bk][:, 0:384], 6, 64),
                            in1=bc12(ex[pp][:, 24 + dr * 12 + g * 6:24 + dr * 12 + g * 6 + 6]), op=ALU.mult), [PB[bk], B_ex[pp]], [B_yo[pp][dr]], part=(g > 0))
                state_update(pp, 0, (S3, S2), hfb, B_hfb)
                done = len(R.capture)
                for g in range(2):
                    R.add("pe", lambda g=g: nc.tensor.matmul(PS[S2][:, g * 128:(g + 1) * 128], lhsT=bcb[pp][:, g, :], rhs=bcb[pp][:, 2 + g, :], start=True, stop=True),
                          [B_bcb[pp]], [PB[S2]], part=(g > 0))
                R.add("dve", lambda: nc.vector.tensor_copy(out=cbs[pp][:].rearrange("p a b -> p (a b)"), in_=PS[S2][:, 0:256]), [PB[S2]], [B_cbs[pp]])
                k_ = 0
                for dr in range(2):
                    um = UIN if dr == 0 else UTR
                    de, deo = ("pool", nc.gpsimd) if dr == 0 else ("dve", nc.vector)
                    R.add(de, lambda dr=dr, um=um, deo=deo: deo.tensor_tensor(
                        out=Dm[pp][dr][:], in0=fap(cb_t[:, um, :], [[0, 12], [1, 128]]), in1=bc12(av[pp][:, dr * 12:dr * 12 + 12], 128), op=ALU.mult),
                        [B_cb, B_av[pp]], [B_Dm[pp][dr]])
                    for q in range(3):
                        bk = S3 if k_ % 2 == 0 else S2
                        k_ += 1
                        R.add("pe", lambda q=q, dr=dr, bk=bk: nc.tensor.matmul(PS[bk][:, :], lhsT=cb_t[:, ONES, :], rhs=Dm[pp][dr][:, q * 4:(q + 1) * 4, :].rearrange("p a b -> p (a b)"),
                                                                             start=True, stop=False), [B_cb, B_Dm[pp][dr]], [PB[bk]])
                        R.add("pe", lambda q=q, dr=dr, bk=bk: nc.tensor.matmul(PS[bk][:, :], lhsT=cb_t[:, IDENT, :], rhs=mrep[:, dr, :, :].rearrange("p a b -> p (a b)"),
                                                                             start=False, stop=True), [B_cb, B_mrep], [PB[bk]], part=True)
                        for hh in range(4):
                            h = q * 4 + hh
                            R.add("act", lambda h=h, hh=hh, dr=dr, bk=bk: nc.scalar.activation(
                                out=Et[pp][dr][:, h, :], in_=PS[bk][:, hh * 128:(hh + 1) * 128], func=AF.Exp,
                                bias=ncum[pp][:, dr * 12 + h:dr * 12 + h + 1], scale=1.0), [PB[bk], B_ncum[pp]], [B_Et[pp][dr]], part=(h > 0))
                    for g in range(2):
                        R.add("dve", lambda g=g, dr=dr: nc.vector.tensor_tensor(out=Mt[pp][dr][:, g * 6:(g + 1) * 6, :], in0=Et[pp][dr][:, g * 6:(g + 1) * 6, :],
                                                                              in1=fap(cbs[pp][:, g, :], [[0, 6], [1, 128]]), op=ALU.mult),
                              [B_Et[pp][dr], B_cbs[pp]], [B_Mt[pp][dr]], part=(g > 0))
                for h in range(12):
                    bk, col = (S0, h * 64) if h < 8 else (S1, (h - 8) * 64)
                    for dr in range(2):
                        R.add("pe", lambda h=h, dr=dr, bk=bk, col=col: nc.tensor.matmul(
                            PS[bk][:, col:col + 64], lhsT=Mt[pp][dr][:, h, :], rhs=xw[pp][2 + dr][:, h * 64:(h + 1) * 64], start=(dr == 0), stop=(dr == 1)),
                            [B_Mt[pp][dr], B_xw[pp][2 + dr]], [PB[bk]], part=not (dr == 0 and h in (0, 8)))
                skip_out = last and c >= 32
                if not skip_out:
                    R.add("dve", lambda: nc.vector.tensor_tensor(out=yv[pp][:, 0:512], in0=PS[S0][:, 0:512], in1=yo[pp][0][:, 0:512], op=ALU.add), [PB[S0], B_yo[pp][0]], [B_yv[pp]])
                    R.add("dve", lambda: nc.vector.tensor_tensor(out=yv[pp][:, 512:768], in0=PS[S1][:, 0:256], in1=yo[pp][0][:, 512:768], op=ALU.add),
                          [PB[S1], B_yo[pp][0]], [B_yv[pp]], part=True)
                    R.add("dve", lambda: nc.vector.tensor_tensor(out=yv[pp][:], in0=yv[pp][:], in1=yo[pp][1][:], op=ALU.add), [B_yv[pp], B_yo[pp][1]], [B_yv[pp]])
                    R.add("pool", lambda: nc.gpsimd.tensor_tensor(out=v3(t3[pp][:, 0:768], 12, 64), in0=v3(xs[pp][:, 0:768], 12, 64), in1=bc12(Dbc[:, 0:12]), op=ALU.mult),
                          [B_xs[pp], B_Dbc], [B_t3[pp]])
                    R.add("dve", lambda: nc.vector.tensor_tensor(out=yv[pp][:], in0=yv[pp][:], in1=t3[pp][:], op=ALU.add), [B_yv[pp], B_t3[pp]], [B_yv[pp]])
                    R.add("pool", lambda: nc.gpsimd.tensor_tensor(out=yv[pp][:], in0=yv[pp][:], in1=zc[pp][:], op=ALU.mult), [B_yv[pp], B_zc[pp]], [B_yv[pp]])
                    R.add("act", lambda: nc.scalar.activation(out=junk[pp][:], in_=yv[pp][:], func=AF.Square, accum_out=st[pp][:, 0:1]), [B_yv[pp]], [B_junk[pp], B_st[pp]])
                    R.add("act", lambda: nc.scalar.activation(out=st[pp][:, 1:2], in_=st[pp][:, 0:1], func=AF.Ln, bias=eps_t[:, 0:1], scale=1.0 / 768),
                          [B_st[pp], B_eps], [B_st[pp]], part=True)
                    R.add("act", lambda: nc.scalar.activation(out=st[pp][:, 2:3], in_=st[pp][:, 1:2], func=AF.Exp, scale=-0.5), [B_st[pp]], [B_st[pp]], part=True)
                    R.add("dve", lambda: nc.vector.scalar_tensor_tensor(out=yn[pp][:], in0=yv[pp][:], scalar=st[pp][:, 2:3], in1=Gbc[:], op0=ALU.mult, op1=ALU.mult),
                          [B_yv[pp], B_st[pp], B_Gbc], [B_yn[pp]])
                    for i in range(6):
                        bk, col = (S2, i * 128) if i < 4 else (S3, (i - 4) * 128)
                        R.add("pe", lambda i=i, bk=bk, col=col: nc.tensor.transpose(PS[bk][:, col:col + 128], yn[pp][:, i * 128:(i + 1) * 128], c_t[:, IDENT, :]),
                              [B_yn[pp], B_c], [PB[bk]], part=(i not in (0, 4)))
                    R.add("act", lambda: nc.scalar.activation(out=catT[pp][:, 0:4, :].rearrange("p a b -> p (a b)"), in_=PS[S2][:, 0:512], func=AF.Copy), [PB[S2]], [B_catT[pp]])
                    R.add("act", lambda: nc.scalar.activation(out=catT[pp][:, 4:6, :].rearrange("p a b -> p (a b)"), in_=PS[S3][:, 0:256], func=AF.Copy), [PB[S3]], [B_catT[pp]], part=True)
                    R.add("sp", lambda: nc.sync.dma_start(out=cat_d[0:768, c * 128:(c + 1) * 128].rearrange("(c p) t -> p c t", p=128), in_=catT[pp][:]),
                          [B_catT[pp]], [B_cat], dma=B_catT[pp], part=True)
                return need, done
            ops, (need, done) = capture(body)
            lists.append((ops, need, done))
        R.interleave(lists)
        R.barrier()
        R.release(B_xT + B_bcT + B_dtc + B_zc + B_hbin + B_hbb + B_catT + [B_Abc, B_Dbc, B_Gbc])


    def phaseD(li, last):
        sb = SB(nc, WORK0)
        vec = sb.alloc("vecD", [128, 16], F32); B_vec = Buf("vecD")
        R.add("sp", lambda: nc.sync.dma_start(out=vec[:], in_=cmv[li].rearrange("p a b -> p (a b)")), [B_in], [B_vec], dma=True)
        pws = sb.alloc("pws", [128, 4, 512], F32); B_pws = Buf("pws")
        pwb = sb.alloc("pwb", [128, 4, 512], BF16); B_pwb = Buf("pwb")
        R.add("sp", lambda: nc.sync.dma_start(out=pws[:], in_=cpw[li].rearrange("(c p) n -> p c n", p=128)), [B_in], [B_pws], dma=True)
        R.add("pool", lambda: nc.gpsimd.tensor_copy(out=pwb[:], in_=pws[:]), [B_pws], [B_pwb])
        P2 = range(2)
        cvt = [sb.alloc("cvt", [128, 4, 512], F32) for _ in P2]; B_cvt = [Buf("cvt%d" % i) for i in P2]
        gct = [sb.alloc("gct", [128, 4, 512], F32) for _ in P2]; B_gct = [Buf("gct%d" % i) for i in P2]
        sq = sb.alloc("sq", [128, 4, 512], F32); B_sq = Buf("sq")
        mean = sb.alloc("mean", [128, 512], F32); B_mean = Buf("mean")
        m2 = sb.alloc("m2", [128, 512], F32); B_m2 = Buf("m2")
        var = sb.alloc("var", [128, 512], F32); B_var = Buf("var")
        sdv = sb.alloc("sdv", [128, 512], F32); B_sdv = Buf("sdv")
        rstd = sb.alloc("rstd", [128, 512], F32); B_rstd = Buf("rstd")
        xc_ = sb.alloc("xc", [128, 512], F32); B_xc = Buf("xc")
        xn_ = sb.alloc("xn", [128, 512], F32); B_xn = Buf("xnD")
        act = sb.alloc("act", [128, 4, 512], BF16); B_act = Buf("act")
        oD = [sb.alloc("oD", [128, 512], BF16) for _ in P2]; B_oD = [Buf("oD%d" % i) for i in P2]
        k_ = 0
        for j in range(9):
            if last and j == 8:
                continue
            w_ = 512 if j < 8 else 256
            cs = slice(j * 512, j * 512 + w_)
            pp = j % 2
            R.add("sp", lambda pp=pp, cs=cs, w_=w_: nc.sync.dma_start(out=cvt[pp][:, :, 0:w_], in_=cv_d[:, cs].rearrange("(c p) t -> p c t", p=128)), [B_cv], [B_cvt[pp]], dma=True)
            R.add("sp", lambda pp=pp, cs=cs, w_=w_: nc.sync.dma_start(out=gct[pp][:, :, 0:w_], in_=gcs_d[:, cs].rearrange("(c p) t -> p c t", p=128)), [B_gcs], [B_gct[pp]], dma=True)
            R.add("act", lambda pp=pp, w_=w_: nc.scalar.activation(out=sq[:, :, 0:w_], in_=cvt[pp][:, :, 0:w_], func=AF.Square), [B_cvt[pp]], [B_sq])
            for c in range(4):
                R.add("pe", lambda c=c, pp=pp, w_=w_: nc.tensor.matmul(PS[0][:, 0:w_], lhsT=c_t[:, ONES, :], rhs=cvt[pp][:, c, 0:w_], start=(c == 0), stop=(c == 3)),
                      [B_c, B_cvt[pp]], [PB[0]], part=(c > 0))
            for c in range(4):
                R.add("pe", lambda c=c, w_=w_: nc.tensor.matmul(PS[1][:, 0:w_], lhsT=c_t[:, ONES, :], rhs=sq[:, c, 0:w_], start=(c == 0), stop=(c == 3)),
                      [B_c, B_sq], [PB[1]], part=(c > 0))
            R.add("dve", lambda w_=w_: nc.vector.tensor_scalar(out=mean[:, 0:w_], in0=PS[0][:, 0:w_], scalar1=1.0 / 512, scalar2=None, op0=ALU.mult), [PB[0]], [B_mean])
            R.add("dve", lambda w_=w_: nc.vector.tensor_tensor(out=m2[:, 0:w_], in0=mean[:, 0:w_], in1=mean[:, 0:w_], op=ALU.mult), [B_mean], [B_m2])
            R.add("dve", lambda w_=w_: nc.vector.scalar_tensor_tensor(out=var[:, 0:w_], in0=PS[1][:, 0:w_], scalar=1.0 / 512, in1=m2[:, 0:w_], op0=ALU.mult, op1=ALU.subtract),
                  [PB[1], B_m2], [B_var])
            R.add("act", lambda w_=w_: nc.scalar.activation(out=sdv[:, 0:w_], in_=var[:, 0:w_], func=AF.Ln, bias=eps_t[:, 0:1], scale=1.0), [B_var, B_eps], [B_sdv])
            R.add("act", lambda w_=w_: nc.scalar.activation(out=rstd[:, 0:w_], in_=sdv[:, 0:w_], func=AF.Exp, scale=-0.5), [B_sdv], [B_rstd])
            for c in range(4):
                R.add("dve", lambda c=c, pp=pp, w_=w_: nc.vector.tensor_tensor(out=xc_[:, 0:w_], in0=cvt[pp][:, c, 0:w_], in1=mean[:, 0:w_], op=ALU.subtract),
                      [B_cvt[pp], B_mean], [B_xc])
                R.add("pool", lambda w_=w_: nc.gpsimd.tensor_tensor(out=xn_[:, 0:w_], in0=xc_[:, 0:w_], in1=rstd[:, 0:w_], op=ALU.mult), [B_xc, B_rstd], [B_xn])
                R.add("act", lambda c=c, w_=w_: nc.scalar.activation(out=act[:, c, 0:w_], in_=xn_[:, 0:w_], func=AF.Silu,
                                                                     bias=vec[:, c * 4 + 2:c * 4 + 3], scale=vec[:, c * 4 + 1:c * 4 + 2]), [B_xn, B_vec], [B_act], part=(c > 0))
            for oc in range(4):
                bk = 2 + oc % 2
                for c in range(4):
                    R.add("pe", lambda oc=oc, c=c, bk=bk, w_=w_: nc.tensor.matmul(PS[bk][:, 0:w_], lhsT=pwb[:, c, oc * 128:(oc + 1) * 128], rhs=act[:, c, 0:w_],
                                                                                 start=(c == 0), stop=(c == 3)), [B_pwb, B_act], [PB[bk]], part=(c > 0))
                o_ = k_ % 2
                k_ += 1
                R.add("dve", lambda oc=oc, bk=bk, o_=o_, pp=pp, w_=w_: nc.vector.scalar_tensor_tensor(
                    out=oD[o_][:, 0:w_], in0=PS[bk][:, 0:w_], scalar=vec[:, oc * 4 + 3:oc * 4 + 4], in1=gct[pp][:, oc, 0:w_], op0=ALU.add, op1=ALU.mult),
                    [PB[bk], B_vec, B_gct[pp]], [B_oD[o_]])
                R.add("sp", lambda oc=oc, o_=o_, cs=cs, w_=w_: nc.sync.dma_start(out=cat_d[1536 + oc * 128:1536 + (oc + 1) * 128, cs], in_=oD[o_][:, 0:w_]),
                      [B_oD[o_]], [B_cat], dma=B_oD[o_], part=True)
        R.barrier()
        R.release([B_vec, B_pws] + B_cvt + B_gct + B_oD)

    def phaseE(li, last):
        sb = SB(nc, WORK0)
        wos = sb.alloc("wos", [128, 4, D], F32); B_wos = Buf("wos")
        wo = sb.alloc("wo", [128, 16, D], BF16); B_wo = Buf("wo")
        for q in range(4):
            R.add("sp", lambda q=q: nc.sync.dma_start(out=wos[:], in_=w_out[li, q * 512:(q + 1) * 512, :].rearrange("(c p) n -> p c n", p=128)), [B_in], [B_wos], dma=True)
            R.add("pool", lambda q=q: nc.gpsimd.tensor_copy(out=wo[:, q * 4:(q + 1) * 4, :], in_=wos[:]), [B_wos], [B_wo], part=(q > 0))
        gB = sb.alloc("gB", [128, 2, D], F32); B_gB = Buf("gB")
        for j in range(2):
            R.add("sp", lambda j=j: nc.sync.dma_start(out=gB[:, j, :], in_=gate_d[li, j]), [B_gate], [B_gB], dma=True, part=(j > 0))
        fg = sb.alloc("fg", [128, D], F32); B_fg = Buf("fg")
        R.add("sp", lambda: nc.sync.dma_start(out=fg[:], in_=bass.AP(fng.ap().tensor, 0, [[0, 128], [1, D]])), [B_in], [B_fg], dma=True)
        P2 = range(2)
        ct_ = [sb.alloc("ctE", [128, 16, 512], BF16) for _ in P2]; B_ct = [Buf("ctE%d" % i) for i in P2]
        xr = [sb.alloc("xr", [128, D], F32) for _ in P2]; B_xr = [Buf("xr%d" % i) for i in P2]
        ty = [sb.alloc("ty", [128, D], F32) for _ in P2]; B_ty = [Buf("ty%d" % i) for i in P2]
        xo = [sb.alloc("xo", [128, D], F32) for _ in P2]; B_xo = [Buf("xo%d" % i) for i in P2]
        junk = sb.alloc("junkE", [128, D], BF16); B_junk = Buf("junkE")
        st = [sb.alloc("stE", [128, 4], F32) for _ in P2]; B_st = [Buf("stE%d" % i) for i in P2]
        fo = [sb.alloc("fo", [128, D], F32) for _ in P2]; B_fo = [Buf("fo%d" % i) for i in P2]
        for j in range(9):
            if last and j == 8:
                continue
            w_ = 512 if j < 8 else 256
            jp = j % 2
            R.add("sp", lambda jp=jp, j=j, w_=w_: nc.sync.dma_start(out=ct_[jp][:, :, 0:w_], in_=cat_d[:, j * 512:j * 512 + w_].rearrange("(c p) t -> p c t", p=128)),
                  [B_cat], [B_ct[jp]], dma=True)
            for q in range(w_ // 128):
                tt = j * 4 + q
                pp = tt % 2
                jj = 0 if tt < 32 else 1
                if li == 0:
                    src = x_in[tt * 128:(tt + 1) * 128, :] if tt < 32 else ctx_in[(tt - 32) * 128:(tt - 31) * 128, :]
                    srcb = B_in
                else:
                    src = x1_d[tt * 128:(tt + 1) * 128, :]
                    srcb = B_x1
                R.add("sp", lambda pp=pp, src=src: nc.sync.dma_start(out=xr[pp][:], in_=src), [srcb], [B_xr[pp]], dma=True)
                for hf_ in range(2):
                    bk = 2 * pp + hf_
                    for c in range(16):
                        R.add("pe", lambda c=c, hf_=hf_, bk=bk, jp=jp, q=q: nc.tensor.matmul(
                            PS[bk][:, :], lhsT=ct_[jp][:, c, q * 128:(q + 1) * 128], rhs=wo[:, c, hf_ * 512:(hf_ + 1) * 512], start=(c == 0), stop=(c == 15)),
                            [B_ct[jp], B_wo], [PB[bk]], part=(c > 0))
                    R.add("dve", lambda hf_=hf_, bk=bk, pp=pp, jj=jj: nc.vector.tensor_tensor(
                        out=ty[pp][:, hf_ * 512:(hf_ + 1) * 512], in0=PS[bk][:, :], in1=gB[:, jj, hf_ * 512:(hf_ + 1) * 512], op=ALU.mult),
                        [PB[bk], B_gB], [B_ty[pp]], part=(hf_ > 0))
                R.add("pool", lambda pp=pp: nc.gpsimd.tensor_tensor(out=xo[pp][:], in0=ty[pp][:], in1=xr[pp][:], op=ALU.add), [B_ty[pp], B_xr[pp]], [B_xo[pp]])
                if not last:
                    R.add("sp", lambda pp=pp, tt=tt: nc.sync.dma_start(out=x1_d[tt * 128:(tt + 1) * 128, :], in_=xo[pp][:]), [B_xo[pp]], [B_x1], dma=B_xo[pp], part=True)
                else:
                    R.add("act", lambda pp=pp: nc.scalar.activation(out=junk[:], in_=xo[pp][:], func=AF.Square, accum_out=st[pp][:, 0:1]), [B_xo[pp]], [B_junk, B_st[pp]])
                    R.add("act", lambda pp=pp: nc.scalar.activation(out=st[pp][:, 1:2], in_=st[pp][:, 0:1], func=AF.Sqrt, bias=eps_t[:, 0:1], scale=1.0 / D),
                          [B_st[pp], B_eps], [B_st[pp]], part=True)
                    R.add("dve", lambda pp=pp: nc.vector.reciprocal(out=st[pp][:, 2:3], in_=st[pp][:, 1:2]), [B_st[pp]], [B_st[pp]], part=True)
                    R.add("dve", lambda pp=pp: nc.vector.scalar_tensor_tensor(out=fo[pp][:], in0=xo[pp][:], scalar=st[pp][:, 2:3], in1=fg[:], op0=ALU.mult, op1=ALU.mult),
                          [B_xo[pp], B_st[pp], B_fg], [B_fo[pp]])
                    R.add("sp", lambda pp=pp, tt=tt: nc.sync.dma_start(out=out_d[tt * 128:(tt + 1) * 128, :], in_=fo[pp][:]), [B_fo[pp]], [B_out], dma=B_fo[pp], part=True)
        R.barrier()
        R.release([B_wos, B_gB, B_fg] + B_ct + B_xr + B_xo + B_fo)


    R.barrier()
    for li in range(nlayers):
        phase1(li)
        R.barrier()
        if phases >= 2:
            phaseA(li)
        last = (li == DEPTH - 1) and nlayers == DEPTH
        if phases >= 3 and "skipB" not in taps:
            phaseB(li, last)
        if phases >= 4 and "skipC" not in taps:
            phaseC(li, last)
        if phases >= 5:
            phaseD(li, last)
        if phases >= 6:
            phaseE(li, last)
    dbg_fence = []
    if "hT" in taps:
        hT_dbg = nc.dram_tensor("hT_dbg", [128, 8, T], BF16, kind="ExternalOutput")
        B_dbg = Buf("dbg")
        R.add("sp", lambda: nc.sync.dma_start(out=hT_dbg.ap(), in_=hT[:]), B_hT, [B_dbg], dma=True)
        dbg_fence.append(B_dbg)
    R.barrier()
    R.emit()
    return dram_in


def host_inputs(inp, b):
    f = np.float32
    fm = lambda v, nch: np.ascontiguousarray(np.asarray(v, f).reshape(nch, 128).T)
    m = {}
    m["x"] = np.ascontiguousarray(inp["x"][b], f)
    m["ctx"] = np.ascontiguousarray(inp["ctx"][b], f)
    m["cvec"] = np.ascontiguousarray(np.stack([fm(inp["c"][b], 8), fm(inp["c_ctx"], 8)], axis=-1))
    m["ada_w"] = np.ascontiguousarray(inp["ada_w"], f)
    m["ada_b_fm"] = np.stack([fm(inp["ada_b"][l], 24) for l in range(DEPTH)])
    m["ada_b_gate"] = np.ascontiguousarray(inp["ada_b"][:, 2048:3072], f)
    m["norm_g_fm"] = np.stack([fm(inp["norm_g"][l], 8) for l in range(DEPTH)])
    m["w_in"] = np.ascontiguousarray(inp["w_in"], f)
    m["ssd_conv_w_fm"] = np.ascontiguousarray(
        np.asarray(inp["ssd_conv_w"], f).reshape(DEPTH, 5, 10, 128).transpose(0, 3, 2, 1))
    m["ssd_conv_b_fm"] = np.stack([fm(inp["ssd_conv_b"][l], 10) for l in range(DEPTH)])
    m["ssd_dt_bias"] = np.ascontiguousarray(inp["ssd_dt_bias"], f)
    m["ssd_a_log"] = np.ascontiguousarray(inp["ssd_a_log"], f)
    m["ssd_d"] = np.ascontiguousarray(inp["ssd_d"], f)
    m["ssd_norm_g"] = np.ascontiguousarray(inp["ssd_norm_g"], f)
    qg = np.asarray(inp["q_norm_g"], f)
    kg = np.asarray(inp["k_norm_g"], f)
    m["qk_g_fm"] = np.ascontiguousarray(np.stack([np.tile(qg, (1, 2)), np.tile(kg, (1, 2))], axis=-1))
    m["cm_conv_w_fm"] = np.ascontiguousarray(
        np.asarray(inp["cm_conv_w"], f).reshape(DEPTH, 31, 4, 128).transpose(0, 3, 2, 1))
    m["cm_vec_fm"] = np.ascontiguousarray(np.stack(
        [np.stack([fm(inp[k][l], 4) for k in ("cm_conv_b", "cm_ln_g", "cm_ln_b", "cm_pw_b")], axis=-1)
         for l in range(DEPTH)]))
    m["cm_pw_w"] = np.ascontiguousarray(inp["cm_pw_w"], f)
    m["w_out"] = np.ascontiguousarray(inp["w_out"], f)
    m["final_norm_g"] = np.ascontiguousarray(inp["final_norm_g"], f)
    m.update(const_inputs())
    return m


_CONST = None


def const_inputs():
    global _CONST
    if _CONST is not None:
        return _CONST
    f = np.float32
    i = np.arange(128)
    ident = np.eye(128, dtype=f)
    U = (i[:, None] <= i[None, :]).astype(f)
    UT = (i[:, None] >= i[None, :]).astype(f)
    d = i % 64
    half = (d % 32) // 16
    partner = np.where(half == 0, i + 16, i - 16)
    perm = np.zeros((128, 128), f)
    perm[partner, i] = 1.0
    bd = ((i[:, None] // 64) == (i[None, :] // 64)).astype(f) / 64.0
    ones = np.ones((128, 128), f)
    consts = np.stack([ident, U, UT, perm, bd, ones], axis=1)
    mf = np.where(i[:, None] <= i[None, :], 0.0, NEG).astype(f)
    mb = np.where(i[:, None] >= i[None, :], 0.0, NEG).astype(f)
    masks = np.stack([mf, mb], axis=1)
    u = np.arange(SEQ)
    pos = np.stack([u // 64, u % 64], axis=0).astype(np.float64)
    inv = 10000.0 ** (-np.arange(16, dtype=np.float64) / 16)
    fq = d % 16
    axis = d // 32
    ang = pos[axis][:, :] * inv[fq][:, None]
    ang = (pos[axis].astype(f) * inv.astype(f)[fq][:, None]).astype(f)
    cos = np.cos(ang).astype(f)
    sin = np.sin(ang).astype(f)
    sgn = np.where(half == 0, -1.0, 1.0).astype(f)[:, None]
    COS = np.concatenate([cos, np.ones((128, NCTX), f)], axis=1)
    SIN = np.concatenate([sin * sgn, np.zeros((128, NCTX), f)], axis=1)
    rope = np.stack([COS, SIN], axis=1)
    _CONST = {"consts": np.ascontiguousarray(consts), "masks": np.ascontiguousarray(masks),
              "rope": np.ascontiguousarray(rope)}
    return _CONST


def kernel(**inputs):
    inputs = {k: np.asarray(v) for k, v in inputs.items()}
    nc = bass.Bass("TRN2", target_bir_lowering=False)
    build(nc)
    in_maps = [host_inputs(inputs, c % 4) for c in range(8)]
    res = run_bass_kernel_spmd(nc, in_maps, core_ids=list(range(8)))
    return np.stack([res.results[b]["out"] for b in range(4)], axis=0).astype(np.float32)
```

```python
import numpy as np
import ml_dtypes
import concourse.bass as bass
import concourse.mybir as mybir
from concourse.bass_utils import run_bass_kernel_spmd

F32 = mybir.dt.float32
BF16 = mybir.dt.bfloat16
AF = mybir.ActivationFunctionType
ALU = mybir.AluOpType
AX = mybir.AxisListType

D = 1024
SEQ = 4096
NCTX = 256
T = SEQ + NCTX
DEPTH = 2
INW = 5656
EPS = 1e-6
O_Z, O_X, O_B, O_C, O_DT, O_Q, O_K, O_V, O_GA, O_UA, O_UB, O_GC = 0, 768, 1536, 1792, 2048, 2072, 2840, 3096, 3352, 4120, 4632, 5144
NEG = -30000.0


class Buf:
    __slots__ = ("name", "w", "r", "excl", "semi", "dcount")

    def __init__(self, name, excl=False):
        self.name = name
        self.w = {}
        self.r = {}
        self.excl = excl
        self.semi = None
        self.dcount = 0


class Op:
    __slots__ = ("eng", "idx", "fn", "deps", "dma", "sig", "semi", "dval")


def _merge(dst, src):
    for k, v in src.items():
        if dst.get(k, -1) < v:
            dst[k] = v


class Rec:
    ENG = ("pe", "act", "dve", "pool", "sp")

    def __init__(self, nc):
        self.nc = nc
        self.ops = {e: [] for e in self.ENG}
        self.ndma_sems = 0
        self.free = []
        self.side = []
        self.side_every = 0
        self.side_cnt = 0
        self.capture = None
        self.semcount = {}
        self.eobj = {"pe": nc.tensor, "act": nc.scalar, "dve": nc.vector, "pool": nc.gpsimd, "sp": nc.sync}

    def add(self, eng, fn, reads=(), writes=(), dma=None, part=False):
        if dma is True:
            dma = writes[0]
        if self.capture is not None:
            self.capture.append((eng, fn, tuple(reads), tuple(writes), dma, part))
            return None
        op = Op()
        op.eng, op.fn, op.dma, op.sig = eng, fn, dma is not None, False
        op.idx = len(self.ops[eng])
        raw, oth = {}, {}
        for b in reads:
            _merge(raw, b.w)
            if b.excl:
                _merge(oth, b.r)
        for b in writes:
            _merge(oth, b.r)
            if not part or b.excl:
                _merge(oth, b.w)
        if dma is None:
            oth.pop(("c", eng), None)
            if eng == "pe":
                raw.pop(("c", eng), None)
        deps = raw
        _merge(deps, oth)
        op.deps = deps
        if dma is not None:
            b = dma
            if b.semi is None:
                if self.free:
                    b.semi, b.dcount = self.free.pop()
                    _merge(deps, {("d", b.semi): b.dcount})
                else:
                    b.semi = self.ndma_sems
                    self.ndma_sems += 1
            b.dcount += 16
            self.semcount[b.semi] = b.dcount
            op.semi, op.dval = b.semi, b.dcount
            my = {("d", b.semi): b.dcount}
        else:
            my = {("c", eng): op.idx}
        for b in reads:
            _merge(b.r, my)
        for b in writes:
            if part:
                _merge(b.w, my)
            else:
                b.w = dict(my)
                b.r = {}
        self.ops[eng].append(op)
        if self.side and self.side_every:
            self.side_cnt += 1
            if self.side_cnt >= self.side_every:
                self.side_cnt = 0
                so = self.side.pop(0)
                ev, self.side_every = self.side_every, 0
                self.add(*so[:4], dma=so[4], part=so[5])
                self.side_every = ev
        return op

    def flush_side(self):
        self.side_every = 0
        while self.side:
            so = self.side.pop(0)
            self.add(*so[:4], dma=so[4], part=so[5])

    def interleave(self, lists):
        cur = [0] * len(lists)
        lo = 0
        n = len(lists)
        while lo < n:
            act = [i for i in (lo, lo + 1) if i < n]
            progressed = False
            for i in act:
                ops, need, done = lists[i]
                if cur[i] >= len(ops):
                    continue
                if i > lo and cur[i] >= need and cur[i - 1] < lists[i - 1][2]:
                    continue
                if i > lo and cur[i] == 0 and cur[lo] < len(lists[lo][0]) // 2:
                    continue
                self.add(*ops[cur[i]][:4], dma=ops[cur[i]][4], part=ops[cur[i]][5])
                cur[i] += 1
                progressed = True
            while lo < n and cur[lo] >= len(lists[lo][0]):
                lo += 1
            assert progressed or lo >= n

    def release(self, bufs):
        for b in bufs:
            if b.semi is not None:
                self.free.append((b.semi, b.dcount))
                b.semi = None

    def barrier(self):
        deps = {("c", e): len(self.ops[e]) - 1 for e in self.ENG if self.ops[e]}
        for s_, c_ in self.semcount.items():
            deps[("d", s_)] = c_
        for e in self.ENG:
            op = Op()
            op.eng, op.fn, op.dma, op.sig, op.idx = e, None, False, False, len(self.ops[e])
            op.deps = {k: v for k, v in deps.items() if k != ("c", e)}
            self.ops[e].append(op)

    def emit(self):
        nc = self.nc
        for e in self.ENG:
            for op in self.ops[e]:
                nd = {}
                for k, v in op.deps.items():
                    if k[0] == "c":
                        lst = self.ops[k[1]]
                        while v >= 0 and lst[v].fn is None:
                            v -= 1
                        if v < 0:
                            continue
                        lst[v].sig = True
                    nd[k] = v
                op.deps = nd
        pref = {}
        for e in self.ENG:
            c = 0
            arr = []
            for op in self.ops[e]:
                if op.sig and not op.dma:
                    c += 1
                arr.append(c)
            pref[e] = arr
        assert self.ndma_sems + 5 <= 98, self.ndma_sems
        esem = {e: nc.alloc_semaphore(name="es_" + e) for e in self.ENG}
        dsem = [nc.alloc_semaphore(name="ds_%d" % i) for i in range(self.ndma_sems)]
        for e in self.ENG:
            eo = self.eobj[e]
            known = {}
            for op in self.ops[e]:
                for k, v in op.deps.items():
                    if k[0] == "c":
                        sem, val, key = esem[k[1]], pref[k[1]][v], k
                    else:
                        sem, val, key = dsem[k[1]], v, k
                    if known.get(key, 0) >= val:
                        continue
                    known[key] = val
                    eo.wait_ge(sem, val)
                if op.fn is None:
                    continue
                ins = op.fn()
                if op.dma:
                    ins.then_inc(dsem[op.semi], 16)
                elif op.sig:
                    ins.then_inc(esem[e], 1)


class SB:
    ARENA = None
    ABYTES = 204800

    def __init__(self, nc, base=0, limit=None):
        if SB.ARENA is None or SB.ARENA[0] is not nc:
            SB.ARENA = (nc, nc.alloc_sbuf_tensor("arena", [128, SB.ABYTES // 4], F32))
        self.nc, self.off, self.limit = nc, base, (limit or SB.ABYTES)

    def alloc(self, name, shape, dt):
        return self.at(name, shape, dt, None)

    def at(self, name, shape, dt, off):
        esz = 4 if dt == F32 else 2
        n = int(np.prod(shape[1:]))
        nb = (n * esz + 63) // 64 * 64
        if off is None:
            off = self.off
            self.off += nb
        assert off % 4 == 0 and off + nb <= self.limit, (name, off, nb, self.limit)
        A = SB.ARENA[1]
        ap = A[0:shape[0], off // 4: off // 4 + nb // 4]
        if dt != F32:
            ap = ap.bitcast(dt)
        ap = ap[:, 0:n]
        if len(shape) > 2:
            names = " ".join("d%d" % i for i in range(len(shape) - 1))
            kw = {"d%d" % i: int(shape[i + 1]) for i in range(len(shape) - 1)}
            ap = ap.rearrange("p (%s) -> p %s" % (names, names), **kw)
        return ap


def fap(ap, dims):
    return bass.AP(ap.tensor, ap.offset, [list(ap.ap[0])] + [list(d) for d in dims])


def build(nc, phases=99, nlayers=DEPTH, taps=()):
    R = Rec(nc)
    dram_in = {}

    def din(name, shape, dt=F32):
        dram_in[name] = nc.dram_tensor(name, list(shape), dt, kind="ExternalInput")
        return dram_in[name]

    x_in = din("x", [SEQ, D])
    ctx_in = din("ctx", [NCTX, D])
    cvec = din("cvec", [128, 8, 2])
    ada_w = din("ada_w", [DEPTH, D, 3 * D])
    ada_b_fm = din("ada_b_fm", [DEPTH, 128, 24])
    ada_b_gate = din("ada_b_gate", [DEPTH, D])
    norm_g_fm = din("norm_g_fm", [DEPTH, 128, 8])
    w_in = din("w_in", [DEPTH, D, INW])
    scw = din("ssd_conv_w_fm", [DEPTH, 128, 10, 5])
    scb = din("ssd_conv_b_fm", [DEPTH, 128, 10])
    dtb = din("ssd_dt_bias", [DEPTH, 24])
    alog = din("ssd_a_log", [DEPTH, 24])
    ssdd = din("ssd_d", [DEPTH, 12])
    ssdng = din("ssd_norm_g", [DEPTH, 768])
    qkg = din("qk_g_fm", [DEPTH, 128, 2])
    ccw = din("cm_conv_w_fm", [DEPTH, 128, 4, 31])
    cmv = din("cm_vec_fm", [DEPTH, 128, 4, 4])
    cpw = din("cm_pw_w", [DEPTH, 512, 512])
    w_out = din("w_out", [DEPTH, 2048, D])
    fng = din("final_norm_g", [D])
    consts = din("consts", [128, 6, 128])
    masks = din("masks", [128, 2, 128])
    rope = din("rope", [128, 2, T])
    out_d = nc.dram_tensor("out", [SEQ, D], F32, kind="ExternalOutput")

    def scratch(name, shape, dt):
        kind = "ExternalOutput" if name in taps else "Internal"
        return nc.dram_tensor(name, list(shape), dt, kind=kind)

    gate_d = scratch("gate_d", [DEPTH, 2, 128, D], F32)
    x1_d = scratch("x1_d", [T, D], F32)
    xbc_d = scratch("xbc_d", [1280, T], F32)
    dt_d = scratch("dt_d", [T, 24], F32)
    zs_d = scratch("zs_d", [T, 768], F32)
    qT_d = scratch("qT_d", [768, T], BF16)
    kT_d = scratch("kT_d", [256, T], BF16)
    v_d = scratch("v_d", [T, 256], BF16)
    gas_d = scratch("gas_d", [768, T], F32)
    cv_d = scratch("cv_d", [512, T], F32)
    gcs_d = scratch("gcs_d", [512, T], F32)
    cat_d = scratch("cat_d", [2048, T], BF16)
    hb_d = scratch("hb_d", [34, 128, 768], BF16)
    B_gate = Buf("gate_d"); B_x1 = Buf("x1_d"); B_xbc = Buf("xbc_d"); B_dt = Buf("dt_d"); B_zs = Buf("zs_d")
    B_qT = Buf("qT_d"); B_kT = Buf("kT_d"); B_v = Buf("v_d"); B_gas = Buf("gas_d"); B_cv = Buf("cv_d")
    B_gcs = Buf("gcs_d"); B_cat = Buf("cat_d"); B_hb = Buf("hb_d"); B_out = Buf("out")
    B_in = Buf("inputs")

    PS = [nc.alloc_psum_tensor("ps%d" % i, [128, 512], F32) for i in range(8)]
    PB = [Buf("psb%d" % i, excl=True) for i in range(8)]

    sbp = SB(nc, 0, 24 * 1024)
    c_t = sbp.alloc("consts", [128, 6, 128], F32)
    cb_t = sbp.alloc("constsb", [128, 6, 128], BF16)
    mk_t = sbp.alloc("masks", [128, 2, 128], F32)
    eps_t = sbp.alloc("eps", [128, 2], F32)
    AB_t = sbp.alloc("AB", [128, DEPTH, 2, 8, 2], F32)
    B_c = Buf("consts"); B_AB = Buf("AB")
    IDENT, UIN, UTR, PERM, BDM, ONES = range(6)
    R.add("sp", lambda: nc.sync.dma_start(out=c_t[:], in_=consts.ap()), [B_in], [B_c], dma=True)
    B_mk = Buf("mk")
    R.add("sp", lambda: nc.sync.dma_start(out=mk_t[:], in_=masks.ap()), [B_in], [B_mk], dma=True)
    B_cb = Buf("cb")
    R.add("dve", lambda: nc.vector.tensor_copy(out=cb_t[:], in_=c_t[:]), [B_c], [B_cb])
    B_eps = Buf("eps")
    R.add("pool", lambda: nc.gpsimd.memset(eps_t[:, 0:1], EPS), [], [B_eps])
    R.add("pool", lambda: nc.gpsimd.memset(eps_t[:, 1:2], 1.0), [], [B_eps], part=True)

    WORK0 = 24 * 1024

    def phase0():
        sb = SB(nc, WORK0)
        aw = sb.alloc("aw", [128, 8, 3072], F32)
        B_aw = [Buf("aw%d" % k) for k in range(8)]
        cv = sb.alloc("cv", [128, 8, 2], F32)
        sc = sb.alloc("sc", [128, 8, 2], F32)
        screp = sb.alloc("screp", [128, 8, 2, 128], F32)
        abf = sb.alloc("abf", [128, 24], F32)
        ngf = sb.alloc("ngf", [128, 8], F32)
        abg = sb.alloc("abg", [128, D], F32)
        mod = sb.alloc("mod", [128, 16, 2], F32)
        gt = sb.alloc("gt", [128, 2, D], F32)
        B_cv, B_sc, B_screp, B_abf, B_ngf, B_abg, B_mod, B_gt = [Buf(n) for n in "cv sc screp abf ngf abg mod gt".split()]
        R.add("sp", lambda: nc.sync.dma_start(out=cv[:], in_=cvec.ap()), [B_in], [B_cv], dma=True)
        R.add("act", lambda: nc.scalar.activation(out=sc[:], in_=cv[:], func=AF.Silu), [B_cv], [B_sc])
        for kc in range(8):
            for j in range(2):
                R.add("dve", lambda kc=kc, j=j: nc.vector.tensor_copy(
                    out=screp[:, kc, j, :], in_=fap(sc[:, kc, j:j + 1], [[0, 128]])), [B_sc], [B_screp], part=True)
        for li in range(nlayers):
            for kc in range(8):
                R.add("sp", lambda kc=kc, li=li: nc.sync.dma_start(
                    out=aw[:, kc, :], in_=ada_w[li, kc * 128:(kc + 1) * 128, :]), [B_in], [B_aw[kc]], dma=True)
            R.add("sp", lambda li=li: nc.sync.dma_start(out=abf[:], in_=ada_b_fm[li]), [B_in], [B_abf], dma=True)
            R.add("sp", lambda li=li: nc.sync.dma_start(out=ngf[:], in_=norm_g_fm[li]), [B_in], [B_ngf], dma=True)
            R.add("sp", lambda li=li: nc.sync.dma_start(
                out=abg[:], in_=bass.AP(ada_b_gate.ap().tensor, li * D, [[0, 128], [1, D]])), [B_in], [B_abg], dma=True)
            for fc in range(16):
                for kc in range(8):
                    R.add("pe", lambda fc=fc, kc=kc: nc.tensor.matmul(
                        PS[0][:, fc * 2:fc * 2 + 2], lhsT=aw[:, kc, fc * 128:(fc + 1) * 128], rhs=sc[:, kc, :],
                        start=(kc == 0), stop=(kc == 7)), [B_aw[kc], B_sc], [PB[0]], part=not (fc == 0 and kc == 0))
            R.add("dve", lambda: nc.vector.tensor_tensor(
                out=mod[:], in0=fap(PS[0][:, 0:32], [[2, 16], [1, 2]]), in1=fap(abf[:, 0:16], [[1, 16], [0, 2]]),
                op=ALU.add), [PB[0], B_abf], [B_mod])
            R.add("dve", lambda li=li: nc.vector.scalar_tensor_tensor(
                out=AB_t[:, li, 0, :, :], in0=mod[:, 8:16, :], scalar=1.0, in1=fap(ngf[:, 0:8], [[1, 8], [0, 2]]),
                op0=ALU.add, op1=ALU.mult), [B_mod, B_ngf], [B_AB], part=True)
            R.add("dve", lambda li=li: nc.vector.tensor_copy(out=AB_t[:, li, 1, :, :], in_=mod[:, 0:8, :]),
                  [B_mod], [B_AB], part=True)
            for j in range(2):
                for cc in range(2):
                    pb = 1 + (j * 2 + cc) % 2
                    for kc in range(8):
                        R.add("pe", lambda j=j, cc=cc, kc=kc, pb=pb: nc.tensor.matmul(
                            PS[pb][:, :], lhsT=screp[:, kc, j, :], rhs=aw[:, kc, 2048 + cc * 512:2048 + (cc + 1) * 512],
                            start=(kc == 0), stop=(kc == 7)), [B_screp, B_aw[kc]], [PB[pb]], part=(kc > 0))
                    R.add("dve", lambda j=j, cc=cc, pb=pb: nc.vector.tensor_tensor(
                        out=gt[:, j, cc * 512:(cc + 1) * 512], in0=PS[pb][:, :], in1=abg[:, cc * 512:(cc + 1) * 512],
                        op=ALU.add), [PB[pb], B_abg], [B_gt], part=not (j == 0 and cc == 0))
            for j in range(2):
                R.add("sp", lambda li=li, j=j: nc.sync.dma_start(out=gate_d[li, j], in_=gt[:, j, :]),
                      [B_gt], [B_gate], dma=B_gt, part=True)

    phase0()
    if phases <= 0:
        R.emit()
        return dram_in

    HT_OFF = WORK0
    hT = SB(nc).at("hT", [128, 8, T], BF16, HT_OFF)
    HT_BYTES = 8 * T * 2
    B_hT = [Buf("hT%d" % i) for i in range(34)]
    WORK1 = HT_OFF + HT_BYTES

    def phase1(li):
        sb = SB(nc, WORK1)
        xt = [sb.alloc("xt", [128, D], F32) for _ in range(2)]
        xn = [sb.alloc("xn", [128, D], F32) for _ in range(2)]
        junk = sb.alloc("junk", [128, D], BF16)
        st = [sb.alloc("st", [128, 4], F32) for _ in range(2)]
        B_xt = [Buf("xt%d" % i) for i in range(2)]
        B_xn = [Buf("xn%d" % i) for i in range(2)]
        B_junk = Buf("junk")
        B_st = [Buf("st%d" % i) for i in range(2)]
        for tt in range(34):
            s = tt % 2
            j = 0 if tt < 32 else 1
            if li == 0:
                src = x_in[tt * 128:(tt + 1) * 128, :] if tt < 32 else ctx_in[(tt - 32) * 128:(tt - 31) * 128, :]
                srcb = B_in
            else:
                src = x1_d[tt * 128:(tt + 1) * 128, :]
                srcb = B_x1
            R.add("sp", lambda s=s, src=src: nc.sync.dma_start(out=xt[s][:], in_=src), [srcb], [B_xt[s]], dma=True)
            R.add("act", lambda s=s: nc.scalar.activation(out=junk[:], in_=xt[s][:], func=AF.Square,
                                                          accum_out=st[s][:, 0:1]), [B_xt[s]], [B_junk, B_st[s]])
            R.add("act", lambda s=s: nc.scalar.activation(out=st[s][:, 1:2], in_=st[s][:, 0:1], func=AF.Sqrt,
                                                          bias=eps_t[:, 0:1], scale=1.0 / D), [B_st[s], B_eps], [B_st[s]], part=True)
            R.add("dve", lambda s=s: nc.vector.reciprocal(out=st[s][:, 2:3], in_=st[s][:, 1:2]), [B_st[s]], [B_st[s]], part=True)
            R.add("dve", lambda s=s: nc.vector.tensor_scalar(out=xn[s][:], in0=xt[s][:], scalar1=st[s][:, 2:3], scalar2=None,
                                                             op0=ALU.mult), [B_xt[s], B_st[s]], [B_xn[s]])
            pb0 = 2 * (tt % 2)
            for c in range(8):
                pb = pb0 + c // 4
                R.add("pe", lambda s=s, c=c, pb=pb: nc.tensor.transpose(
                    PS[pb][:, (c % 4) * 128:(c % 4 + 1) * 128], xn[s][:, c * 128:(c + 1) * 128], c_t[:, IDENT, :]),
                    [B_xn[s], B_c], [PB[pb]], part=(c % 4 > 0))
            for c in range(8):
                pb = pb0 + c // 4
                if c % 2 == 0:
                    R.add("act", lambda c=c, pb=pb, tt=tt, j=j: nc.scalar.activation(
                        out=hT[:, c, tt * 128:(tt + 1) * 128], in_=PS[pb][:, (c % 4) * 128:(c % 4 + 1) * 128],
                        func=AF.Identity, bias=AB_t[:, li, 1, c, j:j + 1], scale=AB_t[:, li, 0, c, j:j + 1]),
                        [PB[pb], B_AB], [B_hT[tt]], part=True)
                else:
                    R.add("dve", lambda c=c, pb=pb, tt=tt, j=j: nc.vector.tensor_scalar(
                        out=hT[:, c, tt * 128:(tt + 1) * 128], in0=PS[pb][:, (c % 4) * 128:(c % 4 + 1) * 128],
                        scalar1=AB_t[:, li, 0, c, j:j + 1], scalar2=AB_t[:, li, 1, c, j:j + 1],
                        op0=ALU.mult, op1=ALU.add), [PB[pb], B_AB], [B_hT[tt]], part=True)

    def wsrc(li, c0, ncol):
        return w_in[li, :, c0:c0 + ncol].rearrange("(kc p) n -> p kc n", p=128)

    def phaseA(li):
        sb = SB(nc, WORK1)
        wst = [sb.alloc("wst", [128, 8, 128], F32) for _ in range(2)]
        wbf = [sb.alloc("wbf", [128, 8, 128], BF16) for _ in range(2)]
        B_wst = [Buf("wst%d" % i) for i in range(2)]
        B_wbf = [Buf("wbf%d" % i) for i in range(2)]
        ot = [sb.alloc("ot", [128, 512], F32) for _ in range(3)]
        B_ot = [Buf("ot%d" % i) for i in range(3)]
        obt = [sb.alloc("obt", [128, 512], BF16) for _ in range(2)]
        B_obt = [Buf("obt%d" % i) for i in range(2)]
        vecs = sb.alloc("vecs", [128, 10 * 5 + 10 + 2 + 4 * 31 + 16], F32)
        B_vecs = Buf("vecs")
        V_SCW, V_SCB, V_QKG, V_CCW, V_CMV = 0, 50, 60, 62, 62 + 124
        R.add("sp", lambda: nc.sync.dma_start(out=vecs[:, V_SCW:V_SCW + 50], in_=scw[li].rearrange("p a b -> p (a b)")), [B_in], [B_vecs], dma=True)
        R.add("sp", lambda: nc.sync.dma_start(out=vecs[:, V_SCB:V_SCB + 10], in_=scb[li]), [B_in], [B_vecs], dma=True, part=True)
        R.add("sp", lambda: nc.sync.dma_start(out=vecs[:, V_QKG:V_QKG + 2], in_=qkg[li]), [B_in], [B_vecs], dma=True, part=True)
        R.add("sp", lambda: nc.sync.dma_start(out=vecs[:, V_CCW:V_CCW + 124], in_=ccw[li].rearrange("p a b -> p (a b)")), [B_in], [B_vecs], dma=True, part=True)
        R.add("sp", lambda: nc.sync.dma_start(out=vecs[:, V_CMV:V_CMV + 16], in_=cmv[li].rearrange("p a b -> p (a b)")), [B_in], [B_vecs], dma=True, part=True)
        sub0 = sb.off
        cnt = {"w": 0, "ot": 0, "obt": 0, "ps": 0}

        def wload(c0):
            s_ = cnt["w"] % 2
            cnt["w"] += 1
            R.add("sp", lambda: nc.sync.dma_start(out=wst[s_][:], in_=wsrc(li, c0, 128)), [B_in], [B_wst[s_]], dma=True)
            R.add("pool", lambda: nc.gpsimd.tensor_copy(out=wbf[s_][:], in_=wst[s_][:]), [B_wst[s_]], [B_wbf[s_]])
            return s_

        def proj(ws, j, pb):
            w_ = 512 if j < 8 else 256
            for kc in range(8):
                R.add("pe", lambda kc=kc: nc.tensor.matmul(PS[pb][:, 0:w_], lhsT=wbf[ws][:, kc, :], rhs=hT[:, kc, j * 512:j * 512 + w_],
                                                          start=(kc == 0), stop=(kc == 7)),
                      [B_wbf[ws]] + B_hT[j * 4:j * 4 + w_ // 128], [PB[pb]], part=(kc > 0))
            return w_

        def next_ot():
            s_ = cnt["ot"] % 3
            cnt["ot"] += 1
            return s_

        def next_obt():
            s_ = cnt["obt"] % 2
            cnt["obt"] += 1
            return s_

        for (c0, nch, dst, B_dst) in (((O_GA, 6, gas_d, B_gas), (O_GC, 4, gcs_d, B_gcs)) if "noA1" not in taps else ()):
            for ch in range(nch):
                ws = wload(c0 + ch * 128)
                for j in range(9):
                    pb = cnt["ps"] % 2
                    cnt["ps"] += 1
                    w_ = proj(ws, j, pb)
                    o_ = next_ot()
                    R.add("act", lambda pb=pb, o_=o_, w_=w_: nc.scalar.activation(out=ot[o_][:, 0:w_], in_=PS[pb][:, 0:w_], func=AF.Silu),
                          [PB[pb]], [B_ot[o_]])
                    R.add("sp", lambda o_=o_, w_=w_, ch=ch, j=j, dst=dst: nc.sync.dma_start(
                        out=dst[ch * 128:(ch + 1) * 128, j * 512:j * 512 + w_], in_=ot[o_][:, 0:w_]), [B_ot[o_]], [B_dst], dma=B_ot[o_], part=True)

        sbq = SB(nc, sub0)
        ropet = sbq.alloc("rope", [128, 2, T], F32)
        B_rope = Buf("rope")
        R.add("sp", lambda: nc.sync.dma_start(out=ropet[:, 0, :], in_=rope[:, 0, :]), [B_in], [B_rope], dma=True)
        R.add("sp", lambda: nc.sync.dma_start(out=ropet[:, 1, :], in_=rope[:, 1, :]), [B_in], [B_rope], dma=True, part=True)
        P2 = range(2)
        sqb = [sbq.alloc("sqb", [128, 512], BF16) for _ in P2]; B_sqb = [Buf("sqb%d" % i) for i in P2]
        sd = [sbq.alloc("sd", [128, 512], F32) for _ in P2]; B_sd = [Buf("sd%d" % i) for i in P2]
        rs = [sbq.alloc("rs", [128, 512], F32) for _ in P2]; B_rs = [Buf("rs%d" % i) for i in P2]
        qnb = [sbq.alloc("qnb", [128, 512], BF16) for _ in P2]; B_qnb = [Buf("qnb%d" % i) for i in P2]
        t1 = [sbq.alloc("t1", [128, 512], F32) for _ in P2]; B_t1 = [Buf("t1%d" % i) for i in P2]
        t2 = [sbq.alloc("t2", [128, 512], F32) for _ in P2]; B_t2 = [Buf("t2%d" % i) for i in P2]
        lists = []
        tile_no = 0
        qk_chunks = []
        for (c0, nch, dst, B_dst, gi) in (((O_Q, 6, qT_d, B_qT, 0), (O_K, 2, kT_d, B_kT, 1)) if "noA2" not in taps else ()):
            for ch in range(nch):
                qk_chunks.append((c0 + ch * 128, ch, dst, B_dst, gi))
        ws_of = {}
        for kq, (wc0, ch, dst, B_dst, gi) in enumerate(qk_chunks):
            if True:
                for j in range(9):
                    pp = tile_no % 2
                    tile_no += 1
                    ws = None

                    def body(kq=kq, j=j, pp=pp, ch=ch, dst=dst, B_dst=B_dst, gi=gi):
                        P0, P1, P2_ = 3 * pp, 3 * pp + 1, 3 * pp + 2
                        if j == 0:
                            if kq == 0:
                                ws_of[0] = wload(qk_chunks[0][0])
                            if kq + 1 < len(qk_chunks):
                                ws_of[kq + 1] = wload(qk_chunks[kq + 1][0])
                        ws = ws_of[kq]
                        w_ = proj(ws, j, P0)
                        cs = slice(j * 512, j * 512 + w_)
                        R.add("act", lambda: nc.scalar.activation(out=sqb[pp][:, 0:w_], in_=PS[P0][:, 0:w_], func=AF.Square), [PB[P0]], [B_sqb[pp]])
                        R.add("pe", lambda: nc.tensor.matmul(PS[P1][:, 0:w_], lhsT=cb_t[:, BDM, :], rhs=sqb[pp][:, 0:w_], start=True, stop=True),
                              [B_cb, B_sqb[pp]], [PB[P1]])
                        R.add("act", lambda: nc.scalar.activation(out=sd[pp][:, 0:w_], in_=PS[P1][:, 0:w_], func=AF.Ln, bias=eps_t[:, 0:1], scale=1.0),
                              [PB[P1], B_eps], [B_sd[pp]])
                        R.add("act", lambda: nc.scalar.activation(out=rs[pp][:, 0:w_], in_=sd[pp][:, 0:w_], func=AF.Exp, scale=-0.5), [B_sd[pp]], [B_rs[pp]])
                        R.add("dve", lambda: nc.vector.scalar_tensor_tensor(
                            out=qnb[pp][:, 0:w_], in0=PS[P0][:, 0:w_], scalar=vecs[:, V_QKG + gi:V_QKG + gi + 1], in1=rs[pp][:, 0:w_], op0=ALU.mult, op1=ALU.mult),
                            [PB[P0], B_vecs, B_rs[pp]], [B_qnb[pp]])
                        R.add("pe", lambda: nc.tensor.matmul(PS[P2_][:, 0:w_], lhsT=cb_t[:, PERM, :], rhs=qnb[pp][:, 0:w_], start=True, stop=True),
                              [B_cb, B_qnb[pp]], [PB[P2_]])
                        R.add("pool", lambda: nc.gpsimd.tensor_tensor(out=t1[pp][:, 0:w_], in0=qnb[pp][:, 0:w_], in1=ropet[:, 0, cs], op=ALU.mult),
                              [B_qnb[pp], B_rope], [B_t1[pp]])
                        R.add("dve", lambda: nc.vector.tensor_tensor(out=t2[pp][:, 0:w_], in0=PS[P2_][:, 0:w_], in1=ropet[:, 1, cs], op=ALU.mult),
                              [PB[P2_], B_rope], [B_t2[pp]])
                        R.add("dve", lambda: nc.vector.tensor_tensor(out=obt[pp][:, 0:w_], in0=t1[pp][:, 0:w_], in1=t2[pp][:, 0:w_], op=ALU.add),
                              [B_t1[pp], B_t2[pp]], [B_obt[pp]])
                        R.add("sp", lambda: nc.sync.dma_start(out=dst[ch * 128:(ch + 1) * 128, cs], in_=obt[pp][:, 0:w_]), [B_obt[pp]], [B_dst], dma=B_obt[pp], part=True)
                    lst = []
                    R.capture = lst
                    body()
                    R.capture = None
                    lists.append((lst, 10 ** 9, 0))
        R.interleave(lists)
        R.barrier()

        sbx = SB(nc, sub0)
        dg5 = sbx.alloc("dg5", [128, 10, 5, 128], BF16); B_dg5 = Buf("dg5")
        RBW = 2 + SEQ + 2 + 2 + NCTX + 2
        rb = [sbx.alloc("rb", [128, RBW], BF16) for _ in range(2)]
        B_rb = [Buf("rb%d" % i) for i in range(2)]
        for ch in range(10):
            for k in range(5):
                R.add("dve", lambda ch=ch, k=k: nc.vector.tensor_scalar(
                    out=dg5[:, ch, k, :], in0=c_t[:, IDENT, :], scalar1=vecs[:, V_SCW + ch * 5 + k:V_SCW + ch * 5 + k + 1], scalar2=None, op0=ALU.mult),
                    [B_c, B_vecs], [B_dg5], part=True)
        for i in range(2):
            R.add("pool", lambda i=i: nc.gpsimd.memset(rb[i][:], 0.0), [], [B_rb[i]])

        def rbcol(j, pad):
            return pad + j * 512 if j < 8 else pad + SEQ + 2 * pad

        def xproj(ch):
            ws = wload(O_X + ch * 128)
            r_ = ch % 2
            for j in range(9):
                pb = cnt["ps"] % 2
                cnt["ps"] += 1
                w_ = proj(ws, j, pb)
                c0_ = rbcol(j, 2)
                R.add("act", lambda pb=pb, w_=w_, c0_=c0_: nc.scalar.activation(out=rb[r_][:, c0_:c0_ + w_], in_=PS[pb][:, 0:w_], func=AF.Copy),
                      [PB[pb]], [B_rb[r_]], part=(j > 0))

        def xconv(ch):
            r_ = ch % 2
            for j in range(9):
                w_ = 512 if j < 8 else 256
                pb = 2 + j % 2
                st_ = rbcol(j, 2) - 2
                for k in range(5):
                    R.add("pe", lambda k=k, pb=pb, w_=w_, st_=st_: nc.tensor.matmul(
                        PS[pb][:, 0:w_], lhsT=dg5[:, ch, k, :], rhs=rb[r_][:, st_ + k:st_ + k + w_], start=(k == 0), stop=(k == 4)),
                        [B_dg5, B_rb[r_]], [PB[pb]], part=(k > 0))
                o_ = next_ot()
                R.add("act", lambda pb=pb, o_=o_, w_=w_: nc.scalar.activation(
                    out=ot[o_][:, 0:w_], in_=PS[pb][:, 0:w_], func=AF.Silu, bias=vecs[:, V_SCB + ch:V_SCB + ch + 1], scale=1.0),
                    [PB[pb], B_vecs], [B_ot[o_]])
                R.add("sp", lambda o_=o_, w_=w_, j=j: nc.sync.dma_start(
                    out=xbc_d[ch * 128:(ch + 1) * 128, j * 512:j * 512 + w_], in_=ot[o_][:, 0:w_]), [B_ot[o_]], [B_xbc], dma=B_ot[o_], part=True)

        if "noA3" not in taps:
            xproj(0)
        for ch in (range(10) if "noA3" not in taps else ()):
            if ch + 1 < 10:
                xproj(ch + 1)
            xconv(ch)
        R.barrier()

        sbc = SB(nc, sub0)
        dg31 = sbc.alloc("dg31", [128, 4, 31, 128], BF16); B_dg31 = Buf("dg31")
        RUW = 15 + SEQ + 15 + 15 + NCTX + 15
        ru = [sbc.alloc("ru", [128, RUW], BF16) for _ in range(2)]
        B_ru = [Buf("ru%d" % i) for i in range(2)]
        sg = [sbc.alloc("sg", [128, 512], F32) for _ in range(2)]
        B_sg = [Buf("sg%d" % i) for i in range(2)]
        for ch in range(4):
            for k in range(31):
                eng, eo = ("dve", nc.vector) if k % 2 == 0 else ("pool", nc.gpsimd)
                R.add(eng, lambda ch=ch, k=k, eo=eo: eo.tensor_scalar(
                    out=dg31[:, ch, k, :], in0=c_t[:, IDENT, :], scalar1=vecs[:, V_CCW + ch * 31 + k:V_CCW + ch * 31 + k + 1], scalar2=None, op0=ALU.mult),
                    [B_c, B_vecs], [B_dg31], part=True)
        for i in range(2):
            R.add("pool", lambda i=i: nc.gpsimd.memset(ru[i][:], 0.0), [], [B_ru[i]])

        def uproj(ch):
            wa = wload(O_UA + ch * 128)
            wb = wload(O_UB + ch * 128)
            r_ = ch % 2
            for j in range(9):
                w_ = proj(wa, j, 0)
                proj(wb, j, 1)
                s_ = j % 2
                R.add("act", lambda s_=s_, w_=w_: nc.scalar.activation(out=sg[s_][:, 0:w_], in_=PS[1][:, 0:w_], func=AF.Sigmoid), [PB[1]], [B_sg[s_]])
                c0_ = rbcol(j, 15)
                R.add("dve", lambda s_=s_, w_=w_, c0_=c0_: nc.vector.tensor_tensor(
                    out=ru[r_][:, c0_:c0_ + w_], in0=PS[0][:, 0:w_], in1=sg[s_][:, 0:w_], op=ALU.mult), [PB[0], B_sg[s_]], [B_ru[r_]], part=(j > 0))

        def uconv(ch):
            r_ = ch % 2
            for j in range(9):
                w_ = 512 if j < 8 else 256
                pb = 2 + j % 2
                st_ = rbcol(j, 15) - 15
                for k in range(31):
                    R.add("pe", lambda k=k, pb=pb, w_=w_, st_=st_: nc.tensor.matmul(
                        PS[pb][:, 0:w_], lhsT=dg31[:, ch, k, :], rhs=ru[r_][:, st_ + k:st_ + k + w_], start=(k == 0), stop=(k == 30)),
                        [B_dg31, B_ru[r_]], [PB[pb]], part=(k > 0))
                o_ = next_ot()
                R.add("act", lambda pb=pb, o_=o_, w_=w_: nc.scalar.activation(
                    out=ot[o_][:, 0:w_], in_=PS[pb][:, 0:w_], func=AF.Identity, bias=vecs[:, V_CMV + ch * 4:V_CMV + ch * 4 + 1], scale=1.0),
                    [PB[pb], B_vecs], [B_ot[o_]])
                R.add("sp", lambda o_=o_, w_=w_, j=j: nc.sync.dma_start(
                    out=cv_d[ch * 128:(ch + 1) * 128, j * 512:j * 512 + w_], in_=ot[o_][:, 0:w_]), [B_ot[o_]], [B_cv], dma=B_ot[o_], part=True)

        if "noA4" not in taps:
            uproj(0)
        for ch in (range(4) if "noA4" not in taps else ()):
            if ch + 1 < 4:
                uproj(ch + 1)
            uconv(ch)
        R.barrier()

        sbt = SB(nc, sub0)
        NZ = 768 + 24 + 256
        wzs = sbt.alloc("wzs", [128, 8, 384], F32); B_wzs = Buf("wzs")
        wz = sbt.alloc("wz", [128, 8, NZ], BF16); B_wz = Buf("wz")
        for (dc, c0, n_) in ((0, O_Z, 384), (384, O_Z + 384, 384), (768, O_DT, 24), (792, O_V, 256)):
            R.add("sp", lambda c0=c0, n_=n_: nc.sync.dma_start(out=wzs[:, :, 0:n_], in_=wsrc(li, c0, n_)), [B_in], [B_wzs], dma=True)
            R.add("pool", lambda dc=dc, n_=n_: nc.gpsimd.tensor_copy(out=wz[:, :, dc:dc + n_], in_=wzs[:, :, 0:n_]), [B_wzs], [B_wz], part=(dc > 0))
        zt = [sbt.alloc("zt", [128, 768], F32) for _ in range(2)]
        B_zt = [Buf("zt%d" % i) for i in range(2)]
        vt = [sbt.alloc("vt", [128, 256], BF16) for _ in range(2)]
        B_vt = [Buf("vt%d" % i) for i in range(2)]
        dta = sbt.alloc("dta", [128, 34, 24], F32); B_dta = Buf("dta")
        dtw = sbt.alloc("dtw", [128, 3, 34 * 24], F32); B_dtw = Buf("dtw")
        dtbt = sbt.alloc("dtbt", [128, 24], F32); B_dtbt = Buf("dtbt")
        R.add("sp", lambda: nc.sync.dma_start(out=dtbt[:], in_=bass.AP(dtb.ap().tensor, li * 24, [[0, 128], [1, 24]])), [B_in], [B_dtbt], dma=True)
        for tt in (range(34) if "noA5" not in taps else ()):
            s_ = tt % 2
            pbs = (0, 1, 2) if tt % 2 == 0 else (3, 4, 5)
            for (pb, dc, n_) in ((pbs[0], 0, 384), (pbs[1], 384, 384), (pbs[2], 768, 280)):
                for kc in range(8):
                    R.add("pe", lambda kc=kc, pb=pb, dc=dc, n_=n_, tt=tt: nc.tensor.matmul(
                        PS[pb][:, 0:n_], lhsT=hT[:, kc, tt * 128:(tt + 1) * 128], rhs=wz[:, kc, dc:dc + n_], start=(kc == 0), stop=(kc == 7)),
                        [B_hT[tt], B_wz], [PB[pb]], part=(kc > 0))
            for hf_ in range(2):
                R.add("act", lambda hf_=hf_, s_=s_, pbs=pbs: nc.scalar.activation(
                    out=zt[s_][:, hf_ * 384:(hf_ + 1) * 384], in_=PS[pbs[hf_]][:, 0:384], func=AF.Silu), [PB[pbs[hf_]]], [B_zt[s_]], part=(hf_ > 0))
            R.add("sp", lambda s_=s_, tt=tt: nc.sync.dma_start(out=zs_d[tt * 128:(tt + 1) * 128, :], in_=zt[s_][:]), [B_zt[s_]], [B_zs], dma=B_zt[s_], part=True)
            R.add("dve", lambda s_=s_, pbs=pbs: nc.vector.tensor_copy(out=vt[s_][:], in_=PS[pbs[2]][:, 24:280]), [PB[pbs[2]]], [B_vt[s_]])
            R.add("sp", lambda s_=s_, tt=tt: nc.sync.dma_start(out=v_d[tt * 128:(tt + 1) * 128, :], in_=vt[s_][:]), [B_vt[s_]], [B_v], dma=B_vt[s_], part=True)
            R.add("dve", lambda tt=tt, pbs=pbs: nc.vector.tensor_tensor(out=dta[:, tt, :], in0=PS[pbs[2]][:, 0:24], in1=dtbt[:], op=ALU.add),
                  [PB[pbs[2]], B_dtbt], [B_dta], part=True)
        dflat = dta.rearrange("p a b -> p (a b)")
        R.add("act", lambda: nc.scalar.activation(out=dtw[:, 0, :], in_=dflat, func=AF.Abs), [B_dta], [B_dtw])
        R.add("act", lambda: nc.scalar.activation(out=dtw[:, 1, :], in_=dtw[:, 0, :], func=AF.Exp, scale=-1.0), [B_dtw], [B_dtw], part=True)
        R.add("act", lambda: nc.scalar.activation(out=dtw[:, 2, :], in_=dtw[:, 1, :], func=AF.Ln, bias=eps_t[:, 1:2], scale=1.0), [B_dtw, B_eps], [B_dtw], part=True)
        R.add("dve", lambda: nc.vector.scalar_tensor_tensor(out=dtw[:, 0, :], in0=dflat, scalar=0.0, in1=dtw[:, 2, :], op0=ALU.max, op1=ALU.add),
              [B_dta, B_dtw], [B_dtw], part=True)
        R.add("sp", lambda: nc.sync.dma_start(out=dt_d.ap().rearrange("(a p) h -> p a h", p=128),
                                              in_=dtw[:, 0, :].rearrange("p (a h) -> p a h", h=24)), [B_dtw], [B_dt], dma=B_dtw)
        R.barrier()
        R.release(B_wst + B_ot + B_obt + [B_vecs, B_rope, B_wzs, B_dtbt, B_dtw] + B_zt + B_vt)

    def phaseB(li, last, side=None):
        sb = SB(nc, WORK0, limit=P1_BASE)
        KT = sb.alloc("KT", [128, 4, T], BF16); B_KT = Buf("KT")
        VA = sb.alloc("VA", [128, 34, 4, 128], BF16); B_VA = Buf("VA")
        qs = [sb.alloc("qs", [128, 3, 512], BF16) for _ in range(2)]
        B_qs = [Buf("qs%d" % i) for i in range(2)]
        pT = [sb.alloc("pT", [128, 512], BF16) for _ in range(4)]
        B_pT = [Buf("pT%d" % i) for i in range(4)]
        rsb = [sb.alloc("rsb", [128, 512], F32) for _ in range(2)]
        B_rsb = [Buf("rsb%d" % i) for i in range(2)]
        ob = [sb.alloc("ob", [128, 512], F32) for _ in range(2)]
        B_ob = [Buf("ob%d" % i) for i in range(2)]
        gs = [sb.alloc("gs", [128, 512], F32) for _ in range(2)]
        B_gs = [Buf("gs%d" % i) for i in range(2)]
        obb = [sb.alloc("obb", [128, 512], BF16) for _ in range(2)]
        B_obb = [Buf("obb%d" % i) for i in range(2)]
        R.add("pool", lambda: nc.gpsimd.memset(KT[64:128, :, :], 0.0), [], [B_KT])
        R.add("sp", lambda: nc.sync.dma_start(out=KT[0:64, :, :], in_=kT_d.ap().rearrange("(g d) t -> d g t", d=64)), [B_kT], [B_KT], dma=True, part=True)
        for i in range(2):
            R.add("pool", lambda i=i: nc.gpsimd.memset(qs[i][64:128, :, :], 0.0), [], [B_qs[i]])
        R.add("pool", lambda: nc.gpsimd.memset(VA[:, :, :, 64:128], 1.0), [], [B_VA])
        for tt in range(34):
            R.add("sp", lambda tt=tt: nc.sync.dma_start(out=VA[:, tt, :, 0:64], in_=v_d[tt * 128:(tt + 1) * 128, :].rearrange("p (g d) -> p g d", d=64)),
                  [B_v], [B_VA], dma=True, part=True)
        fin = 0
        if side:
            R.side, R.side_every, R.side_cnt = list(side), 8, 0
        for j in range(9):
            if last and j == 8:
                continue
            w_ = 512 if j < 8 else 256
            kts = list(range(34)) if j < 8 else [32, 33]
            for g in range(4):
                pbase = (g % 2) * 64
                qi = (j * 4 + g) % 2
                R.add("sp", lambda g=g, j=j, w_=w_, qi=qi, pbase=pbase: nc.sync.dma_start(
                    out=qs[qi][0:64, :, 0:w_],
                    in_=qT_d[g * 192:(g + 1) * 192, j * 512:j * 512 + w_].rearrange("(h d) t -> d h t", d=64)), [B_qT], [B_qs[qi]], dma=True, part=True)
                steps = [(kt, hh) for kt in kts for hh in range(3)]
                n = len(steps)

                def S(s_):
                    kt, hh = steps[s_]
                    bk = s_ % 3
                    pk = s_ % 4
                    R.add("pe", lambda kt=kt, hh=hh, bk=bk, g=g, pbase=pbase, qi=qi, w_=w_: nc.tensor.matmul(PS[bk][:, 0:w_], lhsT=KT[:, g, kt * 128:(kt + 1) * 128],
                                                         rhs=qs[qi][:, hh, 0:w_], start=True, stop=True), [B_KT, B_qs[qi]], [PB[bk]])
                    R.add("act", lambda bk=bk, pk=pk, w_=w_: nc.scalar.activation(out=pT[pk][:, 0:w_], in_=PS[bk][:, 0:w_], func=AF.Exp, scale=0.125), [PB[bk]], [B_pT[pk]])

                def PV(s_):
                    kt, hh = steps[s_]
                    bk = s_ % 4
                    R.add("pe", lambda kt=kt, hh=hh, bk=bk, g=g, w_=w_, k0=kts[0], k1=kts[-1]: nc.tensor.matmul(
                        PS[4 + hh][:, 0:w_], lhsT=VA[:, kt, g, :], rhs=pT[bk][:, 0:w_],
                        start=(kt == k0), stop=(kt == k1)), [B_VA, B_pT[bk]], [PB[4 + hh]], part=(kt != kts[0]))

                for s_ in range(n + 2):
                    if s_ < n:
                        S(s_)
                    if s_ >= 2:
                        PV(s_ - 2)
                for hh in range(3):
                    h_ = g * 3 + hh
                    f_ = fin % 2
                    fin += 1
                    R.add("sp", lambda h_=h_, f_=f_, j=j, w_=w_: nc.sync.dma_start(
                        out=gs[f_][0:64, 0:w_], in_=gas_d[h_ * 64:(h_ + 1) * 64, j * 512:j * 512 + w_]), [B_gas], [B_gs[f_]], dma=True)
                    R.add("act", lambda hh=hh, f_=f_, w_=w_: nc.scalar.activation(out=rsb[f_][64:128, 0:w_], in_=PS[4 + hh][64:128, 0:w_], func=AF.Ln), [PB[4 + hh]], [B_rsb[f_]])
                    R.add("act", lambda f_=f_, w_=w_: nc.scalar.activation(out=rsb[f_][64:128, 0:w_], in_=rsb[f_][64:128, 0:w_], func=AF.Exp, scale=-1.0), [B_rsb[f_]], [B_rsb[f_]])
                    R.add("dve", lambda hh=hh, f_=f_, w_=w_: nc.vector.tensor_tensor(out=ob[f_][0:64, 0:w_], in0=PS[4 + hh][0:64, 0:w_], in1=rsb[f_][64:128, 0:w_], op=ALU.mult),
                          [PB[4 + hh], B_rsb[f_]], [B_ob[f_]])
                    R.add("pool", lambda f_=f_, w_=w_: nc.gpsimd.tensor_tensor(out=obb[f_][0:64, 0:w_], in0=ob[f_][0:64, 0:w_], in1=gs[f_][0:64, 0:w_], op=ALU.mult),
                          [B_ob[f_], B_gs[f_]], [B_obb[f_]])
                    R.add("sp", lambda h_=h_, f_=f_, j=j, w_=w_: nc.sync.dma_start(
                        out=cat_d[768 + h_ * 64:768 + (h_ + 1) * 64, j * 512:j * 512 + w_], in_=obb[f_][0:64, 0:w_]), [B_obb[f_]], [B_cat], dma=B_obb[f_], part=True)
        R.flush_side()
        R.barrier()
        R.release([B_KT, B_VA] + B_qs + B_gs + B_obb)

    def bc12(ap2d, n=64):
        return fap(ap2d, [[1, ap2d.shape[1]], [0, n]])

    def v3(ap2d, a, b):
        return fap(ap2d, [[b, a], [1, b]])

    P1_BASE = SB.ABYTES - 46 * 1024

    def phaseC(li, last, mode="p2"):
        full_ = (mode == "p2")
        sb = SB(nc, WORK0) if full_ else SB(nc, P1_BASE)
        Abc = sb.alloc("Abc", [128, 24], F32); B_Abc = Buf("Abc")
        Dbc = sb.alloc("Dbc", [128, 12], F32); B_Dbc = Buf("Dbc")
        Gbc = sb.alloc("Gbc", [128, 768], F32) if full_ else None; B_Gbc = Buf("Gbc")
        mrep = sb.alloc("mrep", [128, 2, 4, 128], BF16) if full_ else None; B_mrep = Buf("mrep")
        R.add("sp", lambda: nc.sync.dma_start(out=Abc[:], in_=bass.AP(alog.ap().tensor, li * 24, [[0, 128], [1, 24]])), [B_in], [B_Abc], dma=True)
        R.add("act", lambda: nc.scalar.activation(out=Abc[:], in_=Abc[:], func=AF.Exp), [B_Abc], [B_Abc])
        R.add("dve", lambda: nc.vector.tensor_scalar(out=Abc[:], in0=Abc[:], scalar1=-1.0, scalar2=None, op0=ALU.mult), [B_Abc], [B_Abc])
        R.add("sp", lambda: nc.sync.dma_start(out=Dbc[:], in_=bass.AP(ssdd.ap().tensor, li * 12, [[0, 128], [1, 12]])), [B_in], [B_Dbc], dma=True)
        if full_:
            R.add("sp", lambda: nc.sync.dma_start(out=Gbc[:], in_=bass.AP(ssdng.ap().tensor, li * 768, [[0, 128], [1, 768]])), [B_in], [B_Gbc], dma=True)
        for dr in (range(2) if full_ else ()):
            R.add("dve", lambda dr=dr: nc.vector.tensor_copy(out=mrep[:, dr, :, :], in_=fap(mk_t[:, dr, :], [[0, 4], [1, 128]])), [B_mk], [B_mrep], part=(dr > 0))
        P2 = range(2)

        def mk(name, shape, dt, n=2, always=False):
            if not (full_ or always):
                return [None] * n, [Buf("%s%d" % (name, i)) for i in range(n)]
            return [sb.alloc(name, shape, dt) for _ in range(n)], [Buf("%s%d" % (name, i)) for i in range(n)]
        _mk = mk
        mk = lambda name, shape, dt, n=2: _mk(name, shape, dt, n, name in ("xT", "bcT", "dtc", "xs", "Btm", "av", "ct", "ex", "dtd", "tmp", "hst", "hbb"))
        xT, B_xT = mk("xT", [128, 6, 128], F32)
        bcT, B_bcT = mk("bcT", [128, 4, 128], F32)
        dtc, B_dtc = mk("dtc", [128, 24], F32)
        zc, B_zc = mk("zc", [128, 768], F32)
        hbin, B_hbin = mk("hbin", [128, 768], BF16)
        xs, B_xs = mk("xs", [128, 768], F32)
        Btm, B_Btm = mk("Btm", [128, 256], BF16)
        bcb, B_bcb = mk("bcb", [128, 4, 128], BF16)
        av, B_av = mk("av", [128, 24], BF16)
        ct, B_ct = mk("ct", [128, 72], F32)
        ex, B_ex = mk("ex", [128, 72], F32)
        ncum, B_ncum = mk("ncum", [128, 24], F32)
        dtd, B_dtd = mk("dtd", [128, 24], F32)
        xw = [[(sb.alloc("xw", [128, 768], BF16) if (full_ or k == 1) else None) for k in range(4)] for _ in P2]
        B_xw = [[Buf("xw%d_%d" % (i, k)) for k in range(4)] for i in P2]
        cbs, B_cbs = mk("cbs", [128, 2, 128], BF16)
        Dm = [[(sb.alloc("Dm", [128, 12, 128], BF16) if full_ else None) for _ in P2] for _ in P2]; B_Dm = [[Buf("Dm%d%d" % (i, k)) for k in P2] for i in P2]
        Et = [[(sb.alloc("Et", [128, 12, 128], BF16) if full_ else None) for _ in P2] for _ in P2]; B_Et = [[Buf("Et%d%d" % (i, k)) for k in P2] for i in P2]
        Mt = [[(sb.alloc("Mt", [128, 12, 128], BF16) if full_ else None) for _ in P2] for _ in P2]; B_Mt = [[Buf("Mt%d%d" % (i, k)) for k in P2] for i in P2]
        yo = [[(sb.alloc("yo", [128, 768], F32) if full_ else None) for _ in P2] for _ in P2]; B_yo = [[Buf("yo%d%d" % (i, k)) for k in P2] for i in P2]
        yv, B_yv = mk("yv", [128, 768], F32)
        t3, B_t3 = mk("t3", [128, 768], F32)
        yn, B_yn = mk("yn", [128, 768], F32)
        junk, B_junk = mk("junkc", [128, 768], BF16)
        st, B_st = mk("stc", [128, 4], F32)
        catT, B_catT = mk("catT", [128, 6, 128], BF16)
        tmp, B_tmp = mk("tmp", [128, 768], F32)
        hst, B_hst = mk("hst", [128, 768], F32)
        hfb = sb.alloc("hfb", [128, 768], BF16) if full_ else None; B_hfb = Buf("hfb")
        hbb, B_hbb = mk("hbb", [128, 768], BF16)
        for i in P2:
            R.add("pool", lambda i=i: nc.gpsimd.memset(hst[i][:], 0.0), [], [B_hst[i]])
            R.add("pool", lambda i=i: nc.gpsimd.memset(hbb[i][:], 0.0), [], [B_hbb[i]])
        if full_:
            R.add("pool", lambda: nc.gpsimd.memset(hfb[:], 0.0), [], [B_hfb])

        def prep(c, pp, full, banks=None):
            S0, S1, S2 = banks if banks is not None else (4 * pp, 4 * pp + 1, 4 * pp + 2)
            cs = slice(c * 128, (c + 1) * 128)
            R.add("sp", lambda: nc.sync.dma_start(out=xT[pp][:], in_=xbc_d[0:768, cs].rearrange("(c p) t -> p c t", p=128)), [B_xbc], [B_xT[pp]], dma=True)
            R.add("sp", lambda: nc.sync.dma_start(out=bcT[pp][:], in_=xbc_d[768:1280, cs].rearrange("(c p) t -> p c t", p=128)), [B_xbc], [B_bcT[pp]], dma=True)
            R.add("sp", lambda: nc.sync.dma_start(out=dtc[pp][:], in_=dt_d[cs, :]), [B_dt], [B_dtc[pp]], dma=True)
            if full:
                R.add("sp", lambda: nc.sync.dma_start(out=zc[pp][:], in_=zs_d[cs, :]), [B_zs], [B_zc[pp]], dma=True)
                R.add("sp", lambda: nc.sync.dma_start(out=hbin[pp][:], in_=hb_d[c]), [B_hb], [B_hbin[pp]], dma=True)
            for i in range(4):
                R.add("pe", lambda i=i: nc.tensor.transpose(PS[S0][:, i * 128:(i + 1) * 128], xT[pp][:, i, :], c_t[:, IDENT, :]),
                      [B_xT[pp], B_c], [PB[S0]], part=(i > 0))
            R.add("act", lambda: nc.scalar.activation(out=xs[pp][:, 0:512], in_=PS[S0][:, 0:512], func=AF.Copy), [PB[S0]], [B_xs[pp]])
            for i in range(4, 6):
                R.add("pe", lambda i=i: nc.tensor.transpose(PS[S1][:, (i - 4) * 128:(i - 3) * 128], xT[pp][:, i, :], c_t[:, IDENT, :]),
                      [B_xT[pp], B_c], [PB[S1]], part=(i > 4))
            for i in range(2):
                R.add("pe", lambda i=i: nc.tensor.transpose(PS[S1][:, 256 + i * 128:384 + i * 128], bcT[pp][:, i, :], c_t[:, IDENT, :]),
                      [B_bcT[pp], B_c], [PB[S1]], part=True)
            R.add("act", lambda: nc.scalar.activation(out=xs[pp][:, 512:768], in_=PS[S1][:, 0:256], func=AF.Copy), [PB[S1]], [B_xs[pp]], part=True)
            R.add("dve", lambda: nc.vector.tensor_copy(out=Btm[pp][:], in_=PS[S1][:, 256:512]), [PB[S1]], [B_Btm[pp]])
            if full:
                R.add("pool", lambda: nc.gpsimd.tensor_copy(out=bcb[pp][:], in_=bcT[pp][:]), [B_bcT[pp]], [B_bcb[pp]])
            R.add("dve", lambda: nc.vector.tensor_tensor(out=av[pp][:], in0=dtc[pp][:], in1=Abc[:], op=ALU.mult), [B_dtc[pp], B_Abc], [B_av[pp]])
            R.add("pe", lambda: nc.tensor.matmul(PS[S2][:, 0:12], lhsT=cb_t[:, UIN, :], rhs=av[pp][:, 0:12], start=True, stop=True), [B_cb, B_av[pp]], [PB[S2]])
            R.add("pe", lambda: nc.tensor.matmul(PS[S2][:, 12:24], lhsT=cb_t[:, UTR, :], rhs=av[pp][:, 12:24], start=True, stop=True), [B_cb, B_av[pp]], [PB[S2]], part=True)
            R.add("pe", lambda: nc.tensor.matmul(PS[S2][:, 24:48], lhsT=cb_t[:, ONES, :], rhs=av[pp][:, 0:24], start=True, stop=True), [B_cb, B_av[pp]], [PB[S2]], part=True)
            R.add("dve", lambda: nc.vector.tensor_copy(out=ct[pp][:, 24:72], in_=PS[S2][:, 0:48]), [PB[S2]], [B_ct[pp]])
            R.add("dve", lambda: nc.vector.tensor_tensor(out=ct[pp][:, 0:24], in0=ct[pp][:, 48:72], in1=ct[pp][:, 24:48], op=ALU.subtract), [B_ct[pp]], [B_ct[pp]], part=True)
            R.add("act", lambda: nc.scalar.activation(out=ex[pp][:], in_=ct[pp][:], func=AF.Exp), [B_ct[pp]], [B_ex[pp]])
            if full:
                R.add("dve", lambda: nc.vector.tensor_scalar(out=ncum[pp][:], in0=ct[pp][:, 24:48], scalar1=-1.0, scalar2=None, op0=ALU.mult), [B_ct[pp]], [B_ncum[pp]])
            R.add("dve", lambda: nc.vector.tensor_tensor(out=dtd[pp][:], in0=dtc[pp][:], in1=ex[pp][:, 0:24], op=ALU.mult), [B_dtc[pp], B_ex[pp]], [B_dtd[pp]])
            xs3 = v3(xs[pp][:, 0:768], 12, 64)
            R.add("dve", lambda: nc.vector.tensor_tensor(out=v3(xw[pp][1][:, 0:768], 12, 64), in0=xs3, in1=bc12(dtd[pp][:, 12:24]), op=ALU.mult),
                  [B_xs[pp], B_dtd[pp]], [B_xw[pp][1]])
            if full:
                R.add("pool", lambda: nc.gpsimd.tensor_tensor(out=v3(xw[pp][0][:, 0:768], 12, 64), in0=xs3, in1=bc12(dtd[pp][:, 0:12]), op=ALU.mult),
                      [B_xs[pp], B_dtd[pp]], [B_xw[pp][0]])
                R.add("dve", lambda: nc.vector.tensor_tensor(out=v3(xw[pp][2][:, 0:768], 12, 64), in0=xs3, in1=bc12(dtc[pp][:, 0:12]), op=ALU.mult),
                      [B_xs[pp], B_dtc[pp]], [B_xw[pp][2]])
                R.add("pool", lambda: nc.gpsimd.tensor_tensor(out=v3(xw[pp][3][:, 0:768], 12, 64), in0=xs3, in1=bc12(dtc[pp][:, 12:24]), op=ALU.mult),
                      [B_xs[pp], B_dtc[pp]], [B_xw[pp][3]])

        def state_update(pp, d, bks, outb, B_outb):
            R.add("dve", lambda: nc.vector.tensor_tensor(out=v3(tmp[pp][:, 0:768], 12, 64), in0=v3(hst[d][:, 0:768], 12, 64),
                                                         in1=bc12(ex[pp][:, 48 + d * 12:60 + d * 12]), op=ALU.mult), [B_hst[d], B_ex[pp]], [B_tmp[pp]])
            for g in range(2):
                R.add("pe", lambda g=g: nc.tensor.matmul(PS[bks[g]][:, 0:384], lhsT=Btm[pp][:, g * 128:(g + 1) * 128], rhs=xw[pp][d][:, g * 384:(g + 1) * 384],
                                                        start=True, stop=True), [B_Btm[pp], B_xw[pp][d]], [PB[bks[g]]])
                R.add("dve", lambda g=g: nc.vector.tensor_tensor(out=hst[d][:, g * 384:(g + 1) * 384], in0=PS[bks[g]][:, 0:384], in1=tmp[pp][:, g * 384:(g + 1) * 384], op=ALU.add),
                      [PB[bks[g]], B_tmp[pp]], [B_hst[d]], part=(g > 0))
            R.add("act", lambda: nc.scalar.activation(out=outb[:], in_=hst[d][:], func=AF.Copy), [B_hst[d]], [B_outb])

        def capture(fn):
            lst = []
            R.capture = lst
            marks = fn()
            R.capture = None
            return lst, marks

        if not full_:
            order_b = [33, 32] + list(range(31, -1, -1))
            lists = []
            for n_, c in enumerate(order_b):
                pp = n_ % 2

                def body(c=c, pp=pp):
                    prep(c, pp, False, banks=(7, 3, 7))
                    need = len(R.capture)
                    R.add("sp", lambda: nc.sync.dma_start(out=hb_d[c], in_=hbb[pp][:]), [B_hbb[pp]], [B_hb], dma=B_hbb[pp], part=True)
                    state_update(pp, 1, (3, 7), hbb[1 - pp], B_hbb[1 - pp])
                    return need, len(R.capture)
                ops, (need, done) = capture(body)
                lists.append((ops, need, done))
            flat = []
            for ops, _, _ in lists:
                flat.extend(ops)
            return flat

        order_f = [32, 33] + list(range(32))
        lists = []
        for n_, c in enumerate(order_f):
            pp = n_ % 2

            def body(c=c, pp=pp):
                S0, S1, S2, S3 = 4 * pp, 4 * pp + 1, 4 * pp + 2, 4 * pp + 3
                prep(c, pp, True)
                need = len(R.capture)
                k_ = 0
                for dr in range(2):
                    hin, B_hin = (hfb, B_hfb) if dr == 0 else (hbin[pp], B_hbin[pp])
                    for g in range(2):
                        bk = S3 if k_ % 2 == 0 else S2
                        k_ += 1
                        R.add("pe", lambda g=g, bk=bk, hin=hin: nc.tensor.matmul(PS[bk][:, 0:384], lhsT=bcb[pp][:, 2 + g, :], rhs=hin[:, g * 384:(g + 1) * 384],
                                                                              start=True, stop=True), [B_bcb[pp], B_hin], [PB[bk]])
                        R.add("dve", lambda g=g, bk=bk, dr=dr: nc.vector.tensor_tensor(
                            out=v3(yo[pp][dr][:, g * 384:(g + 1) * 384], 6, 64), in0=v3(PS[bk][:, 0:384], 6, 64),
                            in1=bc12(ex[pp][:, 24 + dr * 12 + g * 6:24 + dr * 12 + g * 6 + 6]), op=ALU.mult), [PB[bk], B_ex[pp]], [B_yo[pp][dr]], part=(g > 0))
                state_update(pp, 0, (S3, S2), hfb, B_hfb)
                done = len(R.capture)
                for g in range(2):
                    R.add("pe", lambda g=g: nc.tensor.matmul(PS[S2][:, g * 128:(g + 1) * 128], lhsT=bcb[pp][:, g, :], rhs=bcb[pp][:, 2 + g, :], start=True, stop=True),
                          [B_bcb[pp]], [PB[S2]], part=(g > 0))
                R.add("dve", lambda: nc.vector.tensor_copy(out=cbs[pp][:].rearrange("p a b -> p (a b)"), in_=PS[S2][:, 0:256]), [PB[S2]], [B_cbs[pp]])
                k_ = 0
                for dr in range(2):
                    um = UIN if dr == 0 else UTR
                    de, deo = ("pool", nc.gpsimd) if dr == 0 else ("dve", nc.vector)
                    R.add(de, lambda dr=dr, um=um, deo=deo: deo.tensor_tensor(
                        out=Dm[pp][dr][:], in0=fap(cb_t[:, um, :], [[0, 12], [1, 128]]), in1=bc12(av[pp][:, dr * 12:dr * 12 + 12], 128), op=ALU.mult),
                        [B_cb, B_av[pp]], [B_Dm[pp][dr]])
                    for q in range(3):
                        bk = S3 if k_ % 2 == 0 else S2
                        k_ += 1
                        R.add("pe", lambda q=q, dr=dr, bk=bk: nc.tensor.matmul(PS[bk][:, :], lhsT=cb_t[:, ONES, :], rhs=Dm[pp][dr][:, q * 4:(q + 1) * 4, :].rearrange("p a b -> p (a b)"),
                                                                             start=True, stop=False), [B_cb, B_Dm[pp][dr]], [PB[bk]])
                        R.add("pe", lambda q=q, dr=dr, bk=bk: nc.tensor.matmul(PS[bk][:, :], lhsT=cb_t[:, IDENT, :], rhs=mrep[:, dr, :, :].rearrange("p a b -> p (a b)"),
                                                                             start=False, stop=True), [B_cb, B_mrep], [PB[bk]], part=True)
                        for hh in range(4):
                            h = q * 4 + hh
                            R.add("act", lambda h=h, hh=hh, dr=dr, bk=bk: nc.scalar.activation(
                                out=Et[pp][dr][:, h, :], in_=PS[bk][:, hh * 128:(hh + 1) * 128], func=AF.Exp,
                                bias=ncum[pp][:, dr * 12 + h:dr * 12 + h + 1], scale=1.0), [PB[bk], B_ncum[pp]], [B_Et[pp][dr]], part=(h > 0))
                    for g in range(2):
                        R.add("dve", lambda g=g, dr=dr: nc.vector.tensor_tensor(out=Mt[pp][dr][:, g * 6:(g + 1) * 6, :], in0=Et[pp][dr][:, g * 6:(g + 1) * 6, :],
                                                                              in1=fap(cbs[pp][:, g, :], [[0, 6], [1, 128]]), op=ALU.mult),
                              [B_Et[pp][dr], B_cbs[pp]], [B_Mt[pp][dr]], part=(g > 0))
                for h in range(12):
                    bk, col = (S0, h * 64) if h < 8 else (S1, (h - 8) * 64)
                    for dr in range(2):
                        R.add("pe", lambda h=h, dr=dr, bk=bk, col=col: nc.tensor.matmul(
                            PS[bk][:, col:col + 64], lhsT=Mt[pp][dr][:, h, :], rhs=xw[pp][2 + dr][:, h * 64:(h + 1) * 64], start=(dr == 0), stop=(dr == 1)),
                            [B_Mt[pp][dr], B_xw[pp][2 + dr]], [PB[bk]], part=not (dr == 0 and h in (0, 8)))
                skip_out = last and c >= 32
                if not skip_out:
                    R.add("dve", lambda: nc.vector.tensor_tensor(out=yv[pp][:, 0:512], in0=PS[S0][:, 0:512], in1=yo[pp][0][:, 0:512], op=ALU.add), [PB[S0], B_yo[pp][0]], [B_yv[pp]])
                    R.add("dve", lambda: nc.vector.tensor_tensor(out=yv[pp][:, 512:768], in0=PS[S1][:, 0:256], in1=yo[pp][0][:, 512:768], op=ALU.add),
                          [PB[S1], B_yo[pp][0]], [B_yv[pp]], part=True)
                    R.add("dve", lambda: nc.vector.tensor_tensor(out=yv[pp][:], in0=yv[pp][:], in1=yo[pp][1][:], op=ALU.add), [B_yv[pp], B_yo[pp][1]], [B_yv[pp]])
                    R.add("pool", lambda: nc.gpsimd.tensor_tensor(out=v3(t3[pp][:, 0:768], 12, 64), in0=v3(xs[pp][:, 0:768], 12, 64), in1=bc12(Dbc[:, 0:12]), op=ALU.mult),
                          [B_xs[pp], B_Dbc], [B_t3[pp]])
                    R.add("dve", lambda: nc.vector.tensor_tensor(out=yv[pp][:], in0=yv[pp][:], in1=t3[pp][:], op=ALU.add), [B_yv[pp], B_t3[pp]], [B_yv[pp]])
                    R.add("pool", lambda: nc.gpsimd.tensor_tensor(out=yv[pp][:], in0=yv[pp][:], in1=zc[pp][:], op=ALU.mult), [B_yv[pp], B_zc[pp]], [B_yv[pp]])
                    R.add("act", lambda: nc.scalar.activation(out=junk[pp][:], in_=yv[pp][:], func=AF.Square, accum_out=st[pp][:, 0:1]), [B_yv[pp]], [B_junk[pp], B_st[pp]])
                    R.add("act", lambda: nc.scalar.activation(out=st[pp][:, 1:2], in_=st[pp][:, 0:1], func=AF.Ln, bias=eps_t[:, 0:1], scale=1.0 / 768),
                          [B_st[pp], B_eps], [B_st[pp]], part=True)
                    R.add("act", lambda: nc.scalar.activation(out=st[pp][:, 2:3], in_=st[pp][:, 1:2], func=AF.Exp, scale=-0.5), [B_st[pp]], [B_st[pp]], part=True)
                    R.add("dve", lambda: nc.vector.scalar_tensor_tensor(out=yn[pp][:], in0=yv[pp][:], scalar=st[pp][:, 2:3], in1=Gbc[:], op0=ALU.mult, op1=ALU.mult),
                          [B_yv[pp], B_st[pp], B_Gbc], [B_yn[pp]])
                    for i in range(6):
                        bk, col = (S2, i * 128) if i < 4 else (S3, (i - 4) * 128)
                        R.add("pe", lambda i=i, bk=bk, col=col: nc.tensor.transpose(PS[bk][:, col:col + 128], yn[pp][:, i * 128:(i + 1) * 128], c_t[:, IDENT, :]),
                              [B_yn[pp], B_c], [PB[bk]], part=(i not in (0, 4)))
                    R.add("act", lambda: nc.scalar.activation(out=catT[pp][:, 0:4, :].rearrange("p a b -> p (a b)"), in_=PS[S2][:, 0:512], func=AF.Copy), [PB[S2]], [B_catT[pp]])
                    R.add("act", lambda: nc.scalar.activation(out=catT[pp][:, 4:6, :].rearrange("p a b -> p (a b)"), in_=PS[S3][:, 0:256], func=AF.Copy), [PB[S3]], [B_catT[pp]], part=True)
                    R.add("sp", lambda: nc.sync.dma_start(out=cat_d[0:768, c * 128:(c + 1) * 128].rearrange("(c p) t -> p c t", p=128), in_=catT[pp][:]),
                          [B_catT[pp]], [B_cat], dma=B_catT[pp], part=True)
                return need, done
            ops, (need, done) = capture(body)
            lists.append((ops, need, done))
        R.interleave(lists)
        R.barrier()
        R.release(B_xT + B_bcT + B_dtc + B_zc + B_hbin + B_hbb + B_catT + [B_Abc, B_Dbc, B_Gbc])


    def phaseD(li, last):
        sb = SB(nc, WORK0)
        vec = sb.alloc("vecD", [128, 16], F32); B_vec = Buf("vecD")
        R.add("sp", lambda: nc.sync.dma_start(out=vec[:], in_=cmv[li].rearrange("p a b -> p (a b)")), [B_in], [B_vec], dma=True)
        pws = sb.alloc("pws", [128, 4, 512], F32); B_pws = Buf("pws")
        pwb = sb.alloc("pwb", [128, 4, 512], BF16); B_pwb = Buf("pwb")
        R.add("sp", lambda: nc.sync.dma_start(out=pws[:], in_=cpw[li].rearrange("(c p) n -> p c n", p=128)), [B_in], [B_pws], dma=True)
        R.add("pool", lambda: nc.gpsimd.tensor_copy(out=pwb[:], in_=pws[:]), [B_pws], [B_pwb])
        P2 = range(2)
        cvt = [sb.alloc("cvt", [128, 4, 512], F32) for _ in P2]; B_cvt = [Buf("cvt%d" % i) for i in P2]
        gct = [sb.alloc("gct", [128, 4, 512], F32) for _ in P2]; B_gct = [Buf("gct%d" % i) for i in P2]
        sq = sb.alloc("sq", [128, 4, 512], F32); B_sq = Buf("sq")
        mean = sb.alloc("mean", [128, 512], F32); B_mean = Buf("mean")
        m2 = sb.alloc("m2", [128, 512], F32); B_m2 = Buf("m2")
        var = sb.alloc("var", [128, 512], F32); B_var = Buf("var")
        sdv = sb.alloc("sdv", [128, 512], F32); B_sdv = Buf("sdv")
        rstd = sb.alloc("rstd", [128, 512], F32); B_rstd = Buf("rstd")
        xc_ = sb.alloc("xc", [128, 512], F32); B_xc = Buf("xc")
        xn_ = sb.alloc("xn", [128, 512], F32); B_xn = Buf("xnD")
        act = sb.alloc("act", [128, 4, 512], BF16); B_act = Buf("act")
        oD = [sb.alloc("oD", [128, 512], BF16) for _ in P2]; B_oD = [Buf("oD%d" % i) for i in P2]
        k_ = 0
        for j in range(9):
            if last and j == 8:
                continue
            w_ = 512 if j < 8 else 256
            cs = slice(j * 512, j * 512 + w_)
            pp = j % 2
            R.add("sp", lambda pp=pp, cs=cs, w_=w_: nc.sync.dma_start(out=cvt[pp][:, :, 0:w_], in_=cv_d[:, cs].rearrange("(c p) t -> p c t", p=128)), [B_cv], [B_cvt[pp]], dma=True)
            R.add("sp", lambda pp=pp, cs=cs, w_=w_: nc.sync.dma_start(out=gct[pp][:, :, 0:w_], in_=gcs_d[:, cs].rearrange("(c p) t -> p c t", p=128)), [B_gcs], [B_gct[pp]], dma=True)
            R.add("act", lambda pp=pp, w_=w_: nc.scalar.activation(out=sq[:, :, 0:w_], in_=cvt[pp][:, :, 0:w_], func=AF.Square), [B_cvt[pp]], [B_sq])
            for c in range(4):
                R.add("pe", lambda c=c, pp=pp, w_=w_: nc.tensor.matmul(PS[0][:, 0:w_], lhsT=c_t[:, ONES, :], rhs=cvt[pp][:, c, 0:w_], start=(c == 0), stop=(c == 3)),
                      [B_c, B_cvt[pp]], [PB[0]], part=(c > 0))
            for c in range(4):
                R.add("pe", lambda c=c, w_=w_: nc.tensor.matmul(PS[1][:, 0:w_], lhsT=c_t[:, ONES, :], rhs=sq[:, c, 0:w_], start=(c == 0), stop=(c == 3)),
                      [B_c, B_sq], [PB[1]], part=(c > 0))
            R.add("dve", lambda w_=w_: nc.vector.tensor_scalar(out=mean[:, 0:w_], in0=PS[0][:, 0:w_], scalar1=1.0 / 512, scalar2=None, op0=ALU.mult), [PB[0]], [B_mean])
            R.add("dve", lambda w_=w_: nc.vector.tensor_tensor(out=m2[:, 0:w_], in0=mean[:, 0:w_], in1=mean[:, 0:w_], op=ALU.mult), [B_mean], [B_m2])
            R.add("dve", lambda w_=w_: nc.vector.scalar_tensor_tensor(out=var[:, 0:w_], in0=PS[1][:, 0:w_], scalar=1.0 / 512, in1=m2[:, 0:w_], op0=ALU.mult, op1=ALU.subtract),
                  [PB[1], B_m2], [B_var])
            R.add("act", lambda w_=w_: nc.scalar.activation(out=sdv[:, 0:w_], in_=var[:, 0:w_], func=AF.Ln, bias=eps_t[:, 0:1], scale=1.0), [B_var, B_eps], [B_sdv])
            R.add("act", lambda w_=w_: nc.scalar.activation(out=rstd[:, 0:w_], in_=sdv[:, 0:w_], func=AF.Exp, scale=-0.5), [B_sdv], [B_rstd])
            for c in range(4):
                R.add("dve", lambda c=c, pp=pp, w_=w_: nc.vector.tensor_tensor(out=xc_[:, 0:w_], in0=cvt[pp][:, c, 0:w_], in1=mean[:, 0:w_], op=ALU.subtract),
                      [B_cvt[pp], B_mean], [B_xc])
                R.add("pool", lambda w_=w_: nc.gpsimd.tensor_tensor(out=xn_[:, 0:w_], in0=xc_[:, 0:w_], in1=rstd[:, 0:w_], op=ALU.mult), [B_xc, B_rstd], [B_xn])
                R.add("act", lambda c=c, w_=w_: nc.scalar.activation(out=act[:, c, 0:w_], in_=xn_[:, 0:w_], func=AF.Silu,
                                                                     bias=vec[:, c * 4 + 2:c * 4 + 3], scale=vec[:, c * 4 + 1:c * 4 + 2]), [B_xn, B_vec], [B_act], part=(c > 0))
            for oc in range(4):
                bk = 2 + oc % 2
                for c in range(4):
                    R.add("pe", lambda oc=oc, c=c, bk=bk, w_=w_: nc.tensor.matmul(PS[bk][:, 0:w_], lhsT=pwb[:, c, oc * 128:(oc + 1) * 128], rhs=act[:, c, 0:w_],
                                                                                 start=(c == 0), stop=(c == 3)), [B_pwb, B_act], [PB[bk]], part=(c > 0))
                o_ = k_ % 2
                k_ += 1
                R.add("dve", lambda oc=oc, bk=bk, o_=o_, pp=pp, w_=w_: nc.vector.scalar_tensor_tensor(
                    out=oD[o_][:, 0:w_], in0=PS[bk][:, 0:w_], scalar=vec[:, oc * 4 + 3:oc * 4 + 4], in1=gct[pp][:, oc, 0:w_], op0=ALU.add, op1=ALU.mult),
                    [PB[bk], B_vec, B_gct[pp]], [B_oD[o_]])
                R.add("sp", lambda oc=oc, o_=o_, cs=cs, w_=w_: nc.sync.dma_start(out=cat_d[1536 + oc * 128:1536 + (oc + 1) * 128, cs], in_=oD[o_][:, 0:w_]),
                      [B_oD[o_]], [B_cat], dma=B_oD[o_], part=True)
        R.barrier()
        R.release([B_vec, B_pws] + B_cvt + B_gct + B_oD)

    def phaseE(li, last):
        sb = SB(nc, WORK0)
        wos = sb.alloc("wos", [128, 4, D], F32); B_wos = Buf("wos")
        wo = sb.alloc("wo", [128, 16, D], BF16); B_wo = Buf("wo")
        for q in range(4):
            R.add("sp", lambda q=q: nc.sync.dma_start(out=wos[:], in_=w_out[li, q * 512:(q + 1) * 512, :].rearrange("(c p) n -> p c n", p=128)), [B_in], [B_wos], dma=True)
            R.add("pool", lambda q=q: nc.gpsimd.tensor_copy(out=wo[:, q * 4:(q + 1) * 4, :], in_=wos[:]), [B_wos], [B_wo], part=(q > 0))
        gB = sb.alloc("gB", [128, 2, D], F32); B_gB = Buf("gB")
        for j in range(2):
            R.add("sp", lambda j=j: nc.sync.dma_start(out=gB[:, j, :], in_=gate_d[li, j]), [B_gate], [B_gB], dma=True, part=(j > 0))
        fg = sb.alloc("fg", [128, D], F32); B_fg = Buf("fg")
        R.add("sp", lambda: nc.sync.dma_start(out=fg[:], in_=bass.AP(fng.ap().tensor, 0, [[0, 128], [1, D]])), [B_in], [B_fg], dma=True)
        P2 = range(2)
        ct_ = [sb.alloc("ctE", [128, 16, 512], BF16) for _ in P2]; B_ct = [Buf("ctE%d" % i) for i in P2]
        xr = [sb.alloc("xr", [128, D], F32) for _ in P2]; B_xr = [Buf("xr%d" % i) for i in P2]
        ty = [sb.alloc("ty", [128, D], F32) for _ in P2]; B_ty = [Buf("ty%d" % i) for i in P2]
        xo = [sb.alloc("xo", [128, D], F32) for _ in P2]; B_xo = [Buf("xo%d" % i) for i in P2]
        junk = sb.alloc("junkE", [128, D], BF16); B_junk = Buf("junkE")
        st = [sb.alloc("stE", [128, 4], F32) for _ in P2]; B_st = [Buf("stE%d" % i) for i in P2]
        fo = [sb.alloc("fo", [128, D], F32) for _ in P2]; B_fo = [Buf("fo%d" % i) for i in P2]
        for j in range(9):
            if last and j == 8:
                continue
            w_ = 512 if j < 8 else 256
            jp = j % 2
            R.add("sp", lambda jp=jp, j=j, w_=w_: nc.sync.dma_start(out=ct_[jp][:, :, 0:w_], in_=cat_d[:, j * 512:j * 512 + w_].rearrange("(c p) t -> p c t", p=128)),
                  [B_cat], [B_ct[jp]], dma=True)
            for q in range(w_ // 128):
                tt = j * 4 + q
                pp = tt % 2
                jj = 0 if tt < 32 else 1
                if li == 0:
                    src = x_in[tt * 128:(tt + 1) * 128, :] if tt < 32 else ctx_in[(tt - 32) * 128:(tt - 31) * 128, :]
                    srcb = B_in
                else:
                    src = x1_d[tt * 128:(tt + 1) * 128, :]
                    srcb = B_x1
                R.add("sp", lambda pp=pp, src=src: nc.sync.dma_start(out=xr[pp][:], in_=src), [srcb], [B_xr[pp]], dma=True)
                for hf_ in range(2):
                    bk = 2 * pp + hf_
                    for c in range(16):
                        R.add("pe", lambda c=c, hf_=hf_, bk=bk, jp=jp, q=q: nc.tensor.matmul(
                            PS[bk][:, :], lhsT=ct_[jp][:, c, q * 128:(q + 1) * 128], rhs=wo[:, c, hf_ * 512:(hf_ + 1) * 512], start=(c == 0), stop=(c == 15)),
                            [B_ct[jp], B_wo], [PB[bk]], part=(c > 0))
                    R.add("dve", lambda hf_=hf_, bk=bk, pp=pp, jj=jj: nc.vector.tensor_tensor(
                        out=ty[pp][:, hf_ * 512:(hf_ + 1) * 512], in0=PS[bk][:, :], in1=gB[:, jj, hf_ * 512:(hf_ + 1) * 512], op=ALU.mult),
                        [PB[bk], B_gB], [B_ty[pp]], part=(hf_ > 0))
                R.add("pool", lambda pp=pp: nc.gpsimd.tensor_tensor(out=xo[pp][:], in0=ty[pp][:], in1=xr[pp][:], op=ALU.add), [B_ty[pp], B_xr[pp]], [B_xo[pp]])
                if not last:
                    R.add("sp", lambda pp=pp, tt=tt: nc.sync.dma_start(out=x1_d[tt * 128:(tt + 1) * 128, :], in_=xo[pp][:]), [B_xo[pp]], [B_x1], dma=B_xo[pp], part=True)
                else:
                    R.add("act", lambda pp=pp: nc.scalar.activation(out=junk[:], in_=xo[pp][:], func=AF.Square, accum_out=st[pp][:, 0:1]), [B_xo[pp]], [B_junk, B_st[pp]])
                    R.add("act", lambda pp=pp: nc.scalar.activation(out=st[pp][:, 1:2], in_=st[pp][:, 0:1], func=AF.Sqrt, bias=eps_t[:, 0:1], scale=1.0 / D),
                          [B_st[pp], B_eps], [B_st[pp]], part=True)
                    R.add("dve", lambda pp=pp: nc.vector.reciprocal(out=st[pp][:, 2:3], in_=st[pp][:, 1:2]), [B_st[pp]], [B_st[pp]], part=True)
                    R.add("dve", lambda pp=pp: nc.vector.scalar_tensor_tensor(out=fo[pp][:], in0=xo[pp][:], scalar=st[pp][:, 2:3], in1=fg[:], op0=ALU.mult, op1=ALU.mult),
                          [B_xo[pp], B_st[pp], B_fg], [B_fo[pp]])
                    R.add("sp", lambda pp=pp, tt=tt: nc.sync.dma_start(out=out_d[tt * 128:(tt + 1) * 128, :], in_=fo[pp][:]), [B_fo[pp]], [B_out], dma=B_fo[pp], part=True)
        R.barrier()
        R.release([B_wos, B_gB, B_fg] + B_ct + B_xr + B_xo + B_fo)


    R.barrier()
    for li in range(nlayers):
        phase1(li)
        R.barrier()
        if phases >= 2:
            phaseA(li)
        last = (li == DEPTH - 1) and nlayers == DEPTH
        side = None
        if phases >= 4 and "skipC" not in taps:
            side = phaseC(li, last, "p1")
        if phases >= 3 and "skipB" not in taps:
            phaseB(li, last, side)
        else:
            R.side = list(side or [])
            R.flush_side()
            R.barrier()
        if phases >= 4 and "skipC" not in taps:
            phaseC(li, last, "p2")
        if phases >= 5:
            phaseD(li, last)
        if phases >= 6:
            phaseE(li, last)
    dbg_fence = []
    if "hT" in taps:
        hT_dbg = nc.dram_tensor("hT_dbg", [128, 8, T], BF16, kind="ExternalOutput")
        B_dbg = Buf("dbg")
        R.add("sp", lambda: nc.sync.dma_start(out=hT_dbg.ap(), in_=hT[:]), B_hT, [B_dbg], dma=True)
        dbg_fence.append(B_dbg)
    R.barrier()
    R.emit()
    return dram_in


def host_inputs(inp, b):
    f = np.float32
    fm = lambda v, nch: np.ascontiguousarray(np.asarray(v, f).reshape(nch, 128).T)
    m = {}
    m["x"] = np.ascontiguousarray(inp["x"][b], f)
    m["ctx"] = np.ascontiguousarray(inp["ctx"][b], f)
    m["cvec"] = np.ascontiguousarray(np.stack([fm(inp["c"][b], 8), fm(inp["c_ctx"], 8)], axis=-1))
    m["ada_w"] = np.ascontiguousarray(inp["ada_w"], f)
    m["ada_b_fm"] = np.stack([fm(inp["ada_b"][l], 24) for l in range(DEPTH)])
    m["ada_b_gate"] = np.ascontiguousarray(inp["ada_b"][:, 2048:3072], f)
    m["norm_g_fm"] = np.stack([fm(inp["norm_g"][l], 8) for l in range(DEPTH)])
    m["w_in"] = np.ascontiguousarray(inp["w_in"], f)
    m["ssd_conv_w_fm"] = np.ascontiguousarray(
        np.asarray(inp["ssd_conv_w"], f).reshape(DEPTH, 5, 10, 128).transpose(0, 3, 2, 1))
    m["ssd_conv_b_fm"] = np.stack([fm(inp["ssd_conv_b"][l], 10) for l in range(DEPTH)])
    m["ssd_dt_bias"] = np.ascontiguousarray(inp["ssd_dt_bias"], f)
    m["ssd_a_log"] = np.ascontiguousarray(inp["ssd_a_log"], f)
    m["ssd_d"] = np.ascontiguousarray(inp["ssd_d"], f)
    m["ssd_norm_g"] = np.ascontiguousarray(inp["ssd_norm_g"], f)
    qg = np.asarray(inp["q_norm_g"], f)
    kg = np.asarray(inp["k_norm_g"], f)
    m["qk_g_fm"] = np.ascontiguousarray(np.stack([np.tile(qg, (1, 2)), np.tile(kg, (1, 2))], axis=-1))
    m["cm_conv_w_fm"] = np.ascontiguousarray(
        np.asarray(inp["cm_conv_w"], f).reshape(DEPTH, 31, 4, 128).transpose(0, 3, 2, 1))
    m["cm_vec_fm"] = np.ascontiguousarray(np.stack(
        [np.stack([fm(inp[k][l], 4) for k in ("cm_conv_b", "cm_ln_g", "cm_ln_b", "cm_pw_b")], axis=-1)
         for l in range(DEPTH)]))
    m["cm_pw_w"] = np.ascontiguousarray(inp["cm_pw_w"], f)
    m["w_out"] = np.ascontiguousarray(inp["w_out"], f)
    m["final_norm_g"] = np.ascontiguousarray(inp["final_norm_g"], f)
    m.update(const_inputs())
    return m


_CONST = None


def const_inputs():
    global _CONST
    if _CONST is not None:
        return _CONST
    f = np.float32
    i = np.arange(128)
    ident = np.eye(128, dtype=f)
    U = (i[:, None] <= i[None, :]).astype(f)
    UT = (i[:, None] >= i[None, :]).astype(f)
    d = i % 64
    half = (d % 32) // 16
    partner = np.where(half == 0, i + 16, i - 16)
    perm = np.zeros((128, 128), f)
    perm[partner, i] = 1.0
    bd = ((i[:, None] // 64) == (i[None, :] // 64)).astype(f) / 64.0
    ones = np.ones((128, 128), f)
    consts = np.stack([ident, U, UT, perm, bd, ones], axis=1)
    mf = np.where(i[:, None] <= i[None, :], 0.0, NEG).astype(f)
    mb = np.where(i[:, None] >= i[None, :], 0.0, NEG).astype(f)
    masks = np.stack([mf, mb], axis=1)
    u = np.arange(SEQ)
    pos = np.stack([u // 64, u % 64], axis=0).astype(np.float64)
    inv = 10000.0 ** (-np.arange(16, dtype=np.float64) / 16)
    fq = d % 16
    axis = d // 32
    ang = pos[axis][:, :] * inv[fq][:, None]
    ang = (pos[axis].astype(f) * inv.astype(f)[fq][:, None]).astype(f)
    cos = np.cos(ang).astype(f)
    sin = np.sin(ang).astype(f)
    sgn = np.where(half == 0, -1.0, 1.0).astype(f)[:, None]
    COS = np.concatenate([cos, np.ones((128, NCTX), f)], axis=1)
    SIN = np.concatenate([sin * sgn, np.zeros((128, NCTX), f)], axis=1)
    rope = np.stack([COS, SIN], axis=1)
    _CONST = {"consts": np.ascontiguousarray(consts), "masks": np.ascontiguousarray(masks),
              "rope": np.ascontiguousarray(rope)}
    return _CONST


def kernel(**inputs):
    inputs = {k: np.asarray(v) for k, v in inputs.items()}
    nc = bass.Bass("TRN2", target_bir_lowering=False)
    build(nc)
    owners = (0, 1, 4, 5)
    maps = {c: host_inputs(inputs, b) for b, c in enumerate(owners)}
    zero = {k: np.zeros_like(v) for k, v in maps[0].items()}
    in_maps = [maps.get(c, zero) for c in range(8)]
    res = run_bass_kernel_spmd(nc, in_maps, core_ids=list(range(8)))
    return np.stack([res.results[c]["out"] for c in owners], axis=0).astype(np.float32)
```

```python
import numpy as np
import ml_dtypes
import concourse.bass as bass
import concourse.mybir as mybir
from concourse.bass_utils import run_bass_kernel_spmd

F32 = mybir.dt.float32
BF16 = mybir.dt.bfloat16
AF = mybir.ActivationFunctionType
ALU = mybir.AluOpType
AX = mybir.AxisListType

D = 1024
SEQ = 4096
NCTX = 256
T = SEQ + NCTX
DEPTH = 2
INW = 5656
EPS = 1e-6
O_Z, O_X, O_B, O_C, O_DT, O_Q, O_K, O_V, O_GA, O_UA, O_UB, O_GC = 0, 768, 1536, 1792, 2048, 2072, 2840, 3096, 3352, 4120, 4632, 5144
NEG = -30000.0


class Buf:
    __slots__ = ("name", "w", "r", "excl", "semi", "dcount")

    def __init__(self, name, excl=False):
        self.name = name
        self.w = {}
        self.r = {}
        self.excl = excl
        self.semi = None
        self.dcount = 0


class Op:
    __slots__ = ("eng", "idx", "fn", "deps", "dma", "sig", "semi", "dval")


def _merge(dst, src):
    for k, v in src.items():
        if dst.get(k, -1) < v:
            dst[k] = v


class Rec:
    ENG = ("pe", "act", "dve", "pool", "sp")

    def __init__(self, nc):
        self.nc = nc
        self.ops = {e: [] for e in self.ENG}
        self.ndma_sems = 0
        self.free = []
        self.side = []
        self.side_every = 0
        self.side_cnt = 0
        self.capture = None
        self.semcount = {}
        self.eobj = {"pe": nc.tensor, "act": nc.scalar, "dve": nc.vector, "pool": nc.gpsimd, "sp": nc.sync}

    def add(self, eng, fn, reads=(), writes=(), dma=None, part=False):
        if dma is True:
            dma = writes[0]
        if self.capture is not None:
            self.capture.append((eng, fn, tuple(reads), tuple(writes), dma, part))
            return None
        op = Op()
        op.eng, op.fn, op.dma, op.sig = eng, fn, dma is not None, False
        op.idx = len(self.ops[eng])
        raw, oth = {}, {}
        for b in reads:
            _merge(raw, b.w)
            if b.excl:
                _merge(oth, b.r)
        for b in writes:
            _merge(oth, b.r)
            if not part or b.excl:
                _merge(oth, b.w)
        if dma is None:
            oth.pop(("c", eng), None)
            if eng == "pe":
                raw.pop(("c", eng), None)
        deps = raw
        _merge(deps, oth)
        op.deps = deps
        if dma is not None:
            b = dma
            if b.semi is None:
                if self.free:
                    b.semi, b.dcount = self.free.pop()
                    _merge(deps, {("d", b.semi): b.dcount})
                else:
                    b.semi = self.ndma_sems
                    self.ndma_sems += 1
            b.dcount += 16
            self.semcount[b.semi] = b.dcount
            op.semi, op.dval = b.semi, b.dcount
            my = {("d", b.semi): b.dcount}
        else:
            my = {("c", eng): op.idx}
        for b in reads:
            _merge(b.r, my)
        for b in writes:
            if part:
                _merge(b.w, my)
            else:
                b.w = dict(my)
                b.r = {}
        self.ops[eng].append(op)
        if self.side and self.side_every:
            self.side_cnt += 1
            if self.side_cnt >= self.side_every:
                self.side_cnt = 0
                so = self.side.pop(0)
                ev, self.side_every = self.side_every, 0
                self.add(*so[:4], dma=so[4], part=so[5])
                self.side_every = ev
        return op

    def pipeline(self, bodies):
        lists = []
        for body in bodies:
            lst = []
            self.capture = lst
            body()
            self.capture = None
            lists.append((lst, 10 ** 9, 0))
        self.interleave(lists)

    def flush_side(self):
        self.side_every = 0
        while self.side:
            so = self.side.pop(0)
            self.add(*so[:4], dma=so[4], part=so[5])

    def interleave(self, lists):
        cur = [0] * len(lists)
        lo = 0
        n = len(lists)
        while lo < n:
            act = [i for i in (lo, lo + 1) if i < n]
            progressed = False
            for i in act:
                ops, need, done = lists[i]
                if cur[i] >= len(ops):
                    continue
                if i > lo and cur[i] >= need and cur[i - 1] < lists[i - 1][2]:
                    continue
                if i > lo and cur[i] == 0 and cur[lo] < len(lists[lo][0]) // 2:
                    continue
                self.add(*ops[cur[i]][:4], dma=ops[cur[i]][4], part=ops[cur[i]][5])
                cur[i] += 1
                progressed = True
            while lo < n and cur[lo] >= len(lists[lo][0]):
                lo += 1
            assert progressed or lo >= n

    def release(self, bufs):
        for b in bufs:
            if b.semi is not None:
                self.free.append((b.semi, b.dcount))
                b.semi = None

    def barrier(self):
        deps = {("c", e): len(self.ops[e]) - 1 for e in self.ENG if self.ops[e]}
        for s_, c_ in self.semcount.items():
            deps[("d", s_)] = c_
        for e in self.ENG:
            op = Op()
            op.eng, op.fn, op.dma, op.sig, op.idx = e, None, False, False, len(self.ops[e])
            op.deps = {k: v for k, v in deps.items() if k != ("c", e)}
            self.ops[e].append(op)

    def emit(self):
        nc = self.nc
        for e in self.ENG:
            for op in self.ops[e]:
                nd = {}
                for k, v in op.deps.items():
                    if k[0] == "c":
                        lst = self.ops[k[1]]
                        while v >= 0 and lst[v].fn is None:
                            v -= 1
                        if v < 0:
                            continue
                        lst[v].sig = True
                    nd[k] = v
                op.deps = nd
        pref = {}
        for e in self.ENG:
            c = 0
            arr = []
            for op in self.ops[e]:
                if op.sig and not op.dma:
                    c += 1
                arr.append(c)
            pref[e] = arr
        assert self.ndma_sems + 5 <= 98, self.ndma_sems
        esem = {e: nc.alloc_semaphore(name="es_" + e) for e in self.ENG}
        dsem = [nc.alloc_semaphore(name="ds_%d" % i) for i in range(self.ndma_sems)]
        for e in self.ENG:
            eo = self.eobj[e]
            known = {}
            for op in self.ops[e]:
                for k, v in op.deps.items():
                    if k[0] == "c":
                        sem, val, key = esem[k[1]], pref[k[1]][v], k
                    else:
                        sem, val, key = dsem[k[1]], v, k
                    if known.get(key, 0) >= val:
                        continue
                    known[key] = val
                    eo.wait_ge(sem, val)
                if op.fn is None:
                    continue
                ins = op.fn()
                if op.dma:
                    ins.then_inc(dsem[op.semi], 16)
                elif op.sig:
                    ins.then_inc(esem[e], 1)


class SB:
    ARENA = None
    ABYTES = 204800

    def __init__(self, nc, base=0, limit=None):
        if SB.ARENA is None or SB.ARENA[0] is not nc:
            SB.ARENA = (nc, nc.alloc_sbuf_tensor("arena", [128, SB.ABYTES // 4], F32))
        self.nc, self.off, self.limit = nc, base, (limit or SB.ABYTES)

    def alloc(self, name, shape, dt):
        return self.at(name, shape, dt, None)

    def at(self, name, shape, dt, off):
        esz = 4 if dt == F32 else 2
        n = int(np.prod(shape[1:]))
        nb = (n * esz + 63) // 64 * 64
        if off is None:
            off = self.off
            self.off += nb
        assert off % 4 == 0 and off + nb <= self.limit, (name, off, nb, self.limit)
        A = SB.ARENA[1]
        ap = A[0:shape[0], off // 4: off // 4 + nb // 4]
        if dt != F32:
            ap = ap.bitcast(dt)
        ap = ap[:, 0:n]
        if len(shape) > 2:
            names = " ".join("d%d" % i for i in range(len(shape) - 1))
            kw = {"d%d" % i: int(shape[i + 1]) for i in range(len(shape) - 1)}
            ap = ap.rearrange("p (%s) -> p %s" % (names, names), **kw)
        return ap


def fap(ap, dims):
    return bass.AP(ap.tensor, ap.offset, [list(ap.ap[0])] + [list(d) for d in dims])


def build(nc, phases=99, nlayers=DEPTH, taps=()):
    R = Rec(nc)
    dram_in = {}

    def din(name, shape, dt=F32):
        dram_in[name] = nc.dram_tensor(name, list(shape), dt, kind="ExternalInput")
        return dram_in[name]

    x_in = din("x", [SEQ, D])
    ctx_in = din("ctx", [NCTX, D])
    cvec = din("cvec", [128, 8, 2])
    ada_w = din("ada_w", [DEPTH, D, 3 * D])
    ada_b_fm = din("ada_b_fm", [DEPTH, 128, 24])
    ada_b_gate = din("ada_b_gate", [DEPTH, D])
    norm_g_fm = din("norm_g_fm", [DEPTH, 128, 8])
    w_in = din("w_in", [DEPTH, D, INW])
    scw = din("ssd_conv_w_fm", [DEPTH, 128, 10, 5])
    scb = din("ssd_conv_b_fm", [DEPTH, 128, 10])
    dtb = din("ssd_dt_bias", [DEPTH, 24])
    alog = din("ssd_a_log", [DEPTH, 24])
    ssdd = din("ssd_d", [DEPTH, 12])
    ssdng = din("ssd_norm_g", [DEPTH, 768])
    qkg = din("qk_g_fm", [DEPTH, 128, 2])
    ccw = din("cm_conv_w_fm", [DEPTH, 128, 4, 31])
    cmv = din("cm_vec_fm", [DEPTH, 128, 4, 4])
    cpw = din("cm_pw_w", [DEPTH, 512, 512])
    w_out = din("w_out", [DEPTH, 2048, D])
    fng = din("final_norm_g", [D])
    consts = din("consts", [128, 6, 128])
    masks = din("masks", [128, 2, 128])
    rope = din("rope", [128, 2, T])
    out_d = nc.dram_tensor("out", [SEQ, D], F32, kind="ExternalOutput")

    def scratch(name, shape, dt):
        kind = "ExternalOutput" if name in taps else "Internal"
        return nc.dram_tensor(name, list(shape), dt, kind=kind)

    gate_d = scratch("gate_d", [DEPTH, 2, 128, D], F32)
    x1_d = scratch("x1_d", [T, D], F32)
    xbc_d = scratch("xbc_d", [1280, T], F32)
    dt_d = scratch("dt_d", [T, 24], F32)
    zs_d = scratch("zs_d", [T, 768], F32)
    qT_d = scratch("qT_d", [768, T], BF16)
    kT_d = scratch("kT_d", [256, T], BF16)
    v_d = scratch("v_d", [T, 256], BF16)
    gas_d = scratch("gas_d", [768, T], F32)
    cv_d = scratch("cv_d", [512, T], F32)
    gcs_d = scratch("gcs_d", [512, T], F32)
    cat_d = scratch("cat_d", [2048, T], BF16)
    hb_d = scratch("hb_d", [34, 128, 768], BF16)
    B_gate = Buf("gate_d"); B_x1 = Buf("x1_d"); B_xbc = Buf("xbc_d"); B_dt = Buf("dt_d"); B_zs = Buf("zs_d")
    B_qT = Buf("qT_d"); B_kT = Buf("kT_d"); B_v = Buf("v_d"); B_gas = Buf("gas_d"); B_cv = Buf("cv_d")
    B_gcs = Buf("gcs_d"); B_cat = Buf("cat_d"); B_hb = Buf("hb_d"); B_out = Buf("out")
    B_in = Buf("inputs")

    PS = [nc.alloc_psum_tensor("ps%d" % i, [128, 512], F32) for i in range(8)]
    PB = [Buf("psb%d" % i, excl=True) for i in range(8)]

    sbp = SB(nc, 0, 24 * 1024)
    c_t = sbp.alloc("consts", [128, 6, 128], F32)
    cb_t = sbp.alloc("constsb", [128, 6, 128], BF16)
    mk_t = sbp.alloc("masks", [128, 2, 128], F32)
    eps_t = sbp.alloc("eps", [128, 2], F32)
    AB_t = sbp.alloc("AB", [128, DEPTH, 2, 8, 2], F32)
    B_c = Buf("consts"); B_AB = Buf("AB")
    IDENT, UIN, UTR, PERM, BDM, ONES = range(6)
    R.add("sp", lambda: nc.sync.dma_start(out=c_t[:], in_=consts.ap()), [B_in], [B_c], dma=True)
    B_mk = Buf("mk")
    R.add("sp", lambda: nc.sync.dma_start(out=mk_t[:], in_=masks.ap()), [B_in], [B_mk], dma=True)
    B_cb = Buf("cb")
    R.add("dve", lambda: nc.vector.tensor_copy(out=cb_t[:], in_=c_t[:]), [B_c], [B_cb])
    B_eps = Buf("eps")
    R.add("pool", lambda: nc.gpsimd.memset(eps_t[:, 0:1], EPS), [], [B_eps])
    R.add("pool", lambda: nc.gpsimd.memset(eps_t[:, 1:2], 1.0), [], [B_eps], part=True)

    WORK0 = 24 * 1024

    def phase0():
        sb = SB(nc, WORK0)
        aw = sb.alloc("aw", [128, 8, 3072], F32)
        B_aw = [Buf("aw%d" % k) for k in range(8)]
        cv = sb.alloc("cv", [128, 8, 2], F32)
        sc = sb.alloc("sc", [128, 8, 2], F32)
        screp = sb.alloc("screp", [128, 8, 2, 128], F32)
        abf = sb.alloc("abf", [128, 24], F32)
        ngf = sb.alloc("ngf", [128, 8], F32)
        abg = sb.alloc("abg", [128, D], F32)
        mod = sb.alloc("mod", [128, 16, 2], F32)
        gt = sb.alloc("gt", [128, 2, D], F32)
        B_cv, B_sc, B_screp, B_abf, B_ngf, B_abg, B_mod, B_gt = [Buf(n) for n in "cv sc screp abf ngf abg mod gt".split()]
        R.add("sp", lambda: nc.sync.dma_start(out=cv[:], in_=cvec.ap()), [B_in], [B_cv], dma=True)
        R.add("act", lambda: nc.scalar.activation(out=sc[:], in_=cv[:], func=AF.Silu), [B_cv], [B_sc])
        for kc in range(8):
            for j in range(2):
                R.add("dve", lambda kc=kc, j=j: nc.vector.tensor_copy(
                    out=screp[:, kc, j, :], in_=fap(sc[:, kc, j:j + 1], [[0, 128]])), [B_sc], [B_screp], part=True)
        for li in range(nlayers):
            for kc in range(8):
                R.add("sp", lambda kc=kc, li=li: nc.sync.dma_start(
                    out=aw[:, kc, :], in_=ada_w[li, kc * 128:(kc + 1) * 128, :]), [B_in], [B_aw[kc]], dma=True)
            R.add("sp", lambda li=li: nc.sync.dma_start(out=abf[:], in_=ada_b_fm[li]), [B_in], [B_abf], dma=True)
            R.add("sp", lambda li=li: nc.sync.dma_start(out=ngf[:], in_=norm_g_fm[li]), [B_in], [B_ngf], dma=True)
            R.add("sp", lambda li=li: nc.sync.dma_start(
                out=abg[:], in_=bass.AP(ada_b_gate.ap().tensor, li * D, [[0, 128], [1, D]])), [B_in], [B_abg], dma=True)
            for fc in range(16):
                for kc in range(8):
                    R.add("pe", lambda fc=fc, kc=kc: nc.tensor.matmul(
                        PS[0][:, fc * 2:fc * 2 + 2], lhsT=aw[:, kc, fc * 128:(fc + 1) * 128], rhs=sc[:, kc, :],
                        start=(kc == 0), stop=(kc == 7)), [B_aw[kc], B_sc], [PB[0]], part=not (fc == 0 and kc == 0))
            R.add("dve", lambda: nc.vector.tensor_tensor(
                out=mod[:], in0=fap(PS[0][:, 0:32], [[2, 16], [1, 2]]), in1=fap(abf[:, 0:16], [[1, 16], [0, 2]]),
                op=ALU.add), [PB[0], B_abf], [B_mod])
            R.add("dve", lambda li=li: nc.vector.scalar_tensor_tensor(
                out=AB_t[:, li, 0, :, :], in0=mod[:, 8:16, :], scalar=1.0, in1=fap(ngf[:, 0:8], [[1, 8], [0, 2]]),
                op0=ALU.add, op1=ALU.mult), [B_mod, B_ngf], [B_AB], part=True)
            R.add("dve", lambda li=li: nc.vector.tensor_copy(out=AB_t[:, li, 1, :, :], in_=mod[:, 0:8, :]),
                  [B_mod], [B_AB], part=True)
            for j in range(2):
                for cc in range(2):
                    pb = 1 + (j * 2 + cc) % 2
                    for kc in range(8):
                        R.add("pe", lambda j=j, cc=cc, kc=kc, pb=pb: nc.tensor.matmul(
                            PS[pb][:, :], lhsT=screp[:, kc, j, :], rhs=aw[:, kc, 2048 + cc * 512:2048 + (cc + 1) * 512],
                            start=(kc == 0), stop=(kc == 7)), [B_screp, B_aw[kc]], [PB[pb]], part=(kc > 0))
                    R.add("dve", lambda j=j, cc=cc, pb=pb: nc.vector.tensor_tensor(
                        out=gt[:, j, cc * 512:(cc + 1) * 512], in0=PS[pb][:, :], in1=abg[:, cc * 512:(cc + 1) * 512],
                        op=ALU.add), [PB[pb], B_abg], [B_gt], part=not (j == 0 and cc == 0))
            for j in range(2):
                R.add("sp", lambda li=li, j=j: nc.sync.dma_start(out=gate_d[li, j], in_=gt[:, j, :]),
                      [B_gt], [B_gate], dma=B_gt, part=True)

    phase0()
    if phases <= 0:
        R.emit()
        return dram_in

    HT_OFF = WORK0
    hT = SB(nc).at("hT", [128, 8, T], BF16, HT_OFF)
    HT_BYTES = 8 * T * 2
    B_hT = [Buf("hT%d" % i) for i in range(34)]
    WORK1 = HT_OFF + HT_BYTES

    def phase1(li):
        sb = SB(nc, WORK1)
        xt = [sb.alloc("xt", [128, D], F32) for _ in range(2)]
        xn = [sb.alloc("xn", [128, D], F32) for _ in range(2)]
        junk = sb.alloc("junk", [128, D], BF16)
        st = [sb.alloc("st", [128, 4], F32) for _ in range(2)]
        B_xt = [Buf("xt%d" % i) for i in range(2)]
        B_xn = [Buf("xn%d" % i) for i in range(2)]
        B_junk = Buf("junk")
        B_st = [Buf("st%d" % i) for i in range(2)]
        bodies = []
        for tt in range(34):
            def body(tt=tt):
                s = tt % 2
                j = 0 if tt < 32 else 1
                if li == 0:
                    src = x_in[tt * 128:(tt + 1) * 128, :] if tt < 32 else ctx_in[(tt - 32) * 128:(tt - 31) * 128, :]
                    srcb = B_in
                else:
                    src = x1_d[tt * 128:(tt + 1) * 128, :]
                    srcb = B_x1
                R.add("sp", lambda s=s, src=src: nc.sync.dma_start(out=xt[s][:], in_=src), [srcb], [B_xt[s]], dma=True)
                R.add("act", lambda s=s: nc.scalar.activation(out=junk[:], in_=xt[s][:], func=AF.Square,
                                                              accum_out=st[s][:, 0:1]), [B_xt[s]], [B_junk, B_st[s]])
                R.add("act", lambda s=s: nc.scalar.activation(out=st[s][:, 1:2], in_=st[s][:, 0:1], func=AF.Sqrt,
                                                              bias=eps_t[:, 0:1], scale=1.0 / D), [B_st[s], B_eps], [B_st[s]], part=True)
                R.add("dve", lambda s=s: nc.vector.reciprocal(out=st[s][:, 2:3], in_=st[s][:, 1:2]), [B_st[s]], [B_st[s]], part=True)
                R.add("dve", lambda s=s: nc.vector.tensor_scalar(out=xn[s][:], in0=xt[s][:], scalar1=st[s][:, 2:3], scalar2=None,
                                                                 op0=ALU.mult), [B_xt[s], B_st[s]], [B_xn[s]])
                pb0 = 2 * (tt % 2)
                for c in range(8):
                    pb = pb0 + c // 4
                    R.add("pe", lambda s=s, c=c, pb=pb: nc.tensor.transpose(
                        PS[pb][:, (c % 4) * 128:(c % 4 + 1) * 128], xn[s][:, c * 128:(c + 1) * 128], c_t[:, IDENT, :]),
                        [B_xn[s], B_c], [PB[pb]], part=(c % 4 > 0))
                for c in range(8):
                    pb = pb0 + c // 4
                    if c % 2 == 0:
                        R.add("act", lambda c=c, pb=pb, tt=tt, j=j: nc.scalar.activation(
                            out=hT[:, c, tt * 128:(tt + 1) * 128], in_=PS[pb][:, (c % 4) * 128:(c % 4 + 1) * 128],
                            func=AF.Identity, bias=AB_t[:, li, 1, c, j:j + 1], scale=AB_t[:, li, 0, c, j:j + 1]),
                            [PB[pb], B_AB], [B_hT[tt]], part=True)
                    else:
                        R.add("dve", lambda c=c, pb=pb, tt=tt, j=j: nc.vector.tensor_scalar(
                            out=hT[:, c, tt * 128:(tt + 1) * 128], in0=PS[pb][:, (c % 4) * 128:(c % 4 + 1) * 128],
                            scalar1=AB_t[:, li, 0, c, j:j + 1], scalar2=AB_t[:, li, 1, c, j:j + 1],
                            op0=ALU.mult, op1=ALU.add), [PB[pb], B_AB], [B_hT[tt]], part=True)
            bodies.append(body)
        R.pipeline(bodies)

    def wsrc(li, c0, ncol):
        return w_in[li, :, c0:c0 + ncol].rearrange("(kc p) n -> p kc n", p=128)

    def phaseA(li):
        sb = SB(nc, WORK1)
        wst = [sb.alloc("wst", [128, 8, 128], F32) for _ in range(2)]
        wbf = [sb.alloc("wbf", [128, 8, 128], BF16) for _ in range(2)]
        B_wst = [Buf("wst%d" % i) for i in range(2)]
        B_wbf = [Buf("wbf%d" % i) for i in range(2)]
        ot = [sb.alloc("ot", [128, 512], F32) for _ in range(3)]
        B_ot = [Buf("ot%d" % i) for i in range(3)]
        obt = [sb.alloc("obt", [128, 512], BF16) for _ in range(2)]
        B_obt = [Buf("obt%d" % i) for i in range(2)]
        vecs = sb.alloc("vecs", [128, 10 * 5 + 10 + 2 + 4 * 31 + 16], F32)
        B_vecs = Buf("vecs")
        V_SCW, V_SCB, V_QKG, V_CCW, V_CMV = 0, 50, 60, 62, 62 + 124
        R.add("sp", lambda: nc.sync.dma_start(out=vecs[:, V_SCW:V_SCW + 50], in_=scw[li].rearrange("p a b -> p (a b)")), [B_in], [B_vecs], dma=True)
        R.add("sp", lambda: nc.sync.dma_start(out=vecs[:, V_SCB:V_SCB + 10], in_=scb[li]), [B_in], [B_vecs], dma=True, part=True)
        R.add("sp", lambda: nc.sync.dma_start(out=vecs[:, V_QKG:V_QKG + 2], in_=qkg[li]), [B_in], [B_vecs], dma=True, part=True)
        R.add("sp", lambda: nc.sync.dma_start(out=vecs[:, V_CCW:V_CCW + 124], in_=ccw[li].rearrange("p a b -> p (a b)")), [B_in], [B_vecs], dma=True, part=True)
        R.add("sp", lambda: nc.sync.dma_start(out=vecs[:, V_CMV:V_CMV + 16], in_=cmv[li].rearrange("p a b -> p (a b)")), [B_in], [B_vecs], dma=True, part=True)
        sub0 = sb.off
        cnt = {"w": 0, "ot": 0, "obt": 0, "ps": 0}

        def wload(c0):
            s_ = cnt["w"] % 2
            cnt["w"] += 1
            R.add("sp", lambda: nc.sync.dma_start(out=wst[s_][:], in_=wsrc(li, c0, 128)), [B_in], [B_wst[s_]], dma=True)
            R.add("pool", lambda: nc.gpsimd.tensor_copy(out=wbf[s_][:], in_=wst[s_][:]), [B_wst[s_]], [B_wbf[s_]])
            return s_

        def proj(ws, j, pb):
            w_ = 512 if j < 8 else 256
            for kc in range(8):
                R.add("pe", lambda kc=kc: nc.tensor.matmul(PS[pb][:, 0:w_], lhsT=wbf[ws][:, kc, :], rhs=hT[:, kc, j * 512:j * 512 + w_],
                                                          start=(kc == 0), stop=(kc == 7)),
                      [B_wbf[ws]] + B_hT[j * 4:j * 4 + w_ // 128], [PB[pb]], part=(kc > 0))
            return w_

        def next_ot():
            s_ = cnt["ot"] % 3
            cnt["ot"] += 1
            return s_

        def next_obt():
            s_ = cnt["obt"] % 2
            cnt["obt"] += 1
            return s_

        for (c0, nch, dst, B_dst) in (((O_GA, 6, gas_d, B_gas), (O_GC, 4, gcs_d, B_gcs)) if "noA1" not in taps else ()):
            for ch in range(nch):
                ws = wload(c0 + ch * 128)
                for j in range(9):
                    pb = cnt["ps"] % 2
                    cnt["ps"] += 1
                    w_ = proj(ws, j, pb)
                    o_ = next_ot()
                    R.add("act", lambda pb=pb, o_=o_, w_=w_: nc.scalar.activation(out=ot[o_][:, 0:w_], in_=PS[pb][:, 0:w_], func=AF.Silu),
                          [PB[pb]], [B_ot[o_]])
                    R.add("sp", lambda o_=o_, w_=w_, ch=ch, j=j, dst=dst: nc.sync.dma_start(
                        out=dst[ch * 128:(ch + 1) * 128, j * 512:j * 512 + w_], in_=ot[o_][:, 0:w_]), [B_ot[o_]], [B_dst], dma=B_ot[o_], part=True)

        sbq = SB(nc, sub0)
        ropet = sbq.alloc("rope", [128, 2, T], F32)
        B_rope = Buf("rope")
        R.add("sp", lambda: nc.sync.dma_start(out=ropet[:, 0, :], in_=rope[:, 0, :]), [B_in], [B_rope], dma=True)
        R.add("sp", lambda: nc.sync.dma_start(out=ropet[:, 1, :], in_=rope[:, 1, :]), [B_in], [B_rope], dma=True, part=True)
        P2 = range(2)
        sqb = [sbq.alloc("sqb", [128, 512], BF16) for _ in P2]; B_sqb = [Buf("sqb%d" % i) for i in P2]
        sd = [sbq.alloc("sd", [128, 512], F32) for _ in P2]; B_sd = [Buf("sd%d" % i) for i in P2]
        rs = [sbq.alloc("rs", [128, 512], F32) for _ in P2]; B_rs = [Buf("rs%d" % i) for i in P2]
        qnb = [sbq.alloc("qnb", [128, 512], BF16) for _ in P2]; B_qnb = [Buf("qnb%d" % i) for i in P2]
        t1 = [sbq.alloc("t1", [128, 512], F32) for _ in P2]; B_t1 = [Buf("t1%d" % i) for i in P2]
        t2 = [sbq.alloc("t2", [128, 512], F32) for _ in P2]; B_t2 = [Buf("t2%d" % i) for i in P2]
        lists = []
        tile_no = 0
        qk_chunks = []
        for (c0, nch, dst, B_dst, gi) in (((O_Q, 6, qT_d, B_qT, 0), (O_K, 2, kT_d, B_kT, 1)) if "noA2" not in taps else ()):
            for ch in range(nch):
                qk_chunks.append((c0 + ch * 128, ch, dst, B_dst, gi))
        ws_of = {}
        for kq, (wc0, ch, dst, B_dst, gi) in enumerate(qk_chunks):
            if True:
                for j in range(9):
                    pp = tile_no % 2
                    tile_no += 1
                    ws = None

                    def body(kq=kq, j=j, pp=pp, ch=ch, dst=dst, B_dst=B_dst, gi=gi):
                        P0, P1, P2_ = 3 * pp, 3 * pp + 1, 3 * pp + 2
                        if j == 0 and kq == 0:
                            ws_of[0] = wload(qk_chunks[0][0])
                        if j == 1 and kq + 1 < len(qk_chunks):
                            ws_of[kq + 1] = wload(qk_chunks[kq + 1][0])
                        ws = ws_of[kq]
                        w_ = proj(ws, j, P0)
                        cs = slice(j * 512, j * 512 + w_)
                        R.add("act", lambda: nc.scalar.activation(out=sqb[pp][:, 0:w_], in_=PS[P0][:, 0:w_], func=AF.Square), [PB[P0]], [B_sqb[pp]])
                        R.add("pe", lambda: nc.tensor.matmul(PS[P1][:, 0:w_], lhsT=cb_t[:, BDM, :], rhs=sqb[pp][:, 0:w_], start=True, stop=True),
                              [B_cb, B_sqb[pp]], [PB[P1]])
                        R.add("act", lambda: nc.scalar.activation(out=sd[pp][:, 0:w_], in_=PS[P1][:, 0:w_], func=AF.Ln, bias=eps_t[:, 0:1], scale=1.0),
                              [PB[P1], B_eps], [B_sd[pp]])
                        R.add("act", lambda: nc.scalar.activation(out=rs[pp][:, 0:w_], in_=sd[pp][:, 0:w_], func=AF.Exp, scale=-0.5), [B_sd[pp]], [B_rs[pp]])
                        R.add("dve", lambda: nc.vector.scalar_tensor_tensor(
                            out=qnb[pp][:, 0:w_], in0=PS[P0][:, 0:w_], scalar=vecs[:, V_QKG + gi:V_QKG + gi + 1], in1=rs[pp][:, 0:w_], op0=ALU.mult, op1=ALU.mult),
                            [PB[P0], B_vecs, B_rs[pp]], [B_qnb[pp]])
                        R.add("pe", lambda: nc.tensor.matmul(PS[P2_][:, 0:w_], lhsT=cb_t[:, PERM, :], rhs=qnb[pp][:, 0:w_], start=True, stop=True),
                              [B_cb, B_qnb[pp]], [PB[P2_]])
                        R.add("pool", lambda: nc.gpsimd.tensor_tensor(out=t1[pp][:, 0:w_], in0=qnb[pp][:, 0:w_], in1=ropet[:, 0, cs], op=ALU.mult),
                              [B_qnb[pp], B_rope], [B_t1[pp]])
                        R.add("dve", lambda: nc.vector.tensor_tensor(out=t2[pp][:, 0:w_], in0=PS[P2_][:, 0:w_], in1=ropet[:, 1, cs], op=ALU.mult),
                              [PB[P2_], B_rope], [B_t2[pp]])
                        R.add("dve", lambda: nc.vector.tensor_tensor(out=obt[pp][:, 0:w_], in0=t1[pp][:, 0:w_], in1=t2[pp][:, 0:w_], op=ALU.add),
                              [B_t1[pp], B_t2[pp]], [B_obt[pp]])
                        R.add("sp", lambda: nc.sync.dma_start(out=dst[ch * 128:(ch + 1) * 128, cs], in_=obt[pp][:, 0:w_]), [B_obt[pp]], [B_dst], dma=B_obt[pp], part=True)
                    lst = []
                    R.capture = lst
                    body()
                    R.capture = None
                    lists.append((lst, 10 ** 9, 0))
        R.interleave(lists)
        R.barrier()

        sbx = SB(nc, sub0)
        dg5 = sbx.alloc("dg5", [128, 10, 5, 128], BF16); B_dg5 = Buf("dg5")
        RBW = 2 + SEQ + 2 + 2 + NCTX + 2
        rb = [sbx.alloc("rb", [128, RBW], BF16) for _ in range(2)]
        B_rb = [Buf("rb%d" % i) for i in range(2)]
        for ch in range(10):
            for k in range(5):
                R.add("dve", lambda ch=ch, k=k: nc.vector.tensor_scalar(
                    out=dg5[:, ch, k, :], in0=c_t[:, IDENT, :], scalar1=vecs[:, V_SCW + ch * 5 + k:V_SCW + ch * 5 + k + 1], scalar2=None, op0=ALU.mult),
                    [B_c, B_vecs], [B_dg5], part=True)
        for i in range(2):
            R.add("pool", lambda i=i: nc.gpsimd.memset(rb[i][:], 0.0), [], [B_rb[i]])

        def rbcol(j, pad):
            return pad + j * 512 if j < 8 else pad + SEQ + 2 * pad

        def xproj(ch):
            ws = wload(O_X + ch * 128)
            r_ = ch % 2
            for j in range(9):
                pb = cnt["ps"] % 2
                cnt["ps"] += 1
                w_ = proj(ws, j, pb)
                c0_ = rbcol(j, 2)
                R.add("act", lambda pb=pb, w_=w_, c0_=c0_: nc.scalar.activation(out=rb[r_][:, c0_:c0_ + w_], in_=PS[pb][:, 0:w_], func=AF.Copy),
                      [PB[pb]], [B_rb[r_]], part=(j > 0))

        def xconv(ch):
            r_ = ch % 2
            for j in range(9):
                w_ = 512 if j < 8 else 256
                pb = 2 + j % 2
                st_ = rbcol(j, 2) - 2
                for k in range(5):
                    R.add("pe", lambda k=k, pb=pb, w_=w_, st_=st_: nc.tensor.matmul(
                        PS[pb][:, 0:w_], lhsT=dg5[:, ch, k, :], rhs=rb[r_][:, st_ + k:st_ + k + w_], start=(k == 0), stop=(k == 4)),
                        [B_dg5, B_rb[r_]], [PB[pb]], part=(k > 0))
                o_ = next_ot()
                R.add("act", lambda pb=pb, o_=o_, w_=w_: nc.scalar.activation(
                    out=ot[o_][:, 0:w_], in_=PS[pb][:, 0:w_], func=AF.Silu, bias=vecs[:, V_SCB + ch:V_SCB + ch + 1], scale=1.0),
                    [PB[pb], B_vecs], [B_ot[o_]])
                R.add("sp", lambda o_=o_, w_=w_, j=j: nc.sync.dma_start(
                    out=xbc_d[ch * 128:(ch + 1) * 128, j * 512:j * 512 + w_], in_=ot[o_][:, 0:w_]), [B_ot[o_]], [B_xbc], dma=B_ot[o_], part=True)

        if "noA3" not in taps:
            xproj(0)
        for ch in (range(10) if "noA3" not in taps else ()):
            if ch + 1 < 10:
                xproj(ch + 1)
            xconv(ch)
        R.barrier()

        sbc = SB(nc, sub0)
        dg31 = sbc.alloc("dg31", [128, 4, 31, 128], BF16); B_dg31 = Buf("dg31")
        RUW = 15 + SEQ + 15 + 15 + NCTX + 15
        ru = [sbc.alloc("ru", [128, RUW], BF16) for _ in range(2)]
        B_ru = [Buf("ru%d" % i) for i in range(2)]
        sg = [sbc.alloc("sg", [128, 512], F32) for _ in range(2)]
        B_sg = [Buf("sg%d" % i) for i in range(2)]
        for ch in range(4):
            for k in range(31):
                eng, eo = ("dve", nc.vector) if k % 2 == 0 else ("pool", nc.gpsimd)
                R.add(eng, lambda ch=ch, k=k, eo=eo: eo.tensor_scalar(
                    out=dg31[:, ch, k, :], in0=c_t[:, IDENT, :], scalar1=vecs[:, V_CCW + ch * 31 + k:V_CCW + ch * 31 + k + 1], scalar2=None, op0=ALU.mult),
                    [B_c, B_vecs], [B_dg31], part=True)
        for i in range(2):
            R.add("pool", lambda i=i: nc.gpsimd.memset(ru[i][:], 0.0), [], [B_ru[i]])

        def uproj(ch):
            wa = wload(O_UA + ch * 128)
            wb = wload(O_UB + ch * 128)
            r_ = ch % 2
            for j in range(9):
                w_ = proj(wa, j, 0)
                proj(wb, j, 1)
                s_ = j % 2
                R.add("act", lambda s_=s_, w_=w_: nc.scalar.activation(out=sg[s_][:, 0:w_], in_=PS[1][:, 0:w_], func=AF.Sigmoid), [PB[1]], [B_sg[s_]])
                c0_ = rbcol(j, 15)
                R.add("dve", lambda s_=s_, w_=w_, c0_=c0_: nc.vector.tensor_tensor(
                    out=ru[r_][:, c0_:c0_ + w_], in0=PS[0][:, 0:w_], in1=sg[s_][:, 0:w_], op=ALU.mult), [PB[0], B_sg[s_]], [B_ru[r_]], part=(j > 0))

        def uconv(ch):
            r_ = ch % 2
            for j in range(9):
                w_ = 512 if j < 8 else 256
                pb = 2 + j % 2
                st_ = rbcol(j, 15) - 15
                for k in range(31):
                    R.add("pe", lambda k=k, pb=pb, w_=w_, st_=st_: nc.tensor.matmul(
                        PS[pb][:, 0:w_], lhsT=dg31[:, ch, k, :], rhs=ru[r_][:, st_ + k:st_ + k + w_], start=(k == 0), stop=(k == 30)),
                        [B_dg31, B_ru[r_]], [PB[pb]], part=(k > 0))
                o_ = next_ot()
                R.add("act", lambda pb=pb, o_=o_, w_=w_: nc.scalar.activation(
                    out=ot[o_][:, 0:w_], in_=PS[pb][:, 0:w_], func=AF.Identity, bias=vecs[:, V_CMV + ch * 4:V_CMV + ch * 4 + 1], scale=1.0),
                    [PB[pb], B_vecs], [B_ot[o_]])
                R.add("sp", lambda o_=o_, w_=w_, j=j: nc.sync.dma_start(
                    out=cv_d[ch * 128:(ch + 1) * 128, j * 512:j * 512 + w_], in_=ot[o_][:, 0:w_]), [B_ot[o_]], [B_cv], dma=B_ot[o_], part=True)

        if "noA4" not in taps:
            uproj(0)
        for ch in (range(4) if "noA4" not in taps else ()):
            if ch + 1 < 4:
                uproj(ch + 1)
            uconv(ch)
        R.barrier()

        sbt = SB(nc, sub0)
        NZ = 768 + 24 + 256
        wzs = sbt.alloc("wzs", [128, 8, 384], F32); B_wzs = Buf("wzs")
        wz = sbt.alloc("wz", [128, 8, NZ], BF16); B_wz = Buf("wz")
        for (dc, c0, n_) in ((0, O_Z, 384), (384, O_Z + 384, 384), (768, O_DT, 24), (792, O_V, 256)):
            R.add("sp", lambda c0=c0, n_=n_: nc.sync.dma_start(out=wzs[:, :, 0:n_], in_=wsrc(li, c0, n_)), [B_in], [B_wzs], dma=True)
            R.add("pool", lambda dc=dc, n_=n_: nc.gpsimd.tensor_copy(out=wz[:, :, dc:dc + n_], in_=wzs[:, :, 0:n_]), [B_wzs], [B_wz], part=(dc > 0))
        zt = [sbt.alloc("zt", [128, 768], F32) for _ in range(2)]
        B_zt = [Buf("zt%d" % i) for i in range(2)]
        vt = [sbt.alloc("vt", [128, 256], BF16) for _ in range(2)]
        B_vt = [Buf("vt%d" % i) for i in range(2)]
        dta = sbt.alloc("dta", [128, 34, 24], F32); B_dta = Buf("dta")
        dtw = sbt.alloc("dtw", [128, 3, 34 * 24], F32); B_dtw = Buf("dtw")
        dtbt = sbt.alloc("dtbt", [128, 24], F32); B_dtbt = Buf("dtbt")
        R.add("sp", lambda: nc.sync.dma_start(out=dtbt[:], in_=bass.AP(dtb.ap().tensor, li * 24, [[0, 128], [1, 24]])), [B_in], [B_dtbt], dma=True)
        for tt in (range(34) if "noA5" not in taps else ()):
            s_ = tt % 2
            pbs = (0, 1, 2) if tt % 2 == 0 else (3, 4, 5)
            for (pb, dc, n_) in ((pbs[0], 0, 384), (pbs[1], 384, 384), (pbs[2], 768, 280)):
                for kc in range(8):
                    R.add("pe", lambda kc=kc, pb=pb, dc=dc, n_=n_, tt=tt: nc.tensor.matmul(
                        PS[pb][:, 0:n_], lhsT=hT[:, kc, tt * 128:(tt + 1) * 128], rhs=wz[:, kc, dc:dc + n_], start=(kc == 0), stop=(kc == 7)),
                        [B_hT[tt], B_wz], [PB[pb]], part=(kc > 0))
            for hf_ in range(2):
                R.add("act", lambda hf_=hf_, s_=s_, pbs=pbs: nc.scalar.activation(
                    out=zt[s_][:, hf_ * 384:(hf_ + 1) * 384], in_=PS[pbs[hf_]][:, 0:384], func=AF.Silu), [PB[pbs[hf_]]], [B_zt[s_]], part=(hf_ > 0))
            R.add("sp", lambda s_=s_, tt=tt: nc.sync.dma_start(out=zs_d[tt * 128:(tt + 1) * 128, :], in_=zt[s_][:]), [B_zt[s_]], [B_zs], dma=B_zt[s_], part=True)
            R.add("dve", lambda s_=s_, pbs=pbs: nc.vector.tensor_copy(out=vt[s_][:], in_=PS[pbs[2]][:, 24:280]), [PB[pbs[2]]], [B_vt[s_]])
            R.add("sp", lambda s_=s_, tt=tt: nc.sync.dma_start(out=v_d[tt * 128:(tt + 1) * 128, :], in_=vt[s_][:]), [B_vt[s_]], [B_v], dma=B_vt[s_], part=True)
            R.add("dve", lambda tt=tt, pbs=pbs: nc.vector.tensor_tensor(out=dta[:, tt, :], in0=PS[pbs[2]][:, 0:24], in1=dtbt[:], op=ALU.add),
                  [PB[pbs[2]], B_dtbt], [B_dta], part=True)
        dflat = dta.rearrange("p a b -> p (a b)")
        R.add("act", lambda: nc.scalar.activation(out=dtw[:, 0, :], in_=dflat, func=AF.Abs), [B_dta], [B_dtw])
        R.add("act", lambda: nc.scalar.activation(out=dtw[:, 1, :], in_=dtw[:, 0, :], func=AF.Exp, scale=-1.0), [B_dtw], [B_dtw], part=True)
        R.add("act", lambda: nc.scalar.activation(out=dtw[:, 2, :], in_=dtw[:, 1, :], func=AF.Ln, bias=eps_t[:, 1:2], scale=1.0), [B_dtw, B_eps], [B_dtw], part=True)
        R.add("dve", lambda: nc.vector.scalar_tensor_tensor(out=dtw[:, 0, :], in0=dflat, scalar=0.0, in1=dtw[:, 2, :], op0=ALU.max, op1=ALU.add),
              [B_dta, B_dtw], [B_dtw], part=True)
        R.add("sp", lambda: nc.sync.dma_start(out=dt_d.ap().rearrange("(a p) h -> p a h", p=128),
                                              in_=dtw[:, 0, :].rearrange("p (a h) -> p a h", h=24)), [B_dtw], [B_dt], dma=B_dtw)
        R.barrier()
        R.release(B_wst + B_ot + B_obt + [B_vecs, B_rope, B_wzs, B_dtbt, B_dtw] + B_zt + B_vt)

    def phaseB(li, last, side=None):
        sb = SB(nc, WORK0, limit=P1_BASE)
        KT = sb.alloc("KT", [128, 4, T], BF16); B_KT = Buf("KT")
        VA = sb.alloc("VA", [128, 34, 4, 128], BF16); B_VA = Buf("VA")
        qs = [sb.alloc("qs", [128, 3, 512], BF16) for _ in range(2)]
        B_qs = [Buf("qs%d" % i) for i in range(2)]
        pT = [sb.alloc("pT", [128, 512], BF16) for _ in range(4)]
        B_pT = [Buf("pT%d" % i) for i in range(4)]
        rsb = [sb.alloc("rsb", [128, 512], F32) for _ in range(2)]
        B_rsb = [Buf("rsb%d" % i) for i in range(2)]
        ob = [sb.alloc("ob", [128, 512], F32) for _ in range(2)]
        B_ob = [Buf("ob%d" % i) for i in range(2)]
        gs = [sb.alloc("gs", [128, 512], F32) for _ in range(2)]
        B_gs = [Buf("gs%d" % i) for i in range(2)]
        obb = [sb.alloc("obb", [128, 512], BF16) for _ in range(2)]
        B_obb = [Buf("obb%d" % i) for i in range(2)]
        R.add("pool", lambda: nc.gpsimd.memset(KT[64:128, :, :], 0.0), [], [B_KT])
        R.add("sp", lambda: nc.sync.dma_start(out=KT[0:64, :, :], in_=kT_d.ap().rearrange("(g d) t -> d g t", d=64)), [B_kT], [B_KT], dma=True, part=True)
        for i in range(2):
            R.add("pool", lambda i=i: nc.gpsimd.memset(qs[i][64:128, :, :], 0.0), [], [B_qs[i]])
        R.add("pool", lambda: nc.gpsimd.memset(VA[:, :, :, 64:128], 1.0), [], [B_VA])
        for tt in range(34):
            R.add("sp", lambda tt=tt: nc.sync.dma_start(out=VA[:, tt, :, 0:64], in_=v_d[tt * 128:(tt + 1) * 128, :].rearrange("p (g d) -> p g d", d=64)),
                  [B_v], [B_VA], dma=True, part=True)
        fin = 0
        if side:
            R.side, R.side_every, R.side_cnt = list(side), 8, 0
        for j in range(9):
            if last and j == 8:
                continue
            w_ = 512 if j < 8 else 256
            kts = list(range(34)) if j < 8 else [32, 33]
            for g in range(4):
                pbase = (g % 2) * 64
                qi = (j * 4 + g) % 2
                R.add("sp", lambda g=g, j=j, w_=w_, qi=qi, pbase=pbase: nc.sync.dma_start(
                    out=qs[qi][0:64, :, 0:w_],
                    in_=qT_d[g * 192:(g + 1) * 192, j * 512:j * 512 + w_].rearrange("(h d) t -> d h t", d=64)), [B_qT], [B_qs[qi]], dma=True, part=True)
                steps = [(kt, hh) for kt in kts for hh in range(3)]
                n = len(steps)

                def S(s_):
                    kt, hh = steps[s_]
                    bk = s_ % 3
                    pk = s_ % 4
                    R.add("pe", lambda kt=kt, hh=hh, bk=bk, g=g, pbase=pbase, qi=qi, w_=w_: nc.tensor.matmul(PS[bk][:, 0:w_], lhsT=KT[:, g, kt * 128:(kt + 1) * 128],
                                                         rhs=qs[qi][:, hh, 0:w_], start=True, stop=True), [B_KT, B_qs[qi]], [PB[bk]])
                    R.add("act", lambda bk=bk, pk=pk, w_=w_: nc.scalar.activation(out=pT[pk][:, 0:w_], in_=PS[bk][:, 0:w_], func=AF.Exp, scale=0.125), [PB[bk]], [B_pT[pk]])

                def PV(s_):
                    kt, hh = steps[s_]
                    bk = s_ % 4
                    R.add("pe", lambda kt=kt, hh=hh, bk=bk, g=g, w_=w_, k0=kts[0], k1=kts[-1]: nc.tensor.matmul(
                        PS[4 + hh][:, 0:w_], lhsT=VA[:, kt, g, :], rhs=pT[bk][:, 0:w_],
                        start=(kt == k0), stop=(kt == k1)), [B_VA, B_pT[bk]], [PB[4 + hh]], part=(kt != kts[0]))

                for s_ in range(n + 2):
                    if s_ < n:
                        S(s_)
                    if s_ >= 2:
                        PV(s_ - 2)
                for hh in range(3):
                    h_ = g * 3 + hh
                    f_ = fin % 2
                    fin += 1
                    R.add("sp", lambda h_=h_, f_=f_, j=j, w_=w_: nc.sync.dma_start(
                        out=gs[f_][0:64, 0:w_], in_=gas_d[h_ * 64:(h_ + 1) * 64, j * 512:j * 512 + w_]), [B_gas], [B_gs[f_]], dma=True)
                    R.add("act", lambda hh=hh, f_=f_, w_=w_: nc.scalar.activation(out=rsb[f_][64:128, 0:w_], in_=PS[4 + hh][64:128, 0:w_], func=AF.Ln), [PB[4 + hh]], [B_rsb[f_]])
                    R.add("act", lambda f_=f_, w_=w_: nc.scalar.activation(out=rsb[f_][64:128, 0:w_], in_=rsb[f_][64:128, 0:w_], func=AF.Exp, scale=-1.0), [B_rsb[f_]], [B_rsb[f_]])
                    R.add("dve", lambda hh=hh, f_=f_, w_=w_: nc.vector.tensor_tensor(out=ob[f_][0:64, 0:w_], in0=PS[4 + hh][0:64, 0:w_], in1=rsb[f_][64:128, 0:w_], op=ALU.mult),
                          [PB[4 + hh], B_rsb[f_]], [B_ob[f_]])
                    R.add("pool", lambda f_=f_, w_=w_: nc.gpsimd.tensor_tensor(out=obb[f_][0:64, 0:w_], in0=ob[f_][0:64, 0:w_], in1=gs[f_][0:64, 0:w_], op=ALU.mult),
                          [B_ob[f_], B_gs[f_]], [B_obb[f_]])
                    R.add("sp", lambda h_=h_, f_=f_, j=j, w_=w_: nc.sync.dma_start(
                        out=cat_d[768 + h_ * 64:768 + (h_ + 1) * 64, j * 512:j * 512 + w_], in_=obb[f_][0:64, 0:w_]), [B_obb[f_]], [B_cat], dma=B_obb[f_], part=True)
        R.flush_side()
        R.barrier()
        R.release([B_KT, B_VA] + B_qs + B_gs + B_obb)

    def bc12(ap2d, n=64):
        return fap(ap2d, [[1, ap2d.shape[1]], [0, n]])

    def v3(ap2d, a, b):
        return fap(ap2d, [[b, a], [1, b]])

    P1_BASE = SB.ABYTES - 46 * 1024

    def phaseC(li, last, mode="p2"):
        full_ = (mode == "p2")
        sb = SB(nc, WORK0) if full_ else SB(nc, P1_BASE)
        Abc = sb.alloc("Abc", [128, 24], F32); B_Abc = Buf("Abc")
        Dbc = sb.alloc("Dbc", [128, 12], F32); B_Dbc = Buf("Dbc")
        Gbc = sb.alloc("Gbc", [128, 768], F32) if full_ else None; B_Gbc = Buf("Gbc")
        mrep = sb.alloc("mrep", [128, 2, 4, 128], BF16) if full_ else None; B_mrep = Buf("mrep")
        R.add("sp", lambda: nc.sync.dma_start(out=Abc[:], in_=bass.AP(alog.ap().tensor, li * 24, [[0, 128], [1, 24]])), [B_in], [B_Abc], dma=True)
        R.add("act", lambda: nc.scalar.activation(out=Abc[:], in_=Abc[:], func=AF.Exp), [B_Abc], [B_Abc])
        R.add("dve", lambda: nc.vector.tensor_scalar(out=Abc[:], in0=Abc[:], scalar1=-1.0, scalar2=None, op0=ALU.mult), [B_Abc], [B_Abc])
        R.add("sp", lambda: nc.sync.dma_start(out=Dbc[:], in_=bass.AP(ssdd.ap().tensor, li * 12, [[0, 128], [1, 12]])), [B_in], [B_Dbc], dma=True)
        if full_:
            R.add("sp", lambda: nc.sync.dma_start(out=Gbc[:], in_=bass.AP(ssdng.ap().tensor, li * 768, [[0, 128], [1, 768]])), [B_in], [B_Gbc], dma=True)
        for dr in (range(2) if full_ else ()):
            R.add("dve", lambda dr=dr: nc.vector.tensor_copy(out=mrep[:, dr, :, :], in_=fap(mk_t[:, dr, :], [[0, 4], [1, 128]])), [B_mk], [B_mrep], part=(dr > 0))
        P2 = range(2)

        def mk(name, shape, dt, n=2, always=False):
            if not (full_ or always):
                return [None] * n, [Buf("%s%d" % (name, i)) for i in range(n)]
            return [sb.alloc(name, shape, dt) for _ in range(n)], [Buf("%s%d" % (name, i)) for i in range(n)]
        _mk = mk
        mk = lambda name, shape, dt, n=2: _mk(name, shape, dt, n, name in ("xT", "bcT", "dtc", "xs", "Btm", "av", "ct", "ex", "dtd", "tmp", "hst", "hbb"))
        xT, B_xT = mk("xT", [128, 6, 128], F32)
        bcT, B_bcT = mk("bcT", [128, 4, 128], F32)
        dtc, B_dtc = mk("dtc", [128, 24], F32)
        zc, B_zc = mk("zc", [128, 768], F32)
        hbin, B_hbin = mk("hbin", [128, 768], BF16)
        xs, B_xs = mk("xs", [128, 768], F32)
        Btm, B_Btm = mk("Btm", [128, 256], BF16)
        bcb, B_bcb = mk("bcb", [128, 4, 128], BF16)
        av, B_av = mk("av", [128, 24], BF16)
        ct, B_ct = mk("ct", [128, 72], F32)
        ex, B_ex = mk("ex", [128, 72], F32)
        ncum, B_ncum = mk("ncum", [128, 24], F32)
        dtd, B_dtd = mk("dtd", [128, 24], F32)
        xw = [[(sb.alloc("xw", [128, 768], BF16) if (full_ or k == 1) else None) for k in range(4)] for _ in P2]
        B_xw = [[Buf("xw%d_%d" % (i, k)) for k in range(4)] for i in P2]
        cbs, B_cbs = mk("cbs", [128, 2, 128], BF16)
        Dm = [[(sb.alloc("Dm", [128, 12, 128], BF16) if full_ else None) for _ in P2] for _ in P2]; B_Dm = [[Buf("Dm%d%d" % (i, k)) for k in P2] for i in P2]
        Et = [[(sb.alloc("Et", [128, 12, 128], BF16) if full_ else None) for _ in P2] for _ in P2]; B_Et = [[Buf("Et%d%d" % (i, k)) for k in P2] for i in P2]
        Mt = [[(sb.alloc("Mt", [128, 12, 128], BF16) if full_ else None) for _ in P2] for _ in P2]; B_Mt = [[Buf("Mt%d%d" % (i, k)) for k in P2] for i in P2]
        yo = [[(sb.alloc("yo", [128, 768], F32) if full_ else None) for _ in P2] for _ in P2]; B_yo = [[Buf("yo%d%d" % (i, k)) for k in P2] for i in P2]
        yv, B_yv = mk("yv", [128, 768], F32)
        t3, B_t3 = mk("t3", [128, 768], F32)
        yn, B_yn = mk("yn", [128, 768], F32)
        junk, B_junk = mk("junkc", [128, 768], BF16)
        st, B_st = mk("stc", [128, 4], F32)
        catT, B_catT = mk("catT", [128, 6, 128], BF16)
        tmp, B_tmp = mk("tmp", [128, 768], F32)
        hst, B_hst = mk("hst", [128, 768], F32)
        hfb = sb.alloc("hfb", [128, 768], BF16) if full_ else None; B_hfb = Buf("hfb")
        hbb, B_hbb = mk("hbb", [128, 768], BF16)
        for i in P2:
            R.add("pool", lambda i=i: nc.gpsimd.memset(hst[i][:], 0.0), [], [B_hst[i]])
            R.add("pool", lambda i=i: nc.gpsimd.memset(hbb[i][:], 0.0), [], [B_hbb[i]])
        if full_:
            R.add("pool", lambda: nc.gpsimd.memset(hfb[:], 0.0), [], [B_hfb])

        def prep(c, pp, full, banks=None):
            S0, S1, S2 = banks if banks is not None else (4 * pp, 4 * pp + 1, 4 * pp + 2)
            cs = slice(c * 128, (c + 1) * 128)
            R.add("sp", lambda: nc.sync.dma_start(out=xT[pp][:], in_=xbc_d[0:768, cs].rearrange("(c p) t -> p c t", p=128)), [B_xbc], [B_xT[pp]], dma=True)
            R.add("sp", lambda: nc.sync.dma_start(out=bcT[pp][:], in_=xbc_d[768:1280, cs].rearrange("(c p) t -> p c t", p=128)), [B_xbc], [B_bcT[pp]], dma=True)
            R.add("sp", lambda: nc.sync.dma_start(out=dtc[pp][:], in_=dt_d[cs, :]), [B_dt], [B_dtc[pp]], dma=True)
            if full:
                R.add("sp", lambda: nc.sync.dma_start(out=zc[pp][:], in_=zs_d[cs, :]), [B_zs], [B_zc[pp]], dma=True)
                R.add("sp", lambda: nc.sync.dma_start(out=hbin[pp][:], in_=hb_d[c]), [B_hb], [B_hbin[pp]], dma=True)
            for i in range(4):
                R.add("pe", lambda i=i: nc.tensor.transpose(PS[S0][:, i * 128:(i + 1) * 128], xT[pp][:, i, :], c_t[:, IDENT, :]),
                      [B_xT[pp], B_c], [PB[S0]], part=(i > 0))
            R.add("act", lambda: nc.scalar.activation(out=xs[pp][:, 0:512], in_=PS[S0][:, 0:512], func=AF.Copy), [PB[S0]], [B_xs[pp]])
            for i in range(4, 6):
                R.add("pe", lambda i=i: nc.tensor.transpose(PS[S1][:, (i - 4) * 128:(i - 3) * 128], xT[pp][:, i, :], c_t[:, IDENT, :]),
                      [B_xT[pp], B_c], [PB[S1]], part=(i > 4))
            for i in range(2):
                R.add("pe", lambda i=i: nc.tensor.transpose(PS[S1][:, 256 + i * 128:384 + i * 128], bcT[pp][:, i, :], c_t[:, IDENT, :]),
                      [B_bcT[pp], B_c], [PB[S1]], part=True)
            R.add("act", lambda: nc.scalar.activation(out=xs[pp][:, 512:768], in_=PS[S1][:, 0:256], func=AF.Copy), [PB[S1]], [B_xs[pp]], part=True)
            R.add("dve", lambda: nc.vector.tensor_copy(out=Btm[pp][:], in_=PS[S1][:, 256:512]), [PB[S1]], [B_Btm[pp]])
            if full:
                R.add("pool", lambda: nc.gpsimd.tensor_copy(out=bcb[pp][:], in_=bcT[pp][:]), [B_bcT[pp]], [B_bcb[pp]])
            R.add("dve", lambda: nc.vector.tensor_tensor(out=av[pp][:], in0=dtc[pp][:], in1=Abc[:], op=ALU.mult), [B_dtc[pp], B_Abc], [B_av[pp]])
            R.add("pe", lambda: nc.tensor.matmul(PS[S2][:, 0:12], lhsT=cb_t[:, UIN, :], rhs=av[pp][:, 0:12], start=True, stop=True), [B_cb, B_av[pp]], [PB[S2]])
            R.add("pe", lambda: nc.tensor.matmul(PS[S2][:, 12:24], lhsT=cb_t[:, UTR, :], rhs=av[pp][:, 12:24], start=True, stop=True), [B_cb, B_av[pp]], [PB[S2]], part=True)
            R.add("pe", lambda: nc.tensor.matmul(PS[S2][:, 24:48], lhsT=cb_t[:, ONES, :], rhs=av[pp][:, 0:24], start=True, stop=True), [B_cb, B_av[pp]], [PB[S2]], part=True)
            R.add("dve", lambda: nc.vector.tensor_copy(out=ct[pp][:, 24:72], in_=PS[S2][:, 0:48]), [PB[S2]], [B_ct[pp]])
            R.add("dve", lambda: nc.vector.tensor_tensor(out=ct[pp][:, 0:24], in0=ct[pp][:, 48:72], in1=ct[pp][:, 24:48], op=ALU.subtract), [B_ct[pp]], [B_ct[pp]], part=True)
            R.add("act", lambda: nc.scalar.activation(out=ex[pp][:], in_=ct[pp][:], func=AF.Exp), [B_ct[pp]], [B_ex[pp]])
            if full:
                R.add("dve", lambda: nc.vector.tensor_scalar(out=ncum[pp][:], in0=ct[pp][:, 24:48], scalar1=-1.0, scalar2=None, op0=ALU.mult), [B_ct[pp]], [B_ncum[pp]])
            R.add("dve", lambda: nc.vector.tensor_tensor(out=dtd[pp][:], in0=dtc[pp][:], in1=ex[pp][:, 0:24], op=ALU.mult), [B_dtc[pp], B_ex[pp]], [B_dtd[pp]])
            xs3 = v3(xs[pp][:, 0:768], 12, 64)
            R.add("dve", lambda: nc.vector.tensor_tensor(out=v3(xw[pp][1][:, 0:768], 12, 64), in0=xs3, in1=bc12(dtd[pp][:, 12:24]), op=ALU.mult),
                  [B_xs[pp], B_dtd[pp]], [B_xw[pp][1]])
            if full:
                R.add("pool", lambda: nc.gpsimd.tensor_tensor(out=v3(xw[pp][0][:, 0:768], 12, 64), in0=xs3, in1=bc12(dtd[pp][:, 0:12]), op=ALU.mult),
                      [B_xs[pp], B_dtd[pp]], [B_xw[pp][0]])
                R.add("dve", lambda: nc.vector.tensor_tensor(out=v3(xw[pp][2][:, 0:768], 12, 64), in0=xs3, in1=bc12(dtc[pp][:, 0:12]), op=ALU.mult),
                      [B_xs[pp], B_dtc[pp]], [B_xw[pp][2]])
                R.add("pool", lambda: nc.gpsimd.tensor_tensor(out=v3(xw[pp][3][:, 0:768], 12, 64), in0=xs3, in1=bc12(dtc[pp][:, 12:24]), op=ALU.mult),
                      [B_xs[pp], B_dtc[pp]], [B_xw[pp][3]])

        def state_update(pp, d, bks, outb, B_outb):
            R.add("dve", lambda: nc.vector.tensor_tensor(out=v3(tmp[pp][:, 0:768], 12, 64), in0=v3(hst[d][:, 0:768], 12, 64),
                                                         in1=bc12(ex[pp][:, 48 + d * 12:60 + d * 12]), op=ALU.mult), [B_hst[d], B_ex[pp]], [B_tmp[pp]])
            for g in range(2):
                R.add("pe", lambda g=g: nc.tensor.matmul(PS[bks[g]][:, 0:384], lhsT=Btm[pp][:, g * 128:(g + 1) * 128], rhs=xw[pp][d][:, g * 384:(g + 1) * 384],
                                                        start=True, stop=True), [B_Btm[pp], B_xw[pp][d]], [PB[bks[g]]])
                R.add("dve", lambda g=g: nc.vector.tensor_tensor(out=hst[d][:, g * 384:(g + 1) * 384], in0=PS[bks[g]][:, 0:384], in1=tmp[pp][:, g * 384:(g + 1) * 384], op=ALU.add),
                      [PB[bks[g]], B_tmp[pp]], [B_hst[d]], part=(g > 0))
            R.add("act", lambda: nc.scalar.activation(out=outb[:], in_=hst[d][:], func=AF.Copy), [B_hst[d]], [B_outb])

        def capture(fn):
            lst = []
            R.capture = lst
            marks = fn()
            R.capture = None
            return lst, marks

        if not full_:
            order_b = [33, 32] + list(range(31, -1, -1))
            lists = []
            for n_, c in enumerate(order_b):
                pp = n_ % 2

                def body(c=c, pp=pp):
                    prep(c, pp, False, banks=(7, 3, 7))
                    need = len(R.capture)
                    R.add("sp", lambda: nc.sync.dma_start(out=hb_d[c], in_=hbb[pp][:]), [B_hbb[pp]], [B_hb], dma=B_hbb[pp], part=True)
                    state_update(pp, 1, (3, 7), hbb[1 - pp], B_hbb[1 - pp])
                    return need, len(R.capture)
                ops, (need, done) = capture(body)
                lists.append((ops, need, done))
            flat = []
            for ops, _, _ in lists:
                flat.extend(ops)
            return flat

        order_f = [32, 33] + list(range(32))
        lists = []
        for n_, c in enumerate(order_f):
            pp = n_ % 2

            def body(c=c, pp=pp):
                S0, S1, S2, S3 = 4 * pp, 4 * pp + 1, 4 * pp + 2, 4 * pp + 3
                prep(c, pp, True)
                need = len(R.capture)
                k_ = 0
                for dr in range(2):
                    hin, B_hin = (hfb, B_hfb) if dr == 0 else (hbin[pp], B_hbin[pp])
                    for g in range(2):
                        bk = S3 if k_ % 2 == 0 else S2
                        k_ += 1
                        R.add("pe", lambda g=g, bk=bk, hin=hin: nc.tensor.matmul(PS[bk][:, 0:384], lhsT=bcb[pp][:, 2 + g, :], rhs=hin[:, g * 384:(g + 1) * 384],
                                                                              start=True, stop=True), [B_bcb[pp], B_hin], [PB[bk]])
                        R.add("dve", lambda g=g, bk=bk, dr=dr: nc.vector.tensor_tensor(
                            out=v3(yo[pp][dr][:, g * 384:(g + 1) * 384], 6, 64), in0=v3(PS[bk][:, 0:384], 6, 64),
                            in1=bc12(ex[pp][:, 24 + dr * 12 + g * 6:24 + dr * 12 + g * 6 + 6]), op=ALU.mult), [PB[bk], B_ex[pp]], [B_yo[pp][dr]], part=(g > 0))
                state_update(pp, 0, (S3, S2), hfb, B_hfb)
                done = len(R.capture)
                for g in range(2):
                    R.add("pe", lambda g=g: nc.tensor.matmul(PS[S2][:, g * 128:(g + 1) * 128], lhsT=bcb[pp][:, g, :], rhs=bcb[pp][:, 2 + g, :], start=True, stop=True),
                          [B_bcb[pp]], [PB[S2]], part=(g > 0))
                R.add("dve", lambda: nc.vector.tensor_copy(out=cbs[pp][:].rearrange("p a b -> p (a b)"), in_=PS[S2][:, 0:256]), [PB[S2]], [B_cbs[pp]])
                k_ = 0
                for dr in range(2):
                    um = UIN if dr == 0 else UTR
                    de, deo = ("pool", nc.gpsimd) if dr == 0 else ("dve", nc.vector)
                    R.add(de, lambda dr=dr, um=um, deo=deo: deo.tensor_tensor(
                        out=Dm[pp][dr][:], in0=fap(cb_t[:, um, :], [[0, 12], [1, 128]]), in1=bc12(av[pp][:, dr * 12:dr * 12 + 12], 128), op=ALU.mult),
                        [B_cb, B_av[pp]], [B_Dm[pp][dr]])
                    for q in range(3):
                        bk = S3 if k_ % 2 == 0 else S2
                        k_ += 1
                        R.add("pe", lambda q=q, dr=dr, bk=bk: nc.tensor.matmul(PS[bk][:, :], lhsT=cb_t[:, ONES, :], rhs=Dm[pp][dr][:, q * 4:(q + 1) * 4, :].rearrange("p a b -> p (a b)"),
                                                                             start=True, stop=False), [B_cb, B_Dm[pp][dr]], [PB[bk]])
                        R.add("pe", lambda q=q, dr=dr, bk=bk: nc.tensor.matmul(PS[bk][:, :], lhsT=cb_t[:, IDENT, :], rhs=mrep[:, dr, :, :].rearrange("p a b -> p (a b)"),
                                                                             start=False, stop=True), [B_cb, B_mrep], [PB[bk]], part=True)
                        for hh in range(4):
                            h = q * 4 + hh
                            R.add("act", lambda h=h, hh=hh, dr=dr, bk=bk: nc.scalar.activation(
                                out=Et[pp][dr][:, h, :], in_=PS[bk][:, hh * 128:(hh + 1) * 128], func=AF.Exp,
                                bias=ncum[pp][:, dr * 12 + h:dr * 12 + h + 1], scale=1.0), [PB[bk], B_ncum[pp]], [B_Et[pp][dr]], part=(h > 0))
                    for g in range(2):
                        R.add("dve", lambda g=g, dr=dr: nc.vector.tensor_tensor(out=Mt[pp][dr][:, g * 6:(g + 1) * 6, :], in0=Et[pp][dr][:, g * 6:(g + 1) * 6, :],
                                                                              in1=fap(cbs[pp][:, g, :], [[0, 6], [1, 128]]), op=ALU.mult),
                              [B_Et[pp][dr], B_cbs[pp]], [B_Mt[pp][dr]], part=(g > 0))
                for h in range(12):
                    bk, col = (S0, h * 64) if h < 8 else (S1, (h - 8) * 64)
                    for dr in range(2):
                        R.add("pe", lambda h=h, dr=dr, bk=bk, col=col: nc.tensor.matmul(
                            PS[bk][:, col:col + 64], lhsT=Mt[pp][dr][:, h, :], rhs=xw[pp][2 + dr][:, h * 64:(h + 1) * 64], start=(dr == 0), stop=(dr == 1)),
                            [B_Mt[pp][dr], B_xw[pp][2 + dr]], [PB[bk]], part=not (dr == 0 and h in (0, 8)))
                skip_out = last and c >= 32
                if not skip_out:
                    R.add("dve", lambda: nc.vector.tensor_tensor(out=yv[pp][:, 0:512], in0=PS[S0][:, 0:512], in1=yo[pp][0][:, 0:512], op=ALU.add), [PB[S0], B_yo[pp][0]], [B_yv[pp]])
                    R.add("dve", lambda: nc.vector.tensor_tensor(out=yv[pp][:, 512:768], in0=PS[S1][:, 0:256], in1=yo[pp][0][:, 512:768], op=ALU.add),
                          [PB[S1], B_yo[pp][0]], [B_yv[pp]], part=True)
                    R.add("dve", lambda: nc.vector.tensor_tensor(out=yv[pp][:], in0=yv[pp][:], in1=yo[pp][1][:], op=ALU.add), [B_yv[pp], B_yo[pp][1]], [B_yv[pp]])
                    R.add("pool", lambda: nc.gpsimd.tensor_tensor(out=v3(t3[pp][:, 0:768], 12, 64), in0=v3(xs[pp][:, 0:768], 12, 64), in1=bc12(Dbc[:, 0:12]), op=ALU.mult),
                          [B_xs[pp], B_Dbc], [B_t3[pp]])
                    R.add("dve", lambda: nc.vector.tensor_tensor(out=yv[pp][:], in0=yv[pp][:], in1=t3[pp][:], op=ALU.add), [B_yv[pp], B_t3[pp]], [B_yv[pp]])
                    R.add("pool", lambda: nc.gpsimd.tensor_tensor(out=yv[pp][:], in0=yv[pp][:], in1=zc[pp][:], op=ALU.mult), [B_yv[pp], B_zc[pp]], [B_yv[pp]])
                    R.add("act", lambda: nc.scalar.activation(out=junk[pp][:], in_=yv[pp][:], func=AF.Square, accum_out=st[pp][:, 0:1]), [B_yv[pp]], [B_junk[pp], B_st[pp]])
                    R.add("act", lambda: nc.scalar.activation(out=st[pp][:, 1:2], in_=st[pp][:, 0:1], func=AF.Ln, bias=eps_t[:, 0:1], scale=1.0 / 768),
                          [B_st[pp], B_eps], [B_st[pp]], part=True)
                    R.add("act", lambda: nc.scalar.activation(out=st[pp][:, 2:3], in_=st[pp][:, 1:2], func=AF.Exp, scale=-0.5), [B_st[pp]], [B_st[pp]], part=True)
                    R.add("dve", lambda: nc.vector.scalar_tensor_tensor(out=yn[pp][:], in0=yv[pp][:], scalar=st[pp][:, 2:3], in1=Gbc[:], op0=ALU.mult, op1=ALU.mult),
                          [B_yv[pp], B_st[pp], B_Gbc], [B_yn[pp]])
                    for i in range(6):
                        bk, col = (S2, i * 128) if i < 4 else (S3, (i - 4) * 128)
                        R.add("pe", lambda i=i, bk=bk, col=col: nc.tensor.transpose(PS[bk][:, col:col + 128], yn[pp][:, i * 128:(i + 1) * 128], c_t[:, IDENT, :]),
                              [B_yn[pp], B_c], [PB[bk]], part=(i not in (0, 4)))
                    R.add("act", lambda: nc.scalar.activation(out=catT[pp][:, 0:4, :].rearrange("p a b -> p (a b)"), in_=PS[S2][:, 0:512], func=AF.Copy), [PB[S2]], [B_catT[pp]])
                    R.add("act", lambda: nc.scalar.activation(out=catT[pp][:, 4:6, :].rearrange("p a b -> p (a b)"), in_=PS[S3][:, 0:256], func=AF.Copy), [PB[S3]], [B_catT[pp]], part=True)
                    R.add("sp", lambda: nc.sync.dma_start(out=cat_d[0:768, c * 128:(c + 1) * 128].rearrange("(c p) t -> p c t", p=128), in_=catT[pp][:]),
                          [B_catT[pp]], [B_cat], dma=B_catT[pp], part=True)
                return need, done
            ops, (need, done) = capture(body)
            lists.append((ops, need, done))
        R.interleave(lists)
        R.barrier()
        R.release(B_xT + B_bcT + B_dtc + B_zc + B_hbin + B_hbb + B_catT + [B_Abc, B_Dbc, B_Gbc])


    def phaseD(li, last):
        sb = SB(nc, WORK0)
        vec = sb.alloc("vecD", [128, 16], F32); B_vec = Buf("vecD")
        R.add("sp", lambda: nc.sync.dma_start(out=vec[:], in_=cmv[li].rearrange("p a b -> p (a b)")), [B_in], [B_vec], dma=True)
        pws = sb.alloc("pws", [128, 4, 512], F32); B_pws = Buf("pws")
        pwb = sb.alloc("pwb", [128, 4, 512], BF16); B_pwb = Buf("pwb")
        R.add("sp", lambda: nc.sync.dma_start(out=pws[:], in_=cpw[li].rearrange("(c p) n -> p c n", p=128)), [B_in], [B_pws], dma=True)
        R.add("pool", lambda: nc.gpsimd.tensor_copy(out=pwb[:], in_=pws[:]), [B_pws], [B_pwb])
        P2 = range(2)
        cvt = [sb.alloc("cvt", [128, 4, 512], F32) for _ in P2]; B_cvt = [Buf("cvt%d" % i) for i in P2]
        gct = [sb.alloc("gct", [128, 4, 512], F32) for _ in P2]; B_gct = [Buf("gct%d" % i) for i in P2]
        def mk2(name, shape, dt):
            return [sb.alloc(name, shape, dt) for _ in P2], [Buf("%s%d" % (name, i)) for i in P2]
        sq, B_sq = mk2("sq", [128, 4, 512], F32)
        mean, B_mean = mk2("mean", [128, 512], F32)
        m2, B_m2 = mk2("m2", [128, 512], F32)
        var, B_var = mk2("var", [128, 512], F32)
        sdv, B_sdv = mk2("sdv", [128, 512], F32)
        rstd, B_rstd = mk2("rstd", [128, 512], F32)
        xc_, B_xc = mk2("xc", [128, 512], F32)
        xn_, B_xn = mk2("xnD", [128, 512], F32)
        act, B_act = mk2("act", [128, 4, 512], BF16)
        oD = [[sb.alloc("oD", [128, 512], BF16) for _ in P2] for _ in P2]; B_oD = [[Buf("oD%d%d" % (i, k)) for k in P2] for i in P2]
        bodies = []
        js = [j for j in range(9) if not (last and j == 8)]
        for ji, j in enumerate(js):
            def body(j=j, pp=ji % 2):
                w_ = 512 if j < 8 else 256
                cs = slice(j * 512, j * 512 + w_)
                B0 = 4 * pp
                R.add("sp", lambda: nc.sync.dma_start(out=cvt[pp][:, :, 0:w_], in_=cv_d[:, cs].rearrange("(c p) t -> p c t", p=128)), [B_cv], [B_cvt[pp]], dma=True)
                R.add("sp", lambda: nc.sync.dma_start(out=gct[pp][:, :, 0:w_], in_=gcs_d[:, cs].rearrange("(c p) t -> p c t", p=128)), [B_gcs], [B_gct[pp]], dma=True)
                R.add("act", lambda: nc.scalar.activation(out=sq[pp][:, :, 0:w_], in_=cvt[pp][:, :, 0:w_], func=AF.Square), [B_cvt[pp]], [B_sq[pp]])
                for c in range(4):
                    R.add("pe", lambda c=c: nc.tensor.matmul(PS[B0][:, 0:w_], lhsT=c_t[:, ONES, :], rhs=cvt[pp][:, c, 0:w_], start=(c == 0), stop=(c == 3)),
                          [B_c, B_cvt[pp]], [PB[B0]], part=(c > 0))
                for c in range(4):
                    R.add("pe", lambda c=c: nc.tensor.matmul(PS[B0 + 1][:, 0:w_], lhsT=c_t[:, ONES, :], rhs=sq[pp][:, c, 0:w_], start=(c == 0), stop=(c == 3)),
                          [B_c, B_sq[pp]], [PB[B0 + 1]], part=(c > 0))
                R.add("dve", lambda: nc.vector.tensor_scalar(out=mean[pp][:, 0:w_], in0=PS[B0][:, 0:w_], scalar1=1.0 / 512, scalar2=None, op0=ALU.mult), [PB[B0]], [B_mean[pp]])
                R.add("dve", lambda: nc.vector.tensor_tensor(out=m2[pp][:, 0:w_], in0=mean[pp][:, 0:w_], in1=mean[pp][:, 0:w_], op=ALU.mult), [B_mean[pp]], [B_m2[pp]])
                R.add("dve", lambda: nc.vector.scalar_tensor_tensor(out=var[pp][:, 0:w_], in0=PS[B0 + 1][:, 0:w_], scalar=1.0 / 512, in1=m2[pp][:, 0:w_], op0=ALU.mult, op1=ALU.subtract),
                      [PB[B0 + 1], B_m2[pp]], [B_var[pp]])
                R.add("act", lambda: nc.scalar.activation(out=sdv[pp][:, 0:w_], in_=var[pp][:, 0:w_], func=AF.Ln, bias=eps_t[:, 0:1], scale=1.0), [B_var[pp], B_eps], [B_sdv[pp]])
                R.add("act", lambda: nc.scalar.activation(out=rstd[pp][:, 0:w_], in_=sdv[pp][:, 0:w_], func=AF.Exp, scale=-0.5), [B_sdv[pp]], [B_rstd[pp]])
                for c in range(4):
                    R.add("dve", lambda c=c: nc.vector.tensor_tensor(out=xc_[pp][:, 0:w_], in0=cvt[pp][:, c, 0:w_], in1=mean[pp][:, 0:w_], op=ALU.subtract),
                          [B_cvt[pp], B_mean[pp]], [B_xc[pp]])
                    R.add("pool", lambda: nc.gpsimd.tensor_tensor(out=xn_[pp][:, 0:w_], in0=xc_[pp][:, 0:w_], in1=rstd[pp][:, 0:w_], op=ALU.mult), [B_xc[pp], B_rstd[pp]], [B_xn[pp]])
                    R.add("act", lambda c=c: nc.scalar.activation(out=act[pp][:, c, 0:w_], in_=xn_[pp][:, 0:w_], func=AF.Silu,
                                                                  bias=vec[:, c * 4 + 2:c * 4 + 3], scale=vec[:, c * 4 + 1:c * 4 + 2]), [B_xn[pp], B_vec], [B_act[pp]], part=(c > 0))
                for oc in range(4):
                    bk = B0 + 2 + oc % 2
                    for c in range(4):
                        R.add("pe", lambda oc=oc, c=c, bk=bk: nc.tensor.matmul(PS[bk][:, 0:w_], lhsT=pwb[:, c, oc * 128:(oc + 1) * 128], rhs=act[pp][:, c, 0:w_],
                                                                             start=(c == 0), stop=(c == 3)), [B_pwb, B_act[pp]], [PB[bk]], part=(c > 0))
                    o_ = oc % 2
                    R.add("dve", lambda oc=oc, bk=bk, o_=o_: nc.vector.scalar_tensor_tensor(
                        out=oD[pp][o_][:, 0:w_], in0=PS[bk][:, 0:w_], scalar=vec[:, oc * 4 + 3:oc * 4 + 4], in1=gct[pp][:, oc, 0:w_], op0=ALU.add, op1=ALU.mult),
                        [PB[bk], B_vec, B_gct[pp]], [B_oD[pp][o_]])
                    R.add("sp", lambda oc=oc, o_=o_: nc.sync.dma_start(out=cat_d[1536 + oc * 128:1536 + (oc + 1) * 128, cs], in_=oD[pp][o_][:, 0:w_]),
                          [B_oD[pp][o_]], [B_cat], dma=B_oD[pp][o_], part=True)
            bodies.append(body)
        R.pipeline(bodies)
        R.barrier()
        R.release([B_vec, B_pws] + B_cvt + B_gct + B_oD[0] + B_oD[1])

    def phaseE(li, last):
        sb = SB(nc, WORK0)
        wos = sb.alloc("wos", [128, 4, D], F32); B_wos = Buf("wos")
        wo = sb.alloc("wo", [128, 16, D], BF16); B_wo = Buf("wo")
        for q in range(4):
            R.add("sp", lambda q=q: nc.sync.dma_start(out=wos[:], in_=w_out[li, q * 512:(q + 1) * 512, :].rearrange("(c p) n -> p c n", p=128)), [B_in], [B_wos], dma=True)
            R.add("pool", lambda q=q: nc.gpsimd.tensor_copy(out=wo[:, q * 4:(q + 1) * 4, :], in_=wos[:]), [B_wos], [B_wo], part=(q > 0))
        gB = sb.alloc("gB", [128, 2, D], F32); B_gB = Buf("gB")
        for j in range(2):
            R.add("sp", lambda j=j: nc.sync.dma_start(out=gB[:, j, :], in_=gate_d[li, j]), [B_gate], [B_gB], dma=True, part=(j > 0))
        fg = sb.alloc("fg", [128, D], F32); B_fg = Buf("fg")
        R.add("sp", lambda: nc.sync.dma_start(out=fg[:], in_=bass.AP(fng.ap().tensor, 0, [[0, 128], [1, D]])), [B_in], [B_fg], dma=True)
        P2 = range(2)
        ct_ = [sb.alloc("ctE", [128, 16, 512], BF16) for _ in P2]; B_ct = [Buf("ctE%d" % i) for i in P2]
        xr = [sb.alloc("xr", [128, D], F32) for _ in P2]; B_xr = [Buf("xr%d" % i) for i in P2]
        ty = [sb.alloc("ty", [128, D], F32) for _ in P2]; B_ty = [Buf("ty%d" % i) for i in P2]
        xo = [sb.alloc("xo", [128, D], F32) for _ in P2]; B_xo = [Buf("xo%d" % i) for i in P2]
        junk = sb.alloc("junkE", [128, D], BF16); B_junk = Buf("junkE")
        st = [sb.alloc("stE", [128, 4], F32) for _ in P2]; B_st = [Buf("stE%d" % i) for i in P2]
        fo = [sb.alloc("fo", [128, D], F32) for _ in P2]; B_fo = [Buf("fo%d" % i) for i in P2]
        bodies = []
        js = [j for j in range(9) if not (last and j == 8)]

        def ct_load(j):
            w2 = 512 if j < 8 else 256
            R.add("sp", lambda: nc.sync.dma_start(out=ct_[j % 2][:, :, 0:w2], in_=cat_d[:, j * 512:j * 512 + w2].rearrange("(c p) t -> p c t", p=128)),
                  [B_cat], [B_ct[j % 2]], dma=True)
        for ji, j in enumerate(js):
            w_ = 512 if j < 8 else 256
            jp = j % 2
            for q in range(w_ // 128):
                def body(j=j, w_=w_, jp=jp, q=q, ji=ji):
                    if q == 0 and ji == 0:
                        ct_load(j)
                    if q == 1 and ji + 1 < len(js):
                        ct_load(js[ji + 1])
                    tt = j * 4 + q
                    pp = tt % 2
                    jj = 0 if tt < 32 else 1
                    if li == 0:
                        src = x_in[tt * 128:(tt + 1) * 128, :] if tt < 32 else ctx_in[(tt - 32) * 128:(tt - 31) * 128, :]
                        srcb = B_in
                    else:
                        src = x1_d[tt * 128:(tt + 1) * 128, :]
                        srcb = B_x1
                    R.add("sp", lambda: nc.sync.dma_start(out=xr[pp][:], in_=src), [srcb], [B_xr[pp]], dma=True)
                    for hf_ in range(2):
                        bk = 2 * pp + hf_
                        for c in range(16):
                            R.add("pe", lambda c=c, hf_=hf_, bk=bk: nc.tensor.matmul(
                                PS[bk][:, :], lhsT=ct_[jp][:, c, q * 128:(q + 1) * 128], rhs=wo[:, c, hf_ * 512:(hf_ + 1) * 512], start=(c == 0), stop=(c == 15)),
                                [B_ct[jp], B_wo], [PB[bk]], part=(c > 0))
                        R.add("dve", lambda hf_=hf_, bk=bk: nc.vector.tensor_tensor(
                            out=ty[pp][:, hf_ * 512:(hf_ + 1) * 512], in0=PS[bk][:, :], in1=gB[:, jj, hf_ * 512:(hf_ + 1) * 512], op=ALU.mult),
                            [PB[bk], B_gB], [B_ty[pp]], part=(hf_ > 0))
                    R.add("pool", lambda: nc.gpsimd.tensor_tensor(out=xo[pp][:], in0=ty[pp][:], in1=xr[pp][:], op=ALU.add), [B_ty[pp], B_xr[pp]], [B_xo[pp]])
                    if not last:
                        R.add("sp", lambda: nc.sync.dma_start(out=x1_d[tt * 128:(tt + 1) * 128, :], in_=xo[pp][:]), [B_xo[pp]], [B_x1], dma=B_xo[pp], part=True)
                    else:
                        R.add("act", lambda: nc.scalar.activation(out=junk[:], in_=xo[pp][:], func=AF.Square, accum_out=st[pp][:, 0:1]), [B_xo[pp]], [B_junk, B_st[pp]])
                        R.add("act", lambda: nc.scalar.activation(out=st[pp][:, 1:2], in_=st[pp][:, 0:1], func=AF.Sqrt, bias=eps_t[:, 0:1], scale=1.0 / D),
                              [B_st[pp], B_eps], [B_st[pp]], part=True)
                        R.add("dve", lambda: nc.vector.reciprocal(out=st[pp][:, 2:3], in_=st[pp][:, 1:2]), [B_st[pp]], [B_st[pp]], part=True)
                        R.add("dve", lambda: nc.vector.scalar_tensor_tensor(out=fo[pp][:], in0=xo[pp][:], scalar=st[pp][:, 2:3], in1=fg[:], op0=ALU.mult, op1=ALU.mult),
                              [B_xo[pp], B_st[pp], B_fg], [B_fo[pp]])
                        R.add("sp", lambda: nc.sync.dma_start(out=out_d[tt * 128:(tt + 1) * 128, :], in_=fo[pp][:]), [B_fo[pp]], [B_out], dma=B_fo[pp], part=True)
                bodies.append(body)
        R.pipeline(bodies)
        R.barrier()
        R.release([B_wos, B_gB, B_fg] + B_ct + B_xr + B_xo + B_fo)


    R.barrier()
    for li in range(nlayers):
        phase1(li)
        R.barrier()
        if phases >= 2:
            phaseA(li)
        last = (li == DEPTH - 1) and nlayers == DEPTH
        side = None
        if phases >= 4 and "skipC" not in taps:
            side = phaseC(li, last, "p1")
        if phases >= 3 and "skipB" not in taps:
            phaseB(li, last, side)
        else:
            R.side = list(side or [])
            R.flush_side()
            R.barrier()
        if phases >= 4 and "skipC" not in taps:
            phaseC(li, last, "p2")
        if phases >= 5:
            phaseD(li, last)
        if phases >= 6:
            phaseE(li, last)
    dbg_fence = []
    if "hT" in taps:
        hT_dbg = nc.dram_tensor("hT_dbg", [128, 8, T], BF16, kind="ExternalOutput")
        B_dbg = Buf("dbg")
        R.add("sp", lambda: nc.sync.dma_start(out=hT_dbg.ap(), in_=hT[:]), B_hT, [B_dbg], dma=True)
        dbg_fence.append(B_dbg)
    R.barrier()
    R.emit()
    return dram_in


def host_inputs(inp, b):
    f = np.float32
    fm = lambda v, nch: np.ascontiguousarray(np.asarray(v, f).reshape(nch, 128).T)
    m = {}
    m["x"] = np.ascontiguousarray(inp["x"][b], f)
    m["ctx"] = np.ascontiguousarray(inp["ctx"][b], f)
    m["cvec"] = np.ascontiguousarray(np.stack([fm(inp["c"][b], 8), fm(inp["c_ctx"], 8)], axis=-1))
    m["ada_w"] = np.ascontiguousarray(inp["ada_w"], f)
    m["ada_b_fm"] = np.stack([fm(inp["ada_b"][l], 24) for l in range(DEPTH)])
    m["ada_b_gate"] = np.ascontiguousarray(inp["ada_b"][:, 2048:3072], f)
    m["norm_g_fm"] = np.stack([fm(inp["norm_g"][l], 8) for l in range(DEPTH)])
    m["w_in"] = np.ascontiguousarray(inp["w_in"], f)
    m["ssd_conv_w_fm"] = np.ascontiguousarray(
        np.asarray(inp["ssd_conv_w"], f).reshape(DEPTH, 5, 10, 128).transpose(0, 3, 2, 1))
    m["ssd_conv_b_fm"] = np.stack([fm(inp["ssd_conv_b"][l], 10) for l in range(DEPTH)])
    m["ssd_dt_bias"] = np.ascontiguousarray(inp["ssd_dt_bias"], f)
    m["ssd_a_log"] = np.ascontiguousarray(inp["ssd_a_log"], f)
    m["ssd_d"] = np.ascontiguousarray(inp["ssd_d"], f)
    m["ssd_norm_g"] = np.ascontiguousarray(inp["ssd_norm_g"], f)
    qg = np.asarray(inp["q_norm_g"], f)
    kg = np.asarray(inp["k_norm_g"], f)
    m["qk_g_fm"] = np.ascontiguousarray(np.stack([np.tile(qg, (1, 2)), np.tile(kg, (1, 2))], axis=-1))
    m["cm_conv_w_fm"] = np.ascontiguousarray(
        np.asarray(inp["cm_conv_w"], f).reshape(DEPTH, 31, 4, 128).transpose(0, 3, 2, 1))
    m["cm_vec_fm"] = np.ascontiguousarray(np.stack(
        [np.stack([fm(inp[k][l], 4) for k in ("cm_conv_b", "cm_ln_g", "cm_ln_b", "cm_pw_b")], axis=-1)
         for l in range(DEPTH)]))
    m["cm_pw_w"] = np.ascontiguousarray(inp["cm_pw_w"], f)
    m["w_out"] = np.ascontiguousarray(inp["w_out"], f)
    m["final_norm_g"] = np.ascontiguousarray(inp["final_norm_g"], f)
    m.update(const_inputs())
    return m


_CONST = None


def const_inputs():
    global _CONST
    if _CONST is not None:
        return _CONST
    f = np.float32
    i = np.arange(128)
    ident = np.eye(128, dtype=f)
    U = (i[:, None] <= i[None, :]).astype(f)
    UT = (i[:, None] >= i[None, :]).astype(f)
    d = i % 64
    half = (d % 32) // 16
    partner = np.where(half == 0, i + 16, i - 16)
    perm = np.zeros((128, 128), f)
    perm[partner, i] = 1.0
    bd = ((i[:, None] // 64) == (i[None, :] // 64)).astype(f) / 64.0
    ones = np.ones((128, 128), f)
    consts = np.stack([ident, U, UT, perm, bd, ones], axis=1)
    mf = np.where(i[:, None] <= i[None, :], 0.0, NEG).astype(f)
    mb = np.where(i[:, None] >= i[None, :], 0.0, NEG).astype(f)
    masks = np.stack([mf, mb], axis=1)
    u = np.arange(SEQ)
    pos = np.stack([u // 64, u % 64], axis=0).astype(np.float64)
    inv = 10000.0 ** (-np.arange(16, dtype=np.float64) / 16)
    fq = d % 16
    axis = d // 32
    ang = pos[axis][:, :] * inv[fq][:, None]
    ang = (pos[axis].astype(f) * inv.astype(f)[fq][:, None]).astype(f)
    cos = np.cos(ang).astype(f)
    sin = np.sin(ang).astype(f)
    sgn = np.where(half == 0, -1.0, 1.0).astype(f)[:, None]
    COS = np.concatenate([cos, np.ones((128, NCTX), f)], axis=1)
    SIN = np.concatenate([sin * sgn, np.zeros((128, NCTX), f)], axis=1)
    rope = np.stack([COS, SIN], axis=1)
    _CONST = {"consts": np.ascontiguousarray(consts), "masks": np.ascontiguousarray(masks),
              "rope": np.ascontiguousarray(rope)}
    return _CONST


def kernel(**inputs):
    inputs = {k: np.asarray(v) for k, v in inputs.items()}
    nc = bass.Bass("TRN2", target_bir_lowering=False)
    build(nc)
    owners = (0, 1, 4, 5)
    maps = {c: host_inputs(inputs, b) for b, c in enumerate(owners)}
    zero = {k: np.zeros_like(v) for k, v in maps[0].items()}
    in_maps = [maps.get(c, zero) for c in range(8)]
    res = run_bass_kernel_spmd(nc, in_maps, core_ids=list(range(8)))
    return np.stack([res.results[c]["out"] for c in owners], axis=0).astype(np.float32)
```

```python
import numpy as np
import ml_dtypes
import concourse.bass as bass
import concourse.mybir as mybir
from concourse.bass_utils import run_bass_kernel_spmd

F32 = mybir.dt.float32
BF16 = mybir.dt.bfloat16
AF = mybir.ActivationFunctionType
ALU = mybir.AluOpType
AX = mybir.AxisListType

D = 1024
SEQ = 4096
NCTX = 256
T = SEQ + NCTX
DEPTH = 2
INW = 5656
EPS = 1e-6
O_Z, O_X, O_B, O_C, O_DT, O_Q, O_K, O_V, O_GA, O_UA, O_UB, O_GC = 0, 768, 1536, 1792, 2048, 2072, 2840, 3096, 3352, 4120, 4632, 5144
NEG = -30000.0


class Buf:
    __slots__ = ("name", "w", "r", "excl", "semi", "dcount")

    def __init__(self, name, excl=False):
        self.name = name
        self.w = {}
        self.r = {}
        self.excl = excl
        self.semi = None
        self.dcount = 0


class Op:
    __slots__ = ("eng", "idx", "fn", "deps", "dma", "sig", "semi", "dval")


def _merge(dst, src):
    for k, v in src.items():
        if dst.get(k, -1) < v:
            dst[k] = v


class Rec:
    ENG = ("pe", "act", "dve", "pool", "sp")

    def __init__(self, nc):
        self.nc = nc
        self.ops = {e: [] for e in self.ENG}
        self.ndma_sems = 0
        self.free = []
        self.side = []
        self.side_every = 0
        self.side_cnt = 0
        self.capture = None
        self.semcount = {}
        self.eobj = {"pe": nc.tensor, "act": nc.scalar, "dve": nc.vector, "pool": nc.gpsimd, "sp": nc.sync}

    def add(self, eng, fn, reads=(), writes=(), dma=None, part=False):
        if dma is True:
            dma = writes[0]
        if self.capture is not None:
            self.capture.append((eng, fn, tuple(reads), tuple(writes), dma, part))
            return None
        op = Op()
        op.eng, op.fn, op.dma, op.sig = eng, fn, dma is not None, False
        op.idx = len(self.ops[eng])
        raw, oth = {}, {}
        for b in reads:
            _merge(raw, b.w)
            if b.excl:
                _merge(oth, b.r)
        for b in writes:
            _merge(oth, b.r)
            if not part or b.excl:
                _merge(oth, b.w)
        if dma is None:
            oth.pop(("c", eng), None)
            if eng == "pe":
                raw.pop(("c", eng), None)
        deps = raw
        _merge(deps, oth)
        op.deps = deps
        if dma is not None:
            b = dma
            if b.semi is None:
                if self.free:
                    b.semi, b.dcount = self.free.pop()
                    _merge(deps, {("d", b.semi): b.dcount})
                else:
                    b.semi = self.ndma_sems
                    self.ndma_sems += 1
            b.dcount += 16
            self.semcount[b.semi] = b.dcount
            op.semi, op.dval = b.semi, b.dcount
            my = {("d", b.semi): b.dcount}
        else:
            my = {("c", eng): op.idx}
        for b in reads:
            _merge(b.r, my)
        for b in writes:
            if part:
                _merge(b.w, my)
            else:
                b.w = dict(my)
                b.r = {}
        self.ops[eng].append(op)
        if self.side and self.side_every:
            self.side_cnt += 1
            if self.side_cnt >= self.side_every:
                self.side_cnt = 0
                so = self.side.pop(0)
                ev, self.side_every = self.side_every, 0
                self.add(*so[:4], dma=so[4], part=so[5])
                self.side_every = ev
        return op

    def pipeline(self, bodies, width=2):
        lists = []
        for body in bodies:
            lst = []
            self.capture = lst
            body()
            self.capture = None
            lists.append((lst, 10 ** 9, 0))
        self.interleave(lists, width)

    def flush_side(self):
        self.side_every = 0
        while self.side:
            so = self.side.pop(0)
            self.add(*so[:4], dma=so[4], part=so[5])

    def interleave(self, lists, width=2):
        cur = [0] * len(lists)
        lo = 0
        n = len(lists)
        while lo < n:
            act = [i for i in range(lo, lo + width) if i < n]
            progressed = False
            for i in act:
                ops, need, done = lists[i]
                if cur[i] >= len(ops):
                    continue
                if i > lo and cur[i] >= need and cur[i - 1] < lists[i - 1][2]:
                    continue
                if i > lo and cur[i] == 0 and cur[i - 1] < max(1, len(lists[i - 1][0]) // width):
                    continue
                self.add(*ops[cur[i]][:4], dma=ops[cur[i]][4], part=ops[cur[i]][5])
                cur[i] += 1
                progressed = True
            while lo < n and cur[lo] >= len(lists[lo][0]):
                lo += 1
            assert progressed or lo >= n

    def release(self, bufs):
        for b in bufs:
            if b.semi is not None:
                self.free.append((b.semi, b.dcount))
                b.semi = None

    def barrier(self):
        deps = {("c", e): len(self.ops[e]) - 1 for e in self.ENG if self.ops[e]}
        for s_, c_ in self.semcount.items():
            deps[("d", s_)] = c_
        for e in self.ENG:
            op = Op()
            op.eng, op.fn, op.dma, op.sig, op.idx = e, None, False, False, len(self.ops[e])
            op.deps = {k: v for k, v in deps.items() if k != ("c", e)}
            self.ops[e].append(op)

    def emit(self):
        nc = self.nc
        for e in self.ENG:
            for op in self.ops[e]:
                nd = {}
                for k, v in op.deps.items():
                    if k[0] == "c":
                        lst = self.ops[k[1]]
                        while v >= 0 and lst[v].fn is None:
                            v -= 1
                        if v < 0:
                            continue
                        lst[v].sig = True
                    nd[k] = v
                op.deps = nd
        pref = {}
        for e in self.ENG:
            c = 0
            arr = []
            for op in self.ops[e]:
                if op.sig and not op.dma:
                    c += 1
                arr.append(c)
            pref[e] = arr
        assert self.ndma_sems + 5 <= 98, self.ndma_sems
        esem = {e: nc.alloc_semaphore(name="es_" + e) for e in self.ENG}
        dsem = [nc.alloc_semaphore(name="ds_%d" % i) for i in range(self.ndma_sems)]
        for e in self.ENG:
            eo = self.eobj[e]
            known = {}
            for op in self.ops[e]:
                for k, v in op.deps.items():
                    if k[0] == "c":
                        sem, val, key = esem[k[1]], pref[k[1]][v], k
                    else:
                        sem, val, key = dsem[k[1]], v, k
                    if known.get(key, 0) >= val:
                        continue
                    known[key] = val
                    eo.wait_ge(sem, val)
                if op.fn is None:
                    continue
                ins = op.fn()
                if op.dma:
                    ins.then_inc(dsem[op.semi], 16)
                elif op.sig:
                    ins.then_inc(esem[e], 1)


class SB:
    ARENA = None
    ABYTES = 204800

    def __init__(self, nc, base=0, limit=None):
        if SB.ARENA is None or SB.ARENA[0] is not nc:
            SB.ARENA = (nc, nc.alloc_sbuf_tensor("arena", [128, SB.ABYTES // 4], F32))
        self.nc, self.off, self.limit = nc, base, (limit or SB.ABYTES)

    def alloc(self, name, shape, dt):
        return self.at(name, shape, dt, None)

    def at(self, name, shape, dt, off):
        esz = 4 if dt == F32 else 2
        n = int(np.prod(shape[1:]))
        nb = (n * esz + 63) // 64 * 64
        if off is None:
            off = self.off
            self.off += nb
        assert off % 4 == 0 and off + nb <= self.limit, (name, off, nb, self.limit)
        A = SB.ARENA[1]
        ap = A[0:shape[0], off // 4: off // 4 + nb // 4]
        if dt != F32:
            ap = ap.bitcast(dt)
        ap = ap[:, 0:n]
        if len(shape) > 2:
            names = " ".join("d%d" % i for i in range(len(shape) - 1))
            kw = {"d%d" % i: int(shape[i + 1]) for i in range(len(shape) - 1)}
            ap = ap.rearrange("p (%s) -> p %s" % (names, names), **kw)
        return ap


def fap(ap, dims):
    return bass.AP(ap.tensor, ap.offset, [list(ap.ap[0])] + [list(d) for d in dims])


def build(nc, phases=99, nlayers=DEPTH, taps=()):
    R = Rec(nc)
    dram_in = {}

    def din(name, shape, dt=F32):
        dram_in[name] = nc.dram_tensor(name, list(shape), dt, kind="ExternalInput")
        return dram_in[name]

    x_in = din("x", [SEQ, D])
    ctx_in = din("ctx", [NCTX, D])
    cvec = din("cvec", [128, 8, 2])
    ada_w = din("ada_w", [DEPTH, D, 3 * D])
    ada_b_fm = din("ada_b_fm", [DEPTH, 128, 24])
    ada_b_gate = din("ada_b_gate", [DEPTH, D])
    norm_g_fm = din("norm_g_fm", [DEPTH, 128, 8])
    w_in = din("w_in", [DEPTH, D, INW])
    scw = din("ssd_conv_w_fm", [DEPTH, 128, 10, 5])
    scb = din("ssd_conv_b_fm", [DEPTH, 128, 10])
    dtb = din("ssd_dt_bias", [DEPTH, 24])
    alog = din("ssd_a_log", [DEPTH, 24])
    ssdd = din("ssd_d", [DEPTH, 12])
    ssdng = din("ssd_norm_g", [DEPTH, 768])
    qkg = din("qk_g_fm", [DEPTH, 128, 2])
    ccw = din("cm_conv_w_fm", [DEPTH, 128, 4, 31])
    cmv = din("cm_vec_fm", [DEPTH, 128, 4, 4])
    cpw = din("cm_pw_w", [DEPTH, 512, 512])
    w_out = din("w_out", [DEPTH, 2048, D])
    fng = din("final_norm_g", [D])
    consts = din("consts", [128, 6, 128])
    masks = din("masks", [128, 2, 128])
    rope = din("rope", [128, 2, T])
    out_d = nc.dram_tensor("out", [SEQ, D], F32, kind="ExternalOutput")

    def scratch(name, shape, dt):
        kind = "ExternalOutput" if name in taps else "Internal"
        return nc.dram_tensor(name, list(shape), dt, kind=kind)

    gate_d = scratch("gate_d", [DEPTH, 2, 128, D], F32)
    x1_d = scratch("x1_d", [T, D], F32)
    xbc_d = scratch("xbc_d", [1280, T], F32)
    dt_d = scratch("dt_d", [T, 24], F32)
    zs_d = scratch("zs_d", [T, 768], F32)
    qT_d = scratch("qT_d", [768, T], BF16)
    kT_d = scratch("kT_d", [256, T], BF16)
    v_d = scratch("v_d", [T, 256], BF16)
    gas_d = scratch("gas_d", [768, T], F32)
    cv_d = scratch("cv_d", [512, T], F32)
    gcs_d = scratch("gcs_d", [512, T], F32)
    cat_d = scratch("cat_d", [2048, T], BF16)
    hb_d = scratch("hb_d", [34, 128, 768], BF16)
    B_gate = Buf("gate_d"); B_x1 = Buf("x1_d"); B_xbc = Buf("xbc_d"); B_dt = Buf("dt_d"); B_zs = Buf("zs_d")
    B_qT = Buf("qT_d"); B_kT = Buf("kT_d"); B_v = Buf("v_d"); B_gas = Buf("gas_d"); B_cv = Buf("cv_d")
    B_gcs = Buf("gcs_d"); B_cat = Buf("cat_d"); B_hb = Buf("hb_d"); B_out = Buf("out")
    B_in = Buf("inputs")

    PS = [nc.alloc_psum_tensor("ps%d" % i, [128, 512], F32) for i in range(8)]
    PB = [Buf("psb%d" % i, excl=True) for i in range(8)]

    sbp = SB(nc, 0, 24 * 1024)
    c_t = sbp.alloc("consts", [128, 6, 128], F32)
    cb_t = sbp.alloc("constsb", [128, 6, 128], BF16)
    mk_t = sbp.alloc("masks", [128, 2, 128], F32)
    eps_t = sbp.alloc("eps", [128, 2], F32)
    AB_t = sbp.alloc("AB", [128, DEPTH, 2, 8, 2], F32)
    B_c = Buf("consts"); B_AB = Buf("AB")
    IDENT, UIN, UTR, PERM, BDM, ONES = range(6)
    R.add("sp", lambda: nc.sync.dma_start(out=c_t[:], in_=consts.ap()), [B_in], [B_c], dma=True)
    B_mk = Buf("mk")
    R.add("sp", lambda: nc.sync.dma_start(out=mk_t[:], in_=masks.ap()), [B_in], [B_mk], dma=True)
    B_cb = Buf("cb")
    R.add("dve", lambda: nc.vector.tensor_copy(out=cb_t[:], in_=c_t[:]), [B_c], [B_cb])
    B_eps = Buf("eps")
    R.add("pool", lambda: nc.gpsimd.memset(eps_t[:, 0:1], EPS), [], [B_eps])
    R.add("pool", lambda: nc.gpsimd.memset(eps_t[:, 1:2], 1.0), [], [B_eps], part=True)

    WORK0 = 24 * 1024

    def phase0():
        sb = SB(nc, WORK0)
        aw = sb.alloc("aw", [128, 8, 3072], F32)
        B_aw = [Buf("aw%d" % k) for k in range(8)]
        cv = sb.alloc("cv", [128, 8, 2], F32)
        sc = sb.alloc("sc", [128, 8, 2], F32)
        screp = sb.alloc("screp", [128, 8, 2, 128], F32)
        abf = sb.alloc("abf", [128, 24], F32)
        ngf = sb.alloc("ngf", [128, 8], F32)
        abg = sb.alloc("abg", [128, D], F32)
        mod = sb.alloc("mod", [128, 16, 2], F32)
        gt = sb.alloc("gt", [128, 2, D], F32)
        B_cv, B_sc, B_screp, B_abf, B_ngf, B_abg, B_mod, B_gt = [Buf(n) for n in "cv sc screp abf ngf abg mod gt".split()]
        R.add("sp", lambda: nc.sync.dma_start(out=cv[:], in_=cvec.ap()), [B_in], [B_cv], dma=True)
        R.add("act", lambda: nc.scalar.activation(out=sc[:], in_=cv[:], func=AF.Silu), [B_cv], [B_sc])
        for kc in range(8):
            for j in range(2):
                R.add("dve", lambda kc=kc, j=j: nc.vector.tensor_copy(
                    out=screp[:, kc, j, :], in_=fap(sc[:, kc, j:j + 1], [[0, 128]])), [B_sc], [B_screp], part=True)
        for li in range(nlayers):
            for kc in range(8):
                R.add("sp", lambda kc=kc, li=li: nc.sync.dma_start(
                    out=aw[:, kc, :], in_=ada_w[li, kc * 128:(kc + 1) * 128, :]), [B_in], [B_aw[kc]], dma=True)
            R.add("sp", lambda li=li: nc.sync.dma_start(out=abf[:], in_=ada_b_fm[li]), [B_in], [B_abf], dma=True)
            R.add("sp", lambda li=li: nc.sync.dma_start(out=ngf[:], in_=norm_g_fm[li]), [B_in], [B_ngf], dma=True)
            R.add("sp", lambda li=li: nc.sync.dma_start(
                out=abg[:], in_=bass.AP(ada_b_gate.ap().tensor, li * D, [[0, 128], [1, D]])), [B_in], [B_abg], dma=True)
            for fc in range(16):
                for kc in range(8):
                    R.add("pe", lambda fc=fc, kc=kc: nc.tensor.matmul(
                        PS[0][:, fc * 2:fc * 2 + 2], lhsT=aw[:, kc, fc * 128:(fc + 1) * 128], rhs=sc[:, kc, :],
                        start=(kc == 0), stop=(kc == 7)), [B_aw[kc], B_sc], [PB[0]], part=not (fc == 0 and kc == 0))
            R.add("dve", lambda: nc.vector.tensor_tensor(
                out=mod[:], in0=fap(PS[0][:, 0:32], [[2, 16], [1, 2]]), in1=fap(abf[:, 0:16], [[1, 16], [0, 2]]),
                op=ALU.add), [PB[0], B_abf], [B_mod])
            R.add("dve", lambda li=li: nc.vector.scalar_tensor_tensor(
                out=AB_t[:, li, 0, :, :], in0=mod[:, 8:16, :], scalar=1.0, in1=fap(ngf[:, 0:8], [[1, 8], [0, 2]]),
                op0=ALU.add, op1=ALU.mult), [B_mod, B_ngf], [B_AB], part=True)
            R.add("dve", lambda li=li: nc.vector.tensor_copy(out=AB_t[:, li, 1, :, :], in_=mod[:, 0:8, :]),
                  [B_mod], [B_AB], part=True)
            for j in range(2):
                for cc in range(2):
                    pb = 1 + (j * 2 + cc) % 2
                    for kc in range(8):
                        R.add("pe", lambda j=j, cc=cc, kc=kc, pb=pb: nc.tensor.matmul(
                            PS[pb][:, :], lhsT=screp[:, kc, j, :], rhs=aw[:, kc, 2048 + cc * 512:2048 + (cc + 1) * 512],
                            start=(kc == 0), stop=(kc == 7)), [B_screp, B_aw[kc]], [PB[pb]], part=(kc > 0))
                    R.add("dve", lambda j=j, cc=cc, pb=pb: nc.vector.tensor_tensor(
                        out=gt[:, j, cc * 512:(cc + 1) * 512], in0=PS[pb][:, :], in1=abg[:, cc * 512:(cc + 1) * 512],
                        op=ALU.add), [PB[pb], B_abg], [B_gt], part=not (j == 0 and cc == 0))
            for j in range(2):
                R.add("sp", lambda li=li, j=j: nc.sync.dma_start(out=gate_d[li, j], in_=gt[:, j, :]),
                      [B_gt], [B_gate], dma=B_gt, part=True)

    phase0()
    if phases <= 0:
        R.emit()
        return dram_in

    HT_OFF = WORK0
    hT = SB(nc).at("hT", [128, 8, T], BF16, HT_OFF)
    HT_BYTES = 8 * T * 2
    B_hT = [Buf("hT%d" % i) for i in range(34)]
    WORK1 = HT_OFF + HT_BYTES

    def phase1(li):
        sb = SB(nc, WORK1)
        xt = [sb.alloc("xt", [128, D], F32) for _ in range(4)]
        xn = [sb.alloc("xn", [128, D], F32) for _ in range(4)]
        junk = sb.alloc("junk", [128, D], BF16)
        st = [sb.alloc("st", [128, 4], F32) for _ in range(4)]
        B_xt = [Buf("xt%d" % i) for i in range(4)]
        B_xn = [Buf("xn%d" % i) for i in range(4)]
        B_junk = Buf("junk")
        B_st = [Buf("st%d" % i) for i in range(4)]
        bodies = []
        for tt in range(34):
            def body(tt=tt):
                s = tt % 4
                j = 0 if tt < 32 else 1
                if li == 0:
                    src = x_in[tt * 128:(tt + 1) * 128, :] if tt < 32 else ctx_in[(tt - 32) * 128:(tt - 31) * 128, :]
                    srcb = B_in
                else:
                    src = x1_d[tt * 128:(tt + 1) * 128, :]
                    srcb = B_x1
                R.add("sp", lambda s=s, src=src: nc.sync.dma_start(out=xt[s][:], in_=src), [srcb], [B_xt[s]], dma=True)
                R.add("act", lambda s=s: nc.scalar.activation(out=junk[:], in_=xt[s][:], func=AF.Square,
                                                              accum_out=st[s][:, 0:1]), [B_xt[s]], [B_junk, B_st[s]])
                R.add("act", lambda s=s: nc.scalar.activation(out=st[s][:, 1:2], in_=st[s][:, 0:1], func=AF.Sqrt,
                                                              bias=eps_t[:, 0:1], scale=1.0 / D), [B_st[s], B_eps], [B_st[s]], part=True)
                R.add("dve", lambda s=s: nc.vector.reciprocal(out=st[s][:, 2:3], in_=st[s][:, 1:2]), [B_st[s]], [B_st[s]], part=True)
                R.add("dve", lambda s=s: nc.vector.tensor_scalar(out=xn[s][:], in0=xt[s][:], scalar1=st[s][:, 2:3], scalar2=None,
                                                                 op0=ALU.mult), [B_xt[s], B_st[s]], [B_xn[s]])
                pb0 = 2 * (tt % 4)
                for c in range(8):
                    pb = pb0 + c // 4
                    R.add("pe", lambda s=s, c=c, pb=pb: nc.tensor.transpose(
                        PS[pb][:, (c % 4) * 128:(c % 4 + 1) * 128], xn[s][:, c * 128:(c + 1) * 128], c_t[:, IDENT, :]),
                        [B_xn[s], B_c], [PB[pb]], part=(c % 4 > 0))
                for c in range(8):
                    pb = pb0 + c // 4
                    if c % 2 == 0:
                        R.add("act", lambda c=c, pb=pb, tt=tt, j=j: nc.scalar.activation(
                            out=hT[:, c, tt * 128:(tt + 1) * 128], in_=PS[pb][:, (c % 4) * 128:(c % 4 + 1) * 128],
                            func=AF.Identity, bias=AB_t[:, li, 1, c, j:j + 1], scale=AB_t[:, li, 0, c, j:j + 1]),
                            [PB[pb], B_AB], [B_hT[tt]], part=True)
                    else:
                        R.add("dve", lambda c=c, pb=pb, tt=tt, j=j: nc.vector.tensor_scalar(
                            out=hT[:, c, tt * 128:(tt + 1) * 128], in0=PS[pb][:, (c % 4) * 128:(c % 4 + 1) * 128],
                            scalar1=AB_t[:, li, 0, c, j:j + 1], scalar2=AB_t[:, li, 1, c, j:j + 1],
                            op0=ALU.mult, op1=ALU.add), [PB[pb], B_AB], [B_hT[tt]], part=True)
            bodies.append(body)
        R.pipeline(bodies, 4)

    def wsrc(li, c0, ncol):
        return w_in[li, :, c0:c0 + ncol].rearrange("(kc p) n -> p kc n", p=128)

    def phaseA(li):
        sb = SB(nc, WORK1)
        wst = [sb.alloc("wst", [128, 8, 128], F32) for _ in range(2)]
        wbf = [sb.alloc("wbf", [128, 8, 128], BF16) for _ in range(2)]
        B_wst = [Buf("wst%d" % i) for i in range(2)]
        B_wbf = [Buf("wbf%d" % i) for i in range(2)]
        ot = [sb.alloc("ot", [128, 512], F32) for _ in range(3)]
        B_ot = [Buf("ot%d" % i) for i in range(3)]
        obt = [sb.alloc("obt", [128, 512], BF16) for _ in range(2)]
        B_obt = [Buf("obt%d" % i) for i in range(2)]
        vecs = sb.alloc("vecs", [128, 10 * 5 + 10 + 2 + 4 * 31 + 16], F32)
        B_vecs = Buf("vecs")
        V_SCW, V_SCB, V_QKG, V_CCW, V_CMV = 0, 50, 60, 62, 62 + 124
        R.add("sp", lambda: nc.sync.dma_start(out=vecs[:, V_SCW:V_SCW + 50], in_=scw[li].rearrange("p a b -> p (a b)")), [B_in], [B_vecs], dma=True)
        R.add("sp", lambda: nc.sync.dma_start(out=vecs[:, V_SCB:V_SCB + 10], in_=scb[li]), [B_in], [B_vecs], dma=True, part=True)
        R.add("sp", lambda: nc.sync.dma_start(out=vecs[:, V_QKG:V_QKG + 2], in_=qkg[li]), [B_in], [B_vecs], dma=True, part=True)
        R.add("sp", lambda: nc.sync.dma_start(out=vecs[:, V_CCW:V_CCW + 124], in_=ccw[li].rearrange("p a b -> p (a b)")), [B_in], [B_vecs], dma=True, part=True)
        R.add("sp", lambda: nc.sync.dma_start(out=vecs[:, V_CMV:V_CMV + 16], in_=cmv[li].rearrange("p a b -> p (a b)")), [B_in], [B_vecs], dma=True, part=True)
        sub0 = sb.off
        cnt = {"w": 0, "ot": 0, "obt": 0, "ps": 0}

        def wload(c0):
            s_ = cnt["w"] % 2
            cnt["w"] += 1
            R.add("sp", lambda: nc.sync.dma_start(out=wst[s_][:], in_=wsrc(li, c0, 128)), [B_in], [B_wst[s_]], dma=True)
            R.add("pool", lambda: nc.gpsimd.tensor_copy(out=wbf[s_][:], in_=wst[s_][:]), [B_wst[s_]], [B_wbf[s_]])
            return s_

        def proj(ws, j, pb):
            w_ = 512 if j < 8 else 256
            for kc in range(8):
                R.add("pe", lambda kc=kc: nc.tensor.matmul(PS[pb][:, 0:w_], lhsT=wbf[ws][:, kc, :], rhs=hT[:, kc, j * 512:j * 512 + w_],
                                                          start=(kc == 0), stop=(kc == 7)),
                      [B_wbf[ws]] + B_hT[j * 4:j * 4 + w_ // 128], [PB[pb]], part=(kc > 0))
            return w_

        def next_ot():
            s_ = cnt["ot"] % 3
            cnt["ot"] += 1
            return s_

        def next_obt():
            s_ = cnt["obt"] % 2
            cnt["obt"] += 1
            return s_

        for (c0, nch, dst, B_dst) in (((O_GA, 6, gas_d, B_gas), (O_GC, 4, gcs_d, B_gcs)) if "noA1" not in taps else ()):
            for ch in range(nch):
                ws = wload(c0 + ch * 128)
                for j in range(9):
                    pb = cnt["ps"] % 2
                    cnt["ps"] += 1
                    w_ = proj(ws, j, pb)
                    o_ = next_ot()
                    R.add("act", lambda pb=pb, o_=o_, w_=w_: nc.scalar.activation(out=ot[o_][:, 0:w_], in_=PS[pb][:, 0:w_], func=AF.Silu),
                          [PB[pb]], [B_ot[o_]])
                    R.add("sp", lambda o_=o_, w_=w_, ch=ch, j=j, dst=dst: nc.sync.dma_start(
                        out=dst[ch * 128:(ch + 1) * 128, j * 512:j * 512 + w_], in_=ot[o_][:, 0:w_]), [B_ot[o_]], [B_dst], dma=B_ot[o_], part=True)

        sbq = SB(nc, sub0)
        ropet = sbq.alloc("rope", [128, 2, T], F32)
        B_rope = Buf("rope")
        R.add("sp", lambda: nc.sync.dma_start(out=ropet[:, 0, :], in_=rope[:, 0, :]), [B_in], [B_rope], dma=True)
        R.add("sp", lambda: nc.sync.dma_start(out=ropet[:, 1, :], in_=rope[:, 1, :]), [B_in], [B_rope], dma=True, part=True)
        P2 = range(4)
        obt = [sbq.alloc("obtq", [128, 512], BF16) for _ in P2]; B_obt = [Buf("obtq%d" % i) for i in P2]
        sqb = [sbq.alloc("sqb", [128, 512], BF16) for _ in P2]; B_sqb = [Buf("sqb%d" % i) for i in P2]
        sd = [sbq.alloc("sd", [128, 512], F32) for _ in P2]; B_sd = [Buf("sd%d" % i) for i in P2]
        rs = [sbq.alloc("rs", [128, 512], F32) for _ in P2]; B_rs = [Buf("rs%d" % i) for i in P2]
        qnb = [sbq.alloc("qnb", [128, 512], BF16) for _ in P2]; B_qnb = [Buf("qnb%d" % i) for i in P2]
        t1 = [sbq.alloc("t1", [128, 512], F32) for _ in P2]; B_t1 = [Buf("t1%d" % i) for i in P2]
        t2 = [sbq.alloc("t2", [128, 512], F32) for _ in P2]; B_t2 = [Buf("t2%d" % i) for i in P2]
        lists = []
        tile_no = 0
        qk_chunks = []
        for (c0, nch, dst, B_dst, gi) in (((O_Q, 6, qT_d, B_qT, 0), (O_K, 2, kT_d, B_kT, 1)) if "noA2" not in taps else ()):
            for ch in range(nch):
                qk_chunks.append((c0 + ch * 128, ch, dst, B_dst, gi))
        ws_of = {}
        for kq, (wc0, ch, dst, B_dst, gi) in enumerate(qk_chunks):
            if True:
                for j in range(9):
                    pp = tile_no % 4
                    tile_no += 1
                    ws = None

                    def body(kq=kq, j=j, pp=pp, ch=ch, dst=dst, B_dst=B_dst, gi=gi):
                        P0, P1, P2_ = 2 * pp, 2 * pp + 1, 2 * pp + 1
                        if j == 0 and kq == 0:
                            ws_of[0] = wload(qk_chunks[0][0])
                        if j == 4 and kq + 1 < len(qk_chunks):
                            ws_of[kq + 1] = wload(qk_chunks[kq + 1][0])
                        ws = ws_of[kq]
                        w_ = proj(ws, j, P0)
                        cs = slice(j * 512, j * 512 + w_)
                        R.add("act", lambda: nc.scalar.activation(out=sqb[pp][:, 0:w_], in_=PS[P0][:, 0:w_], func=AF.Square), [PB[P0]], [B_sqb[pp]])
                        R.add("pe", lambda: nc.tensor.matmul(PS[P1][:, 0:w_], lhsT=cb_t[:, BDM, :], rhs=sqb[pp][:, 0:w_], start=True, stop=True),
                              [B_cb, B_sqb[pp]], [PB[P1]])
                        R.add("act", lambda: nc.scalar.activation(out=sd[pp][:, 0:w_], in_=PS[P1][:, 0:w_], func=AF.Ln, bias=eps_t[:, 0:1], scale=1.0),
                              [PB[P1], B_eps], [B_sd[pp]])
                        R.add("act", lambda: nc.scalar.activation(out=rs[pp][:, 0:w_], in_=sd[pp][:, 0:w_], func=AF.Exp, scale=-0.5), [B_sd[pp]], [B_rs[pp]])
                        R.add("dve", lambda: nc.vector.scalar_tensor_tensor(
                            out=qnb[pp][:, 0:w_], in0=PS[P0][:, 0:w_], scalar=vecs[:, V_QKG + gi:V_QKG + gi + 1], in1=rs[pp][:, 0:w_], op0=ALU.mult, op1=ALU.mult),
                            [PB[P0], B_vecs, B_rs[pp]], [B_qnb[pp]])
                        R.add("pe", lambda: nc.tensor.matmul(PS[P2_][:, 0:w_], lhsT=cb_t[:, PERM, :], rhs=qnb[pp][:, 0:w_], start=True, stop=True),
                              [B_cb, B_qnb[pp]], [PB[P2_]])
                        R.add("pool", lambda: nc.gpsimd.tensor_tensor(out=t1[pp][:, 0:w_], in0=qnb[pp][:, 0:w_], in1=ropet[:, 0, cs], op=ALU.mult),
                              [B_qnb[pp], B_rope], [B_t1[pp]])
                        R.add("dve", lambda: nc.vector.tensor_tensor(out=t2[pp][:, 0:w_], in0=PS[P2_][:, 0:w_], in1=ropet[:, 1, cs], op=ALU.mult),
                              [PB[P2_], B_rope], [B_t2[pp]])
                        R.add("dve", lambda: nc.vector.tensor_tensor(out=obt[pp][:, 0:w_], in0=t1[pp][:, 0:w_], in1=t2[pp][:, 0:w_], op=ALU.add),
                              [B_t1[pp], B_t2[pp]], [B_obt[pp]])
                        R.add("sp", lambda: nc.sync.dma_start(out=dst[ch * 128:(ch + 1) * 128, cs], in_=obt[pp][:, 0:w_]), [B_obt[pp]], [B_dst], dma=B_obt[pp], part=True)
                    lst = []
                    R.capture = lst
                    body()
                    R.capture = None
                    lists.append((lst, 10 ** 9, 0))
        R.interleave(lists, 4)
        R.barrier()

        sbx = SB(nc, sub0)
        dg5 = sbx.alloc("dg5", [128, 10, 5, 128], BF16); B_dg5 = Buf("dg5")
        RBW = 2 + SEQ + 2 + 2 + NCTX + 2
        rb = [sbx.alloc("rb", [128, RBW], BF16) for _ in range(2)]
        B_rb = [Buf("rb%d" % i) for i in range(2)]
        for ch in range(10):
            for k in range(5):
                R.add("dve", lambda ch=ch, k=k: nc.vector.tensor_scalar(
                    out=dg5[:, ch, k, :], in0=c_t[:, IDENT, :], scalar1=vecs[:, V_SCW + ch * 5 + k:V_SCW + ch * 5 + k + 1], scalar2=None, op0=ALU.mult),
                    [B_c, B_vecs], [B_dg5], part=True)
        for i in range(2):
            R.add("pool", lambda i=i: nc.gpsimd.memset(rb[i][:], 0.0), [], [B_rb[i]])

        def rbcol(j, pad):
            return pad + j * 512 if j < 8 else pad + SEQ + 2 * pad

        def xproj(ch):
            ws = wload(O_X + ch * 128)
            r_ = ch % 2
            for j in range(9):
                pb = cnt["ps"] % 2
                cnt["ps"] += 1
                w_ = proj(ws, j, pb)
                c0_ = rbcol(j, 2)
                R.add("act", lambda pb=pb, w_=w_, c0_=c0_: nc.scalar.activation(out=rb[r_][:, c0_:c0_ + w_], in_=PS[pb][:, 0:w_], func=AF.Copy),
                      [PB[pb]], [B_rb[r_]], part=(j > 0))

        def xconv(ch):
            r_ = ch % 2
            for j in range(9):
                w_ = 512 if j < 8 else 256
                pb = 2 + j % 2
                st_ = rbcol(j, 2) - 2
                for k in range(5):
                    R.add("pe", lambda k=k, pb=pb, w_=w_, st_=st_: nc.tensor.matmul(
                        PS[pb][:, 0:w_], lhsT=dg5[:, ch, k, :], rhs=rb[r_][:, st_ + k:st_ + k + w_], start=(k == 0), stop=(k == 4)),
                        [B_dg5, B_rb[r_]], [PB[pb]], part=(k > 0))
                o_ = next_ot()
                R.add("act", lambda pb=pb, o_=o_, w_=w_: nc.scalar.activation(
                    out=ot[o_][:, 0:w_], in_=PS[pb][:, 0:w_], func=AF.Silu, bias=vecs[:, V_SCB + ch:V_SCB + ch + 1], scale=1.0),
                    [PB[pb], B_vecs], [B_ot[o_]])
                R.add("sp", lambda o_=o_, w_=w_, j=j: nc.sync.dma_start(
                    out=xbc_d[ch * 128:(ch + 1) * 128, j * 512:j * 512 + w_], in_=ot[o_][:, 0:w_]), [B_ot[o_]], [B_xbc], dma=B_ot[o_], part=True)

        if "noA3" not in taps:
            xproj(0)
        for ch in (range(10) if "noA3" not in taps else ()):
            if ch + 1 < 10:
                xproj(ch + 1)
            xconv(ch)
        R.barrier()

        sbc = SB(nc, sub0)
        dg31 = sbc.alloc("dg31", [128, 4, 31, 128], BF16); B_dg31 = Buf("dg31")
        RUW = 15 + SEQ + 15 + 15 + NCTX + 15
        ru = [sbc.alloc("ru", [128, RUW], BF16) for _ in range(2)]
        B_ru = [Buf("ru%d" % i) for i in range(2)]
        sg = [sbc.alloc("sg", [128, 512], F32) for _ in range(2)]
        B_sg = [Buf("sg%d" % i) for i in range(2)]
        for ch in range(4):
            for k in range(31):
                eng, eo = ("dve", nc.vector) if k % 2 == 0 else ("pool", nc.gpsimd)
                R.add(eng, lambda ch=ch, k=k, eo=eo: eo.tensor_scalar(
                    out=dg31[:, ch, k, :], in0=c_t[:, IDENT, :], scalar1=vecs[:, V_CCW + ch * 31 + k:V_CCW + ch * 31 + k + 1], scalar2=None, op0=ALU.mult),
                    [B_c, B_vecs], [B_dg31], part=True)
        for i in range(2):
            R.add("pool", lambda i=i: nc.gpsimd.memset(ru[i][:], 0.0), [], [B_ru[i]])

        def uproj(ch):
            wa = wload(O_UA + ch * 128)
            wb = wload(O_UB + ch * 128)
            r_ = ch % 2
            for j in range(9):
                w_ = proj(wa, j, 0)
                proj(wb, j, 1)
                s_ = j % 2
                R.add("act", lambda s_=s_, w_=w_: nc.scalar.activation(out=sg[s_][:, 0:w_], in_=PS[1][:, 0:w_], func=AF.Sigmoid), [PB[1]], [B_sg[s_]])
                c0_ = rbcol(j, 15)
                R.add("dve", lambda s_=s_, w_=w_, c0_=c0_: nc.vector.tensor_tensor(
                    out=ru[r_][:, c0_:c0_ + w_], in0=PS[0][:, 0:w_], in1=sg[s_][:, 0:w_], op=ALU.mult), [PB[0], B_sg[s_]], [B_ru[r_]], part=(j > 0))

        def uconv(ch):
            r_ = ch % 2
            for j in range(9):
                w_ = 512 if j < 8 else 256
                pb = 2 + j % 2
                st_ = rbcol(j, 15) - 15
                for k in range(31):
                    R.add("pe", lambda k=k, pb=pb, w_=w_, st_=st_: nc.tensor.matmul(
                        PS[pb][:, 0:w_], lhsT=dg31[:, ch, k, :], rhs=ru[r_][:, st_ + k:st_ + k + w_], start=(k == 0), stop=(k == 30)),
                        [B_dg31, B_ru[r_]], [PB[pb]], part=(k > 0))
                o_ = next_ot()
                R.add("act", lambda pb=pb, o_=o_, w_=w_: nc.scalar.activation(
                    out=ot[o_][:, 0:w_], in_=PS[pb][:, 0:w_], func=AF.Identity, bias=vecs[:, V_CMV + ch * 4:V_CMV + ch * 4 + 1], scale=1.0),
                    [PB[pb], B_vecs], [B_ot[o_]])
                R.add("sp", lambda o_=o_, w_=w_, j=j: nc.sync.dma_start(
                    out=cv_d[ch * 128:(ch + 1) * 128, j * 512:j * 512 + w_], in_=ot[o_][:, 0:w_]), [B_ot[o_]], [B_cv], dma=B_ot[o_], part=True)

        if "noA4" not in taps:
            uproj(0)
        for ch in (range(4) if "noA4" not in taps else ()):
            if ch + 1 < 4:
                uproj(ch + 1)
            uconv(ch)
        R.barrier()

        sbt = SB(nc, sub0)
        NZ = 768 + 24 + 256
        wzs = sbt.alloc("wzs", [128, 8, 384], F32); B_wzs = Buf("wzs")
        wz = sbt.alloc("wz", [128, 8, NZ], BF16); B_wz = Buf("wz")
        for (dc, c0, n_) in ((0, O_Z, 384), (384, O_Z + 384, 384), (768, O_DT, 24), (792, O_V, 256)):
            R.add("sp", lambda c0=c0, n_=n_: nc.sync.dma_start(out=wzs[:, :, 0:n_], in_=wsrc(li, c0, n_)), [B_in], [B_wzs], dma=True)
            R.add("pool", lambda dc=dc, n_=n_: nc.gpsimd.tensor_copy(out=wz[:, :, dc:dc + n_], in_=wzs[:, :, 0:n_]), [B_wzs], [B_wz], part=(dc > 0))
        zt = [sbt.alloc("zt", [128, 768], F32) for _ in range(2)]
        B_zt = [Buf("zt%d" % i) for i in range(2)]
        vt = [sbt.alloc("vt", [128, 256], BF16) for _ in range(2)]
        B_vt = [Buf("vt%d" % i) for i in range(2)]
        dta = sbt.alloc("dta", [128, 34, 24], F32); B_dta = Buf("dta")
        dtw = sbt.alloc("dtw", [128, 3, 34 * 24], F32); B_dtw = Buf("dtw")
        dtbt = sbt.alloc("dtbt", [128, 24], F32); B_dtbt = Buf("dtbt")
        R.add("sp", lambda: nc.sync.dma_start(out=dtbt[:], in_=bass.AP(dtb.ap().tensor, li * 24, [[0, 128], [1, 24]])), [B_in], [B_dtbt], dma=True)
        for tt in (range(34) if "noA5" not in taps else ()):
            s_ = tt % 2
            pbs = (0, 1, 2) if tt % 2 == 0 else (3, 4, 5)
            for (pb, dc, n_) in ((pbs[0], 0, 384), (pbs[1], 384, 384), (pbs[2], 768, 280)):
                for kc in range(8):
                    R.add("pe", lambda kc=kc, pb=pb, dc=dc, n_=n_, tt=tt: nc.tensor.matmul(
                        PS[pb][:, 0:n_], lhsT=hT[:, kc, tt * 128:(tt + 1) * 128], rhs=wz[:, kc, dc:dc + n_], start=(kc == 0), stop=(kc == 7)),
                        [B_hT[tt], B_wz], [PB[pb]], part=(kc > 0))
            for hf_ in range(2):
                R.add("act", lambda hf_=hf_, s_=s_, pbs=pbs: nc.scalar.activation(
                    out=zt[s_][:, hf_ * 384:(hf_ + 1) * 384], in_=PS[pbs[hf_]][:, 0:384], func=AF.Silu), [PB[pbs[hf_]]], [B_zt[s_]], part=(hf_ > 0))
            R.add("sp", lambda s_=s_, tt=tt: nc.sync.dma_start(out=zs_d[tt * 128:(tt + 1) * 128, :], in_=zt[s_][:]), [B_zt[s_]], [B_zs], dma=B_zt[s_], part=True)
            R.add("dve", lambda s_=s_, pbs=pbs: nc.vector.tensor_copy(out=vt[s_][:], in_=PS[pbs[2]][:, 24:280]), [PB[pbs[2]]], [B_vt[s_]])
            R.add("sp", lambda s_=s_, tt=tt: nc.sync.dma_start(out=v_d[tt * 128:(tt + 1) * 128, :], in_=vt[s_][:]), [B_vt[s_]], [B_v], dma=B_vt[s_], part=True)
            R.add("dve", lambda tt=tt, pbs=pbs: nc.vector.tensor_tensor(out=dta[:, tt, :], in0=PS[pbs[2]][:, 0:24], in1=dtbt[:], op=ALU.add),
                  [PB[pbs[2]], B_dtbt], [B_dta], part=True)
        dflat = dta.rearrange("p a b -> p (a b)")
        R.add("act", lambda: nc.scalar.activation(out=dtw[:, 0, :], in_=dflat, func=AF.Abs), [B_dta], [B_dtw])
        R.add("act", lambda: nc.scalar.activation(out=dtw[:, 1, :], in_=dtw[:, 0, :], func=AF.Exp, scale=-1.0), [B_dtw], [B_dtw], part=True)
        R.add("act", lambda: nc.scalar.activation(out=dtw[:, 2, :], in_=dtw[:, 1, :], func=AF.Ln, bias=eps_t[:, 1:2], scale=1.0), [B_dtw, B_eps], [B_dtw], part=True)
        R.add("dve", lambda: nc.vector.scalar_tensor_tensor(out=dtw[:, 0, :], in0=dflat, scalar=0.0, in1=dtw[:, 2, :], op0=ALU.max, op1=ALU.add),
              [B_dta, B_dtw], [B_dtw], part=True)
        R.add("sp", lambda: nc.sync.dma_start(out=dt_d.ap().rearrange("(a p) h -> p a h", p=128),
                                              in_=dtw[:, 0, :].rearrange("p (a h) -> p a h", h=24)), [B_dtw], [B_dt], dma=B_dtw)
        R.barrier()
        R.release(B_wst + B_ot + B_obt + [B_vecs, B_rope, B_wzs, B_dtbt, B_dtw] + B_zt + B_vt)

    def phaseB(li, last, side=None):
        sb = SB(nc, WORK0, limit=P1_BASE)
        KT = sb.alloc("KT", [128, 4, T], BF16); B_KT = Buf("KT")
        VA = sb.alloc("VA", [128, 34, 4, 128], BF16); B_VA = Buf("VA")
        qs = [sb.alloc("qs", [128, 3, 512], BF16) for _ in range(2)]
        B_qs = [Buf("qs%d" % i) for i in range(2)]
        pT = [sb.alloc("pT", [128, 512], BF16) for _ in range(4)]
        B_pT = [Buf("pT%d" % i) for i in range(4)]
        rsb = [sb.alloc("rsb", [128, 512], F32) for _ in range(2)]
        B_rsb = [Buf("rsb%d" % i) for i in range(2)]
        ob = [sb.alloc("ob", [128, 512], F32) for _ in range(2)]
        B_ob = [Buf("ob%d" % i) for i in range(2)]
        gs = [sb.alloc("gs", [128, 512], F32) for _ in range(2)]
        B_gs = [Buf("gs%d" % i) for i in range(2)]
        obb = [sb.alloc("obb", [128, 512], BF16) for _ in range(2)]
        B_obb = [Buf("obb%d" % i) for i in range(2)]
        R.add("pool", lambda: nc.gpsimd.memset(KT[64:128, :, :], 0.0), [], [B_KT])
        R.add("sp", lambda: nc.sync.dma_start(out=KT[0:64, :, :], in_=kT_d.ap().rearrange("(g d) t -> d g t", d=64)), [B_kT], [B_KT], dma=True, part=True)
        for i in range(2):
            R.add("pool", lambda i=i: nc.gpsimd.memset(qs[i][64:128, :, :], 0.0), [], [B_qs[i]])
        R.add("pool", lambda: nc.gpsimd.memset(VA[:, :, :, 64:128], 1.0), [], [B_VA])
        for tt in range(34):
            R.add("sp", lambda tt=tt: nc.sync.dma_start(out=VA[:, tt, :, 0:64], in_=v_d[tt * 128:(tt + 1) * 128, :].rearrange("p (g d) -> p g d", d=64)),
                  [B_v], [B_VA], dma=True, part=True)
        fin = 0
        if side:
            R.side, R.side_every, R.side_cnt = list(side), 8, 0
        for j in range(9):
            if last and j == 8:
                continue
            w_ = 512 if j < 8 else 256
            kts = list(range(34)) if j < 8 else [32, 33]
            for g in range(4):
                pbase = (g % 2) * 64
                qi = (j * 4 + g) % 2
                R.add("sp", lambda g=g, j=j, w_=w_, qi=qi, pbase=pbase: nc.sync.dma_start(
                    out=qs[qi][0:64, :, 0:w_],
                    in_=qT_d[g * 192:(g + 1) * 192, j * 512:j * 512 + w_].rearrange("(h d) t -> d h t", d=64)), [B_qT], [B_qs[qi]], dma=True, part=True)
                steps = [(kt, hh) for kt in kts for hh in range(3)]
                n = len(steps)

                def S(s_):
                    kt, hh = steps[s_]
                    bk = s_ % 3
                    pk = s_ % 4
                    R.add("pe", lambda kt=kt, hh=hh, bk=bk, g=g, pbase=pbase, qi=qi, w_=w_: nc.tensor.matmul(PS[bk][:, 0:w_], lhsT=KT[:, g, kt * 128:(kt + 1) * 128],
                                                         rhs=qs[qi][:, hh, 0:w_], start=True, stop=True), [B_KT, B_qs[qi]], [PB[bk]])
                    R.add("act", lambda bk=bk, pk=pk, w_=w_: nc.scalar.activation(out=pT[pk][:, 0:w_], in_=PS[bk][:, 0:w_], func=AF.Exp, scale=0.125), [PB[bk]], [B_pT[pk]])

                def PV(s_):
                    kt, hh = steps[s_]
                    bk = s_ % 4
                    R.add("pe", lambda kt=kt, hh=hh, bk=bk, g=g, w_=w_, k0=kts[0], k1=kts[-1]: nc.tensor.matmul(
                        PS[4 + hh][:, 0:w_], lhsT=VA[:, kt, g, :], rhs=pT[bk][:, 0:w_],
                        start=(kt == k0), stop=(kt == k1)), [B_VA, B_pT[bk]], [PB[4 + hh]], part=(kt != kts[0]))

                for s_ in range(n + 2):
                    if s_ < n:
                        S(s_)
                    if s_ >= 2:
                        PV(s_ - 2)
                for hh in range(3):
                    h_ = g * 3 + hh
                    f_ = fin % 2
                    fin += 1
                    R.add("sp", lambda h_=h_, f_=f_, j=j, w_=w_: nc.sync.dma_start(
                        out=gs[f_][0:64, 0:w_], in_=gas_d[h_ * 64:(h_ + 1) * 64, j * 512:j * 512 + w_]), [B_gas], [B_gs[f_]], dma=True)
                    R.add("act", lambda hh=hh, f_=f_, w_=w_: nc.scalar.activation(out=rsb[f_][64:128, 0:w_], in_=PS[4 + hh][64:128, 0:w_], func=AF.Ln), [PB[4 + hh]], [B_rsb[f_]])
                    R.add("act", lambda f_=f_, w_=w_: nc.scalar.activation(out=rsb[f_][64:128, 0:w_], in_=rsb[f_][64:128, 0:w_], func=AF.Exp, scale=-1.0), [B_rsb[f_]], [B_rsb[f_]])
                    R.add("dve", lambda hh=hh, f_=f_, w_=w_: nc.vector.tensor_tensor(out=ob[f_][0:64, 0:w_], in0=PS[4 + hh][0:64, 0:w_], in1=rsb[f_][64:128, 0:w_], op=ALU.mult),
                          [PB[4 + hh], B_rsb[f_]], [B_ob[f_]])
                    R.add("pool", lambda f_=f_, w_=w_: nc.gpsimd.tensor_tensor(out=obb[f_][0:64, 0:w_], in0=ob[f_][0:64, 0:w_], in1=gs[f_][0:64, 0:w_], op=ALU.mult),
                          [B_ob[f_], B_gs[f_]], [B_obb[f_]])
                    R.add("sp", lambda h_=h_, f_=f_, j=j, w_=w_: nc.sync.dma_start(
                        out=cat_d[768 + h_ * 64:768 + (h_ + 1) * 64, j * 512:j * 512 + w_], in_=obb[f_][0:64, 0:w_]), [B_obb[f_]], [B_cat], dma=B_obb[f_], part=True)
        R.flush_side()
        R.barrier()
        R.release([B_KT, B_VA] + B_qs + B_gs + B_obb)

    def bc12(ap2d, n=64):
        return fap(ap2d, [[1, ap2d.shape[1]], [0, n]])

    def v3(ap2d, a, b):
        return fap(ap2d, [[b, a], [1, b]])

    P1_BASE = SB.ABYTES - 46 * 1024

    def phaseC(li, last, mode="p2"):
        full_ = (mode == "p2")
        sb = SB(nc, WORK0) if full_ else SB(nc, P1_BASE)
        Abc = sb.alloc("Abc", [128, 24], F32); B_Abc = Buf("Abc")
        Dbc = sb.alloc("Dbc", [128, 12], F32); B_Dbc = Buf("Dbc")
        Gbc = sb.alloc("Gbc", [128, 768], F32) if full_ else None; B_Gbc = Buf("Gbc")
        mrep = sb.alloc("mrep", [128, 2, 4, 128], BF16) if full_ else None; B_mrep = Buf("mrep")
        R.add("sp", lambda: nc.sync.dma_start(out=Abc[:], in_=bass.AP(alog.ap().tensor, li * 24, [[0, 128], [1, 24]])), [B_in], [B_Abc], dma=True)
        R.add("act", lambda: nc.scalar.activation(out=Abc[:], in_=Abc[:], func=AF.Exp), [B_Abc], [B_Abc])
        R.add("dve", lambda: nc.vector.tensor_scalar(out=Abc[:], in0=Abc[:], scalar1=-1.0, scalar2=None, op0=ALU.mult), [B_Abc], [B_Abc])
        R.add("sp", lambda: nc.sync.dma_start(out=Dbc[:], in_=bass.AP(ssdd.ap().tensor, li * 12, [[0, 128], [1, 12]])), [B_in], [B_Dbc], dma=True)
        if full_:
            R.add("sp", lambda: nc.sync.dma_start(out=Gbc[:], in_=bass.AP(ssdng.ap().tensor, li * 768, [[0, 128], [1, 768]])), [B_in], [B_Gbc], dma=True)
        for dr in (range(2) if full_ else ()):
            R.add("dve", lambda dr=dr: nc.vector.tensor_copy(out=mrep[:, dr, :, :], in_=fap(mk_t[:, dr, :], [[0, 4], [1, 128]])), [B_mk], [B_mrep], part=(dr > 0))
        P2 = range(2)

        def mk(name, shape, dt, n=2, always=False):
            if not (full_ or always):
                return [None] * n, [Buf("%s%d" % (name, i)) for i in range(n)]
            return [sb.alloc(name, shape, dt) for _ in range(n)], [Buf("%s%d" % (name, i)) for i in range(n)]
        _mk = mk
        mk = lambda name, shape, dt, n=2: _mk(name, shape, dt, n, name in ("xT", "bcT", "dtc", "xs", "Btm", "av", "ct", "ex", "dtd", "tmp", "hst", "hbb"))
        xT, B_xT = mk("xT", [128, 6, 128], F32)
        bcT, B_bcT = mk("bcT", [128, 4, 128], F32)
        dtc, B_dtc = mk("dtc", [128, 24], F32)
        zc, B_zc = mk("zc", [128, 768], F32)
        hbin, B_hbin = mk("hbin", [128, 768], BF16)
        xs, B_xs = mk("xs", [128, 768], F32)
        Btm, B_Btm = mk("Btm", [128, 256], BF16)
        bcb, B_bcb = mk("bcb", [128, 4, 128], BF16)
        av, B_av = mk("av", [128, 24], BF16)
        ct, B_ct = mk("ct", [128, 72], F32)
        ex, B_ex = mk("ex", [128, 72], F32)
        ncum, B_ncum = mk("ncum", [128, 24], F32)
        dtd, B_dtd = mk("dtd", [128, 24], F32)
        xw = [[(sb.alloc("xw", [128, 768], BF16) if (full_ or k == 1) else None) for k in range(4)] for _ in P2]
        B_xw = [[Buf("xw%d_%d" % (i, k)) for k in range(4)] for i in P2]
        cbs, B_cbs = mk("cbs", [128, 2, 128], BF16)
        Dm = [[(sb.alloc("Dm", [128, 12, 128], BF16) if full_ else None) for _ in P2] for _ in P2]; B_Dm = [[Buf("Dm%d%d" % (i, k)) for k in P2] for i in P2]
        Et = [[(sb.alloc("Et", [128, 12, 128], BF16) if full_ else None) for _ in P2] for _ in P2]; B_Et = [[Buf("Et%d%d" % (i, k)) for k in P2] for i in P2]
        Mt = [[(sb.alloc("Mt", [128, 12, 128], BF16) if full_ else None) for _ in P2] for _ in P2]; B_Mt = [[Buf("Mt%d%d" % (i, k)) for k in P2] for i in P2]
        yo = [[(sb.alloc("yo", [128, 768], F32) if full_ else None) for _ in P2] for _ in P2]; B_yo = [[Buf("yo%d%d" % (i, k)) for k in P2] for i in P2]
        yv, B_yv = mk("yv", [128, 768], F32)
        t3, B_t3 = mk("t3", [128, 768], F32)
        yn, B_yn = mk("yn", [128, 768], F32)
        junk, B_junk = mk("junkc", [128, 768], BF16)
        st, B_st = mk("stc", [128, 4], F32)
        catT, B_catT = mk("catT", [128, 6, 128], BF16)
        tmp, B_tmp = mk("tmp", [128, 768], F32)
        hst, B_hst = mk("hst", [128, 768], F32)
        hfb = sb.alloc("hfb", [128, 768], BF16) if full_ else None; B_hfb = Buf("hfb")
        hbb, B_hbb = mk("hbb", [128, 768], BF16)
        for i in P2:
            R.add("pool", lambda i=i: nc.gpsimd.memset(hst[i][:], 0.0), [], [B_hst[i]])
            R.add("pool", lambda i=i: nc.gpsimd.memset(hbb[i][:], 0.0), [], [B_hbb[i]])
        if full_:
            R.add("pool", lambda: nc.gpsimd.memset(hfb[:], 0.0), [], [B_hfb])

        def prep(c, pp, full, banks=None):
            S0, S1, S2 = banks if banks is not None else (4 * pp, 4 * pp + 1, 4 * pp + 2)
            cs = slice(c * 128, (c + 1) * 128)
            R.add("sp", lambda: nc.sync.dma_start(out=xT[pp][:], in_=xbc_d[0:768, cs].rearrange("(c p) t -> p c t", p=128)), [B_xbc], [B_xT[pp]], dma=True)
            R.add("sp", lambda: nc.sync.dma_start(out=bcT[pp][:], in_=xbc_d[768:1280, cs].rearrange("(c p) t -> p c t", p=128)), [B_xbc], [B_bcT[pp]], dma=True)
            R.add("sp", lambda: nc.sync.dma_start(out=dtc[pp][:], in_=dt_d[cs, :]), [B_dt], [B_dtc[pp]], dma=True)
            if full:
                R.add("sp", lambda: nc.sync.dma_start(out=zc[pp][:], in_=zs_d[cs, :]), [B_zs], [B_zc[pp]], dma=True)
                R.add("sp", lambda: nc.sync.dma_start(out=hbin[pp][:], in_=hb_d[c]), [B_hb], [B_hbin[pp]], dma=True)
            for i in range(4):
                R.add("pe", lambda i=i: nc.tensor.transpose(PS[S0][:, i * 128:(i + 1) * 128], xT[pp][:, i, :], c_t[:, IDENT, :]),
                      [B_xT[pp], B_c], [PB[S0]], part=(i > 0))
            R.add("act", lambda: nc.scalar.activation(out=xs[pp][:, 0:512], in_=PS[S0][:, 0:512], func=AF.Copy), [PB[S0]], [B_xs[pp]])
            for i in range(4, 6):
                R.add("pe", lambda i=i: nc.tensor.transpose(PS[S1][:, (i - 4) * 128:(i - 3) * 128], xT[pp][:, i, :], c_t[:, IDENT, :]),
                      [B_xT[pp], B_c], [PB[S1]], part=(i > 4))
            for i in range(2):
                R.add("pe", lambda i=i: nc.tensor.transpose(PS[S1][:, 256 + i * 128:384 + i * 128], bcT[pp][:, i, :], c_t[:, IDENT, :]),
                      [B_bcT[pp], B_c], [PB[S1]], part=True)
            R.add("act", lambda: nc.scalar.activation(out=xs[pp][:, 512:768], in_=PS[S1][:, 0:256], func=AF.Copy), [PB[S1]], [B_xs[pp]], part=True)
            R.add("dve", lambda: nc.vector.tensor_copy(out=Btm[pp][:], in_=PS[S1][:, 256:512]), [PB[S1]], [B_Btm[pp]])
            if full:
                R.add("pool", lambda: nc.gpsimd.tensor_copy(out=bcb[pp][:], in_=bcT[pp][:]), [B_bcT[pp]], [B_bcb[pp]])
            R.add("dve", lambda: nc.vector.tensor_tensor(out=av[pp][:], in0=dtc[pp][:], in1=Abc[:], op=ALU.mult), [B_dtc[pp], B_Abc], [B_av[pp]])
            R.add("pe", lambda: nc.tensor.matmul(PS[S2][:, 0:12], lhsT=cb_t[:, UIN, :], rhs=av[pp][:, 0:12], start=True, stop=True), [B_cb, B_av[pp]], [PB[S2]])
            R.add("pe", lambda: nc.tensor.matmul(PS[S2][:, 12:24], lhsT=cb_t[:, UTR, :], rhs=av[pp][:, 12:24], start=True, stop=True), [B_cb, B_av[pp]], [PB[S2]], part=True)
            R.add("pe", lambda: nc.tensor.matmul(PS[S2][:, 24:48], lhsT=cb_t[:, ONES, :], rhs=av[pp][:, 0:24], start=True, stop=True), [B_cb, B_av[pp]], [PB[S2]], part=True)
            R.add("dve", lambda: nc.vector.tensor_copy(out=ct[pp][:, 24:72], in_=PS[S2][:, 0:48]), [PB[S2]], [B_ct[pp]])
            R.add("dve", lambda: nc.vector.tensor_tensor(out=ct[pp][:, 0:24], in0=ct[pp][:, 48:72], in1=ct[pp][:, 24:48], op=ALU.subtract), [B_ct[pp]], [B_ct[pp]], part=True)
            R.add("act", lambda: nc.scalar.activation(out=ex[pp][:], in_=ct[pp][:], func=AF.Exp), [B_ct[pp]], [B_ex[pp]])
            if full:
                R.add("dve", lambda: nc.vector.tensor_scalar(out=ncum[pp][:], in0=ct[pp][:, 24:48], scalar1=-1.0, scalar2=None, op0=ALU.mult), [B_ct[pp]], [B_ncum[pp]])
            R.add("dve", lambda: nc.vector.tensor_tensor(out=dtd[pp][:], in0=dtc[pp][:], in1=ex[pp][:, 0:24], op=ALU.mult), [B_dtc[pp], B_ex[pp]], [B_dtd[pp]])
            xs3 = v3(xs[pp][:, 0:768], 12, 64)
            R.add("dve", lambda: nc.vector.tensor_tensor(out=v3(xw[pp][1][:, 0:768], 12, 64), in0=xs3, in1=bc12(dtd[pp][:, 12:24]), op=ALU.mult),
                  [B_xs[pp], B_dtd[pp]], [B_xw[pp][1]])
            if full:
                R.add("pool", lambda: nc.gpsimd.tensor_tensor(out=v3(xw[pp][0][:, 0:768], 12, 64), in0=xs3, in1=bc12(dtd[pp][:, 0:12]), op=ALU.mult),
                      [B_xs[pp], B_dtd[pp]], [B_xw[pp][0]])
                R.add("dve", lambda: nc.vector.tensor_tensor(out=v3(xw[pp][2][:, 0:768], 12, 64), in0=xs3, in1=bc12(dtc[pp][:, 0:12]), op=ALU.mult),
                      [B_xs[pp], B_dtc[pp]], [B_xw[pp][2]])
                R.add("pool", lambda: nc.gpsimd.tensor_tensor(out=v3(xw[pp][3][:, 0:768], 12, 64), in0=xs3, in1=bc12(dtc[pp][:, 12:24]), op=ALU.mult),
                      [B_xs[pp], B_dtc[pp]], [B_xw[pp][3]])

        def state_update(pp, d, bks, outb, B_outb):
            R.add("dve", lambda: nc.vector.tensor_tensor(out=v3(tmp[pp][:, 0:768], 12, 64), in0=v3(hst[d][:, 0:768], 12, 64),
                                                         in1=bc12(ex[pp][:, 48 + d * 12:60 + d * 12]), op=ALU.mult), [B_hst[d], B_ex[pp]], [B_tmp[pp]])
            for g in range(2):
                R.add("pe", lambda g=g: nc.tensor.matmul(PS[bks[g]][:, 0:384], lhsT=Btm[pp][:, g * 128:(g + 1) * 128], rhs=xw[pp][d][:, g * 384:(g + 1) * 384],
                                                        start=True, stop=True), [B_Btm[pp], B_xw[pp][d]], [PB[bks[g]]])
                R.add("dve", lambda g=g: nc.vector.tensor_tensor(out=hst[d][:, g * 384:(g + 1) * 384], in0=PS[bks[g]][:, 0:384], in1=tmp[pp][:, g * 384:(g + 1) * 384], op=ALU.add),
                      [PB[bks[g]], B_tmp[pp]], [B_hst[d]], part=(g > 0))
            R.add("act", lambda: nc.scalar.activation(out=outb[:], in_=hst[d][:], func=AF.Copy), [B_hst[d]], [B_outb])

        def capture(fn):
            lst = []
            R.capture = lst
            marks = fn()
            R.capture = None
            return lst, marks

        if not full_:
            order_b = [33, 32] + list(range(31, -1, -1))
            lists = []
            for n_, c in enumerate(order_b):
                pp = n_ % 2

                def body(c=c, pp=pp):
                    prep(c, pp, False, banks=(7, 3, 7))
                    need = len(R.capture)
                    R.add("sp", lambda: nc.sync.dma_start(out=hb_d[c], in_=hbb[pp][:]), [B_hbb[pp]], [B_hb], dma=B_hbb[pp], part=True)
                    state_update(pp, 1, (3, 7), hbb[1 - pp], B_hbb[1 - pp])
                    return need, len(R.capture)
                ops, (need, done) = capture(body)
                lists.append((ops, need, done))
            flat = []
            for ops, _, _ in lists:
                flat.extend(ops)
            return flat

        order_f = [32, 33] + list(range(32))
        lists = []
        for n_, c in enumerate(order_f):
            pp = n_ % 2

            def body(c=c, pp=pp):
                S0, S1, S2, S3 = 4 * pp, 4 * pp + 1, 4 * pp + 2, 4 * pp + 3
                prep(c, pp, True)
                need = len(R.capture)
                k_ = 0
                for dr in range(2):
                    hin, B_hin = (hfb, B_hfb) if dr == 0 else (hbin[pp], B_hbin[pp])
                    for g in range(2):
                        bk = S3 if k_ % 2 == 0 else S2
                        k_ += 1
                        R.add("pe", lambda g=g, bk=bk, hin=hin: nc.tensor.matmul(PS[bk][:, 0:384], lhsT=bcb[pp][:, 2 + g, :], rhs=hin[:, g * 384:(g + 1) * 384],
                                                                              start=True, stop=True), [B_bcb[pp], B_hin], [PB[bk]])
                        R.add("dve", lambda g=g, bk=bk, dr=dr: nc.vector.tensor_tensor(
                            out=v3(yo[pp][dr][:, g * 384:(g + 1) * 384], 6, 64), in0=v3(PS[bk][:, 0:384], 6, 64),
                            in1=bc12(ex[pp][:, 24 + dr * 12 + g * 6:24 + dr * 12 + g * 6 + 6]), op=ALU.mult), [PB[bk], B_ex[pp]], [B_yo[pp][dr]], part=(g > 0))
                state_update(pp, 0, (S3, S2), hfb, B_hfb)
                done = len(R.capture)
                for g in range(2):
                    R.add("pe", lambda g=g: nc.tensor.matmul(PS[S2][:, g * 128:(g + 1) * 128], lhsT=bcb[pp][:, g, :], rhs=bcb[pp][:, 2 + g, :], start=True, stop=True),
                          [B_bcb[pp]], [PB[S2]], part=(g > 0))
                R.add("dve", lambda: nc.vector.tensor_copy(out=cbs[pp][:].rearrange("p a b -> p (a b)"), in_=PS[S2][:, 0:256]), [PB[S2]], [B_cbs[pp]])
                k_ = 0
                for dr in range(2):
                    um = UIN if dr == 0 else UTR
                    de, deo = ("pool", nc.gpsimd) if dr == 0 else ("dve", nc.vector)
                    R.add(de, lambda dr=dr, um=um, deo=deo: deo.tensor_tensor(
                        out=Dm[pp][dr][:], in0=fap(cb_t[:, um, :], [[0, 12], [1, 128]]), in1=bc12(av[pp][:, dr * 12:dr * 12 + 12], 128), op=ALU.mult),
                        [B_cb, B_av[pp]], [B_Dm[pp][dr]])
                    for q in range(3):
                        bk = S3 if k_ % 2 == 0 else S2
                        k_ += 1
                        R.add("pe", lambda q=q, dr=dr, bk=bk: nc.tensor.matmul(PS[bk][:, :], lhsT=cb_t[:, ONES, :], rhs=Dm[pp][dr][:, q * 4:(q + 1) * 4, :].rearrange("p a b -> p (a b)"),
                                                                             start=True, stop=False), [B_cb, B_Dm[pp][dr]], [PB[bk]])
                        R.add("pe", lambda q=q, dr=dr, bk=bk: nc.tensor.matmul(PS[bk][:, :], lhsT=cb_t[:, IDENT, :], rhs=mrep[:, dr, :, :].rearrange("p a b -> p (a b)"),
                                                                             start=False, stop=True), [B_cb, B_mrep], [PB[bk]], part=True)
                        for hh in range(4):
                            h = q * 4 + hh
                            R.add("act", lambda h=h, hh=hh, dr=dr, bk=bk: nc.scalar.activation(
                                out=Et[pp][dr][:, h, :], in_=PS[bk][:, hh * 128:(hh + 1) * 128], func=AF.Exp,
                                bias=ncum[pp][:, dr * 12 + h:dr * 12 + h + 1], scale=1.0), [PB[bk], B_ncum[pp]], [B_Et[pp][dr]], part=(h > 0))
                    for g in range(2):
                        R.add("dve", lambda g=g, dr=dr: nc.vector.tensor_tensor(out=Mt[pp][dr][:, g * 6:(g + 1) * 6, :], in0=Et[pp][dr][:, g * 6:(g + 1) * 6, :],
                                                                              in1=fap(cbs[pp][:, g, :], [[0, 6], [1, 128]]), op=ALU.mult),
                              [B_Et[pp][dr], B_cbs[pp]], [B_Mt[pp][dr]], part=(g > 0))
                for h in range(12):
                    bk, col = (S0, h * 64) if h < 8 else (S1, (h - 8) * 64)
                    for dr in range(2):
                        R.add("pe", lambda h=h, dr=dr, bk=bk, col=col: nc.tensor.matmul(
                            PS[bk][:, col:col + 64], lhsT=Mt[pp][dr][:, h, :], rhs=xw[pp][2 + dr][:, h * 64:(h + 1) * 64], start=(dr == 0), stop=(dr == 1)),
                            [B_Mt[pp][dr], B_xw[pp][2 + dr]], [PB[bk]], part=not (dr == 0 and h in (0, 8)))
                skip_out = last and c >= 32
                if not skip_out:
                    R.add("dve", lambda: nc.vector.tensor_tensor(out=yv[pp][:, 0:512], in0=PS[S0][:, 0:512], in1=yo[pp][0][:, 0:512], op=ALU.add), [PB[S0], B_yo[pp][0]], [B_yv[pp]])
                    R.add("dve", lambda: nc.vector.tensor_tensor(out=yv[pp][:, 512:768], in0=PS[S1][:, 0:256], in1=yo[pp][0][:, 512:768], op=ALU.add),
                          [PB[S1], B_yo[pp][0]], [B_yv[pp]], part=True)
                    R.add("dve", lambda: nc.vector.tensor_tensor(out=yv[pp][:], in0=yv[pp][:], in1=yo[pp][1][:], op=ALU.add), [B_yv[pp], B_yo[pp][1]], [B_yv[pp]])
                    R.add("pool", lambda: nc.gpsimd.tensor_tensor(out=v3(t3[pp][:, 0:768], 12, 64), in0=v3(xs[pp][:, 0:768], 12, 64), in1=bc12(Dbc[:, 0:12]), op=ALU.mult),
                          [B_xs[pp], B_Dbc], [B_t3[pp]])
                    R.add("dve", lambda: nc.vector.tensor_tensor(out=yv[pp][:], in0=yv[pp][:], in1=t3[pp][:], op=ALU.add), [B_yv[pp], B_t3[pp]], [B_yv[pp]])
                    R.add("pool", lambda: nc.gpsimd.tensor_tensor(out=yv[pp][:], in0=yv[pp][:], in1=zc[pp][:], op=ALU.mult), [B_yv[pp], B_zc[pp]], [B_yv[pp]])
                    R.add("act", lambda: nc.scalar.activation(out=junk[pp][:], in_=yv[pp][:], func=AF.Square, accum_out=st[pp][:, 0:1]), [B_yv[pp]], [B_junk[pp], B_st[pp]])
                    R.add("act", lambda: nc.scalar.activation(out=st[pp][:, 1:2], in_=st[pp][:, 0:1], func=AF.Ln, bias=eps_t[:, 0:1], scale=1.0 / 768),
                          [B_st[pp], B_eps], [B_st[pp]], part=True)
                    R.add("act", lambda: nc.scalar.activation(out=st[pp][:, 2:3], in_=st[pp][:, 1:2], func=AF.Exp, scale=-0.5), [B_st[pp]], [B_st[pp]], part=True)
                    R.add("dve", lambda: nc.vector.scalar_tensor_tensor(out=yn[pp][:], in0=yv[pp][:], scalar=st[pp][:, 2:3], in1=Gbc[:], op0=ALU.mult, op1=ALU.mult),
                          [B_yv[pp], B_st[pp], B_Gbc], [B_yn[pp]])
                    for i in range(6):
                        bk, col = (S2, i * 128) if i < 4 else (S3, (i - 4) * 128)
                        R.add("pe", lambda i=i, bk=bk, col=col: nc.tensor.transpose(PS[bk][:, col:col + 128], yn[pp][:, i * 128:(i + 1) * 128], c_t[:, IDENT, :]),
                              [B_yn[pp], B_c], [PB[bk]], part=(i not in (0, 4)))
                    R.add("act", lambda: nc.scalar.activation(out=catT[pp][:, 0:4, :].rearrange("p a b -> p (a b)"), in_=PS[S2][:, 0:512], func=AF.Copy), [PB[S2]], [B_catT[pp]])
                    R.add("act", lambda: nc.scalar.activation(out=catT[pp][:, 4:6, :].rearrange("p a b -> p (a b)"), in_=PS[S3][:, 0:256], func=AF.Copy), [PB[S3]], [B_catT[pp]], part=True)
                    R.add("sp", lambda: nc.sync.dma_start(out=cat_d[0:768, c * 128:(c + 1) * 128].rearrange("(c p) t -> p c t", p=128), in_=catT[pp][:]),
                          [B_catT[pp]], [B_cat], dma=B_catT[pp], part=True)
                return need, done
            ops, (need, done) = capture(body)
            lists.append((ops, need, done))
        R.interleave(lists)
        R.barrier()
        R.release(B_xT + B_bcT + B_dtc + B_zc + B_hbin + B_hbb + B_catT + [B_Abc, B_Dbc, B_Gbc])


    def phaseD(li, last):
        sb = SB(nc, WORK0)
        vec = sb.alloc("vecD", [128, 16], F32); B_vec = Buf("vecD")
        R.add("sp", lambda: nc.sync.dma_start(out=vec[:], in_=cmv[li].rearrange("p a b -> p (a b)")), [B_in], [B_vec], dma=True)
        pws = sb.alloc("pws", [128, 4, 512], F32); B_pws = Buf("pws")
        pwb = sb.alloc("pwb", [128, 4, 512], BF16); B_pwb = Buf("pwb")
        R.add("sp", lambda: nc.sync.dma_start(out=pws[:], in_=cpw[li].rearrange("(c p) n -> p c n", p=128)), [B_in], [B_pws], dma=True)
        R.add("pool", lambda: nc.gpsimd.tensor_copy(out=pwb[:], in_=pws[:]), [B_pws], [B_pwb])
        P2 = range(2)
        cvt = [sb.alloc("cvt", [128, 4, 512], F32) for _ in P2]; B_cvt = [Buf("cvt%d" % i) for i in P2]
        gct = [sb.alloc("gct", [128, 4, 512], F32) for _ in P2]; B_gct = [Buf("gct%d" % i) for i in P2]
        def mk2(name, shape, dt):
            return [sb.alloc(name, shape, dt) for _ in P2], [Buf("%s%d" % (name, i)) for i in P2]
        sq, B_sq = mk2("sq", [128, 4, 512], F32)
        mean, B_mean = mk2("mean", [128, 512], F32)
        m2, B_m2 = mk2("m2", [128, 512], F32)
        var, B_var = mk2("var", [128, 512], F32)
        sdv, B_sdv = mk2("sdv", [128, 512], F32)
        rstd, B_rstd = mk2("rstd", [128, 512], F32)
        xc_, B_xc = mk2("xc", [128, 512], F32)
        xn_, B_xn = mk2("xnD", [128, 512], F32)
        act, B_act = mk2("act", [128, 4, 512], BF16)
        oD = [[sb.alloc("oD", [128, 512], BF16) for _ in P2] for _ in P2]; B_oD = [[Buf("oD%d%d" % (i, k)) for k in P2] for i in P2]
        bodies = []
        js = [j for j in range(9) if not (last and j == 8)]
        for ji, j in enumerate(js):
            def body(j=j, pp=ji % 2):
                w_ = 512 if j < 8 else 256
                cs = slice(j * 512, j * 512 + w_)
                B0 = 4 * pp
                R.add("sp", lambda: nc.sync.dma_start(out=cvt[pp][:, :, 0:w_], in_=cv_d[:, cs].rearrange("(c p) t -> p c t", p=128)), [B_cv], [B_cvt[pp]], dma=True)
                R.add("sp", lambda: nc.sync.dma_start(out=gct[pp][:, :, 0:w_], in_=gcs_d[:, cs].rearrange("(c p) t -> p c t", p=128)), [B_gcs], [B_gct[pp]], dma=True)
                R.add("act", lambda: nc.scalar.activation(out=sq[pp][:, :, 0:w_], in_=cvt[pp][:, :, 0:w_], func=AF.Square), [B_cvt[pp]], [B_sq[pp]])
                for c in range(4):
                    R.add("pe", lambda c=c: nc.tensor.matmul(PS[B0][:, 0:w_], lhsT=c_t[:, ONES, :], rhs=cvt[pp][:, c, 0:w_], start=(c == 0), stop=(c == 3)),
                          [B_c, B_cvt[pp]], [PB[B0]], part=(c > 0))
                for c in range(4):
                    R.add("pe", lambda c=c: nc.tensor.matmul(PS[B0 + 1][:, 0:w_], lhsT=c_t[:, ONES, :], rhs=sq[pp][:, c, 0:w_], start=(c == 0), stop=(c == 3)),
                          [B_c, B_sq[pp]], [PB[B0 + 1]], part=(c > 0))
                R.add("dve", lambda: nc.vector.tensor_scalar(out=mean[pp][:, 0:w_], in0=PS[B0][:, 0:w_], scalar1=1.0 / 512, scalar2=None, op0=ALU.mult), [PB[B0]], [B_mean[pp]])
                R.add("dve", lambda: nc.vector.tensor_tensor(out=m2[pp][:, 0:w_], in0=mean[pp][:, 0:w_], in1=mean[pp][:, 0:w_], op=ALU.mult), [B_mean[pp]], [B_m2[pp]])
                R.add("dve", lambda: nc.vector.scalar_tensor_tensor(out=var[pp][:, 0:w_], in0=PS[B0 + 1][:, 0:w_], scalar=1.0 / 512, in1=m2[pp][:, 0:w_], op0=ALU.mult, op1=ALU.subtract),
                      [PB[B0 + 1], B_m2[pp]], [B_var[pp]])
                R.add("act", lambda: nc.scalar.activation(out=sdv[pp][:, 0:w_], in_=var[pp][:, 0:w_], func=AF.Ln, bias=eps_t[:, 0:1], scale=1.0), [B_var[pp], B_eps], [B_sdv[pp]])
                R.add("act", lambda: nc.scalar.activation(out=rstd[pp][:, 0:w_], in_=sdv[pp][:, 0:w_], func=AF.Exp, scale=-0.5), [B_sdv[pp]], [B_rstd[pp]])
                for c in range(4):
                    R.add("dve", lambda c=c: nc.vector.tensor_tensor(out=xc_[pp][:, 0:w_], in0=cvt[pp][:, c, 0:w_], in1=mean[pp][:, 0:w_], op=ALU.subtract),
                          [B_cvt[pp], B_mean[pp]], [B_xc[pp]])
                    R.add("pool", lambda: nc.gpsimd.tensor_tensor(out=xn_[pp][:, 0:w_], in0=xc_[pp][:, 0:w_], in1=rstd[pp][:, 0:w_], op=ALU.mult), [B_xc[pp], B_rstd[pp]], [B_xn[pp]])
                    R.add("act", lambda c=c: nc.scalar.activation(out=act[pp][:, c, 0:w_], in_=xn_[pp][:, 0:w_], func=AF.Silu,
                                                                  bias=vec[:, c * 4 + 2:c * 4 + 3], scale=vec[:, c * 4 + 1:c * 4 + 2]), [B_xn[pp], B_vec], [B_act[pp]], part=(c > 0))
                for oc in range(4):
                    bk = B0 + 2 + oc % 2
                    for c in range(4):
                        R.add("pe", lambda oc=oc, c=c, bk=bk: nc.tensor.matmul(PS[bk][:, 0:w_], lhsT=pwb[:, c, oc * 128:(oc + 1) * 128], rhs=act[pp][:, c, 0:w_],
                                                                             start=(c == 0), stop=(c == 3)), [B_pwb, B_act[pp]], [PB[bk]], part=(c > 0))
                    o_ = oc % 2
                    R.add("dve", lambda oc=oc, bk=bk, o_=o_: nc.vector.scalar_tensor_tensor(
                        out=oD[pp][o_][:, 0:w_], in0=PS[bk][:, 0:w_], scalar=vec[:, oc * 4 + 3:oc * 4 + 4], in1=gct[pp][:, oc, 0:w_], op0=ALU.add, op1=ALU.mult),
                        [PB[bk], B_vec, B_gct[pp]], [B_oD[pp][o_]])
                    R.add("sp", lambda oc=oc, o_=o_: nc.sync.dma_start(out=cat_d[1536 + oc * 128:1536 + (oc + 1) * 128, cs], in_=oD[pp][o_][:, 0:w_]),
                          [B_oD[pp][o_]], [B_cat], dma=B_oD[pp][o_], part=True)
            bodies.append(body)
        R.pipeline(bodies)
        R.barrier()
        R.release([B_vec, B_pws] + B_cvt + B_gct + B_oD[0] + B_oD[1])

    def phaseE(li, last):
        sb = SB(nc, WORK0)
        wos = sb.alloc("wos", [128, 4, D], F32); B_wos = Buf("wos")
        wo = sb.alloc("wo", [128, 16, D], BF16); B_wo = Buf("wo")
        for q in range(4):
            R.add("sp", lambda q=q: nc.sync.dma_start(out=wos[:], in_=w_out[li, q * 512:(q + 1) * 512, :].rearrange("(c p) n -> p c n", p=128)), [B_in], [B_wos], dma=True)
            R.add("pool", lambda q=q: nc.gpsimd.tensor_copy(out=wo[:, q * 4:(q + 1) * 4, :], in_=wos[:]), [B_wos], [B_wo], part=(q > 0))
        gB = sb.alloc("gB", [128, 2, D], F32); B_gB = Buf("gB")
        for j in range(2):
            R.add("sp", lambda j=j: nc.sync.dma_start(out=gB[:, j, :], in_=gate_d[li, j]), [B_gate], [B_gB], dma=True, part=(j > 0))
        fg = sb.alloc("fg", [128, D], F32); B_fg = Buf("fg")
        R.add("sp", lambda: nc.sync.dma_start(out=fg[:], in_=bass.AP(fng.ap().tensor, 0, [[0, 128], [1, D]])), [B_in], [B_fg], dma=True)
        P2 = range(2)
        ct_ = [sb.alloc("ctE", [128, 16, 512], BF16) for _ in P2]; B_ct = [Buf("ctE%d" % i) for i in P2]
        xr = [sb.alloc("xr", [128, D], F32) for _ in P2]; B_xr = [Buf("xr%d" % i) for i in P2]
        ty = [sb.alloc("ty", [128, D], F32) for _ in P2]; B_ty = [Buf("ty%d" % i) for i in P2]
        xo = [sb.alloc("xo", [128, D], F32) for _ in P2]; B_xo = [Buf("xo%d" % i) for i in P2]
        junk = sb.alloc("junkE", [128, D], BF16); B_junk = Buf("junkE")
        st = [sb.alloc("stE", [128, 4], F32) for _ in P2]; B_st = [Buf("stE%d" % i) for i in P2]
        fo = [sb.alloc("fo", [128, D], F32) for _ in P2]; B_fo = [Buf("fo%d" % i) for i in P2]
        bodies = []
        js = [j for j in range(9) if not (last and j == 8)]

        def ct_load(j):
            w2 = 512 if j < 8 else 256
            R.add("sp", lambda: nc.sync.dma_start(out=ct_[j % 2][:, :, 0:w2], in_=cat_d[:, j * 512:j * 512 + w2].rearrange("(c p) t -> p c t", p=128)),
                  [B_cat], [B_ct[j % 2]], dma=True)
        for ji, j in enumerate(js):
            w_ = 512 if j < 8 else 256
            jp = j % 2
            for q in range(w_ // 128):
                def body(j=j, w_=w_, jp=jp, q=q, ji=ji):
                    if q == 0 and ji == 0:
                        ct_load(j)
                    if q == 1 and ji + 1 < len(js):
                        ct_load(js[ji + 1])
                    tt = j * 4 + q
                    pp = tt % 2
                    jj = 0 if tt < 32 else 1
                    if li == 0:
                        src = x_in[tt * 128:(tt + 1) * 128, :] if tt < 32 else ctx_in[(tt - 32) * 128:(tt - 31) * 128, :]
                        srcb = B_in
                    else:
                        src = x1_d[tt * 128:(tt + 1) * 128, :]
                        srcb = B_x1
                    R.add("sp", lambda: nc.sync.dma_start(out=xr[pp][:], in_=src), [srcb], [B_xr[pp]], dma=True)
                    for hf_ in range(2):
                        bk = 2 * pp + hf_
                        for c in range(16):
                            R.add("pe", lambda c=c, hf_=hf_, bk=bk: nc.tensor.matmul(
                                PS[bk][:, :], lhsT=ct_[jp][:, c, q * 128:(q + 1) * 128], rhs=wo[:, c, hf_ * 512:(hf_ + 1) * 512], start=(c == 0), stop=(c == 15)),
                                [B_ct[jp], B_wo], [PB[bk]], part=(c > 0))
                        R.add("dve", lambda hf_=hf_, bk=bk: nc.vector.tensor_tensor(
                            out=ty[pp][:, hf_ * 512:(hf_ + 1) * 512], in0=PS[bk][:, :], in1=gB[:, jj, hf_ * 512:(hf_ + 1) * 512], op=ALU.mult),
                            [PB[bk], B_gB], [B_ty[pp]], part=(hf_ > 0))
                    R.add("pool", lambda: nc.gpsimd.tensor_tensor(out=xo[pp][:], in0=ty[pp][:], in1=xr[pp][:], op=ALU.add), [B_ty[pp], B_xr[pp]], [B_xo[pp]])
                    if not last:
                        R.add("sp", lambda: nc.sync.dma_start(out=x1_d[tt * 128:(tt + 1) * 128, :], in_=xo[pp][:]), [B_xo[pp]], [B_x1], dma=B_xo[pp], part=True)
                    else:
                        R.add("act", lambda: nc.scalar.activation(out=junk[:], in_=xo[pp][:], func=AF.Square, accum_out=st[pp][:, 0:1]), [B_xo[pp]], [B_junk, B_st[pp]])
                        R.add("act", lambda: nc.scalar.activation(out=st[pp][:, 1:2], in_=st[pp][:, 0:1], func=AF.Sqrt, bias=eps_t[:, 0:1], scale=1.0 / D),
                              [B_st[pp], B_eps], [B_st[pp]], part=True)
                        R.add("dve", lambda: nc.vector.reciprocal(out=st[pp][:, 2:3], in_=st[pp][:, 1:2]), [B_st[pp]], [B_st[pp]], part=True)
                        R.add("dve", lambda: nc.vector.scalar_tensor_tensor(out=fo[pp][:], in0=xo[pp][:], scalar=st[pp][:, 2:3], in1=fg[:], op0=ALU.mult, op1=ALU.mult),
                              [B_xo[pp], B_st[pp], B_fg], [B_fo[pp]])
                        R.add("sp", lambda: nc.sync.dma_start(out=out_d[tt * 128:(tt + 1) * 128, :], in_=fo[pp][:]), [B_fo[pp]], [B_out], dma=B_fo[pp], part=True)
                bodies.append(body)
        R.pipeline(bodies)
        R.barrier()
        R.release([B_wos, B_gB, B_fg] + B_ct + B_xr + B_xo + B_fo)


    R.barrier()
    for li in range(nlayers):
        phase1(li)
        R.barrier()
        if phases >= 2:
            phaseA(li)
        last = (li == DEPTH - 1) and nlayers == DEPTH
        side = None
        if phases >= 4 and "skipC" not in taps:
            side = phaseC(li, last, "p1")
        if phases >= 3 and "skipB" not in taps:
            phaseB(li, last, side)
        else:
            R.side = list(side or [])
            R.flush_side()
            R.barrier()
        if phases >= 4 and "skipC" not in taps:
            phaseC(li, last, "p2")
        if phases >= 5:
            phaseD(li, last)
        if phases >= 6:
            phaseE(li, last)
    dbg_fence = []
    if "hT" in taps:
        hT_dbg = nc.dram_tensor("hT_dbg", [128, 8, T], BF16, kind="ExternalOutput")
        B_dbg = Buf("dbg")
        R.add("sp", lambda: nc.sync.dma_start(out=hT_dbg.ap(), in_=hT[:]), B_hT, [B_dbg], dma=True)
        dbg_fence.append(B_dbg)
    R.barrier()
    R.emit()
    return dram_in


def host_inputs(inp, b):
    f = np.float32
    fm = lambda v, nch: np.ascontiguousarray(np.asarray(v, f).reshape(nch, 128).T)
    m = {}
    m["x"] = np.ascontiguousarray(inp["x"][b], f)
    m["ctx"] = np.ascontiguousarray(inp["ctx"][b], f)
    m["cvec"] = np.ascontiguousarray(np.stack([fm(inp["c"][b], 8), fm(inp["c_ctx"], 8)], axis=-1))
    m["ada_w"] = np.ascontiguousarray(inp["ada_w"], f)
    m["ada_b_fm"] = np.stack([fm(inp["ada_b"][l], 24) for l in range(DEPTH)])
    m["ada_b_gate"] = np.ascontiguousarray(inp["ada_b"][:, 2048:3072], f)
    m["norm_g_fm"] = np.stack([fm(inp["norm_g"][l], 8) for l in range(DEPTH)])
    m["w_in"] = np.ascontiguousarray(inp["w_in"], f)
    m["ssd_conv_w_fm"] = np.ascontiguousarray(
        np.asarray(inp["ssd_conv_w"], f).reshape(DEPTH, 5, 10, 128).transpose(0, 3, 2, 1))
    m["ssd_conv_b_fm"] = np.stack([fm(inp["ssd_conv_b"][l], 10) for l in range(DEPTH)])
    m["ssd_dt_bias"] = np.ascontiguousarray(inp["ssd_dt_bias"], f)
    m["ssd_a_log"] = np.ascontiguousarray(inp["ssd_a_log"], f)
    m["ssd_d"] = np.ascontiguousarray(inp["ssd_d"], f)
    m["ssd_norm_g"] = np.ascontiguousarray(inp["ssd_norm_g"], f)
    qg = np.asarray(inp["q_norm_g"], f)
    kg = np.asarray(inp["k_norm_g"], f)
    m["qk_g_fm"] = np.ascontiguousarray(np.stack([np.tile(qg, (1, 2)), np.tile(kg, (1, 2))], axis=-1))
    m["cm_conv_w_fm"] = np.ascontiguousarray(
        np.asarray(inp["cm_conv_w"], f).reshape(DEPTH, 31, 4, 128).transpose(0, 3, 2, 1))
    m["cm_vec_fm"] = np.ascontiguousarray(np.stack(
        [np.stack([fm(inp[k][l], 4) for k in ("cm_conv_b", "cm_ln_g", "cm_ln_b", "cm_pw_b")], axis=-1)
         for l in range(DEPTH)]))
    m["cm_pw_w"] = np.ascontiguousarray(inp["cm_pw_w"], f)
    m["w_out"] = np.ascontiguousarray(inp["w_out"], f)
    m["final_norm_g"] = np.ascontiguousarray(inp["final_norm_g"], f)
    m.update(const_inputs())
    return m


_CONST = None


def const_inputs():
    global _CONST
    if _CONST is not None:
        return _CONST
    f = np.float32
    i = np.arange(128)
    ident = np.eye(128, dtype=f)
    U = (i[:, None] <= i[None, :]).astype(f)
    UT = (i[:, None] >= i[None, :]).astype(f)
    d = i % 64
    half = (d % 32) // 16
    partner = np.where(half == 0, i + 16, i - 16)
    perm = np.zeros((128, 128), f)
    perm[partner, i] = 1.0
    bd = ((i[:, None] // 64) == (i[None, :] // 64)).astype(f) / 64.0
    ones = np.ones((128, 128), f)
    consts = np.stack([ident, U, UT, perm, bd, ones], axis=1)
    mf = np.where(i[:, None] <= i[None, :], 0.0, NEG).astype(f)
    mb = np.where(i[:, None] >= i[None, :], 0.0, NEG).astype(f)
    masks = np.stack([mf, mb], axis=1)
    u = np.arange(SEQ)
    pos = np.stack([u // 64, u % 64], axis=0).astype(np.float64)
    inv = 10000.0 ** (-np.arange(16, dtype=np.float64) / 16)
    fq = d % 16
    axis = d // 32
    ang = pos[axis][:, :] * inv[fq][:, None]
    ang = (pos[axis].astype(f) * inv.astype(f)[fq][:, None]).astype(f)
    cos = np.cos(ang).astype(f)
    sin = np.sin(ang).astype(f)
    sgn = np.where(half == 0, -1.0, 1.0).astype(f)[:, None]
    COS = np.concatenate([cos, np.ones((128, NCTX), f)], axis=1)
    SIN = np.concatenate([sin * sgn, np.zeros((128, NCTX), f)], axis=1)
    rope = np.stack([COS, SIN], axis=1)
    _CONST = {"consts": np.ascontiguousarray(consts), "masks": np.ascontiguousarray(masks),
              "rope": np.ascontiguousarray(rope)}
    return _CONST


def kernel(**inputs):
    inputs = {k: np.asarray(v) for k, v in inputs.items()}
    nc = bass.Bass("TRN2", target_bir_lowering=False)
    build(nc)
    owners = (0, 1, 4, 5)
    maps = {c: host_inputs(inputs, b) for b, c in enumerate(owners)}
    zero = {k: np.zeros_like(v) for k, v in maps[0].items()}
    in_maps = [maps.get(c, zero) for c in range(8)]
    res = run_bass_kernel_spmd(nc, in_maps, core_ids=list(range(8)))
    return np.stack([res.results[c]["out"] for c in owners], axis=0).astype(np.float32)
```

```python
import numpy as np
import ml_dtypes
import concourse.bass as bass
import concourse.mybir as mybir
from concourse.bass_utils import run_bass_kernel_spmd

F32 = mybir.dt.float32
BF16 = mybir.dt.bfloat16
AF = mybir.ActivationFunctionType
ALU = mybir.AluOpType
AX = mybir.AxisListType

D = 1024
SEQ = 4096
NCTX = 256
T = SEQ + NCTX
DEPTH = 2
INW = 5656
EPS = 1e-6
O_Z, O_X, O_B, O_C, O_DT, O_Q, O_K, O_V, O_GA, O_UA, O_UB, O_GC = 0, 768, 1536, 1792, 2048, 2072, 2840, 3096, 3352, 4120, 4632, 5144
NEG = -30000.0


class Buf:
    __slots__ = ("name", "w", "r", "excl", "semi", "dcount")

    def __init__(self, name, excl=False):
        self.name = name
        self.w = {}
        self.r = {}
        self.excl = excl
        self.semi = None
        self.dcount = 0


class Op:
    __slots__ = ("eng", "idx", "fn", "deps", "dma", "sig", "semi", "dval")


def _merge(dst, src):
    for k, v in src.items():
        if dst.get(k, -1) < v:
            dst[k] = v


class Rec:
    ENG = ("pe", "act", "dve", "pool", "sp")

    def __init__(self, nc):
        self.nc = nc
        self.ops = {e: [] for e in self.ENG}
        self.ndma_sems = 0
        self.free = []
        self.side = []
        self.side_every = 0
        self.side_cnt = 0
        self.capture = None
        self.semcount = {}
        self.eobj = {"pe": nc.tensor, "act": nc.scalar, "dve": nc.vector, "pool": nc.gpsimd, "sp": nc.sync}

    def add(self, eng, fn, reads=(), writes=(), dma=None, part=False, dinc=16):
        if dma is True:
            dma = writes[0]
        if self.capture is not None:
            assert dinc == 16
            self.capture.append((eng, fn, tuple(reads), tuple(writes), dma, part))
            return None
        op = Op()
        op.eng, op.fn, op.dma, op.sig = eng, fn, dma is not None, False
        op.idx = len(self.ops[eng])
        raw, oth = {}, {}
        for b in reads:
            _merge(raw, b.w)
            if b.excl:
                _merge(oth, b.r)
        for b in writes:
            _merge(oth, b.r)
            if not part or b.excl:
                _merge(oth, b.w)
        if dma is None:
            oth.pop(("c", eng), None)
            if eng == "pe":
                raw.pop(("c", eng), None)
        deps = raw
        _merge(deps, oth)
        op.deps = deps
        if dma is not None:
            b = dma
            if b.semi is None:
                if self.free:
                    b.semi, b.dcount = self.free.pop()
                    _merge(deps, {("d", b.semi): b.dcount})
                else:
                    b.semi = self.ndma_sems
                    self.ndma_sems += 1
            b.dcount += dinc
            self.semcount[b.semi] = b.dcount
            op.semi, op.dval = b.semi, dinc
            my = {("d", b.semi): b.dcount}
        else:
            my = {("c", eng): op.idx}
        for b in reads:
            _merge(b.r, my)
        for b in writes:
            if part:
                _merge(b.w, my)
            else:
                b.w = dict(my)
                b.r = {}
        self.ops[eng].append(op)
        if self.side and self.side_every:
            self.side_cnt += 1
            if self.side_cnt >= self.side_every:
                self.side_cnt = 0
                so = self.side.pop(0)
                ev, self.side_every = self.side_every, 0
                self.add(*so[:4], dma=so[4], part=so[5])
                self.side_every = ev
        return op

    def pipeline(self, bodies, width=2):
        lists = []
        for body in bodies:
            lst = []
            self.capture = lst
            body()
            self.capture = None
            lists.append((lst, 10 ** 9, 0))
        self.interleave(lists, width)

    def flush_side(self):
        self.side_every = 0
        while self.side:
            so = self.side.pop(0)
            self.add(*so[:4], dma=so[4], part=so[5])

    def interleave(self, lists, width=2):
        cur = [0] * len(lists)
        lo = 0
        n = len(lists)
        while lo < n:
            act = [i for i in range(lo, lo + width) if i < n]
            progressed = False
            for i in act:
                ops, need, done = lists[i]
                if cur[i] >= len(ops):
                    continue
                if i > lo and cur[i] >= need and cur[i - 1] < lists[i - 1][2]:
                    continue
                if i > lo and cur[i] == 0 and cur[i - 1] < max(1, len(lists[i - 1][0]) // width):
                    continue
                self.add(*ops[cur[i]][:4], dma=ops[cur[i]][4], part=ops[cur[i]][5])
                cur[i] += 1
                progressed = True
            while lo < n and cur[lo] >= len(lists[lo][0]):
                lo += 1
            assert progressed or lo >= n

    def release(self, bufs):
        for b in bufs:
            if b.semi is not None:
                self.free.append((b.semi, b.dcount))
                b.semi = None

    def barrier(self):
        deps = {("c", e): len(self.ops[e]) - 1 for e in self.ENG if self.ops[e]}
        for s_, c_ in self.semcount.items():
            deps[("d", s_)] = c_
        for e in self.ENG:
            op = Op()
            op.eng, op.fn, op.dma, op.sig, op.idx = e, None, False, False, len(self.ops[e])
            op.deps = {k: v for k, v in deps.items() if k != ("c", e)}
            self.ops[e].append(op)

    def emit(self):
        nc = self.nc
        for e in self.ENG:
            for op in self.ops[e]:
                nd = {}
                for k, v in op.deps.items():
                    if k[0] == "c":
                        lst = self.ops[k[1]]
                        while v >= 0 and lst[v].fn is None:
                            v -= 1
                        if v < 0:
                            continue
                        lst[v].sig = True
                    nd[k] = v
                op.deps = nd
        pref = {}
        for e in self.ENG:
            c = 0
            arr = []
            for op in self.ops[e]:
                if op.sig and not op.dma:
                    c += 1
                arr.append(c)
            pref[e] = arr
        assert self.ndma_sems + 5 <= 98, self.ndma_sems
        esem = {e: nc.alloc_semaphore(name="es_" + e) for e in self.ENG}
        dsem = [nc.alloc_semaphore(name="ds_%d" % i) for i in range(self.ndma_sems)]
        for e in self.ENG:
            eo = self.eobj[e]
            known = {}
            for op in self.ops[e]:
                for k, v in op.deps.items():
                    if k[0] == "c":
                        sem, val, key = esem[k[1]], pref[k[1]][v], k
                    else:
                        sem, val, key = dsem[k[1]], v, k
                    if known.get(key, 0) >= val:
                        continue
                    known[key] = val
                    eo.wait_ge(sem, val)
                if op.fn is None:
                    continue
                ins = op.fn()
                if op.dma:
                    ins.then_inc(dsem[op.semi], op.dval)
                elif op.sig:
                    ins.then_inc(esem[e], 1)


class SB:
    ARENA = None
    ABYTES = 204800

    def __init__(self, nc, base=0, limit=None):
        if SB.ARENA is None or SB.ARENA[0] is not nc:
            SB.ARENA = (nc, nc.alloc_sbuf_tensor("arena", [128, SB.ABYTES // 4], F32))
        self.nc, self.off, self.limit = nc, base, (limit or SB.ABYTES)

    def alloc(self, name, shape, dt):
        return self.at(name, shape, dt, None)

    def at(self, name, shape, dt, off):
        esz = 4 if dt == F32 else 2
        n = int(np.prod(shape[1:]))
        nb = (n * esz + 63) // 64 * 64
        if off is None:
            off = self.off
            self.off += nb
        assert off % 4 == 0 and off + nb <= self.limit, (name, off, nb, self.limit)
        A = SB.ARENA[1]
        ap = A[0:shape[0], off // 4: off // 4 + nb // 4]
        if dt != F32:
            ap = ap.bitcast(dt)
        ap = ap[:, 0:n]
        if len(shape) > 2:
            names = " ".join("d%d" % i for i in range(len(shape) - 1))
            kw = {"d%d" % i: int(shape[i + 1]) for i in range(len(shape) - 1)}
            ap = ap.rearrange("p (%s) -> p %s" % (names, names), **kw)
        return ap


def fap(ap, dims):
    return bass.AP(ap.tensor, ap.offset, [list(ap.ap[0])] + [list(d) for d in dims])


def build(nc, phases=99, nlayers=DEPTH, taps=()):
    R = Rec(nc)
    dram_in = {}

    def din(name, shape, dt=F32):
        dram_in[name] = nc.dram_tensor(name, list(shape), dt, kind="ExternalInput")
        return dram_in[name]

    x_in = din("x", [SEQ, D])
    ctx_in = din("ctx", [NCTX, D])
    cvec = din("cvec", [128, 8, 2])
    ada_w = din("ada_w", [DEPTH, D, 3 * D])
    ada_b_fm = din("ada_b_fm", [DEPTH, 128, 24])
    ada_b_gate = din("ada_b_gate", [DEPTH, D])
    norm_g_fm = din("norm_g_fm", [DEPTH, 128, 8])
    w_in = din("w_in", [DEPTH, D, INW])
    scw = din("ssd_conv_w_fm", [DEPTH, 128, 10, 5])
    scb = din("ssd_conv_b_fm", [DEPTH, 128, 10])
    dtb = din("ssd_dt_bias", [DEPTH, 24])
    alog = din("ssd_a_log", [DEPTH, 24])
    ssdd = din("ssd_d", [DEPTH, 12])
    ssdng = din("ssd_norm_g", [DEPTH, 768])
    qkg = din("qk_g_fm", [DEPTH, 128, 2])
    ccw = din("cm_conv_w_fm", [DEPTH, 128, 4, 31])
    cmv = din("cm_vec_fm", [DEPTH, 128, 4, 4])
    cpw = din("cm_pw_w", [DEPTH, 512, 512])
    w_out = din("w_out", [DEPTH, 2048, D])
    fng = din("final_norm_g", [D])
    consts = din("consts", [128, 6, 128])
    masks = din("masks", [128, 2, 128])
    rope = din("rope", [128, 2, T])
    out_d = nc.dram_tensor("out", [SEQ, D], F32, kind="ExternalOutput")

    def scratch(name, shape, dt):
        kind = "ExternalOutput" if name in taps else "Internal"
        return nc.dram_tensor(name, list(shape), dt, kind=kind)

    gate_d = scratch("gate_d", [DEPTH, 2, 128, D], F32)
    x1_d = scratch("x1_d", [T, D], F32)
    xbc_d = scratch("xbc_d", [1280, T], F32)
    dt_d = scratch("dt_d", [T, 24], F32)
    zs_d = scratch("zs_d", [T, 768], F32)
    qT_d = scratch("qT_d", [384, T], BF16)
    kT_d = scratch("kT_d", [128, T], BF16)
    v_d = scratch("v_d", [T, 128], BF16)
    gas_d = scratch("gas_d", [384, T], F32)
    att_half_l = [scratch("att_half_d%d" % i, [384, T], BF16) for i in range(DEPTH)]
    att_all_l = [scratch("att_all_d%d" % i, [8 * 384, T], BF16) for i in range(DEPTH)]
    B_att = Buf("att_half"); B_attall_l = [Buf("att_all%d" % i) for i in range(DEPTH)]; B_attcp = Buf("att_cp")
    cv_d = scratch("cv_d", [512, T], F32)
    gcs_d = scratch("gcs_d", [512, T], F32)
    cat_d = scratch("cat_d", [2048, T], BF16)
    hb_d = scratch("hb_d", [34, 128, 768], BF16)
    B_gate = Buf("gate_d"); B_x1 = Buf("x1_d"); B_xbc = Buf("xbc_d"); B_dt = Buf("dt_d"); B_zs = Buf("zs_d")
    B_qT = Buf("qT_d"); B_kT = Buf("kT_d"); B_v = Buf("v_d"); B_gas = Buf("gas_d"); B_cv = Buf("cv_d")
    B_gcs = Buf("gcs_d"); B_cat = Buf("cat_d"); B_hb = Buf("hb_d"); B_out = Buf("out")
    B_in = Buf("inputs")

    PS = [nc.alloc_psum_tensor("ps%d" % i, [128, 512], F32) for i in range(8)]
    PB = [Buf("psb%d" % i, excl=True) for i in range(8)]

    sbp = SB(nc, 0, 24 * 1024)
    c_t = sbp.alloc("consts", [128, 6, 128], F32)
    cb_t = sbp.alloc("constsb", [128, 6, 128], BF16)
    mk_t = sbp.alloc("masks", [128, 2, 128], F32)
    eps_t = sbp.alloc("eps", [128, 2], F32)
    AB_t = sbp.alloc("AB", [128, DEPTH, 2, 8, 2], F32)
    B_c = Buf("consts"); B_AB = Buf("AB")
    IDENT, UIN, UTR, PERM, BDM, ONES = range(6)
    R.add("sp", lambda: nc.sync.dma_start(out=c_t[:], in_=consts.ap()), [B_in], [B_c], dma=True)
    B_mk = Buf("mk")
    R.add("sp", lambda: nc.sync.dma_start(out=mk_t[:], in_=masks.ap()), [B_in], [B_mk], dma=True)
    B_cb = Buf("cb")
    R.add("dve", lambda: nc.vector.tensor_copy(out=cb_t[:], in_=c_t[:]), [B_c], [B_cb])
    B_eps = Buf("eps")
    R.add("pool", lambda: nc.gpsimd.memset(eps_t[:, 0:1], EPS), [], [B_eps])
    R.add("pool", lambda: nc.gpsimd.memset(eps_t[:, 1:2], 1.0), [], [B_eps], part=True)

    WORK0 = 24 * 1024

    def phase0():
        sb = SB(nc, WORK0)
        aw = sb.alloc("aw", [128, 8, 3072], F32)
        B_aw = [Buf("aw%d" % k) for k in range(8)]
        cv = sb.alloc("cv", [128, 8, 2], F32)
        sc = sb.alloc("sc", [128, 8, 2], F32)
        screp = sb.alloc("screp", [128, 8, 2, 128], F32)
        abf = sb.alloc("abf", [128, 24], F32)
        ngf = sb.alloc("ngf", [128, 8], F32)
        abg = sb.alloc("abg", [128, D], F32)
        mod = sb.alloc("mod", [128, 16, 2], F32)
        gt = sb.alloc("gt", [128, 2, D], F32)
        B_cv, B_sc, B_screp, B_abf, B_ngf, B_abg, B_mod, B_gt = [Buf(n) for n in "cv sc screp abf ngf abg mod gt".split()]
        R.add("sp", lambda: nc.sync.dma_start(out=cv[:], in_=cvec.ap()), [B_in], [B_cv], dma=True)
        R.add("act", lambda: nc.scalar.activation(out=sc[:], in_=cv[:], func=AF.Silu), [B_cv], [B_sc])
        for kc in range(8):
            for j in range(2):
                R.add("dve", lambda kc=kc, j=j: nc.vector.tensor_copy(
                    out=screp[:, kc, j, :], in_=fap(sc[:, kc, j:j + 1], [[0, 128]])), [B_sc], [B_screp], part=True)
        for li in range(nlayers):
            for kc in range(8):
                R.add("sp", lambda kc=kc, li=li: nc.sync.dma_start(
                    out=aw[:, kc, :], in_=ada_w[li, kc * 128:(kc + 1) * 128, :]), [B_in], [B_aw[kc]], dma=True)
            R.add("sp", lambda li=li: nc.sync.dma_start(out=abf[:], in_=ada_b_fm[li]), [B_in], [B_abf], dma=True)
            R.add("sp", lambda li=li: nc.sync.dma_start(out=ngf[:], in_=norm_g_fm[li]), [B_in], [B_ngf], dma=True)
            R.add("sp", lambda li=li: nc.sync.dma_start(
                out=abg[:], in_=bass.AP(ada_b_gate.ap().tensor, li * D, [[0, 128], [1, D]])), [B_in], [B_abg], dma=True)
            for fc in range(16):
                for kc in range(8):
                    R.add("pe", lambda fc=fc, kc=kc: nc.tensor.matmul(
                        PS[0][:, fc * 2:fc * 2 + 2], lhsT=aw[:, kc, fc * 128:(fc + 1) * 128], rhs=sc[:, kc, :],
                        start=(kc == 0), stop=(kc == 7)), [B_aw[kc], B_sc], [PB[0]], part=not (fc == 0 and kc == 0))
            R.add("dve", lambda: nc.vector.tensor_tensor(
                out=mod[:], in0=fap(PS[0][:, 0:32], [[2, 16], [1, 2]]), in1=fap(abf[:, 0:16], [[1, 16], [0, 2]]),
                op=ALU.add), [PB[0], B_abf], [B_mod])
            R.add("dve", lambda li=li: nc.vector.scalar_tensor_tensor(
                out=AB_t[:, li, 0, :, :], in0=mod[:, 8:16, :], scalar=1.0, in1=fap(ngf[:, 0:8], [[1, 8], [0, 2]]),
                op0=ALU.add, op1=ALU.mult), [B_mod, B_ngf], [B_AB], part=True)
            R.add("dve", lambda li=li: nc.vector.tensor_copy(out=AB_t[:, li, 1, :, :], in_=mod[:, 0:8, :]),
                  [B_mod], [B_AB], part=True)
            for j in range(2):
                for cc in range(2):
                    pb = 1 + (j * 2 + cc) % 2
                    for kc in range(8):
                        R.add("pe", lambda j=j, cc=cc, kc=kc, pb=pb: nc.tensor.matmul(
                            PS[pb][:, :], lhsT=screp[:, kc, j, :], rhs=aw[:, kc, 2048 + cc * 512:2048 + (cc + 1) * 512],
                            start=(kc == 0), stop=(kc == 7)), [B_screp, B_aw[kc]], [PB[pb]], part=(kc > 0))
                    R.add("dve", lambda j=j, cc=cc, pb=pb: nc.vector.tensor_tensor(
                        out=gt[:, j, cc * 512:(cc + 1) * 512], in0=PS[pb][:, :], in1=abg[:, cc * 512:(cc + 1) * 512],
                        op=ALU.add), [PB[pb], B_abg], [B_gt], part=not (j == 0 and cc == 0))
            for j in range(2):
                R.add("sp", lambda li=li, j=j: nc.sync.dma_start(out=gate_d[li, j], in_=gt[:, j, :]),
                      [B_gt], [B_gate], dma=B_gt, part=True)

    phase0()
    if phases <= 0:
        R.emit()
        return dram_in

    HT_OFF = WORK0
    hT = SB(nc).at("hT", [128, 8, T], BF16, HT_OFF)
    HT_BYTES = 8 * T * 2
    B_hT = [Buf("hT%d" % i) for i in range(34)]
    WORK1 = HT_OFF + HT_BYTES

    def phase1(li):
        sb = SB(nc, WORK1)
        xt = [sb.alloc("xt", [128, D], F32) for _ in range(4)]
        xn = [sb.alloc("xn", [128, D], F32) for _ in range(4)]
        junk = sb.alloc("junk", [128, D], BF16)
        st = [sb.alloc("st", [128, 4], F32) for _ in range(4)]
        B_xt = [Buf("xt%d" % i) for i in range(4)]
        B_xn = [Buf("xn%d" % i) for i in range(4)]
        B_junk = Buf("junk")
        B_st = [Buf("st%d" % i) for i in range(4)]
        bodies = []
        for tt in range(34):
            def body(tt=tt):
                s = tt % 4
                j = 0 if tt < 32 else 1
                if li == 0:
                    src = x_in[tt * 128:(tt + 1) * 128, :] if tt < 32 else ctx_in[(tt - 32) * 128:(tt - 31) * 128, :]
                    srcb = B_in
                else:
                    src = x1_d[tt * 128:(tt + 1) * 128, :]
                    srcb = B_x1
                R.add("sp", lambda s=s, src=src: nc.sync.dma_start(out=xt[s][:], in_=src), [srcb], [B_xt[s]], dma=True)
                R.add("act", lambda s=s: nc.scalar.activation(out=junk[:], in_=xt[s][:], func=AF.Square,
                                                              accum_out=st[s][:, 0:1]), [B_xt[s]], [B_junk, B_st[s]])
                R.add("act", lambda s=s: nc.scalar.activation(out=st[s][:, 1:2], in_=st[s][:, 0:1], func=AF.Sqrt,
                                                              bias=eps_t[:, 0:1], scale=1.0 / D), [B_st[s], B_eps], [B_st[s]], part=True)
                R.add("dve", lambda s=s: nc.vector.reciprocal(out=st[s][:, 2:3], in_=st[s][:, 1:2]), [B_st[s]], [B_st[s]], part=True)
                R.add("dve", lambda s=s: nc.vector.tensor_scalar(out=xn[s][:], in0=xt[s][:], scalar1=st[s][:, 2:3], scalar2=None,
                                                                 op0=ALU.mult), [B_xt[s], B_st[s]], [B_xn[s]])
                pb0 = 2 * (tt % 4)
                for c in range(8):
                    pb = pb0 + c // 4
                    R.add("pe", lambda s=s, c=c, pb=pb: nc.tensor.transpose(
                        PS[pb][:, (c % 4) * 128:(c % 4 + 1) * 128], xn[s][:, c * 128:(c + 1) * 128], c_t[:, IDENT, :]),
                        [B_xn[s], B_c], [PB[pb]], part=(c % 4 > 0))
                for c in range(8):
                    pb = pb0 + c // 4
                    if c % 2 == 0:
                        R.add("act", lambda c=c, pb=pb, tt=tt, j=j: nc.scalar.activation(
                            out=hT[:, c, tt * 128:(tt + 1) * 128], in_=PS[pb][:, (c % 4) * 128:(c % 4 + 1) * 128],
                            func=AF.Identity, bias=AB_t[:, li, 1, c, j:j + 1], scale=AB_t[:, li, 0, c, j:j + 1]),
                            [PB[pb], B_AB], [B_hT[tt]], part=True)
                    else:
                        R.add("dve", lambda c=c, pb=pb, tt=tt, j=j: nc.vector.tensor_scalar(
                            out=hT[:, c, tt * 128:(tt + 1) * 128], in0=PS[pb][:, (c % 4) * 128:(c % 4 + 1) * 128],
                            scalar1=AB_t[:, li, 0, c, j:j + 1], scalar2=AB_t[:, li, 1, c, j:j + 1],
                            op0=ALU.mult, op1=ALU.add), [PB[pb], B_AB], [B_hT[tt]], part=True)
            bodies.append(body)
        R.pipeline(bodies, 4)

    def wsrc(li, c0, ncol):
        return w_in[li, :, c0:c0 + ncol].rearrange("(kc p) n -> p kc n", p=128)

    def phaseA(li):
        sb = SB(nc, WORK1)
        wst = [sb.alloc("wst", [128, 8, 128], F32) for _ in range(2)]
        wbf = [sb.alloc("wbf", [128, 8, 128], BF16) for _ in range(2)]
        B_wst = [Buf("wst%d" % i) for i in range(2)]
        B_wbf = [Buf("wbf%d" % i) for i in range(2)]
        ot = [sb.alloc("ot", [128, 512], F32) for _ in range(3)]
        B_ot = [Buf("ot%d" % i) for i in range(3)]
        obt = [sb.alloc("obt", [128, 512], BF16) for _ in range(2)]
        B_obt = [Buf("obt%d" % i) for i in range(2)]
        vecs = sb.alloc("vecs", [128, 10 * 5 + 10 + 2 + 4 * 31 + 16], F32)
        B_vecs = Buf("vecs")
        V_SCW, V_SCB, V_QKG, V_CCW, V_CMV = 0, 50, 60, 62, 62 + 124
        R.add("sp", lambda: nc.sync.dma_start(out=vecs[:, V_SCW:V_SCW + 50], in_=scw[li].rearrange("p a b -> p (a b)")), [B_in], [B_vecs], dma=True)
        R.add("sp", lambda: nc.sync.dma_start(out=vecs[:, V_SCB:V_SCB + 10], in_=scb[li]), [B_in], [B_vecs], dma=True, part=True)
        R.add("sp", lambda: nc.sync.dma_start(out=vecs[:, V_QKG:V_QKG + 2], in_=qkg[li]), [B_in], [B_vecs], dma=True, part=True)
        R.add("sp", lambda: nc.sync.dma_start(out=vecs[:, V_CCW:V_CCW + 124], in_=ccw[li].rearrange("p a b -> p (a b)")), [B_in], [B_vecs], dma=True, part=True)
        R.add("sp", lambda: nc.sync.dma_start(out=vecs[:, V_CMV:V_CMV + 16], in_=cmv[li].rearrange("p a b -> p (a b)")), [B_in], [B_vecs], dma=True, part=True)
        sub0 = sb.off
        cnt = {"w": 0, "ot": 0, "obt": 0, "ps": 0}

        def wload(c0):
            s_ = cnt["w"] % 2
            cnt["w"] += 1
            R.add("sp", lambda: nc.sync.dma_start(out=wst[s_][:], in_=wsrc(li, c0, 128)), [B_in], [B_wst[s_]], dma=True)
            R.add("pool", lambda: nc.gpsimd.tensor_copy(out=wbf[s_][:], in_=wst[s_][:]), [B_wst[s_]], [B_wbf[s_]])
            return s_

        def proj(ws, j, pb):
            w_ = 512 if j < 8 else 256
            for kc in range(8):
                R.add("pe", lambda kc=kc: nc.tensor.matmul(PS[pb][:, 0:w_], lhsT=wbf[ws][:, kc, :], rhs=hT[:, kc, j * 512:j * 512 + w_],
                                                          start=(kc == 0), stop=(kc == 7)),
                      [B_wbf[ws]] + B_hT[j * 4:j * 4 + w_ // 128], [PB[pb]], part=(kc > 0))
            return w_

        def next_ot():
            s_ = cnt["ot"] % 3
            cnt["ot"] += 1
            return s_

        def next_obt():
            s_ = cnt["obt"] % 2
            cnt["obt"] += 1
            return s_

        for (c0, nch, dst, B_dst) in (((O_GA, 3, gas_d, B_gas), (O_GC, 4, gcs_d, B_gcs)) if "noA1" not in taps else ()):
            for ch in range(nch):
                ws = wload(c0 + ch * 128)
                for j in range(9):
                    pb = cnt["ps"] % 2
                    cnt["ps"] += 1
                    w_ = proj(ws, j, pb)
                    o_ = next_ot()
                    R.add("act", lambda pb=pb, o_=o_, w_=w_: nc.scalar.activation(out=ot[o_][:, 0:w_], in_=PS[pb][:, 0:w_], func=AF.Silu),
                          [PB[pb]], [B_ot[o_]])
                    R.add("sp", lambda o_=o_, w_=w_, ch=ch, j=j, dst=dst: nc.sync.dma_start(
                        out=dst[ch * 128:(ch + 1) * 128, j * 512:j * 512 + w_], in_=ot[o_][:, 0:w_]), [B_ot[o_]], [B_dst], dma=B_ot[o_], part=True)

        sbq = SB(nc, sub0)
        ropet = sbq.alloc("rope", [128, 2, T], F32)
        B_rope = Buf("rope")
        R.add("sp", lambda: nc.sync.dma_start(out=ropet[:, 0, :], in_=rope[:, 0, :]), [B_in], [B_rope], dma=True)
        R.add("sp", lambda: nc.sync.dma_start(out=ropet[:, 1, :], in_=rope[:, 1, :]), [B_in], [B_rope], dma=True, part=True)
        P2 = range(4)
        obt = [sbq.alloc("obtq", [128, 512], BF16) for _ in P2]; B_obt = [Buf("obtq%d" % i) for i in P2]
        sqb = [sbq.alloc("sqb", [128, 512], BF16) for _ in P2]; B_sqb = [Buf("sqb%d" % i) for i in P2]
        sd = [sbq.alloc("sd", [128, 512], F32) for _ in P2]; B_sd = [Buf("sd%d" % i) for i in P2]
        rs = [sbq.alloc("rs", [128, 512], F32) for _ in P2]; B_rs = [Buf("rs%d" % i) for i in P2]
        qnb = [sbq.alloc("qnb", [128, 512], BF16) for _ in P2]; B_qnb = [Buf("qnb%d" % i) for i in P2]
        t1 = [sbq.alloc("t1", [128, 512], F32) for _ in P2]; B_t1 = [Buf("t1%d" % i) for i in P2]
        t2 = [sbq.alloc("t2", [128, 512], F32) for _ in P2]; B_t2 = [Buf("t2%d" % i) for i in P2]
        lists = []
        tile_no = 0
        qk_chunks = []
        for (c0, nch, dst, B_dst, gi) in (((O_Q, 3, qT_d, B_qT, 0), (O_K, 1, kT_d, B_kT, 1)) if "noA2" not in taps else ()):
            for ch in range(nch):
                qk_chunks.append((c0 + ch * 128, ch, dst, B_dst, gi))
        ws_of = {}
        for kq, (wc0, ch, dst, B_dst, gi) in enumerate(qk_chunks):
            if True:
                for j in range(9):
                    pp = tile_no % 4
                    tile_no += 1
                    ws = None

                    def body(kq=kq, j=j, pp=pp, ch=ch, dst=dst, B_dst=B_dst, gi=gi):
                        P0, P1, P2_ = 2 * pp, 2 * pp + 1, 2 * pp + 1
                        if j == 0 and kq == 0:
                            ws_of[0] = wload(qk_chunks[0][0])
                        if j == 4 and kq + 1 < len(qk_chunks):
                            ws_of[kq + 1] = wload(qk_chunks[kq + 1][0])
                        ws = ws_of[kq]
                        w_ = proj(ws, j, P0)
                        cs = slice(j * 512, j * 512 + w_)
                        R.add("act", lambda: nc.scalar.activation(out=sqb[pp][:, 0:w_], in_=PS[P0][:, 0:w_], func=AF.Square), [PB[P0]], [B_sqb[pp]])
                        R.add("pe", lambda: nc.tensor.matmul(PS[P1][:, 0:w_], lhsT=cb_t[:, BDM, :], rhs=sqb[pp][:, 0:w_], start=True, stop=True),
                              [B_cb, B_sqb[pp]], [PB[P1]])
                        R.add("act", lambda: nc.scalar.activation(out=sd[pp][:, 0:w_], in_=PS[P1][:, 0:w_], func=AF.Ln, bias=eps_t[:, 0:1], scale=1.0),
                              [PB[P1], B_eps], [B_sd[pp]])
                        R.add("act", lambda: nc.scalar.activation(out=rs[pp][:, 0:w_], in_=sd[pp][:, 0:w_], func=AF.Exp, scale=-0.5), [B_sd[pp]], [B_rs[pp]])
                        R.add("dve", lambda: nc.vector.scalar_tensor_tensor(
                            out=qnb[pp][:, 0:w_], in0=PS[P0][:, 0:w_], scalar=vecs[:, V_QKG + gi:V_QKG + gi + 1], in1=rs[pp][:, 0:w_], op0=ALU.mult, op1=ALU.mult),
                            [PB[P0], B_vecs, B_rs[pp]], [B_qnb[pp]])
                        R.add("pe", lambda: nc.tensor.matmul(PS[P2_][:, 0:w_], lhsT=cb_t[:, PERM, :], rhs=qnb[pp][:, 0:w_], start=True, stop=True),
                              [B_cb, B_qnb[pp]], [PB[P2_]])
                        R.add("pool", lambda: nc.gpsimd.tensor_tensor(out=t1[pp][:, 0:w_], in0=qnb[pp][:, 0:w_], in1=ropet[:, 0, cs], op=ALU.mult),
                              [B_qnb[pp], B_rope], [B_t1[pp]])
                        R.add("dve", lambda: nc.vector.tensor_tensor(out=t2[pp][:, 0:w_], in0=PS[P2_][:, 0:w_], in1=ropet[:, 1, cs], op=ALU.mult),
                              [PB[P2_], B_rope], [B_t2[pp]])
                        R.add("dve", lambda: nc.vector.tensor_tensor(out=obt[pp][:, 0:w_], in0=t1[pp][:, 0:w_], in1=t2[pp][:, 0:w_], op=ALU.add),
                              [B_t1[pp], B_t2[pp]], [B_obt[pp]])
                        R.add("sp", lambda: nc.sync.dma_start(out=dst[ch * 128:(ch + 1) * 128, cs], in_=obt[pp][:, 0:w_]), [B_obt[pp]], [B_dst], dma=B_obt[pp], part=True)
                    lst = []
                    R.capture = lst
                    body()
                    R.capture = None
                    lists.append((lst, 10 ** 9, 0))
        R.interleave(lists, 4)
        R.barrier()

        sbx = SB(nc, sub0)
        dg5 = sbx.alloc("dg5", [128, 10, 5, 128], BF16); B_dg5 = Buf("dg5")
        RBW = 2 + SEQ + 2 + 2 + NCTX + 2
        rb = [sbx.alloc("rb", [128, RBW], BF16) for _ in range(2)]
        B_rb = [Buf("rb%d" % i) for i in range(2)]
        for ch in range(10):
            for k in range(5):
                R.add("dve", lambda ch=ch, k=k: nc.vector.tensor_scalar(
                    out=dg5[:, ch, k, :], in0=c_t[:, IDENT, :], scalar1=vecs[:, V_SCW + ch * 5 + k:V_SCW + ch * 5 + k + 1], scalar2=None, op0=ALU.mult),
                    [B_c, B_vecs], [B_dg5], part=True)
        for i in range(2):
            R.add("pool", lambda i=i: nc.gpsimd.memset(rb[i][:], 0.0), [], [B_rb[i]])

        def rbcol(j, pad):
            return pad + j * 512 if j < 8 else pad + SEQ + 2 * pad

        def xproj(ch):
            ws = wload(O_X + ch * 128)
            r_ = ch % 2
            for j in range(9):
                pb = cnt["ps"] % 2
                cnt["ps"] += 1
                w_ = proj(ws, j, pb)
                c0_ = rbcol(j, 2)
                R.add("act", lambda pb=pb, w_=w_, c0_=c0_: nc.scalar.activation(out=rb[r_][:, c0_:c0_ + w_], in_=PS[pb][:, 0:w_], func=AF.Copy),
                      [PB[pb]], [B_rb[r_]], part=(j > 0))

        def xconv(ch):
            r_ = ch % 2
            for j in range(9):
                w_ = 512 if j < 8 else 256
                pb = 2 + j % 2
                st_ = rbcol(j, 2) - 2
                for k in range(5):
                    R.add("pe", lambda k=k, pb=pb, w_=w_, st_=st_: nc.tensor.matmul(
                        PS[pb][:, 0:w_], lhsT=dg5[:, ch, k, :], rhs=rb[r_][:, st_ + k:st_ + k + w_], start=(k == 0), stop=(k == 4)),
                        [B_dg5, B_rb[r_]], [PB[pb]], part=(k > 0))
                o_ = next_ot()
                R.add("act", lambda pb=pb, o_=o_, w_=w_: nc.scalar.activation(
                    out=ot[o_][:, 0:w_], in_=PS[pb][:, 0:w_], func=AF.Silu, bias=vecs[:, V_SCB + ch:V_SCB + ch + 1], scale=1.0),
                    [PB[pb], B_vecs], [B_ot[o_]])
                R.add("sp", lambda o_=o_, w_=w_, j=j: nc.sync.dma_start(
                    out=xbc_d[ch * 128:(ch + 1) * 128, j * 512:j * 512 + w_], in_=ot[o_][:, 0:w_]), [B_ot[o_]], [B_xbc], dma=B_ot[o_], part=True)

        if "noA3" not in taps:
            xproj(0)
        for ch in (range(10) if "noA3" not in taps else ()):
            if ch + 1 < 10:
                xproj(ch + 1)
            xconv(ch)
        R.barrier()

        sbc = SB(nc, sub0)
        dg31 = sbc.alloc("dg31", [128, 4, 31, 128], BF16); B_dg31 = Buf("dg31")
        RUW = 15 + SEQ + 15 + 15 + NCTX + 15
        ru = [sbc.alloc("ru", [128, RUW], BF16) for _ in range(2)]
        B_ru = [Buf("ru%d" % i) for i in range(2)]
        sg = [sbc.alloc("sg", [128, 512], F32) for _ in range(2)]
        B_sg = [Buf("sg%d" % i) for i in range(2)]
        for ch in range(4):
            for k in range(31):
                eng, eo = ("dve", nc.vector) if k % 2 == 0 else ("pool", nc.gpsimd)
                R.add(eng, lambda ch=ch, k=k, eo=eo: eo.tensor_scalar(
                    out=dg31[:, ch, k, :], in0=c_t[:, IDENT, :], scalar1=vecs[:, V_CCW + ch * 31 + k:V_CCW + ch * 31 + k + 1], scalar2=None, op0=ALU.mult),
                    [B_c, B_vecs], [B_dg31], part=True)
        for i in range(2):
            R.add("pool", lambda i=i: nc.gpsimd.memset(ru[i][:], 0.0), [], [B_ru[i]])

        def uproj(ch):
            wa = wload(O_UA + ch * 128)
            wb = wload(O_UB + ch * 128)
            r_ = ch % 2
            for j in range(9):
                w_ = proj(wa, j, 0)
                proj(wb, j, 1)
                s_ = j % 2
                R.add("act", lambda s_=s_, w_=w_: nc.scalar.activation(out=sg[s_][:, 0:w_], in_=PS[1][:, 0:w_], func=AF.Sigmoid), [PB[1]], [B_sg[s_]])
                c0_ = rbcol(j, 15)
                R.add("dve", lambda s_=s_, w_=w_, c0_=c0_: nc.vector.tensor_tensor(
                    out=ru[r_][:, c0_:c0_ + w_], in0=PS[0][:, 0:w_], in1=sg[s_][:, 0:w_], op=ALU.mult), [PB[0], B_sg[s_]], [B_ru[r_]], part=(j > 0))

        def uconv(ch):
            r_ = ch % 2
            for j in range(9):
                w_ = 512 if j < 8 else 256
                pb = 2 + j % 2
                st_ = rbcol(j, 15) - 15
                for k in range(31):
                    R.add("pe", lambda k=k, pb=pb, w_=w_, st_=st_: nc.tensor.matmul(
                        PS[pb][:, 0:w_], lhsT=dg31[:, ch, k, :], rhs=ru[r_][:, st_ + k:st_ + k + w_], start=(k == 0), stop=(k == 30)),
                        [B_dg31, B_ru[r_]], [PB[pb]], part=(k > 0))
                o_ = next_ot()
                R.add("act", lambda pb=pb, o_=o_, w_=w_: nc.scalar.activation(
                    out=ot[o_][:, 0:w_], in_=PS[pb][:, 0:w_], func=AF.Identity, bias=vecs[:, V_CMV + ch * 4:V_CMV + ch * 4 + 1], scale=1.0),
                    [PB[pb], B_vecs], [B_ot[o_]])
                R.add("sp", lambda o_=o_, w_=w_, j=j: nc.sync.dma_start(
                    out=cv_d[ch * 128:(ch + 1) * 128, j * 512:j * 512 + w_], in_=ot[o_][:, 0:w_]), [B_ot[o_]], [B_cv], dma=B_ot[o_], part=True)

        if "noA4" not in taps:
            uproj(0)
        for ch in (range(4) if "noA4" not in taps else ()):
            if ch + 1 < 4:
                uproj(ch + 1)
            uconv(ch)
        R.barrier()

        sbt = SB(nc, sub0)
        NZ = 768 + 24 + 128
        wzs = sbt.alloc("wzs", [128, 8, 384], F32); B_wzs = Buf("wzs")
        wz = sbt.alloc("wz", [128, 8, NZ], BF16); B_wz = Buf("wz")
        for (dc, c0, n_) in ((0, O_Z, 384), (384, O_Z + 384, 384), (768, O_DT, 24), (792, O_V, 128)):
            R.add("sp", lambda c0=c0, n_=n_: nc.sync.dma_start(out=wzs[:, :, 0:n_], in_=wsrc(li, c0, n_)), [B_in], [B_wzs], dma=True)
            R.add("pool", lambda dc=dc, n_=n_: nc.gpsimd.tensor_copy(out=wz[:, :, dc:dc + n_], in_=wzs[:, :, 0:n_]), [B_wzs], [B_wz], part=(dc > 0))
        zt = [sbt.alloc("zt", [128, 768], F32) for _ in range(2)]
        B_zt = [Buf("zt%d" % i) for i in range(2)]
        vt = [sbt.alloc("vt", [128, 128], BF16) for _ in range(2)]
        B_vt = [Buf("vt%d" % i) for i in range(2)]
        dta = sbt.alloc("dta", [128, 34, 24], F32); B_dta = Buf("dta")
        dtw = sbt.alloc("dtw", [128, 3, 34 * 24], F32); B_dtw = Buf("dtw")
        dtbt = sbt.alloc("dtbt", [128, 24], F32); B_dtbt = Buf("dtbt")
        R.add("sp", lambda: nc.sync.dma_start(out=dtbt[:], in_=bass.AP(dtb.ap().tensor, li * 24, [[0, 128], [1, 24]])), [B_in], [B_dtbt], dma=True)
        for tt in (range(34) if "noA5" not in taps else ()):
            s_ = tt % 2
            pbs = (0, 1, 2) if tt % 2 == 0 else (3, 4, 5)
            for (pb, dc, n_) in ((pbs[0], 0, 384), (pbs[1], 384, 384), (pbs[2], 768, 152)):
                for kc in range(8):
                    R.add("pe", lambda kc=kc, pb=pb, dc=dc, n_=n_, tt=tt: nc.tensor.matmul(
                        PS[pb][:, 0:n_], lhsT=hT[:, kc, tt * 128:(tt + 1) * 128], rhs=wz[:, kc, dc:dc + n_], start=(kc == 0), stop=(kc == 7)),
                        [B_hT[tt], B_wz], [PB[pb]], part=(kc > 0))
            for hf_ in range(2):
                R.add("act", lambda hf_=hf_, s_=s_, pbs=pbs: nc.scalar.activation(
                    out=zt[s_][:, hf_ * 384:(hf_ + 1) * 384], in_=PS[pbs[hf_]][:, 0:384], func=AF.Silu), [PB[pbs[hf_]]], [B_zt[s_]], part=(hf_ > 0))
            R.add("sp", lambda s_=s_, tt=tt: nc.sync.dma_start(out=zs_d[tt * 128:(tt + 1) * 128, :], in_=zt[s_][:]), [B_zt[s_]], [B_zs], dma=B_zt[s_], part=True)
            R.add("dve", lambda s_=s_, pbs=pbs: nc.vector.tensor_copy(out=vt[s_][:], in_=PS[pbs[2]][:, 24:152]), [PB[pbs[2]]], [B_vt[s_]])
            R.add("sp", lambda s_=s_, tt=tt: nc.sync.dma_start(out=v_d[tt * 128:(tt + 1) * 128, :], in_=vt[s_][:]), [B_vt[s_]], [B_v], dma=B_vt[s_], part=True)
            R.add("dve", lambda tt=tt, pbs=pbs: nc.vector.tensor_tensor(out=dta[:, tt, :], in0=PS[pbs[2]][:, 0:24], in1=dtbt[:], op=ALU.add),
                  [PB[pbs[2]], B_dtbt], [B_dta], part=True)
        dflat = dta.rearrange("p a b -> p (a b)")
        R.add("act", lambda: nc.scalar.activation(out=dtw[:, 0, :], in_=dflat, func=AF.Abs), [B_dta], [B_dtw])
        R.add("act", lambda: nc.scalar.activation(out=dtw[:, 1, :], in_=dtw[:, 0, :], func=AF.Exp, scale=-1.0), [B_dtw], [B_dtw], part=True)
        R.add("act", lambda: nc.scalar.activation(out=dtw[:, 2, :], in_=dtw[:, 1, :], func=AF.Ln, bias=eps_t[:, 1:2], scale=1.0), [B_dtw, B_eps], [B_dtw], part=True)
        R.add("dve", lambda: nc.vector.scalar_tensor_tensor(out=dtw[:, 0, :], in0=dflat, scalar=0.0, in1=dtw[:, 2, :], op0=ALU.max, op1=ALU.add),
              [B_dta, B_dtw], [B_dtw], part=True)
        R.add("sp", lambda: nc.sync.dma_start(out=dt_d.ap().rearrange("(a p) h -> p a h", p=128),
                                              in_=dtw[:, 0, :].rearrange("p (a h) -> p a h", h=24)), [B_dtw], [B_dt], dma=B_dtw)
        R.barrier()
        R.release(B_wst + B_ot + B_obt + [B_vecs, B_rope, B_wzs, B_dtbt, B_dtw] + B_zt + B_vt)

    def phaseB(li, last, side=None):
        att_half_d, att_all_d = att_half_l[li], att_all_l[li]
        B_attall = B_attall_l[li]
        sb = SB(nc, WORK0, limit=P1_BASE)
        KT = sb.alloc("KT", [128, 2, T], BF16); B_KT = Buf("KT")
        VA = sb.alloc("VA", [128, 34, 2, 128], BF16); B_VA = Buf("VA")
        qs = [sb.alloc("qs", [128, 3, 512], BF16) for _ in range(2)]
        B_qs = [Buf("qs%d" % i) for i in range(2)]
        pT = [sb.alloc("pT", [128, 512], BF16) for _ in range(4)]
        B_pT = [Buf("pT%d" % i) for i in range(4)]
        rsb = [sb.alloc("rsb", [128, 512], F32) for _ in range(2)]
        B_rsb = [Buf("rsb%d" % i) for i in range(2)]
        ob = [sb.alloc("ob", [128, 512], F32) for _ in range(2)]
        B_ob = [Buf("ob%d" % i) for i in range(2)]
        gs = [sb.alloc("gs", [128, 512], F32) for _ in range(2)]
        B_gs = [Buf("gs%d" % i) for i in range(2)]
        obb = [sb.alloc("obb", [128, 512], BF16) for _ in range(2)]
        B_obb = [Buf("obb%d" % i) for i in range(2)]
        R.add("pool", lambda: nc.gpsimd.memset(KT[64:128, :, :], 0.0), [], [B_KT])
        R.add("sp", lambda: nc.sync.dma_start(out=KT[0:64, :, :], in_=kT_d.ap().rearrange("(g d) t -> d g t", d=64)), [B_kT], [B_KT], dma=True, part=True)
        for i in range(2):
            R.add("pool", lambda i=i: nc.gpsimd.memset(qs[i][64:128, :, :], 0.0), [], [B_qs[i]])
        R.add("pool", lambda: nc.gpsimd.memset(VA[:, :, :, 64:128], 1.0), [], [B_VA])
        for tt in range(34):
            R.add("sp", lambda tt=tt: nc.sync.dma_start(out=VA[:, tt, :, 0:64], in_=v_d[tt * 128:(tt + 1) * 128, :].rearrange("p (g d) -> p g d", d=64)),
                  [B_v], [B_VA], dma=True, part=True)
        fin = 0
        if side:
            R.side, R.side_every, R.side_cnt = list(side), 4, 0
        for j in range(9):
            if last and j == 8:
                continue
            w_ = 512 if j < 8 else 256
            kts = list(range(34)) if j < 8 else [32, 33]
            for g in range(2):
                pbase = (g % 2) * 64
                qi = (j * 2 + g) % 2
                R.add("sp", lambda g=g, j=j, w_=w_, qi=qi, pbase=pbase: nc.sync.dma_start(
                    out=qs[qi][0:64, :, 0:w_],
                    in_=qT_d[g * 192:(g + 1) * 192, j * 512:j * 512 + w_].rearrange("(h d) t -> d h t", d=64)), [B_qT], [B_qs[qi]], dma=True, part=True)
                steps = [(kt, hh) for kt in kts for hh in range(3)]
                n = len(steps)

                def S(s_):
                    kt, hh = steps[s_]
                    bk = s_ % 3
                    pk = s_ % 4
                    R.add("pe", lambda kt=kt, hh=hh, bk=bk, g=g, pbase=pbase, qi=qi, w_=w_: nc.tensor.matmul(PS[bk][:, 0:w_], lhsT=KT[:, g, kt * 128:(kt + 1) * 128],
                                                         rhs=qs[qi][:, hh, 0:w_], start=True, stop=True), [B_KT, B_qs[qi]], [PB[bk]])
                    R.add("act", lambda bk=bk, pk=pk, w_=w_: nc.scalar.activation(out=pT[pk][:, 0:w_], in_=PS[bk][:, 0:w_], func=AF.Exp, scale=0.125), [PB[bk]], [B_pT[pk]])

                def PV(s_):
                    kt, hh = steps[s_]
                    bk = s_ % 4
                    R.add("pe", lambda kt=kt, hh=hh, bk=bk, g=g, w_=w_, k0=kts[0], k1=kts[-1]: nc.tensor.matmul(
                        PS[4 + hh][:, 0:w_], lhsT=VA[:, kt, g, :], rhs=pT[bk][:, 0:w_],
                        start=(kt == k0), stop=(kt == k1)), [B_VA, B_pT[bk]], [PB[4 + hh]], part=(kt != kts[0]))

                for s_ in range(n + 2):
                    if s_ < n:
                        S(s_)
                    if s_ >= 2:
                        PV(s_ - 2)
                for hh in range(3):
                    h_ = g * 3 + hh
                    f_ = fin % 2
                    fin += 1
                    R.add("sp", lambda h_=h_, f_=f_, j=j, w_=w_: nc.sync.dma_start(
                        out=gs[f_][0:64, 0:w_], in_=gas_d[h_ * 64:(h_ + 1) * 64, j * 512:j * 512 + w_]), [B_gas], [B_gs[f_]], dma=True)
                    R.add("act", lambda hh=hh, f_=f_, w_=w_: nc.scalar.activation(out=rsb[f_][64:128, 0:w_], in_=PS[4 + hh][64:128, 0:w_], func=AF.Ln), [PB[4 + hh]], [B_rsb[f_]])
                    R.add("act", lambda f_=f_, w_=w_: nc.scalar.activation(out=rsb[f_][64:128, 0:w_], in_=rsb[f_][64:128, 0:w_], func=AF.Exp, scale=-1.0), [B_rsb[f_]], [B_rsb[f_]])
                    R.add("dve", lambda hh=hh, f_=f_, w_=w_: nc.vector.tensor_tensor(out=ob[f_][0:64, 0:w_], in0=PS[4 + hh][0:64, 0:w_], in1=rsb[f_][64:128, 0:w_], op=ALU.mult),
                          [PB[4 + hh], B_rsb[f_]], [B_ob[f_]])
                    R.add("pool", lambda f_=f_, w_=w_: nc.gpsimd.tensor_tensor(out=obb[f_][0:64, 0:w_], in0=ob[f_][0:64, 0:w_], in1=gs[f_][0:64, 0:w_], op=ALU.mult),
                          [B_ob[f_], B_gs[f_]], [B_obb[f_]])
                    R.add("sp", lambda h_=h_, f_=f_, j=j, w_=w_: nc.sync.dma_start(
                        out=att_half_d[h_ * 64:(h_ + 1) * 64, j * 512:j * 512 + w_], in_=obb[f_][0:64, 0:w_]), [B_obb[f_]], [B_att], dma=B_obb[f_], part=True)
        R.flush_side()
        if "nocc" in taps:
            R.add("sp", lambda: nc.sync.dma_start(out=cat_d[768:1152, :], in_=att_half_d.ap()), [B_att], [B_cat], dma=B_attcp, part=True)
            R.barrier()
            R.release([B_KT, B_VA] + B_qs + B_gs + B_obb)
            return
        R.barrier()
        R.add("pool", lambda: nc.gpsimd.collective_compute("AllGather", ALU.bypass, replica_groups=[list(range(8))],
                                                           ins=[att_half_d.ap().opt()], outs=[att_all_d.ap().opt()]),
              [B_att], [B_attall], dma=B_attall, dinc=1)

        def pair_copy():
            pid = nc.gpsimd.partition_id()
            base = (pid // 2) * 768
            return nc.gpsimd.dma_start(out=cat_d[768:1536, :], in_=att_all_d[bass.ds(base, 768), :])
        R.add("pool", pair_copy, [B_attall], [B_cat], dma=B_attcp, part=True)
        R.barrier()
        R.release([B_KT, B_VA] + B_qs + B_gs + B_obb)

    def bc12(ap2d, n=64):
        return fap(ap2d, [[1, ap2d.shape[1]], [0, n]])

    def v3(ap2d, a, b):
        return fap(ap2d, [[b, a], [1, b]])

    P1_BASE = SB.ABYTES - 46 * 1024

    def phaseC(li, last, mode="p2"):
        full_ = (mode == "p2")
        sb = SB(nc, WORK0) if full_ else SB(nc, P1_BASE)
        Abc = sb.alloc("Abc", [128, 24], F32); B_Abc = Buf("Abc")
        Dbc = sb.alloc("Dbc", [128, 12], F32); B_Dbc = Buf("Dbc")
        Gbc = sb.alloc("Gbc", [128, 768], F32) if full_ else None; B_Gbc = Buf("Gbc")
        mrep = sb.alloc("mrep", [128, 2, 4, 128], BF16) if full_ else None; B_mrep = Buf("mrep")
        R.add("sp", lambda: nc.sync.dma_start(out=Abc[:], in_=bass.AP(alog.ap().tensor, li * 24, [[0, 128], [1, 24]])), [B_in], [B_Abc], dma=True)
        R.add("act", lambda: nc.scalar.activation(out=Abc[:], in_=Abc[:], func=AF.Exp), [B_Abc], [B_Abc])
        R.add("dve", lambda: nc.vector.tensor_scalar(out=Abc[:], in0=Abc[:], scalar1=-1.0, scalar2=None, op0=ALU.mult), [B_Abc], [B_Abc])
        R.add("sp", lambda: nc.sync.dma_start(out=Dbc[:], in_=bass.AP(ssdd.ap().tensor, li * 12, [[0, 128], [1, 12]])), [B_in], [B_Dbc], dma=True)
        if full_:
            R.add("sp", lambda: nc.sync.dma_start(out=Gbc[:], in_=bass.AP(ssdng.ap().tensor, li * 768, [[0, 128], [1, 768]])), [B_in], [B_Gbc], dma=True)
        for dr in (range(2) if full_ else ()):
            R.add("dve", lambda dr=dr: nc.vector.tensor_copy(out=mrep[:, dr, :, :], in_=fap(mk_t[:, dr, :], [[0, 4], [1, 128]])), [B_mk], [B_mrep], part=(dr > 0))
        P2 = range(2)

        def mk(name, shape, dt, n=2, always=False):
            if not (full_ or always):
                return [None] * n, [Buf("%s%d" % (name, i)) for i in range(n)]
            return [sb.alloc(name, shape, dt) for _ in range(n)], [Buf("%s%d" % (name, i)) for i in range(n)]
        _mk = mk
        mk = lambda name, shape, dt, n=2: _mk(name, shape, dt, n, name in ("xT", "bcT", "dtc", "xs", "Btm", "av", "ct", "ex", "dtd", "tmp", "hst", "hbb"))
        xT, B_xT = mk("xT", [128, 6, 128], F32)
        bcT, B_bcT = mk("bcT", [128, 4, 128], F32)
        dtc, B_dtc = mk("dtc", [128, 24], F32)
        zc, B_zc = mk("zc", [128, 768], F32)
        hbin, B_hbin = mk("hbin", [128, 768], BF16)
        xs, B_xs = mk("xs", [128, 768], F32)
        Btm, B_Btm = mk("Btm", [128, 256], BF16)
        bcb, B_bcb = mk("bcb", [128, 4, 128], BF16)
        av, B_av = mk("av", [128, 24], BF16)
        ct, B_ct = mk("ct", [128, 72], F32)
        ex, B_ex = mk("ex", [128, 72], F32)
        ncum, B_ncum = mk("ncum", [128, 24], F32)
        dtd, B_dtd = mk("dtd", [128, 24], F32)
        xw = [[(sb.alloc("xw", [128, 768], BF16) if (full_ or k == 1) else None) for k in range(4)] for _ in P2]
        B_xw = [[Buf("xw%d_%d" % (i, k)) for k in range(4)] for i in P2]
        cbs, B_cbs = mk("cbs", [128, 2, 128], BF16)
        Dm = [[(sb.alloc("Dm", [128, 12, 128], BF16) if full_ else None) for _ in P2] for _ in P2]; B_Dm = [[Buf("Dm%d%d" % (i, k)) for k in P2] for i in P2]
        Et = [[(sb.alloc("Et", [128, 12, 128], BF16) if full_ else None) for _ in P2] for _ in P2]; B_Et = [[Buf("Et%d%d" % (i, k)) for k in P2] for i in P2]
        Mt = [[(sb.alloc("Mt", [128, 12, 128], BF16) if full_ else None) for _ in P2] for _ in P2]; B_Mt = [[Buf("Mt%d%d" % (i, k)) for k in P2] for i in P2]
        yo = [[(sb.alloc("yo", [128, 768], F32) if full_ else None) for _ in P2] for _ in P2]; B_yo = [[Buf("yo%d%d" % (i, k)) for k in P2] for i in P2]
        yv, B_yv = mk("yv", [128, 768], F32)
        t3, B_t3 = mk("t3", [128, 768], F32)
        yn, B_yn = mk("yn", [128, 768], F32)
        junk, B_junk = mk("junkc", [128, 768], BF16)
        st, B_st = mk("stc", [128, 4], F32)
        catT, B_catT = mk("catT", [128, 6, 128], BF16)
        tmp, B_tmp = mk("tmp", [128, 768], F32)
        hst, B_hst = mk("hst", [128, 768], F32)
        hfb = sb.alloc("hfb", [128, 768], BF16) if full_ else None; B_hfb = Buf("hfb")
        hbb, B_hbb = mk("hbb", [128, 768], BF16)
        for i in P2:
            R.add("pool", lambda i=i: nc.gpsimd.memset(hst[i][:], 0.0), [], [B_hst[i]])
            R.add("pool", lambda i=i: nc.gpsimd.memset(hbb[i][:], 0.0), [], [B_hbb[i]])
        if full_:
            R.add("pool", lambda: nc.gpsimd.memset(hfb[:], 0.0), [], [B_hfb])

        def prep(c, pp, full, banks=None):
            S0, S1, S2 = banks if banks is not None else (4 * pp, 4 * pp + 1, 4 * pp + 2)
            cs = slice(c * 128, (c + 1) * 128)
            R.add("sp", lambda: nc.sync.dma_start(out=xT[pp][:], in_=xbc_d[0:768, cs].rearrange("(c p) t -> p c t", p=128)), [B_xbc], [B_xT[pp]], dma=True)
            R.add("sp", lambda: nc.sync.dma_start(out=bcT[pp][:], in_=xbc_d[768:1280, cs].rearrange("(c p) t -> p c t", p=128)), [B_xbc], [B_bcT[pp]], dma=True)
            R.add("sp", lambda: nc.sync.dma_start(out=dtc[pp][:], in_=dt_d[cs, :]), [B_dt], [B_dtc[pp]], dma=True)
            if full:
                R.add("sp", lambda: nc.sync.dma_start(out=zc[pp][:], in_=zs_d[cs, :]), [B_zs], [B_zc[pp]], dma=True)
                R.add("sp", lambda: nc.sync.dma_start(out=hbin[pp][:], in_=hb_d[c]), [B_hb], [B_hbin[pp]], dma=True)
            for i in range(4):
                R.add("pe", lambda i=i: nc.tensor.transpose(PS[S0][:, i * 128:(i + 1) * 128], xT[pp][:, i, :], c_t[:, IDENT, :]),
                      [B_xT[pp], B_c], [PB[S0]], part=(i > 0))
            R.add("act", lambda: nc.scalar.activation(out=xs[pp][:, 0:512], in_=PS[S0][:, 0:512], func=AF.Copy), [PB[S0]], [B_xs[pp]])
            for i in range(4, 6):
                R.add("pe", lambda i=i: nc.tensor.transpose(PS[S1][:, (i - 4) * 128:(i - 3) * 128], xT[pp][:, i, :], c_t[:, IDENT, :]),
                      [B_xT[pp], B_c], [PB[S1]], part=(i > 4))
            for i in range(2):
                R.add("pe", lambda i=i: nc.tensor.transpose(PS[S1][:, 256 + i * 128:384 + i * 128], bcT[pp][:, i, :], c_t[:, IDENT, :]),
                      [B_bcT[pp], B_c], [PB[S1]], part=True)
            R.add("act", lambda: nc.scalar.activation(out=xs[pp][:, 512:768], in_=PS[S1][:, 0:256], func=AF.Copy), [PB[S1]], [B_xs[pp]], part=True)
            R.add("dve", lambda: nc.vector.tensor_copy(out=Btm[pp][:], in_=PS[S1][:, 256:512]), [PB[S1]], [B_Btm[pp]])
            if full:
                R.add("pool", lambda: nc.gpsimd.tensor_copy(out=bcb[pp][:], in_=bcT[pp][:]), [B_bcT[pp]], [B_bcb[pp]])
            R.add("dve", lambda: nc.vector.tensor_tensor(out=av[pp][:], in0=dtc[pp][:], in1=Abc[:], op=ALU.mult), [B_dtc[pp], B_Abc], [B_av[pp]])
            R.add("pe", lambda: nc.tensor.matmul(PS[S2][:, 0:12], lhsT=cb_t[:, UIN, :], rhs=av[pp][:, 0:12], start=True, stop=True), [B_cb, B_av[pp]], [PB[S2]])
            R.add("pe", lambda: nc.tensor.matmul(PS[S2][:, 12:24], lhsT=cb_t[:, UTR, :], rhs=av[pp][:, 12:24], start=True, stop=True), [B_cb, B_av[pp]], [PB[S2]], part=True)
            R.add("pe", lambda: nc.tensor.matmul(PS[S2][:, 24:48], lhsT=cb_t[:, ONES, :], rhs=av[pp][:, 0:24], start=True, stop=True), [B_cb, B_av[pp]], [PB[S2]], part=True)
            R.add("dve", lambda: nc.vector.tensor_copy(out=ct[pp][:, 24:72], in_=PS[S2][:, 0:48]), [PB[S2]], [B_ct[pp]])
            R.add("dve", lambda: nc.vector.tensor_tensor(out=ct[pp][:, 0:24], in0=ct[pp][:, 48:72], in1=ct[pp][:, 24:48], op=ALU.subtract), [B_ct[pp]], [B_ct[pp]], part=True)
            R.add("act", lambda: nc.scalar.activation(out=ex[pp][:], in_=ct[pp][:], func=AF.Exp), [B_ct[pp]], [B_ex[pp]])
            if full:
                R.add("dve", lambda: nc.vector.tensor_scalar(out=ncum[pp][:], in0=ct[pp][:, 24:48], scalar1=-1.0, scalar2=None, op0=ALU.mult), [B_ct[pp]], [B_ncum[pp]])
            R.add("dve", lambda: nc.vector.tensor_tensor(out=dtd[pp][:], in0=dtc[pp][:], in1=ex[pp][:, 0:24], op=ALU.mult), [B_dtc[pp], B_ex[pp]], [B_dtd[pp]])
            xs3 = v3(xs[pp][:, 0:768], 12, 64)
            R.add("dve", lambda: nc.vector.tensor_tensor(out=v3(xw[pp][1][:, 0:768], 12, 64), in0=xs3, in1=bc12(dtd[pp][:, 12:24]), op=ALU.mult),
                  [B_xs[pp], B_dtd[pp]], [B_xw[pp][1]])
            if full:
                R.add("pool", lambda: nc.gpsimd.tensor_tensor(out=v3(xw[pp][0][:, 0:768], 12, 64), in0=xs3, in1=bc12(dtd[pp][:, 0:12]), op=ALU.mult),
                      [B_xs[pp], B_dtd[pp]], [B_xw[pp][0]])
                R.add("dve", lambda: nc.vector.tensor_tensor(out=v3(xw[pp][2][:, 0:768], 12, 64), in0=xs3, in1=bc12(dtc[pp][:, 0:12]), op=ALU.mult),
                      [B_xs[pp], B_dtc[pp]], [B_xw[pp][2]])
                R.add("pool", lambda: nc.gpsimd.tensor_tensor(out=v3(xw[pp][3][:, 0:768], 12, 64), in0=xs3, in1=bc12(dtc[pp][:, 12:24]), op=ALU.mult),
                      [B_xs[pp], B_dtc[pp]], [B_xw[pp][3]])

        def state_update(pp, d, bks, outb, B_outb):
            R.add("dve", lambda: nc.vector.tensor_tensor(out=v3(tmp[pp][:, 0:768], 12, 64), in0=v3(hst[d][:, 0:768], 12, 64),
                                                         in1=bc12(ex[pp][:, 48 + d * 12:60 + d * 12]), op=ALU.mult), [B_hst[d], B_ex[pp]], [B_tmp[pp]])
            for g in range(2):
                R.add("pe", lambda g=g: nc.tensor.matmul(PS[bks[g]][:, 0:384], lhsT=Btm[pp][:, g * 128:(g + 1) * 128], rhs=xw[pp][d][:, g * 384:(g + 1) * 384],
                                                        start=True, stop=True), [B_Btm[pp], B_xw[pp][d]], [PB[bks[g]]])
                R.add("dve", lambda g=g: nc.vector.tensor_tensor(out=hst[d][:, g * 384:(g + 1) * 384], in0=PS[bks[g]][:, 0:384], in1=tmp[pp][:, g * 384:(g + 1) * 384], op=ALU.add),
                      [PB[bks[g]], B_tmp[pp]], [B_hst[d]], part=(g > 0))
            R.add("act", lambda: nc.scalar.activation(out=outb[:], in_=hst[d][:], func=AF.Copy), [B_hst[d]], [B_outb])

        def capture(fn):
            lst = []
            R.capture = lst
            marks = fn()
            R.capture = None
            return lst, marks

        if not full_:
            order_b = [33, 32] + list(range(31, -1, -1))
            lists = []
            for n_, c in enumerate(order_b):
                pp = n_ % 2

                def body(c=c, pp=pp):
                    prep(c, pp, False, banks=(7, 3, 7))
                    need = len(R.capture)
                    R.add("sp", lambda: nc.sync.dma_start(out=hb_d[c], in_=hbb[pp][:]), [B_hbb[pp]], [B_hb], dma=B_hbb[pp], part=True)
                    state_update(pp, 1, (3, 7), hbb[1 - pp], B_hbb[1 - pp])
                    return need, len(R.capture)
                ops, (need, done) = capture(body)
                lists.append((ops, need, done))
            flat = []
            for ops, _, _ in lists:
                flat.extend(ops)
            return flat

        order_f = [32, 33] + list(range(32))
        lists = []
        for n_, c in enumerate(order_f):
            pp = n_ % 2

            def body(c=c, pp=pp):
                S0, S1, S2, S3 = 4 * pp, 4 * pp + 1, 4 * pp + 2, 4 * pp + 3
                prep(c, pp, True)
                need = len(R.capture)
                k_ = 0
                for dr in range(2):
                    hin, B_hin = (hfb, B_hfb) if dr == 0 else (hbin[pp], B_hbin[pp])
                    for g in range(2):
                        bk = S3 if k_ % 2 == 0 else S2
                        k_ += 1
                        R.add("pe", lambda g=g, bk=bk, hin=hin: nc.tensor.matmul(PS[bk][:, 0:384], lhsT=bcb[pp][:, 2 + g, :], rhs=hin[:, g * 384:(g + 1) * 384],
                                                                              start=True, stop=True), [B_bcb[pp], B_hin], [PB[bk]])
                        R.add("dve", lambda g=g, bk=bk, dr=dr: nc.vector.tensor_tensor(
                            out=v3(yo[pp][dr][:, g * 384:(g + 1) * 384], 6, 64), in0=v3(PS[bk][:, 0:384], 6, 64),
                            in1=bc12(ex[pp][:, 24 + dr * 12 + g * 6:24 + dr * 12 + g * 6 + 6]), op=ALU.mult), [PB[bk], B_ex[pp]], [B_yo[pp][dr]], part=(g > 0))
                state_update(pp, 0, (S3, S2), hfb, B_hfb)
                done = len(R.capture)
                for g in range(2):
                    R.add("pe", lambda g=g: nc.tensor.matmul(PS[S2][:, g * 128:(g + 1) * 128], lhsT=bcb[pp][:, g, :], rhs=bcb[pp][:, 2 + g, :], start=True, stop=True),
                          [B_bcb[pp]], [PB[S2]], part=(g > 0))
                R.add("dve", lambda: nc.vector.tensor_copy(out=cbs[pp][:].rearrange("p a b -> p (a b)"), in_=PS[S2][:, 0:256]), [PB[S2]], [B_cbs[pp]])
                k_ = 0
                for dr in range(2):
                    um = UIN if dr == 0 else UTR
                    de, deo = ("pool", nc.gpsimd) if dr == 0 else ("dve", nc.vector)
                    R.add(de, lambda dr=dr, um=um, deo=deo: deo.tensor_tensor(
                        out=Dm[pp][dr][:], in0=fap(cb_t[:, um, :], [[0, 12], [1, 128]]), in1=bc12(av[pp][:, dr * 12:dr * 12 + 12], 128), op=ALU.mult),
                        [B_cb, B_av[pp]], [B_Dm[pp][dr]])
                    for q in range(3):
                        bk = S3 if k_ % 2 == 0 else S2
                        k_ += 1
                        R.add("pe", lambda q=q, dr=dr, bk=bk: nc.tensor.matmul(PS[bk][:, :], lhsT=cb_t[:, ONES, :], rhs=Dm[pp][dr][:, q * 4:(q + 1) * 4, :].rearrange("p a b -> p (a b)"),
                                                                             start=True, stop=False), [B_cb, B_Dm[pp][dr]], [PB[bk]])
                        R.add("pe", lambda q=q, dr=dr, bk=bk: nc.tensor.matmul(PS[bk][:, :], lhsT=cb_t[:, IDENT, :], rhs=mrep[:, dr, :, :].rearrange("p a b -> p (a b)"),
                                                                             start=False, stop=True), [B_cb, B_mrep], [PB[bk]], part=True)
                        for hh in range(4):
                            h = q * 4 + hh
                            R.add("act", lambda h=h, hh=hh, dr=dr, bk=bk: nc.scalar.activation(
                                out=Et[pp][dr][:, h, :], in_=PS[bk][:, hh * 128:(hh + 1) * 128], func=AF.Exp,
                                bias=ncum[pp][:, dr * 12 + h:dr * 12 + h + 1], scale=1.0), [PB[bk], B_ncum[pp]], [B_Et[pp][dr]], part=(h > 0))
                    for g in range(2):
                        R.add("dve", lambda g=g, dr=dr: nc.vector.tensor_tensor(out=Mt[pp][dr][:, g * 6:(g + 1) * 6, :], in0=Et[pp][dr][:, g * 6:(g + 1) * 6, :],
                                                                              in1=fap(cbs[pp][:, g, :], [[0, 6], [1, 128]]), op=ALU.mult),
                              [B_Et[pp][dr], B_cbs[pp]], [B_Mt[pp][dr]], part=(g > 0))
                for h in range(12):
                    bk, col = (S0, h * 64) if h < 8 else (S1, (h - 8) * 64)
                    for dr in range(2):
                        R.add("pe", lambda h=h, dr=dr, bk=bk, col=col: nc.tensor.matmul(
                            PS[bk][:, col:col + 64], lhsT=Mt[pp][dr][:, h, :], rhs=xw[pp][2 + dr][:, h * 64:(h + 1) * 64], start=(dr == 0), stop=(dr == 1)),
                            [B_Mt[pp][dr], B_xw[pp][2 + dr]], [PB[bk]], part=not (dr == 0 and h in (0, 8)))
                skip_out = last and c >= 32
                if not skip_out:
                    R.add("dve", lambda: nc.vector.tensor_tensor(out=yv[pp][:, 0:512], in0=PS[S0][:, 0:512], in1=yo[pp][0][:, 0:512], op=ALU.add), [PB[S0], B_yo[pp][0]], [B_yv[pp]])
                    R.add("dve", lambda: nc.vector.tensor_tensor(out=yv[pp][:, 512:768], in0=PS[S1][:, 0:256], in1=yo[pp][0][:, 512:768], op=ALU.add),
                          [PB[S1], B_yo[pp][0]], [B_yv[pp]], part=True)
                    R.add("dve", lambda: nc.vector.tensor_tensor(out=yv[pp][:], in0=yv[pp][:], in1=yo[pp][1][:], op=ALU.add), [B_yv[pp], B_yo[pp][1]], [B_yv[pp]])
                    R.add("pool", lambda: nc.gpsimd.tensor_tensor(out=v3(t3[pp][:, 0:768], 12, 64), in0=v3(xs[pp][:, 0:768], 12, 64), in1=bc12(Dbc[:, 0:12]), op=ALU.mult),
                          [B_xs[pp], B_Dbc], [B_t3[pp]])
                    R.add("dve", lambda: nc.vector.tensor_tensor(out=yv[pp][:], in0=yv[pp][:], in1=t3[pp][:], op=ALU.add), [B_yv[pp], B_t3[pp]], [B_yv[pp]])
                    R.add("pool", lambda: nc.gpsimd.tensor_tensor(out=yv[pp][:], in0=yv[pp][:], in1=zc[pp][:], op=ALU.mult), [B_yv[pp], B_zc[pp]], [B_yv[pp]])
                    R.add("act", lambda: nc.scalar.activation(out=junk[pp][:], in_=yv[pp][:], func=AF.Square, accum_out=st[pp][:, 0:1]), [B_yv[pp]], [B_junk[pp], B_st[pp]])
                    R.add("act", lambda: nc.scalar.activation(out=st[pp][:, 1:2], in_=st[pp][:, 0:1], func=AF.Ln, bias=eps_t[:, 0:1], scale=1.0 / 768),
                          [B_st[pp], B_eps], [B_st[pp]], part=True)
                    R.add("act", lambda: nc.scalar.activation(out=st[pp][:, 2:3], in_=st[pp][:, 1:2], func=AF.Exp, scale=-0.5), [B_st[pp]], [B_st[pp]], part=True)
                    R.add("dve", lambda: nc.vector.scalar_tensor_tensor(out=yn[pp][:], in0=yv[pp][:], scalar=st[pp][:, 2:3], in1=Gbc[:], op0=ALU.mult, op1=ALU.mult),
                          [B_yv[pp], B_st[pp], B_Gbc], [B_yn[pp]])
                    for i in range(6):
                        bk, col = (S2, i * 128) if i < 4 else (S3, (i - 4) * 128)
                        R.add("pe", lambda i=i, bk=bk, col=col: nc.tensor.transpose(PS[bk][:, col:col + 128], yn[pp][:, i * 128:(i + 1) * 128], c_t[:, IDENT, :]),
                              [B_yn[pp], B_c], [PB[bk]], part=(i not in (0, 4)))
                    R.add("act", lambda: nc.scalar.activation(out=catT[pp][:, 0:4, :].rearrange("p a b -> p (a b)"), in_=PS[S2][:, 0:512], func=AF.Copy), [PB[S2]], [B_catT[pp]])
                    R.add("act", lambda: nc.scalar.activation(out=catT[pp][:, 4:6, :].rearrange("p a b -> p (a b)"), in_=PS[S3][:, 0:256], func=AF.Copy), [PB[S3]], [B_catT[pp]], part=True)
                    R.add("sp", lambda: nc.sync.dma_start(out=cat_d[0:768, c * 128:(c + 1) * 128].rearrange("(c p) t -> p c t", p=128), in_=catT[pp][:]),
                          [B_catT[pp]], [B_cat], dma=B_catT[pp], part=True)
                return need, done
            ops, (need, done) = capture(body)
            lists.append((ops, need, done))
        R.interleave(lists)
        R.barrier()
        R.release(B_xT + B_bcT + B_dtc + B_zc + B_hbin + B_hbb + B_catT + [B_Abc, B_Dbc, B_Gbc])


    def phaseD(li, last):
        sb = SB(nc, WORK0)
        vec = sb.alloc("vecD", [128, 16], F32); B_vec = Buf("vecD")
        R.add("sp", lambda: nc.sync.dma_start(out=vec[:], in_=cmv[li].rearrange("p a b -> p (a b)")), [B_in], [B_vec], dma=True)
        pws = sb.alloc("pws", [128, 4, 512], F32); B_pws = Buf("pws")
        pwb = sb.alloc("pwb", [128, 4, 512], BF16); B_pwb = Buf("pwb")
        R.add("sp", lambda: nc.sync.dma_start(out=pws[:], in_=cpw[li].rearrange("(c p) n -> p c n", p=128)), [B_in], [B_pws], dma=True)
        R.add("pool", lambda: nc.gpsimd.tensor_copy(out=pwb[:], in_=pws[:]), [B_pws], [B_pwb])
        P2 = range(2)
        cvt = [sb.alloc("cvt", [128, 4, 512], F32) for _ in P2]; B_cvt = [Buf("cvt%d" % i) for i in P2]
        gct = [sb.alloc("gct", [128, 4, 512], F32) for _ in P2]; B_gct = [Buf("gct%d" % i) for i in P2]
        def mk2(name, shape, dt):
            return [sb.alloc(name, shape, dt) for _ in P2], [Buf("%s%d" % (name, i)) for i in P2]
        sq, B_sq = mk2("sq", [128, 4, 512], F32)
        mean, B_mean = mk2("mean", [128, 512], F32)
        m2, B_m2 = mk2("m2", [128, 512], F32)
        var, B_var = mk2("var", [128, 512], F32)
        sdv, B_sdv = mk2("sdv", [128, 512], F32)
        rstd, B_rstd = mk2("rstd", [128, 512], F32)
        xc_, B_xc = mk2("xc", [128, 512], F32)
        xn_, B_xn = mk2("xnD", [128, 512], F32)
        act, B_act = mk2("act", [128, 4, 512], BF16)
        oD = [[sb.alloc("oD", [128, 512], BF16) for _ in P2] for _ in P2]; B_oD = [[Buf("oD%d%d" % (i, k)) for k in P2] for i in P2]
        bodies = []
        js = [j for j in range(9) if not (last and j == 8)]
        for ji, j in enumerate(js):
            def body(j=j, pp=ji % 2):
                w_ = 512 if j < 8 else 256
                cs = slice(j * 512, j * 512 + w_)
                B0 = 4 * pp
                R.add("sp", lambda: nc.sync.dma_start(out=cvt[pp][:, :, 0:w_], in_=cv_d[:, cs].rearrange("(c p) t -> p c t", p=128)), [B_cv], [B_cvt[pp]], dma=True)
                R.add("sp", lambda: nc.sync.dma_start(out=gct[pp][:, :, 0:w_], in_=gcs_d[:, cs].rearrange("(c p) t -> p c t", p=128)), [B_gcs], [B_gct[pp]], dma=True)
                R.add("act", lambda: nc.scalar.activation(out=sq[pp][:, :, 0:w_], in_=cvt[pp][:, :, 0:w_], func=AF.Square), [B_cvt[pp]], [B_sq[pp]])
                for c in range(4):
                    R.add("pe", lambda c=c: nc.tensor.matmul(PS[B0][:, 0:w_], lhsT=c_t[:, ONES, :], rhs=cvt[pp][:, c, 0:w_], start=(c == 0), stop=(c == 3)),
                          [B_c, B_cvt[pp]], [PB[B0]], part=(c > 0))
                for c in range(4):
                    R.add("pe", lambda c=c: nc.tensor.matmul(PS[B0 + 1][:, 0:w_], lhsT=c_t[:, ONES, :], rhs=sq[pp][:, c, 0:w_], start=(c == 0), stop=(c == 3)),
                          [B_c, B_sq[pp]], [PB[B0 + 1]], part=(c > 0))
                R.add("dve", lambda: nc.vector.tensor_scalar(out=mean[pp][:, 0:w_], in0=PS[B0][:, 0:w_], scalar1=1.0 / 512, scalar2=None, op0=ALU.mult), [PB[B0]], [B_mean[pp]])
                R.add("dve", lambda: nc.vector.tensor_tensor(out=m2[pp][:, 0:w_], in0=mean[pp][:, 0:w_], in1=mean[pp][:, 0:w_], op=ALU.mult), [B_mean[pp]], [B_m2[pp]])
                R.add("dve", lambda: nc.vector.scalar_tensor_tensor(out=var[pp][:, 0:w_], in0=PS[B0 + 1][:, 0:w_], scalar=1.0 / 512, in1=m2[pp][:, 0:w_], op0=ALU.mult, op1=ALU.subtract),
                      [PB[B0 + 1], B_m2[pp]], [B_var[pp]])
                R.add("act", lambda: nc.scalar.activation(out=sdv[pp][:, 0:w_], in_=var[pp][:, 0:w_], func=AF.Ln, bias=eps_t[:, 0:1], scale=1.0), [B_var[pp], B_eps], [B_sdv[pp]])
                R.add("act", lambda: nc.scalar.activation(out=rstd[pp][:, 0:w_], in_=sdv[pp][:, 0:w_], func=AF.Exp, scale=-0.5), [B_sdv[pp]], [B_rstd[pp]])
                for c in range(4):
                    R.add("dve", lambda c=c: nc.vector.tensor_tensor(out=xc_[pp][:, 0:w_], in0=cvt[pp][:, c, 0:w_], in1=mean[pp][:, 0:w_], op=ALU.subtract),
                          [B_cvt[pp], B_mean[pp]], [B_xc[pp]])
                    R.add("pool", lambda: nc.gpsimd.tensor_tensor(out=xn_[pp][:, 0:w_], in0=xc_[pp][:, 0:w_], in1=rstd[pp][:, 0:w_], op=ALU.mult), [B_xc[pp], B_rstd[pp]], [B_xn[pp]])
                    R.add("act", lambda c=c: nc.scalar.activation(out=act[pp][:, c, 0:w_], in_=xn_[pp][:, 0:w_], func=AF.Silu,
                                                                  bias=vec[:, c * 4 + 2:c * 4 + 3], scale=vec[:, c * 4 + 1:c * 4 + 2]), [B_xn[pp], B_vec], [B_act[pp]], part=(c > 0))
                for oc in range(4):
                    bk = B0 + 2 + oc % 2
                    for c in range(4):
                        R.add("pe", lambda oc=oc, c=c, bk=bk: nc.tensor.matmul(PS[bk][:, 0:w_], lhsT=pwb[:, c, oc * 128:(oc + 1) * 128], rhs=act[pp][:, c, 0:w_],
                                                                             start=(c == 0), stop=(c == 3)), [B_pwb, B_act[pp]], [PB[bk]], part=(c > 0))
                    o_ = oc % 2
                    R.add("dve", lambda oc=oc, bk=bk, o_=o_: nc.vector.scalar_tensor_tensor(
                        out=oD[pp][o_][:, 0:w_], in0=PS[bk][:, 0:w_], scalar=vec[:, oc * 4 + 3:oc * 4 + 4], in1=gct[pp][:, oc, 0:w_], op0=ALU.add, op1=ALU.mult),
                        [PB[bk], B_vec, B_gct[pp]], [B_oD[pp][o_]])
                    R.add("sp", lambda oc=oc, o_=o_: nc.sync.dma_start(out=cat_d[1536 + oc * 128:1536 + (oc + 1) * 128, cs], in_=oD[pp][o_][:, 0:w_]),
                          [B_oD[pp][o_]], [B_cat], dma=B_oD[pp][o_], part=True)
            bodies.append(body)
        R.pipeline(bodies)
        R.barrier()
        R.release([B_vec, B_pws] + B_cvt + B_gct + B_oD[0] + B_oD[1])

    def phaseE(li, last):
        sb = SB(nc, WORK0)
        wos = sb.alloc("wos", [128, 4, D], F32); B_wos = Buf("wos")
        wo = sb.alloc("wo", [128, 16, D], BF16); B_wo = Buf("wo")
        for q in range(4):
            R.add("sp", lambda q=q: nc.sync.dma_start(out=wos[:], in_=w_out[li, q * 512:(q + 1) * 512, :].rearrange("(c p) n -> p c n", p=128)), [B_in], [B_wos], dma=True)
            R.add("pool", lambda q=q: nc.gpsimd.tensor_copy(out=wo[:, q * 4:(q + 1) * 4, :], in_=wos[:]), [B_wos], [B_wo], part=(q > 0))
        gB = sb.alloc("gB", [128, 2, D], F32); B_gB = Buf("gB")
        for j in range(2):
            R.add("sp", lambda j=j: nc.sync.dma_start(out=gB[:, j, :], in_=gate_d[li, j]), [B_gate], [B_gB], dma=True, part=(j > 0))
        fg = sb.alloc("fg", [128, D], F32); B_fg = Buf("fg")
        R.add("sp", lambda: nc.sync.dma_start(out=fg[:], in_=bass.AP(fng.ap().tensor, 0, [[0, 128], [1, D]])), [B_in], [B_fg], dma=True)
        P2 = range(2)
        ct_ = [sb.alloc("ctE", [128, 16, 512], BF16) for _ in P2]; B_ct = [Buf("ctE%d" % i) for i in P2]
        xr = [sb.alloc("xr", [128, D], F32) for _ in P2]; B_xr = [Buf("xr%d" % i) for i in P2]
        ty = [sb.alloc("ty", [128, D], F32) for _ in P2]; B_ty = [Buf("ty%d" % i) for i in P2]
        xo = [sb.alloc("xo", [128, D], F32) for _ in P2]; B_xo = [Buf("xo%d" % i) for i in P2]
        junk = sb.alloc("junkE", [128, D], BF16); B_junk = Buf("junkE")
        st = [sb.alloc("stE", [128, 4], F32) for _ in P2]; B_st = [Buf("stE%d" % i) for i in P2]
        fo = [sb.alloc("fo", [128, D], F32) for _ in P2]; B_fo = [Buf("fo%d" % i) for i in P2]
        bodies = []
        js = [j for j in range(9) if not (last and j == 8)]

        def ct_load(j):
            w2 = 512 if j < 8 else 256
            R.add("sp", lambda: nc.sync.dma_start(out=ct_[j % 2][:, :, 0:w2], in_=cat_d[:, j * 512:j * 512 + w2].rearrange("(c p) t -> p c t", p=128)),
                  [B_cat], [B_ct[j % 2]], dma=True)
        for ji, j in enumerate(js):
            w_ = 512 if j < 8 else 256
            jp = j % 2
            for q in range(w_ // 128):
                def body(j=j, w_=w_, jp=jp, q=q, ji=ji):
                    if q == 0 and ji == 0:
                        ct_load(j)
                    if q == 1 and ji + 1 < len(js):
                        ct_load(js[ji + 1])
                    tt = j * 4 + q
                    pp = tt % 2
                    jj = 0 if tt < 32 else 1
                    if li == 0:
                        src = x_in[tt * 128:(tt + 1) * 128, :] if tt < 32 else ctx_in[(tt - 32) * 128:(tt - 31) * 128, :]
                        srcb = B_in
                    else:
                        src = x1_d[tt * 128:(tt + 1) * 128, :]
                        srcb = B_x1
                    R.add("sp", lambda: nc.sync.dma_start(out=xr[pp][:], in_=src), [srcb], [B_xr[pp]], dma=True)
                    for hf_ in range(2):
                        bk = 2 * pp + hf_
                        for c in range(16):
                            R.add("pe", lambda c=c, hf_=hf_, bk=bk: nc.tensor.matmul(
                                PS[bk][:, :], lhsT=ct_[jp][:, c, q * 128:(q + 1) * 128], rhs=wo[:, c, hf_ * 512:(hf_ + 1) * 512], start=(c == 0), stop=(c == 15)),
                                [B_ct[jp], B_wo], [PB[bk]], part=(c > 0))
                        R.add("dve", lambda hf_=hf_, bk=bk: nc.vector.tensor_tensor(
                            out=ty[pp][:, hf_ * 512:(hf_ + 1) * 512], in0=PS[bk][:, :], in1=gB[:, jj, hf_ * 512:(hf_ + 1) * 512], op=ALU.mult),
                            [PB[bk], B_gB], [B_ty[pp]], part=(hf_ > 0))
                    R.add("pool", lambda: nc.gpsimd.tensor_tensor(out=xo[pp][:], in0=ty[pp][:], in1=xr[pp][:], op=ALU.add), [B_ty[pp], B_xr[pp]], [B_xo[pp]])
                    if not last:
                        R.add("sp", lambda: nc.sync.dma_start(out=x1_d[tt * 128:(tt + 1) * 128, :], in_=xo[pp][:]), [B_xo[pp]], [B_x1], dma=B_xo[pp], part=True)
                    else:
                        R.add("act", lambda: nc.scalar.activation(out=junk[:], in_=xo[pp][:], func=AF.Square, accum_out=st[pp][:, 0:1]), [B_xo[pp]], [B_junk, B_st[pp]])
                        R.add("act", lambda: nc.scalar.activation(out=st[pp][:, 1:2], in_=st[pp][:, 0:1], func=AF.Sqrt, bias=eps_t[:, 0:1], scale=1.0 / D),
                              [B_st[pp], B_eps], [B_st[pp]], part=True)
                        R.add("dve", lambda: nc.vector.reciprocal(out=st[pp][:, 2:3], in_=st[pp][:, 1:2]), [B_st[pp]], [B_st[pp]], part=True)
                        R.add("dve", lambda: nc.vector.scalar_tensor_tensor(out=fo[pp][:], in0=xo[pp][:], scalar=st[pp][:, 2:3], in1=fg[:], op0=ALU.mult, op1=ALU.mult),
                              [B_xo[pp], B_st[pp], B_fg], [B_fo[pp]])
                        R.add("sp", lambda: nc.sync.dma_start(out=out_d[tt * 128:(tt + 1) * 128, :], in_=fo[pp][:]), [B_fo[pp]], [B_out], dma=B_fo[pp], part=True)
                bodies.append(body)
        R.pipeline(bodies)
        R.barrier()
        R.release([B_wos, B_gB, B_fg] + B_ct + B_xr + B_xo + B_fo)


    R.barrier()
    for li in range(nlayers):
        phase1(li)
        R.barrier()
        if phases >= 2:
            phaseA(li)
        last = (li == DEPTH - 1) and nlayers == DEPTH
        side = None
        if phases >= 4 and "skipC" not in taps:
            side = phaseC(li, last, "p1")
        if phases >= 3 and "skipB" not in taps:
            phaseB(li, last, side)
        else:
            R.side = list(side or [])
            R.flush_side()
            R.barrier()
        if phases >= 4 and "skipC" not in taps:
            phaseC(li, last, "p2")
        if phases >= 5:
            phaseD(li, last)
        if phases >= 6:
            phaseE(li, last)
    dbg_fence = []
    if "hT" in taps:
        hT_dbg = nc.dram_tensor("hT_dbg", [128, 8, T], BF16, kind="ExternalOutput")
        B_dbg = Buf("dbg")
        R.add("sp", lambda: nc.sync.dma_start(out=hT_dbg.ap(), in_=hT[:]), B_hT, [B_dbg], dma=True)
        dbg_fence.append(B_dbg)
    R.barrier()
    R.emit()
    return dram_in


def host_inputs(inp, b, r=0):
    f = np.float32
    fm = lambda v, nch: np.ascontiguousarray(np.asarray(v, f).reshape(nch, 128).T)
    m = {}
    m["x"] = np.ascontiguousarray(inp["x"][b], f)
    m["ctx"] = np.ascontiguousarray(inp["ctx"][b], f)
    m["cvec"] = np.ascontiguousarray(np.stack([fm(inp["c"][b], 8), fm(inp["c_ctx"], 8)], axis=-1))
    m["ada_w"] = np.ascontiguousarray(inp["ada_w"], f)
    m["ada_b_fm"] = np.stack([fm(inp["ada_b"][l], 24) for l in range(DEPTH)])
    m["ada_b_gate"] = np.ascontiguousarray(inp["ada_b"][:, 2048:3072], f)
    m["norm_g_fm"] = np.stack([fm(inp["norm_g"][l], 8) for l in range(DEPTH)])
    w = np.array(inp["w_in"], f)
    for off, width in ((O_Q, 768), (O_K, 256), (O_V, 256), (O_GA, 768)):
        h = width // 2
        own = w[:, :, off + r * h:off + (r + 1) * h].copy()
        oth = w[:, :, off + (1 - r) * h:off + (2 - r) * h].copy()
        w[:, :, off:off + h] = own
        w[:, :, off + h:off + width] = oth
    m["w_in"] = np.ascontiguousarray(w)
    m["ssd_conv_w_fm"] = np.ascontiguousarray(
        np.asarray(inp["ssd_conv_w"], f).reshape(DEPTH, 5, 10, 128).transpose(0, 3, 2, 1))
    m["ssd_conv_b_fm"] = np.stack([fm(inp["ssd_conv_b"][l], 10) for l in range(DEPTH)])
    m["ssd_dt_bias"] = np.ascontiguousarray(inp["ssd_dt_bias"], f)
    m["ssd_a_log"] = np.ascontiguousarray(inp["ssd_a_log"], f)
    m["ssd_d"] = np.ascontiguousarray(inp["ssd_d"], f)
    m["ssd_norm_g"] = np.ascontiguousarray(inp["ssd_norm_g"], f)
    qg = np.asarray(inp["q_norm_g"], f)
    kg = np.asarray(inp["k_norm_g"], f)
    m["qk_g_fm"] = np.ascontiguousarray(np.stack([np.tile(qg, (1, 2)), np.tile(kg, (1, 2))], axis=-1))
    m["cm_conv_w_fm"] = np.ascontiguousarray(
        np.asarray(inp["cm_conv_w"], f).reshape(DEPTH, 31, 4, 128).transpose(0, 3, 2, 1))
    m["cm_vec_fm"] = np.ascontiguousarray(np.stack(
        [np.stack([fm(inp[k][l], 4) for k in ("cm_conv_b", "cm_ln_g", "cm_ln_b", "cm_pw_b")], axis=-1)
         for l in range(DEPTH)]))
    m["cm_pw_w"] = np.ascontiguousarray(inp["cm_pw_w"], f)
    m["w_out"] = np.ascontiguousarray(inp["w_out"], f)
    m["final_norm_g"] = np.ascontiguousarray(inp["final_norm_g"], f)
    m.update(const_inputs())
    return m


_CONST = None


def const_inputs():
    global _CONST
    if _CONST is not None:
        return _CONST
    f = np.float32
    i = np.arange(128)
    ident = np.eye(128, dtype=f)
    U = (i[:, None] <= i[None, :]).astype(f)
    UT = (i[:, None] >= i[None, :]).astype(f)
    d = i % 64
    half = (d % 32) // 16
    partner = np.where(half == 0, i + 16, i - 16)
    perm = np.zeros((128, 128), f)
    perm[partner, i] = 1.0
    bd = ((i[:, None] // 64) == (i[None, :] // 64)).astype(f) / 64.0
    ones = np.ones((128, 128), f)
    consts = np.stack([ident, U, UT, perm, bd, ones], axis=1)
    mf = np.where(i[:, None] <= i[None, :], 0.0, NEG).astype(f)
    mb = np.where(i[:, None] >= i[None, :], 0.0, NEG).astype(f)
    masks = np.stack([mf, mb], axis=1)
    u = np.arange(SEQ)
    pos = np.stack([u // 64, u % 64], axis=0).astype(np.float64)
    inv = 10000.0 ** (-np.arange(16, dtype=np.float64) / 16)
    fq = d % 16
    axis = d // 32
    ang = pos[axis][:, :] * inv[fq][:, None]
    ang = (pos[axis].astype(f) * inv.astype(f)[fq][:, None]).astype(f)
    cos = np.cos(ang).astype(f)
    sin = np.sin(ang).astype(f)
    sgn = np.where(half == 0, -1.0, 1.0).astype(f)[:, None]
    COS = np.concatenate([cos, np.ones((128, NCTX), f)], axis=1)
    SIN = np.concatenate([sin * sgn, np.zeros((128, NCTX), f)], axis=1)
    rope = np.stack([COS, SIN], axis=1)
    _CONST = {"consts": np.ascontiguousarray(consts), "masks": np.ascontiguousarray(masks),
              "rope": np.ascontiguousarray(rope)}
    return _CONST


def kernel(**inputs):
    inputs = {k: np.asarray(v) for k, v in inputs.items()}
    nc = bass.Bass("TRN2", target_bir_lowering=False)
    build(nc)
    in_maps = [host_inputs(inputs, c // 2, c % 2) for c in range(8)]
    res = run_bass_kernel_spmd(nc, in_maps, core_ids=list(range(8)))
    return np.stack([res.results[2 * b]["out"] for b in range(4)], axis=0).astype(np.float32)
```
